# Optimizing a Trainium2 kernel written in Bass

```python
import math
import jax
import jax.numpy as jnp
from jax import lax
import numpy as np

D_MODEL = 1024
BATCH = 4
SEQ = 8192
DEPTH = 2

HEAD_DIM = 64
Q_BLOCK = 128
RMS_EPS = 1e-6
D_FF = 4 * D_MODEL
NEG_INF = -1e30
FORCE_SCORE = 1e6

SB_HEADS = D_MODEL // (2 * HEAD_DIM)
SB_WIDTH = SB_HEADS * HEAD_DIM
DIFF_HEADS = D_MODEL // (4 * HEAD_DIM)
DIFF_WIDTH = DIFF_HEADS * 2 * HEAD_DIM
EVEN_IN = 3 * SB_WIDTH + 3 * DIFF_WIDTH
EVEN_OUT = SB_WIDTH + DIFF_WIDTH

NSA_HEADS = D_MODEL // HEAD_DIM
NSA_GROUPS = 4
NSA_HPG = NSA_HEADS // NSA_GROUPS
NSA_Q_WIDTH = NSA_HEADS * HEAD_DIM
NSA_KV_WIDTH = NSA_GROUPS * HEAD_DIM
CMP_LEN = 32
CMP_STRIDE = 16
CMP_HIDDEN = 4 * HEAD_DIM
SEL_LEN = 64
SEL_TOPK = 16
WINDOW = 512
N_GATES = 3
ODD_IN = NSA_Q_WIDTH + 6 * NSA_KV_WIDTH + N_GATES * NSA_HEADS

N_EVEN = (DEPTH + 1) // 2
N_ODD = DEPTH // 2

kernel_name = "hybrid_sb_diff_nsa_trunk"


def rmsnorm(x, g):
    x32 = x.astype(jnp.float32)
    y = x32 * lax.rsqrt(jnp.mean(x32 * x32, axis=-1, keepdims=True) + RMS_EPS)
    return (y * g.astype(jnp.float32)).astype(x.dtype)


def alibi_slopes(n_heads):
    return jnp.exp2(-8.0 * (jnp.arange(n_heads, dtype=jnp.float32) + 1.0) / n_heads)


def sweep_query_blocks(block_fn, seq):
    out = lax.map(block_fn, jnp.arange(seq // Q_BLOCK))
    out = jnp.moveaxis(out, 0, 1)
    return out.reshape((out.shape[0], seq) + out.shape[3:])


def stick_breaking_attention(q, k, v):
    S = q.shape[1]
    scale = q.shape[-1] ** -0.5
    kpos = jnp.arange(S)

    def block(i):
        t0 = i * Q_BLOCK
        qb = lax.dynamic_slice_in_dim(q, t0, Q_BLOCK, axis=1)
        tpos = t0 + jnp.arange(Q_BLOCK)
        past = kpos[None, :] < tpos[:, None]
        z = jnp.einsum("bqhd,bkhd->bhqk", qb, k).astype(jnp.float32) * scale
        log_beta = jax.nn.log_sigmoid(z)
        log_rem = jnp.where(past, jax.nn.log_sigmoid(-z), 0.0)
        tail = lax.cumsum(log_rem, axis=3, reverse=True) - log_rem
        w = jnp.where(past, jnp.exp(log_beta + tail), 0.0)
        return jnp.einsum("bhqk,bkhd->bqhd", w.astype(v.dtype), v)

    return sweep_query_blocks(block, S)


def differential_attention(q, k, v, lam, slopes):
    S = q.shape[1]
    scale = q.shape[-1] ** -0.5
    kpos = jnp.arange(S)

    def block(i):
        t0 = i * Q_BLOCK
        qb = lax.dynamic_slice_in_dim(q, t0, Q_BLOCK, axis=1)
        tpos = t0 + jnp.arange(Q_BLOCK)
        dist = (tpos[:, None] - kpos[None, :]).astype(jnp.float32)
        s = jnp.einsum("bqhcd,bkhcd->bhcqk", qb, k).astype(jnp.float32) * scale
        s = jnp.where(dist >= 0, s - slopes[None, :, None, None, None] * dist, NEG_INF)
        p = jax.nn.softmax(s, axis=-1)
        a = p[:, :, 0] - lam * p[:, :, 1]
        return jnp.einsum("bhqk,bkhe->bqhe", a.astype(v.dtype), v)

    return sweep_query_blocks(block, S)


def sb_diff_mixer(h, w_in, lam_q1, lam_k1, lam_q2, lam_k2, subln, w_out, layer):
    B, S, _ = h.shape
    proj = h @ w_in
    sb_q, sb_k, sb_v, df_q, df_k, df_v = jnp.split(proj, 6, axis=-1)
    sb_shape = (B, S, SB_HEADS, HEAD_DIM)
    o_sb = stick_breaking_attention(sb_q.reshape(sb_shape), sb_k.reshape(sb_shape),
                                    sb_v.reshape(sb_shape))
    lambda_init = 0.8 - 0.6 * math.exp(-0.3 * layer)
    lam = (jnp.exp(jnp.sum(lam_q1.astype(jnp.float32) * lam_k1.astype(jnp.float32)))
           - jnp.exp(jnp.sum(lam_q2.astype(jnp.float32) * lam_k2.astype(jnp.float32)))
           + lambda_init)
    qk_shape = (B, S, DIFF_HEADS, 2, HEAD_DIM)
    o_df = differential_attention(df_q.reshape(qk_shape), df_k.reshape(qk_shape),
                                  df_v.reshape(B, S, DIFF_HEADS, 2 * HEAD_DIM),
                                  lam, alibi_slopes(DIFF_HEADS))
    o_df = rmsnorm(o_df, subln) * (1.0 - lambda_init)
    o = jnp.concatenate([o_sb.reshape(B, S, SB_WIDTH), o_df.reshape(B, S, DIFF_WIDTH)], axis=-1)
    return o @ w_out


def compress(t, pos, w1, w2):
    B, S, G, Dh = t.shape
    ratio = CMP_LEN // CMP_STRIDE
    n_chunks = S // CMP_STRIDE
    n_cmp = n_chunks - ratio + 1
    chunks = t.reshape(B, n_chunks, CMP_STRIDE, G, Dh)
    blocks = jnp.concatenate([chunks[:, r:r + n_cmp] for r in range(ratio)], axis=2)
    blocks = blocks + pos[None, None, :, None, :]
    flat = jnp.moveaxis(blocks, 3, 2).reshape(B, n_cmp, G, CMP_LEN * Dh)
    return jax.nn.gelu(flat @ w1) @ w2


def nsa_attention(q, kc, vc, ks, vs, kw, vw, gates, slopes):
    B, S, H, Dh = q.shape
    G = kc.shape[2]
    hpg = H // G
    n_cmp = kc.shape[1]
    n_sel = S // SEL_LEN
    topk = min(SEL_TOPK, n_sel)
    scale = Dh ** -0.5
    slope_g = slopes.reshape(G, hpg)
    qg = q.reshape(B, S, G, hpg, Dh)
    gg = gates.reshape(B, S, G, hpg, N_GATES)
    cmp_start = jnp.arange(n_cmp) * CMP_STRIDE
    cmp_end = cmp_start + CMP_LEN - 1
    sel_ids = jnp.arange(n_sel)
    sel_start = sel_ids * SEL_LEN
    overlap = ((cmp_start[:, None] < sel_start[None, :] + SEL_LEN)
               & (sel_start[None, :] <= cmp_end[:, None])).astype(jnp.float32)
    ks_bg = jnp.moveaxis(ks.reshape(B, n_sel, SEL_LEN, G, Dh), 3, 1)
    vs_bg = jnp.moveaxis(vs.reshape(B, n_sel, SEL_LEN, G, Dh), 3, 1)
    gather_blocks = jax.vmap(jax.vmap(lambda blk, idx: blk[idx]))
    pad = ((0, 0), (WINDOW, 0), (0, 0), (0, 0))
    kw_pad = jnp.pad(kw, pad)
    vw_pad = jnp.pad(vw, pad)
    win_off = jnp.arange(Q_BLOCK + WINDOW) - WINDOW
    sel_off = jnp.arange(SEL_LEN)

    def block(i):
        t0 = i * Q_BLOCK
        tpos = t0 + jnp.arange(Q_BLOCK)
        qb = lax.dynamic_slice_in_dim(qg, t0, Q_BLOCK, axis=1)
        gb = lax.dynamic_slice_in_dim(gg, t0, Q_BLOCK, axis=1)
        dc = (tpos[:, None] - cmp_end[None, :]).astype(jnp.float32)
        valid_c = dc >= 0
        sc = jnp.einsum("bqgrd,bngd->bgrqn", qb, kc).astype(jnp.float32) * scale
        sc = jnp.where(valid_c, sc - slope_g[None, :, :, None, None] * dc, NEG_INF)
        pc = jax.nn.softmax(sc, axis=-1) * jnp.any(valid_c, axis=-1)[:, None].astype(jnp.float32)
        o_cmp = jnp.einsum("bgrqn,bngd->bqgrd", pc.astype(vc.dtype), vc)
        imp = jnp.einsum("bgrqn,ns->bgqs", pc, overlap)
        cur = tpos // SEL_LEN
        forced = ((sel_ids[None, :] == 0) | (sel_ids[None, :] == cur[:, None])
                  | (sel_ids[None, :] == cur[:, None] - 1))
        imp = jnp.where(forced, FORCE_SCORE, imp)
        imp = jnp.where(sel_ids[None, :] <= cur[:, None], imp, -1.0)
        top_val, top_idx = lax.top_k(imp, topk)
        k_sel = gather_blocks(ks_bg, top_idx)
        v_sel = gather_blocks(vs_bg, top_idx)
        d_sel = (tpos[None, None, :, None, None]
                 - (top_idx[..., None] * SEL_LEN + sel_off)).astype(jnp.float32)
        valid_s = (d_sel >= 0) & (top_val >= 0)[..., None]
        ss = jnp.einsum("bqgrd,bgqkld->bgrqkl", qb, k_sel).astype(jnp.float32) * scale
        ss = jnp.where(valid_s[:, :, None],
                       ss - slope_g[None, :, :, None, None, None] * d_sel[:, :, None], NEG_INF)
        ps = jax.nn.softmax(ss.reshape(ss.shape[:4] + (-1,)), axis=-1).reshape(ss.shape)
        o_sel = jnp.einsum("bgrqkl,bgqkld->bqgrd", ps.astype(vs.dtype), v_sel)
        kwb = lax.dynamic_slice_in_dim(kw_pad, t0, Q_BLOCK + WINDOW, axis=1)
        vwb = lax.dynamic_slice_in_dim(vw_pad, t0, Q_BLOCK + WINDOW, axis=1)
        kpos_w = t0 + win_off
        dw = tpos[:, None] - kpos_w[None, :]
        valid_w = (dw >= 0) & (dw < WINDOW) & (kpos_w[None, :] >= 0)
        sw = jnp.einsum("bqgrd,bkgd->bgrqk", qb, kwb).astype(jnp.float32) * scale
        sw = jnp.where(valid_w, sw - slope_g[None, :, :, None, None] * dw.astype(jnp.float32), NEG_INF)
        pw = jax.nn.softmax(sw, axis=-1)
        o_win = jnp.einsum("bgrqk,bkgd->bqgrd", pw.astype(vw.dtype), vwb)
        return gb[..., 0:1] * o_cmp + gb[..., 1:2] * o_sel + gb[..., 2:3] * o_win

    out = sweep_query_blocks(block, S)
    return out.reshape(B, S, H * Dh)


def nsa_mixer(h, w_in, cmp_pos_k, cmp_k_w1, cmp_k_w2, cmp_pos_v, cmp_v_w1, cmp_v_w2, w_out):
    B, S, _ = h.shape
    proj = h @ w_in
    cuts = [NSA_Q_WIDTH + j * NSA_KV_WIDTH for j in range(7)]
    q, kc, vc, ks, vs, kw, vw, g = jnp.split(proj, cuts, axis=-1)
    kv_shape = (B, S, NSA_GROUPS, HEAD_DIM)
    kc = compress(kc.reshape(kv_shape), cmp_pos_k, cmp_k_w1, cmp_k_w2)
    vc = compress(vc.reshape(kv_shape), cmp_pos_v, cmp_v_w1, cmp_v_w2)
    gates = jax.nn.sigmoid(g.astype(jnp.float32)).astype(h.dtype).reshape(B, S, NSA_HEADS, N_GATES)
    o = nsa_attention(q.reshape(B, S, NSA_HEADS, HEAD_DIM), kc, vc,
                      ks.reshape(kv_shape), vs.reshape(kv_shape),
                      kw.reshape(kv_shape), vw.reshape(kv_shape),
                      gates, alibi_slopes(NSA_HEADS))
    return o @ w_out


def squared_relu_mlp(h, w1, w2):
    return jnp.square(jax.nn.relu(h @ w1)) @ w2


def setup_inputs(seed: int = 0) -> dict:
    key = jax.random.key(seed)
    ks = jax.random.split(key, 21)

    def nrm(k, shape, scale):
        return jax.random.normal(k, shape, jnp.float32) * scale

    return {
        "x": nrm(ks[0], (BATCH, SEQ, D_MODEL), 1.0),
        "attn_norm": 1.0 + nrm(ks[1], (DEPTH, D_MODEL), 0.02),
        "mlp_norm": 1.0 + nrm(ks[2], (DEPTH, D_MODEL), 0.02),
        "final_norm": 1.0 + nrm(ks[3], (D_MODEL,), 0.02),
        "ev_w_in": nrm(ks[4], (N_EVEN, D_MODEL, EVEN_IN), D_MODEL ** -0.5),
        "ev_lam_q1": nrm(ks[5], (N_EVEN, HEAD_DIM), 0.1),
        "ev_lam_k1": nrm(ks[6], (N_EVEN, HEAD_DIM), 0.1),
        "ev_lam_q2": nrm(ks[7], (N_EVEN, HEAD_DIM), 0.1),
        "ev_lam_k2": nrm(ks[8], (N_EVEN, HEAD_DIM), 0.1),
        "ev_subln": 1.0 + nrm(ks[9], (N_EVEN, 2 * HEAD_DIM), 0.02),
        "ev_w_out": nrm(ks[10], (N_EVEN, EVEN_OUT, D_MODEL), EVEN_OUT ** -0.5),
        "od_w_in": nrm(ks[11], (N_ODD, D_MODEL, ODD_IN), D_MODEL ** -0.5),
        "od_cmp_pos_k": nrm(ks[12], (N_ODD, CMP_LEN, HEAD_DIM), 0.1),
        "od_cmp_k_w1": nrm(ks[13], (N_ODD, CMP_LEN * HEAD_DIM, CMP_HIDDEN), (CMP_LEN * HEAD_DIM) ** -0.5),
        "od_cmp_k_w2": nrm(ks[14], (N_ODD, CMP_HIDDEN, HEAD_DIM), CMP_HIDDEN ** -0.5),
        "od_cmp_pos_v": nrm(ks[15], (N_ODD, CMP_LEN, HEAD_DIM), 0.1),
        "od_cmp_v_w1": nrm(ks[16], (N_ODD, CMP_LEN * HEAD_DIM, CMP_HIDDEN), (CMP_LEN * HEAD_DIM) ** -0.5),
        "od_cmp_v_w2": nrm(ks[17], (N_ODD, CMP_HIDDEN, HEAD_DIM), CMP_HIDDEN ** -0.5),
        "od_w_out": nrm(ks[18], (N_ODD, NSA_Q_WIDTH, D_MODEL), NSA_Q_WIDTH ** -0.5),
        "mlp_w1": nrm(ks[19], (DEPTH, D_MODEL, D_FF), D_MODEL ** -0.5),
        "mlp_w2": nrm(ks[20], (DEPTH, D_FF, D_MODEL), D_FF ** -0.5),
    }


def reference(x, attn_norm, mlp_norm, final_norm, ev_w_in, ev_lam_q1, ev_lam_k1, ev_lam_q2,
              ev_lam_k2, ev_subln, ev_w_out, od_w_in, od_cmp_pos_k, od_cmp_k_w1, od_cmp_k_w2,
              od_cmp_pos_v, od_cmp_v_w1, od_cmp_v_w2, od_w_out, mlp_w1, mlp_w2):
    for layer in range(DEPTH):
        h = rmsnorm(x, attn_norm[layer])
        if layer % 2 == 0:
            e = layer // 2
            mix = sb_diff_mixer(h, ev_w_in[e], ev_lam_q1[e], ev_lam_k1[e], ev_lam_q2[e],
                                ev_lam_k2[e], ev_subln[e], ev_w_out[e], layer)
        else:
            o = layer // 2
            mix = nsa_mixer(h, od_w_in[o], od_cmp_pos_k[o], od_cmp_k_w1[o], od_cmp_k_w2[o],
                            od_cmp_pos_v[o], od_cmp_v_w1[o], od_cmp_v_w2[o], od_w_out[o])
        x = x + mix
        x = x + squared_relu_mlp(rmsnorm(x, mlp_norm[layer]), mlp_w1[layer], mlp_w2[layer])
    return rmsnorm(x, final_norm)
```

```python
import math
import numpy as np
import ml_dtypes
from contextlib import ExitStack
import concourse.bass as bass
import concourse.mybir as mybir
from concourse.bass_utils import run_bass_kernel_spmd

F32 = mybir.dt.float32
BF16 = mybir.dt.bfloat16
AF = mybir.ActivationFunctionType
ALU = mybir.AluOpType
bf = ml_dtypes.bfloat16

SEQ = 8192
D = 1024
NQT = SEQ // 512
NKT = SEQ // 128
NEG = -30000.0
SB_BACK = None
SEM_WRAP = 16000
DMA_WRAP = 1000
STOP_AFTER = None


class Buf:
    __slots__ = ("name", "w", "r")

    def __init__(self, name):
        self.name = name
        self.w = []
        self.r = []


class Op:
    __slots__ = ("eng", "fn", "deps", "idx", "needs_inc", "waits", "is_dma", "dkey", "dval")

    def __init__(self, eng, fn):
        self.eng = eng
        self.fn = fn
        self.deps = set()
        self.idx = -1
        self.needs_inc = False
        self.waits = []
        self.is_dma = False
        self.dkey = None
        self.dval = 0


class T:
    __slots__ = ("ap", "b")

    def __init__(self, ap, b):
        self.ap = ap
        self.b = b

    def __getitem__(self, k):
        return self.ap[k]


class Sched:
    ENGS = ("pe", "act", "dve", "pool", "sp")

    def __init__(self, nc, stack):
        self.nc = nc
        self.stack = stack
        self.ops = []
        self.eng_ops = {e: [] for e in self.ENGS}
        self.dma_count = {}
        self.bufs = []

    def buf(self, name):
        b = Buf(name)
        self.bufs.append(b)
        return b

    def op(self, eng, fn, reads=(), writes=(), dma_key=None):
        o = Op(eng, fn)
        oid = len(self.ops)
        for b in reads:
            o.deps.update(b.w)
        for b in writes:
            o.deps.update(b.w)
            o.deps.update(b.r)
        for b in reads:
            b.r.append(oid)
        for b in writes:
            b.w = [oid]
            b.r = []
        if dma_key is not None:
            o.is_dma = True
            o.dkey = dma_key
            n = self.dma_count.get(dma_key, 0) + 1
            self.dma_count[dma_key] = n
            o.dval = n
        o.idx = len(self.eng_ops[eng])
        self.eng_ops[eng].append(o)
        self.ops.append(o)
        return oid

    def barrier(self):
        live = [b for b in self.bufs if b.w or b.r]
        first = True
        sync = self.buf("barrier")
        for e in self.ENGS:
            if first:
                self.op(e, lambda eng: eng.nop(), writes=live + [sync])
                first = False
            else:
                self.op(e, lambda eng: eng.nop(), reads=[sync])
        self.bufs = [sync]

    def finalize(self):
        nc = self.nc
        ops = self.ops
        know = {e: {} for e in self.ENGS}
        comp_know = [None] * len(ops)
        for oid, o in enumerate(ops):
            K = know[o.eng]
            for d in sorted(o.deps):
                p = ops[d]
                if p.is_dma:
                    dom = ("d", p.dkey)
                    val = p.dval
                else:
                    dom = ("e", p.eng)
                    val = p.idx + 1
                    if p.eng == "pe" and o.eng == "pe":
                        continue
                if K.get(dom, 0) >= val:
                    continue
                o.waits.append((dom, val))
                p.needs_inc = True
                for k2, v2 in comp_know[d].items():
                    if K.get(k2, 0) < v2:
                        K[k2] = v2
            ck = dict(K)
            if o.is_dma:
                ck[("d", o.dkey)] = o.dval
            else:
                ck[("e", o.eng)] = o.idx + 1
            comp_know[oid] = ck
        comp_know = None
        eng_sems = {}
        counts = {}
        for e in self.ENGS:
            c = 0
            for o in self.eng_ops[e]:
                if o.is_dma:
                    continue
                if o.needs_inc:
                    c += 1
                counts[(e, o.idx)] = c
            nsem = (c + SEM_WRAP - 1) // SEM_WRAP
            eng_sems[e] = [self.stack.enter_context(nc.semaphore(f"s_{e}_{i}")) for i in range(nsem)]
        dma_sems = {}
        for k, n in self.dma_count.items():
            nsem = (n + DMA_WRAP - 1) // DMA_WRAP
            dma_sems[k] = [self.stack.enter_context(nc.semaphore(f"d_{k}_{i}")) for i in range(nsem)]

        def sem_for(dom, val):
            if dom[0] == "e":
                c = counts[(dom[1], val - 1)]
                return eng_sems[dom[1]][(c - 1) // SEM_WRAP], (c - 1) % SEM_WRAP + 1
            return dma_sems[dom[1]][(val - 1) // DMA_WRAP], ((val - 1) % DMA_WRAP + 1) * 16

        self.n_waits = sum(len(o.waits) for o in ops)
        with nc.Block() as block:
            def make(e):
                def body(eng):
                    for o in self.eng_ops[e]:
                        for dom, val in o.waits:
                            s, v = sem_for(dom, val)
                            eng.wait_ge(s, v)
                        ins = o.fn(eng)
                        if o.is_dma:
                            s, v = sem_for(("d", o.dkey), o.dval)
                            ins.then_inc(s, 16)
                        elif o.needs_inc:
                            c = counts[(e, o.idx)]
                            ins.then_inc(eng_sems[e][(c - 1) // SEM_WRAP], 1)
                return body
            block.tensor(make("pe"))
            block.scalar(make("act"))
            block.vector(make("dve"))
            block.gpsimd(make("pool"))
            block.sync(make("sp"))


class Arena:
    def __init__(self, S, tens, nwords):
        self.S = S
        self.t = tens
        self.n = nwords
        self.off = 0
        self.cnt = 0

    def reset(self):
        self.off = 0

    def alloc(self, parts, shape, dt, name):
        n = 1
        for s in shape:
            n *= s
        words = n if dt == F32 else (n + 1) // 2
        v = self.t[0:parts, self.off:self.off + words]
        self.off += words
        assert self.off <= self.n, f"arena overflow at {name}: {self.off} > {self.n}"
        if dt != F32:
            v = v.bitcast(dt)
        if len(shape) == 2:
            v = v.rearrange("p (a b) -> p a b", a=shape[0])
        elif len(shape) == 3:
            v = v.rearrange("p (a b c) -> p a b c", a=shape[0], b=shape[1])
        self.cnt += 1
        return T(v, self.S.buf(name))


def _split3(v):
    v = np.asarray(v, np.float32)
    hi = v.astype(bf)
    r1 = v - hi.astype(np.float32)
    mid = r1.astype(bf)
    r2 = r1 - mid.astype(np.float32)
    lo = r2.astype(bf)
    return hi, mid, lo


DIFF_SLOPES = [2.0 ** (-8.0 * (h + 1) / 4) for h in range(4)]
NSA_SLOPES = [2.0 ** (-8.0 * (h + 1) / 16) for h in range(16)]
ALL_SLOPES = DIFF_SLOPES + NSA_SLOPES
BIAS_M0 = -3
BIAS_NM = 68


def make_consts():
    c = {}
    j = np.arange(128)
    t = np.arange(512)
    c["ident"] = np.eye(128, dtype=np.float32).astype(bf)
    c["ones"] = np.ones((128, 128), np.float32).astype(bf)
    c["negones"] = (-np.ones((128, 128), np.float32)).astype(bf)
    c["uneg"] = (-(j[:, None] >= j[None, :]).astype(np.float32)).astype(bf)
    c["onesdiv"] = np.full((128, 128), 1.0 / 128, np.float32)
    msb = np.zeros((4, 128, 512), np.float32)
    mc = np.zeros((4, 128, 512), np.float32)
    for o in range(4):
        jj = 128 * o + j[:, None]
        msb[o] = np.where(jj >= t[None, :], NEG, 0.0)
        mc[o] = np.where(jj > t[None, :], NEG, 0.0)
    c["mask_sb"] = np.ascontiguousarray(msb.transpose(1, 0, 2)).astype(bf)
    c["mask_c"] = np.ascontiguousarray(mc.transpose(1, 0, 2)).astype(bf)
    kr = np.zeros((6, SEQ), np.float32)
    kr[0:3] = 1.0
    kr[3:6] = (np.arange(SEQ) % 128)[None, :]
    c["kaug"] = kr.astype(bf)
    kc = np.zeros((6, 512), np.float32)
    kc[0:3] = 1.0
    kc[3:6] = (16 * (np.arange(512) % 128))[None, :]
    c["kaug_cmp"] = kc.astype(bf)
    qa = np.zeros((len(ALL_SLOPES), 6, 512), np.float32).astype(bf)
    for i, s in enumerate(ALL_SLOPES):
        s32 = np.float32(s)
        v = (-(s32 * t.astype(np.float32))).astype(np.float32)
        h3 = _split3(v)
        s3 = _split3(np.full(512, s32, np.float32))
        for r in range(3):
            qa[i, r] = h3[r]
            qa[i, 3 + r] = s3[r]
    c["qaug"] = np.ascontiguousarray(qa.transpose(1, 0, 2))
    bt = np.zeros((len(ALL_SLOPES), BIAS_NM), np.float32)
    for i, s in enumerate(ALL_SLOPES):
        for mi in range(BIAS_NM):
            bt[i, mi] = -np.float32(s) * np.float32(128 * (mi + BIAS_M0))
    c["bias_tab"] = np.broadcast_to(bt.reshape(1, -1), (128, bt.size)).copy()
    bc = np.zeros((16, NQT, 4), np.float32)
    for h, s in enumerate(NSA_SLOPES):
        for qt in range(NQT):
            for cc in range(4):
                bc[h, qt, cc] = -np.float32(s) * np.float32(512 * qt - 2048 * cc - 31)
    c["bias_cmp"] = np.broadcast_to(bc.reshape(1, -1), (128, bc.size)).copy()
    mcm = np.zeros((5, 128, 512), np.float32)
    for rel in range(5):
        mcm[rel] = np.where(512 * rel + t[None, :] >= 16 * j[:, None] + 31, 0.0, NEG)
    c["mask_cmp"] = np.ascontiguousarray(mcm.transpose(1, 0, 2)).astype(bf)
    mw = np.zeros((8, 128, 512), np.float32)
    for oi, o in enumerate(range(-4, 4)):
        dd = t[None, :] - j[:, None] - 128 * o
        mw[oi] = np.where((dd >= 0) & (dd <= 511), 0.0, NEG)
    c["mask_win"] = np.ascontiguousarray(mw.transpose(1, 0, 2)).astype(bf)
    cc = np.arange(SEQ)
    c["ewide"] = (cc[None, :] // 64 == j[:, None]).astype(np.float32).astype(bf)
    n = np.arange(512)
    s_ = np.arange(128)
    ov = ((n[:, None] >= 4 * s_[None, :] - 1) & (n[:, None] <= 4 * s_[None, :] + 3) & (n[:, None] < 511))
    c["ovl"] = np.ascontiguousarray(ov.astype(np.float32).reshape(4, 128, 128).transpose(1, 0, 2)).astype(bf)
    q = np.arange(128)
    u = np.arange(-127, 129)
    cur = (q >= 64).astype(np.int64)
    A = np.zeros((128, 256), np.float32)
    M = np.ones((128, 256), np.float32)
    fut = u[None, :] > cur[:, None]
    A[fut] = -1.0
    M[fut] = 0.0
    f1 = u[None, :] == cur[:, None]
    f2 = u[None, :] == cur[:, None] - 1
    A[f1] = 1.0e6 + 1.0
    M[f1] = 0.0
    A[f2] = 1.0e6 + 2.0
    M[f2] = 0.0
    c["topk_a"] = A
    c["topk_m"] = M
    gs = np.zeros((48, 48, 64), np.float32)
    for r in range(48):
        gs[r, r, :] = 1.0
    c["gsel"] = gs.reshape(48, 48 * 64).astype(bf)
    return c


CONST_SPECS = None


def _dt_of(a):
    return BF16 if a.dtype == bf else F32


def build_program(consts):
    nc = bass.Bass("TRN2", target_bir_lowering=False)
    dr = {}

    def din(name, shape, dt=F32):
        dr[name] = nc.dram_tensor(name, list(shape), dt, kind="ExternalInput").ap()
        return dr[name]

    def dscr(name, shape, dt):
        dr[name] = nc.dram_tensor(name, list(shape), dt, kind="Internal").ap()
        return dr[name]

    x_in = din("x", (SEQ, D))
    attn_norm = din("attn_norm", (2, D))
    mlp_norm = din("mlp_norm", (2, D))
    final_norm = din("final_norm", (D,))
    ev_w_in = din("ev_w_in", (D, 3072))
    lamv = din("lamv", (4, 64))
    ev_subln = din("ev_subln", (128,))
    ev_w_out = din("ev_w_out", (D, D))
    od_w_in = din("od_w_in", (D, 2608))
    pos_k = din("od_cmp_pos_k", (32, 64))
    cw1k = din("od_cmp_k_w1", (2048, 256))
    cw2k = din("od_cmp_k_w2", (256, 64))
    pos_v = din("od_cmp_pos_v", (32, 64))
    cw1v = din("od_cmp_v_w1", (2048, 256))
    cw2v = din("od_cmp_v_w2", (256, 64))
    od_w_out = din("od_w_out", (D, D))
    mlp_w1 = din("mlp_w1", (2, D, 4096))
    mlp_w2 = din("mlp_w2", (2, 4096, D))
    cd = {k: din("c_" + k, v.shape, _dt_of(v)) for k, v in consts.items()}
    out_d = nc.dram_tensor("out", [SEQ, D], F32, kind="ExternalOutput").ap()

    qk0 = dscr("qk0", (32, 64, SEQ), BF16)
    v0 = dscr("v0", (SEQ, 1024), BF16)
    ot = dscr("ot", (D, SEQ), BF16)
    x2 = dscr("x2", (SEQ, D), F32)
    q1 = dscr("q1", (16, 64, SEQ), BF16)
    kf1 = dscr("kf1", (16, 64, SEQ), BF16)
    v1 = dscr("v1", (SEQ, 512), BF16)
    gts = dscr("gts", (48, SEQ), BF16)
    kcmp = dscr("kcmp", (4, 64, 512), BF16)
    vcmp = dscr("vcmp", (4, 512, 64), BF16)
    x1s = dscr("x1s", (SEQ, D), F32)
    uts_d = dscr("uts", (32, 128, SEQ), BF16)

    with ExitStack() as st:
        S = Sched(nc, st)
        NW = 50 * 1024
        arena_t = st.enter_context(nc.sbuf_tensor("arena", [128, NW], F32))
        AR = Arena(S, arena_t, NW)
        pbanks = [st.enter_context(nc.psum_tensor(f"pb{i}", [128, 512], F32)) for i in range(8)]

        def PS(i, name, parts=128, cols=512, dt=F32):
            ap = pbanks[i][0:parts, :]
            if dt != F32:
                ap = ap.bitcast(dt)
            ap = ap[:, 0:cols]
            return T(ap, S.buf(name))

        def dma(q, out, in_, reads, writes, key, slow=False):
            if slow:
                S.op(q, lambda e: e.dma_start(out=out, in_=in_, allow_slow_non_contiguous=True), reads=reads, writes=writes, dma_key=key)
            else:
                S.op(q, lambda e: e.dma_start(out=out, in_=in_), reads=reads, writes=writes, dma_key=key)

        def load_vt(VT, src_cols, width):
            for g4 in range(4):
                dma("sp", VT.ap[:, g4 * 16:(g4 + 1) * 16, 0:width],
                    src_cols[g4 * 2048:(g4 + 1) * 2048, :].rearrange("(t p) c -> p t c", p=128), [], [VT.b], "LVT")

        def load(dst, src, q="sp"):
            dma(q, dst.ap, src, [], [dst.b], "L" + dst.b.name)

        def store(dst, src, ap=None, q="pool"):
            dma(q, dst, src.ap if ap is None else ap, [src.b], [], "S" + src.b.name)

        def mm(out, outap, lhsT, lhsap, rhs, rhsap, start, stop, extra_r=()):
            S.op("pe", lambda e: e.matmul(outap, lhsap, rhsap, start=start, stop=stop),
                 reads=[lhsT.b, rhs.b] + list(extra_r), writes=[out.b])

        class Ring:
            def __init__(self, items):
                self.items = items
                self.i = 0

            def next(self):
                it = self.items[self.i % len(self.items)]
                self.i += 1
                return it

        rr_cast = [0]

        def cast(dst, dstap, src, srcap, scale_ap=None, scale_t=None):
            e = ("dve", "act", "pool")[rr_cast[0] % 3] if scale_ap is None else ("dve", "act")[rr_cast[0] % 2]
            rr_cast[0] += 1
            rd = [src.b] + ([scale_t.b] if scale_t is not None else [])
            if e == "act":
                if scale_ap is None:
                    S.op("act", lambda en: en.copy(dstap, srcap), reads=rd, writes=[dst.b])
                else:
                    S.op("act", lambda en: en.activation(dstap, srcap, AF.Copy, scale=scale_ap), reads=rd, writes=[dst.b])
            elif e == "dve":
                if scale_ap is None:
                    S.op("dve", lambda en: en.tensor_copy(dstap, srcap), reads=rd, writes=[dst.b])
                else:
                    S.op("dve", lambda en: en.tensor_scalar(dstap, srcap, scale_ap, None, ALU.mult), reads=rd, writes=[dst.b])
            else:
                S.op("pool", lambda en: en.tensor_copy(dstap, srcap), reads=rd, writes=[dst.b])

        def load_weight(dst, src_ap, K, N, stage_ring, gain=None, col0=0, ncols=None, dcol0=0):
            ncols = N if ncols is None else ncols
            for k in range(K // 128):
                c = 0
                while c < ncols:
                    w = min(2048, ncols - c)
                    stg = stage_ring.next()
                    dma("sp", stg.ap[:, 0:w], src_ap[k * 128:(k + 1) * 128, col0 + c:col0 + c + w], [], [stg.b], "L" + stg.b.name)
                    cast(dst, dst.ap[:, k, dcol0 + c:dcol0 + c + w], stg, stg.ap[:, 0:w],
                         None if gain is None else gain.ap[:, k:k + 1], gain)
                    c += w

        NWP = 0
        ident = AR.alloc(128, (128,), BF16, "ident")
        load(ident, cd["ident"])
        ones = AR.alloc(128, (128,), BF16, "ones")
        load(ones, cd["ones"])
        negones = AR.alloc(128, (128,), BF16, "negones")
        load(negones, cd["negones"])
        uneg = AR.alloc(128, (128,), BF16, "uneg")
        load(uneg, cd["uneg"])
        onesdiv = AR.alloc(128, (128,), F32, "onesdiv")
        load(onesdiv, cd["onesdiv"])
        bias_tab = AR.alloc(128, (len(ALL_SLOPES) * BIAS_NM,), F32, "bias_tab")
        load(bias_tab, cd["bias_tab"])
        gains = AR.alloc(128, (4, 8), F32, "gains")
        for li in range(2):
            dma("sp", gains.ap[:, 2 * li, :], attn_norm[li].rearrange("(k p) -> p k", p=128), [], [gains.b], "Lgains", slow=True)
            dma("sp", gains.ap[:, 2 * li + 1, :], mlp_norm[li].rearrange("(k p) -> p k", p=128), [], [gains.b], "Lgains", slow=True)
        eps_t = AR.alloc(128, (1,), F32, "eps_t")
        S.op("dve", lambda e: e.memset(eps_t.ap, 1e-6), writes=[eps_t.b])
        persist_off = AR.off

        def bias_ap(si, m):
            col = si * BIAS_NM + (m - BIAS_M0)
            return bias_tab.ap[:, col:col + 1]

        def phase_reset():
            S.barrier()
            AR.off = persist_off

        def norm_transpose(xt, hn, ss, rs, junk, tp, hT, r):
            S.op("act", lambda e: e.activation(junk.ap, xt.ap, AF.Square, accum_out=ss.ap), reads=[xt.b], writes=[junk.b, ss.b])
            S.op("act", lambda e: e.activation(rs.ap, ss.ap, AF.Sqrt, bias=eps_t.ap, scale=1.0 / D), reads=[ss.b, eps_t.b], writes=[rs.b])
            S.op("dve", lambda e: e.reciprocal(rs.ap, rs.ap), reads=[rs.b], writes=[rs.b])
            S.op("act", lambda e: e.activation(hn.ap, xt.ap, AF.Copy, scale=rs.ap), reads=[xt.b, rs.b], writes=[hn.b])
            for k in range(8):
                S.op("pe", lambda e, k=k: e.transpose(tp.ap[:, k * 128:(k + 1) * 128], hn.ap[:, k * 128:(k + 1) * 128], ident.ap),
                     reads=[hn.b, ident.b], writes=[tp.b])
            S.op("dve", lambda e: e.tensor_copy(hT.ap[:, :, r * 128:(r + 1) * 128], tp.ap.rearrange("p (k c) -> p k c", k=8)),
                 reads=[tp.b], writes=[hT.b])

        def phase_proj(x_src, w_src, ncols_total, gain_idx, fm_chunks, tm_chunks):
            stage = Ring([AR.alloc(128, (2048,), F32, f"wstg{i}") for i in range(2)])
            W = AR.alloc(128, (8, ncols_total), BF16, "Win")
            gsl = T(gains.ap[:, gain_idx, :], gains.b)
            load_weight(W, w_src, D, ncols_total, stage, gain=gsl)
            xts = Ring([AR.alloc(128, (D,), F32, f"xt{i}") for i in range(2)])
            hns = Ring([AR.alloc(128, (D,), BF16, f"hn{i}") for i in range(2)])
            junk = AR.alloc(128, (D,), BF16, "junk")
            sss = Ring([AR.alloc(128, (1,), F32, f"ss{i}") for i in range(2)])
            rss = Ring([AR.alloc(128, (1,), F32, f"rs{i}") for i in range(2)])
            hTs = Ring([AR.alloc(128, (8, 512), BF16, f"hT{i}") for i in range(2)])
            tps = Ring([PS(i, f"tp{i}", cols=1024, dt=BF16) for i in (0, 1)])
            pps = Ring([PS(i, f"pp{i}") for i in (2, 3, 4, 5)])
            evs = Ring([AR.alloc(128, (512,), BF16, f"ev{i}") for i in range(4)])
            ev_i = [0]
            for tb in range(NQT):
                hT = hTs.next()
                for r in range(4):
                    xt = xts.next()
                    row0 = tb * 512 + r * 128
                    load(xt, x_src[row0:row0 + 128, :])
                    norm_transpose(xt, hns.next(), sss.next(), rss.next(), junk, tps.next(), hT, r)
                for (col0, M, dsts, scale, func) in fm_chunks:
                    pp = pps.next()
                    for k in range(8):
                        mm(pp, pp.ap[0:M, :], W, W.ap[:, k, col0:col0 + M], hT, hT.ap[:, k, :], k == 0, k == 7)
                    ev = evs.next()
                    eng = ("act", "dve")[ev_i[0] % 2] if func is None else "act"
                    ev_i[0] += 1
                    if eng == "act":
                        S.op("act", lambda e, pp=pp, ev=ev, M=M, scale=scale, func=func: e.activation(
                            ev.ap[0:M, :], pp.ap[0:M, :], AF.Copy if func is None else func, scale=scale),
                            reads=[pp.b], writes=[ev.b])
                    else:
                        S.op("dve", lambda e, pp=pp, ev=ev, M=M, scale=scale: e.tensor_scalar(
                            ev.ap[0:M, :], pp.ap[0:M, :], float(scale), None, ALU.mult), reads=[pp.b], writes=[ev.b])
                    for (dfn, p0, npart) in dsts:
                        store(dfn(tb), ev, ap=ev.ap[p0:p0 + npart, :])
                for (col0, ncols, dfn) in tm_chunks:
                    for r in range(4):
                        pp = pps.next()
                        for k in range(8):
                            mm(pp, pp.ap[:, 0:ncols], hT, hT.ap[:, k, r * 128:(r + 1) * 128], W, W.ap[:, k, col0:col0 + ncols], k == 0, k == 7)
                        ev = evs.next()
                        eng = ("act", "dve")[ev_i[0] % 2]
                        ev_i[0] += 1
                        if eng == "act":
                            S.op("act", lambda e, pp=pp, ev=ev, n=ncols: e.copy(ev.ap[:, 0:n], pp.ap[:, 0:n]), reads=[pp.b], writes=[ev.b])
                        else:
                            S.op("dve", lambda e, pp=pp, ev=ev, n=ncols: e.tensor_copy(ev.ap[:, 0:n], pp.ap[:, 0:n]), reads=[pp.b], writes=[ev.b])
                        store(dfn(tb * 512 + r * 128), ev, ap=ev.ap[:, 0:ncols])

        fm = []
        for jc in range(16):
            col0 = 128 * jc if jc < 8 else 1536 + 128 * (jc - 8)
            isq = (jc < 4) or (8 <= jc < 12)
            dsts = []
            for half in range(2):
                slot = 2 * jc + half
                dsts.append((lambda tb, slot=slot: qk0[slot, :, tb * 512:(tb + 1) * 512], 64 * half, 64))
            fm.append((col0, 128, dsts, 0.125 if isq else 1.0, None))
        tm = [(1024, 512, lambda row0: v0[row0:row0 + 128, 0:512]),
              (2560, 512, lambda row0: v0[row0:row0 + 128, 512:1024])]
        phase_proj(x_in, ev_w_in, 3072, 0, fm, tm)
        phase_reset()

        def ktiles_for(qt, back):
            hi = 4 * qt + 3
            lo = 0 if back is None else max(0, 4 * qt - back)
            return list(range(hi, lo - 1, -1))

        def attention_l0():
            mask_sb = AR.alloc(128, (4, 512), BF16, "mask_sb")
            load(mask_sb, cd["mask_sb"])
            mask_c = AR.alloc(128, (4, 512), BF16, "mask_c")
            load(mask_c, cd["mask_c"])
            KT = AR.alloc(70, (SEQ,), BF16, "KT")
            VT = AR.alloc(128, (NKT, 128), BF16, "VT")
            QTs = Ring([AR.alloc(70, (512,), BF16, f"QT{i}") for i in range(2)])
            Es = Ring([AR.alloc(128, (512,), F32, f"E{i}") for i in range(2)])
            SPs = Ring([AR.alloc(128, (512,), BF16, f"SP{i}") for i in range(2)])
            Ws = Ring([AR.alloc(128, (512,), BF16, f"Wt{i}") for i in range(3)])
            ssum = AR.alloc(128, (512,), F32, "ssum")
            sshs = Ring([AR.alloc(128, (512,), BF16, f"ssh{i}") for i in range(2)])
            oev = Ring([AR.alloc(128, (512,), BF16, f"oev{i}") for i in range(2)])
            Ys = Ring([PS(i, f"Y{i}") for i in (0, 1, 2)])
            Oacc = PS(3, "Oacc")
            for h in range(8):
                load(T(KT.ap[0:64, :], KT.b), qk0[8 + h])
                load_vt(VT, v0[:, h * 64:(h + 1) * 64], 64)
                for qt in range(NQT):
                    QT = QTs.next()
                    dma("sp", QT.ap[0:64, :], qk0[h, :, qt * 512:(qt + 1) * 512], [], [QT.b], "L" + QT.b.name)
                    kts = ktiles_for(qt, SB_BACK)
                    n = len(kts)
                    ssh_prev = None
                    for i, kt in enumerate(kts):
                        Y = Ys.next()
                        o = kt - 4 * qt
                        mm(Y, Y.ap, KT, KT.ap[0:64, kt * 128:(kt + 1) * 128], QT, QT.ap[0:64, :], True, False)
                        if o >= 0:
                            mm(Y, Y.ap, ident, ident.ap, mask_sb, mask_sb.ap[:, o, :], False, False)
                        E = Es.next()
                        SP = SPs.next()
                        S.op("act", lambda e, E=E, Y=Y: e.activation(E.ap, Y.ap, AF.Exp), reads=[Y.b], writes=[E.b])
                        S.op("act", lambda e, E=E, SP=SP: e.activation(SP.ap, E.ap, AF.Ln, bias=1.0), reads=[E.b], writes=[SP.b])
                        mm(Y, Y.ap, uneg, uneg.ap, SP, SP.ap, False, ssh_prev is None)
                        if ssh_prev is not None:
                            mm(Y, Y.ap, negones, negones.ap, ssh_prev, ssh_prev.ap, False, True)
                        Wt = Ws.next()
                        S.op("act", lambda e, Wt=Wt, Y=Y: e.activation(Wt.ap, Y.ap, AF.Exp), reads=[Y.b], writes=[Wt.b])
                        mm(Oacc, Oacc.ap[0:64, :], VT, VT.ap[:, kt, 0:64], Wt, Wt.ap, i == 0, i == n - 1)
                        if i < n - 1:
                            ssh = sshs.next()
                            if i == 0:
                                S.op("pool", lambda e, SP=SP: e.tensor_copy(ssum.ap, SP.ap), reads=[SP.b], writes=[ssum.b])
                                ssh_prev = SP
                            else:
                                S.op("pool", lambda e, SP=SP: e.tensor_tensor(ssum.ap, ssum.ap, SP.ap, ALU.add), reads=[SP.b, ssum.b], writes=[ssum.b])
                                S.op("pool", lambda e, ssh=ssh: e.tensor_copy(ssh.ap, ssum.ap), reads=[ssum.b], writes=[ssh.b])
                                ssh_prev = ssh
                    ev = oev.next()
                    S.op("dve", lambda e, ev=ev: e.tensor_copy(ev.ap[0:64, :], Oacc.ap[0:64, :]), reads=[Oacc.b], writes=[ev.b])
                    store(ot[h * 64:(h + 1) * 64, qt * 512:(qt + 1) * 512], ev, ap=ev.ap[0:64, :])
            lam_t = AR.alloc(128, (4, 64), F32, "lam_t")
            for i in range(4):
                dma("sp", lam_t.ap[:, i, :], lamv[i].partition_broadcast(128), [], [lam_t.b], "Llam")
            lam_p = AR.alloc(128, (2, 64), F32, "lam_p")
            lam_s = AR.alloc(128, (2,), F32, "lam_s")
            neglam = AR.alloc(128, (1,), F32, "neglam")
            S.op("dve", lambda e: e.tensor_tensor(lam_p.ap[:, 0, :], lam_t.ap[:, 0, :], lam_t.ap[:, 1, :], ALU.mult), reads=[lam_t.b], writes=[lam_p.b])
            S.op("dve", lambda e: e.tensor_tensor(lam_p.ap[:, 1, :], lam_t.ap[:, 2, :], lam_t.ap[:, 3, :], ALU.mult), reads=[lam_t.b, lam_p.b], writes=[lam_p.b])
            S.op("dve", lambda e: e.reduce_sum(lam_s.ap, lam_p.ap, axis=mybir.AxisListType.X), reads=[lam_p.b], writes=[lam_s.b])
            S.op("act", lambda e: e.activation(lam_s.ap, lam_s.ap, AF.Exp), reads=[lam_s.b], writes=[lam_s.b])
            lam_init = 0.8 - 0.6 * math.exp(-0.3 * 0)
            S.op("dve", lambda e: e.tensor_tensor(neglam.ap, lam_s.ap[:, 1:2], lam_s.ap[:, 0:1], ALU.subtract), reads=[lam_s.b], writes=[neglam.b])
            S.op("dve", lambda e: e.tensor_scalar(neglam.ap, neglam.ap, -lam_init, None, ALU.add), reads=[neglam.b], writes=[neglam.b])
            sg = AR.alloc(128, (1,), F32, "sg")
            dma("sp", sg.ap, ev_subln.rearrange("(p o) -> p o", o=1), [], [sg.b], "Lsg")
            S.op("dve", lambda e: e.tensor_scalar(sg.ap, sg.ap, 1.0 - lam_init, None, ALU.mult), reads=[sg.b], writes=[sg.b])
            KT2 = [KT, AR.alloc(70, (SEQ,), BF16, "KTb")]
            for kk in KT2:
                dma("sp", kk.ap[64:70, :], cd["kaug"], [], [kk.b], "L" + kk.b.name)
            QT2 = [[AR.alloc(70, (512,), BF16, f"QD{c}{i}") for i in range(2)] for c in range(2)]
            Ps = Ring([AR.alloc(128, (512,), BF16, f"P{i}") for i in range(3)])
            NUM = [PS(3, "NUM0"), PS(4, "NUM1")]
            DEN = [PS(5, "DEN0"), PS(6, "DEN1")]
            MS = PS(7, "MS")
            r_t = [AR.alloc(128, (512,), F32, f"rden{c}") for c in range(2)]
            a_t = [AR.alloc(128, (512,), F32, f"a{c}") for c in range(2)]
            o_t = AR.alloc(128, (512,), F32, "o_t")
            sq_t = AR.alloc(128, (512,), F32, "sq_t")
            for h in range(4):
                for c in range(2):
                    dma("sp", KT2[c].ap[0:64, :], qk0[24 + 2 * h + c], [], [KT2[c].b], "L" + KT2[c].b.name)
                load_vt(VT, v0[:, 512 + h * 128:512 + (h + 1) * 128], 128)
                for qt in range(NQT):
                    for c in range(2):
                        QT = QT2[c][qt % 2]
                        dma("sp", QT.ap[0:64, :], qk0[16 + 2 * h + c, :, qt * 512:(qt + 1) * 512], [], [QT.b], "L" + QT.b.name)
                        dma("sp", QT.ap[64:70, :], cd["qaug"][:, h, :], [], [QT.b], "L" + QT.b.name)
                        kts = ktiles_for(qt, None)
                        n = len(kts)
                        for i, kt in enumerate(kts):
                            Y = Ys.next()
                            o = kt - 4 * qt
                            mm(Y, Y.ap, KT2[c], KT2[c].ap[:, kt * 128:(kt + 1) * 128], QT, QT.ap, True, o < 0)
                            if o >= 0:
                                mm(Y, Y.ap, ident, ident.ap, mask_c, mask_c.ap[:, o, :], False, True)
                            P = Ps.next()
                            S.op("act", lambda e, P=P, Y=Y, m=4 * qt - kt, h=h: e.activation(P.ap, Y.ap, AF.Exp, bias=bias_ap(h, m)),
                                 reads=[Y.b, bias_tab.b], writes=[P.b])
                            mm(NUM[c], NUM[c].ap, VT, VT.ap[:, kt, :], P, P.ap, i == 0, i == n - 1)
                            mm(DEN[c], DEN[c].ap, ones, ones.ap, P, P.ap, i == 0, i == n - 1)
                        S.op("dve", lambda e, c=c: e.reciprocal(r_t[c].ap, DEN[c].ap), reads=[DEN[c].b], writes=[r_t[c].b])
                        S.op("dve", lambda e, c=c: e.tensor_tensor(a_t[c].ap, NUM[c].ap, r_t[c].ap, ALU.mult), reads=[NUM[c].b, r_t[c].b], writes=[a_t[c].b])
                    S.op("dve", lambda e: e.scalar_tensor_tensor(o_t.ap, a_t[1].ap, neglam.ap, a_t[0].ap, ALU.mult, ALU.add),
                         reads=[a_t[0].b, a_t[1].b, neglam.b], writes=[o_t.b])
                    S.op("act", lambda e: e.activation(sq_t.ap, o_t.ap, AF.Square), reads=[o_t.b], writes=[sq_t.b])
                    mm(MS, MS.ap, onesdiv, onesdiv.ap, sq_t, sq_t.ap, True, True)
                    S.op("act", lambda e: e.activation(sq_t.ap, MS.ap, AF.Sqrt, bias=eps_t.ap), reads=[MS.b, eps_t.b], writes=[sq_t.b])
                    S.op("dve", lambda e: e.reciprocal(sq_t.ap, sq_t.ap), reads=[sq_t.b], writes=[sq_t.b])
                    ev = oev.next()
                    S.op("dve", lambda e, ev=ev: e.scalar_tensor_tensor(ev.ap, o_t.ap, sg.ap, sq_t.ap, ALU.mult, ALU.mult),
                         reads=[o_t.b, sg.b, sq_t.b], writes=[ev.b])
                    store(ot[512 + h * 128:512 + (h + 1) * 128, qt * 512:(qt + 1) * 512], ev)

        attention_l0()
        phase_reset()

        def phase_mlp(x_src, wout_src, w1_src, w2_src, gain_idx, dst, final_gain=None):
            stage = Ring([AR.alloc(128, (2048,), F32, f"wstg{i}") for i in range(2)])
            Wo = AR.alloc(128, (8, D), BF16, "Wo")
            W1 = AR.alloc(128, (8, 4096), BF16, "W1")
            gsl = T(gains.ap[:, gain_idx, :], gains.b)
            load_weight(Wo, wout_src, D, D, stage)
            load_weight(W1, w1_src, D, 4096, stage, gain=gsl)
            OTs = Ring([AR.alloc(128, (8, 512), BF16, f"OT{i}") for i in range(2)])
            x1r = Ring([AR.alloc(128, (D,), F32, f"x1r{i}") for i in range(3)])
            xts = Ring([AR.alloc(128, (D,), F32, f"xt{i}") for i in range(2)])
            hns = Ring([AR.alloc(128, (D,), BF16, f"hn{i}") for i in range(2)])
            junk = AR.alloc(128, (D,), BF16, "junk")
            sss = Ring([AR.alloc(128, (1,), F32, f"ss{i}") for i in range(2)])
            rss = Ring([AR.alloc(128, (1,), F32, f"rs{i}") for i in range(2)])
            hTs = Ring([AR.alloc(128, (8, 512), BF16, f"hT{i}") for i in range(2)])
            uts = Ring([AR.alloc(128, (512,), BF16, f"ut{i}") for i in range(4)])
            sqs = Ring([AR.alloc(128, (512,), BF16, f"sq{i}") for i in range(2)])
            tps = Ring([PS(i, f"tp{i}", cols=1024, dt=BF16) for i in (0, 1)])
            pps = Ring([PS(i, f"pp{i}") for i in (2, 3, 4, 5, 6, 7)])
            for tb in range(NQT):
                OT = OTs.next()
                dma("sp", OT.ap, ot[:, tb * 512:(tb + 1) * 512].rearrange("(k p) t -> p k t", p=128), [], [OT.b], "L" + OT.b.name)
                hT = hTs.next()
                for r in range(4):
                    row0 = tb * 512 + r * 128
                    xt = xts.next()
                    load(xt, x_src[row0:row0 + 128, :])
                    x1v = x1r.next()
                    for half in range(2):
                        pp = pps.next()
                        for k in range(8):
                            mm(pp, pp.ap, OT, OT.ap[:, k, r * 128:(r + 1) * 128], Wo, Wo.ap[:, k, half * 512:(half + 1) * 512], k == 0, k == 7)
                        S.op("dve", lambda e, pp=pp, xt=xt, x1v=x1v, half=half: e.tensor_tensor(
                            x1v.ap[:, half * 512:(half + 1) * 512], pp.ap, xt.ap[:, half * 512:(half + 1) * 512], ALU.add),
                            reads=[pp.b, xt.b], writes=[x1v.b])
                    store(x1s[row0:row0 + 128, :], x1v)
                    norm_transpose(x1v, hns.next(), sss.next(), rss.next(), junk, tps.next(), hT, r)
                for fc in range(32):
                    pp = pps.next()
                    for k in range(8):
                        mm(pp, pp.ap, W1, W1.ap[:, k, fc * 128:(fc + 1) * 128], hT, hT.ap[:, k, :], k == 0, k == 7)
                    sq = sqs.next()
                    u = uts.next()
                    S.op("act", lambda e, pp=pp, sq=sq: e.activation(sq.ap, pp.ap, AF.Square), reads=[pp.b], writes=[sq.b])
                    S.op("dve", lambda e, pp=pp, sq=sq, u=u: e.scalar_tensor_tensor(u.ap, pp.ap, 0.0, sq.ap, ALU.is_gt, ALU.mult),
                         reads=[pp.b, sq.b], writes=[u.b])
                    store(uts_d[fc, :, tb * 512:(tb + 1) * 512], u)
            phase_reset()
            stage = Ring([AR.alloc(128, (2048,), F32, f"wstg{i}") for i in range(2)])
            W2 = AR.alloc(128, (32, D), BF16, "W2")
            load_weight(W2, w2_src, 4096, D, stage)
            fg = None
            if final_gain is not None:
                fg = AR.alloc(128, (D,), F32, "fg")
                load(fg, final_gain.partition_broadcast(128))
            UTs = Ring([AR.alloc(128, (32, 512), BF16, f"UT{i}") for i in range(2)])
            x1r = Ring([AR.alloc(128, (D,), F32, f"x1r{i}") for i in range(3)])
            ys = Ring([AR.alloc(128, (D,), F32, f"y{i}") for i in range(3)])
            junk = AR.alloc(128, (D,), BF16, "junk")
            sss = Ring([AR.alloc(128, (1,), F32, f"ss{i}") for i in range(2)])
            rss = Ring([AR.alloc(128, (1,), F32, f"rs{i}") for i in range(2)])
            pps = Ring([PS(i, f"pp{i}") for i in (0, 1, 2, 3, 4, 5)])
            for tb in range(NQT):
                UT = UTs.next()
                for f4 in range(4):
                    dma("sp", UT.ap[:, f4 * 8:(f4 + 1) * 8, :], uts_d[f4 * 8:(f4 + 1) * 8, :, tb * 512:(tb + 1) * 512].rearrange("f p t -> p f t"),
                        [], [UT.b], "L" + UT.b.name)
                for r in range(4):
                    row0 = tb * 512 + r * 128
                    x1v = x1r.next()
                    load(x1v, x1s[row0:row0 + 128, :])
                    y = ys.next()
                    for half in range(2):
                        pp = pps.next()
                        for fc in range(32):
                            mm(pp, pp.ap, UT, UT.ap[:, fc, r * 128:(r + 1) * 128], W2, W2.ap[:, fc, half * 512:(half + 1) * 512], fc == 0, fc == 31)
                        S.op("dve", lambda e, pp=pp, y=y, x1v=x1v, half=half: e.tensor_tensor(
                            y.ap[:, half * 512:(half + 1) * 512], pp.ap, x1v.ap[:, half * 512:(half + 1) * 512], ALU.add),
                            reads=[pp.b, x1v.b], writes=[y.b])
                    if fg is not None:
                        ss = sss.next()
                        rs = rss.next()
                        S.op("act", lambda e, y=y, ss=ss: e.activation(junk.ap, y.ap, AF.Square, accum_out=ss.ap), reads=[y.b], writes=[junk.b, ss.b])
                        S.op("act", lambda e, ss=ss, rs=rs: e.activation(rs.ap, ss.ap, AF.Sqrt, bias=eps_t.ap, scale=1.0 / D), reads=[ss.b, eps_t.b], writes=[rs.b])
                        S.op("dve", lambda e, rs=rs: e.reciprocal(rs.ap, rs.ap), reads=[rs.b], writes=[rs.b])
                        S.op("dve", lambda e, y=y, rs=rs: e.scalar_tensor_tensor(y.ap, y.ap, rs.ap, fg.ap, ALU.mult, ALU.mult),
                             reads=[y.b, rs.b, fg.b], writes=[y.b])
                    store(dst[row0:row0 + 128, :], y)

        phase_mlp(x_in, ev_w_out, mlp_w1[0], mlp_w2[0], 1, x2)
        if STOP_AFTER == "L0":
            S.barrier()
            cp = Ring([AR.alloc(128, (D,), F32, f"cp{i}") for i in range(2)])
            for i in range(NKT):
                c_ = cp.next()
                load(c_, x2[i * 128:(i + 1) * 128, :])
                store(out_d[i * 128:(i + 1) * 128, :], c_)
            S.barrier()
            S.finalize()
            return nc, S
        phase_reset()

        fm = []
        for jc in range(8):
            dsts = [(lambda tb, slot=2 * jc + half: q1[slot, :, tb * 512:(tb + 1) * 512], 64 * half, 64) for half in range(2)]
            fm.append((128 * jc, 128, dsts, 0.125, None))
        for (cbase, sbase) in ((1024, 0), (1280, 4), (1536, 8), (2048, 12)):
            for jc in range(2):
                dsts = [(lambda tb, slot=sbase + 2 * jc + half: kf1[slot, :, tb * 512:(tb + 1) * 512], 64 * half, 64) for half in range(2)]
                fm.append((cbase + 128 * jc, 128, dsts, 1.0, None))
        fm.append((2560, 48, [(lambda tb: gts[:, tb * 512:(tb + 1) * 512], 0, 48)], 1.0, AF.Sigmoid))
        tm = [(1792, 256, lambda row0: v1[row0:row0 + 128, 0:256]),
              (2304, 256, lambda row0: v1[row0:row0 + 128, 256:512])]
        phase_proj(x2, od_w_in, 2608, 2, fm, tm)
        phase_reset()

        def phase_compress():
            stg = AR.alloc(64, (32 * 256,), F32, "cstg")
            W1c = AR.alloc(64, (32, 256), BF16, "W1c")
            stg2 = AR.alloc(128, (2, 64), F32, "cstg2")
            W2c = AR.alloc(128, (2, 64), BF16, "W2c")
            posf = AR.alloc(64, (32,), F32, "posf")
            posT = AR.alloc(64, (32,), BF16, "posT")
            biash = AR.alloc(128, (2,), F32, "biash")
            src = AR.alloc(64, (SEQ,), BF16, "csrc")
            hid = AR.alloc(128, (2, 512), BF16, "hid")
            u_t = AR.alloc(128, (512,), F32, "u_t")
            w_t = AR.alloc(128, (512,), F32, "w_t")
            evk = AR.alloc(64, (512,), BF16, "evk")
            evv = Ring([AR.alloc(128, (64,), BF16, f"evv{i}") for i in range(2)])
            pb_ = PS(0, "cbias")
            ph = Ring([PS(1, "ph0"), PS(2, "ph1")])
            po = Ring([PS(3, "po0"), PS(4, "po1")])
            S.op("dve", lambda e: e.memset(hid.ap, 0.0), writes=[hid.b])
            S.op("dve", lambda e: e.memset(evk.ap, 0.0), writes=[evk.b])
            for kind, (w1d, w2d, posd) in enumerate(((cw1k, cw2k, pos_k), (cw1v, cw2v, pos_v))):
                dma("sp", stg.ap.rearrange("p (l h) -> p l h", l=32), w1d.rearrange("(l d) h -> d l h", d=64), [], [stg.b], "Lcstg")
                for q4 in range(4):
                    cast(W1c, W1c.ap[:, q4 * 8:(q4 + 1) * 8, :], stg, stg.ap.rearrange("p (l h) -> p l h", l=32)[:, q4 * 8:(q4 + 1) * 8, :])
                dma("sp", stg2.ap, w2d.rearrange("(c p) n -> p c n", p=128), [], [stg2.b], "Lcstg2")
                cast(W2c, W2c.ap, stg2, stg2.ap)
                dma("sp", posf.ap, posd.rearrange("l d -> d l"), [], [posf.b], "Lposf", slow=True)
                cast(posT, posT.ap, posf, posf.ap)
                for hc in range(2):
                    for l in range(32):
                        mm(pb_, pb_.ap[:, 0:1], W1c, W1c.ap[:, l, hc * 128:(hc + 1) * 128], posT, posT.ap[:, l:l + 1], l == 0, l == 31)
                    S.op("dve", lambda e, hc=hc: e.tensor_copy(biash.ap[:, hc:hc + 1], pb_.ap[:, 0:1]), reads=[pb_.b], writes=[biash.b])
                for g in range(4):
                    load(src, kf1[4 * kind + g])
                    for hc in range(2):
                        p_ = ph.next()
                        for l in range(32):
                            mm(p_, p_.ap[:, 0:511], W1c, W1c.ap[:, l, hc * 128:(hc + 1) * 128], src, src.ap[:, l:l + 16 * 510 + 1:16], l == 0, l == 31)
                        S.op("act", lambda e, p_=p_, hc=hc: e.activation(u_t.ap[:, 0:511], p_.ap[:, 0:511], AF.Identity, bias=biash.ap[:, hc:hc + 1]),
                             reads=[p_.b, biash.b], writes=[u_t.b])
                        S.op("act", lambda e: e.activation(w_t.ap[:, 0:511], u_t.ap[:, 0:511], AF.Square), reads=[u_t.b], writes=[w_t.b])
                        S.op("dve", lambda e: e.tensor_scalar(w_t.ap[:, 0:511], w_t.ap[:, 0:511], 0.044715, 1.0, ALU.mult, ALU.add), reads=[w_t.b], writes=[w_t.b])
                        S.op("dve", lambda e: e.tensor_tensor(w_t.ap[:, 0:511], w_t.ap[:, 0:511], u_t.ap[:, 0:511], ALU.mult), reads=[w_t.b, u_t.b], writes=[w_t.b])
                        S.op("act", lambda e: e.activation(w_t.ap[:, 0:511], w_t.ap[:, 0:511], AF.Sigmoid, scale=2.0 * 0.7978845608028654), reads=[w_t.b], writes=[w_t.b])
                        S.op("dve", lambda e, hc=hc: e.tensor_tensor(hid.ap[:, hc, 0:511], w_t.ap[:, 0:511], u_t.ap[:, 0:511], ALU.mult), reads=[w_t.b, u_t.b], writes=[hid.b])
                    if kind == 0:
                        p2 = po.next()
                        for hc in range(2):
                            mm(p2, p2.ap[0:64, 0:511], W2c, W2c.ap[:, hc, :], hid, hid.ap[:, hc, 0:511], hc == 0, hc == 1)
                        S.op("dve", lambda e, p2=p2: e.tensor_copy(evk.ap[:, 0:511], p2.ap[0:64, 0:511]), reads=[p2.b], writes=[evk.b])
                        store(kcmp[g], evk)
                    else:
                        for nchunk in range(4):
                            p2 = po.next()
                            for hc in range(2):
                                mm(p2, p2.ap[:, 0:64], hid, hid.ap[:, hc, nchunk * 128:(nchunk + 1) * 128], W2c, W2c.ap[:, hc, :], hc == 0, hc == 1)
                            ev = evv.next()
                            S.op("dve", lambda e, p2=p2, ev=ev: e.tensor_copy(ev.ap, p2.ap[:, 0:64]), reads=[p2.b], writes=[ev.b])
                            store(vcmp[g, nchunk * 128:(nchunk + 1) * 128, :], ev)

        phase_compress()
        phase_reset()

        def attention_l1():
            mask_c = AR.alloc(128, (4, 512), BF16, "mask_c")
            load(mask_c, cd["mask_c"])
            mask_cmp = AR.alloc(128, (5, 512), BF16, "mask_cmp")
            load(mask_cmp, cd["mask_cmp"])
            mask_win = AR.alloc(128, (8, 512), BF16, "mask_win")
            load(mask_win, cd["mask_win"])
            ewide = AR.alloc(128, (SEQ,), BF16, "ewide")
            load(ewide, cd["ewide"])
            ovl = AR.alloc(128, (4, 128), BF16, "ovl")
            load(ovl, cd["ovl"])
            gsel = AR.alloc(48, (48 * 64,), BF16, "gsel")
            load(gsel, cd["gsel"])
            tka = AR.alloc(128, (256,), F32, "tka")
            load(tka, cd["topk_a"])
            tkm = AR.alloc(128, (256,), F32, "tkm")
            load(tkm, cd["topk_m"])
            bcmp = AR.alloc(128, (16 * NQT * 4,), F32, "bcmp")
            load(bcmp, cd["bias_cmp"])
            KcA = AR.alloc(70, (512,), BF16, "KcA")
            dma("sp", KcA.ap[64:70, :], cd["kaug_cmp"], [], [KcA.b], "LKcA")
            Vc = AR.alloc(128, (4, 64), BF16, "Vc")
            KsA = AR.alloc(70, (SEQ,), BF16, "KsA")
            KwA = AR.alloc(70, (SEQ,), BF16, "KwA")
            dma("sp", KsA.ap[64:70, :], cd["kaug"], [], [KsA.b], "LKsA")
            dma("sp", KwA.ap[64:70, :], cd["kaug"], [], [KwA.b], "LKwA")
            Vs = AR.alloc(128, (NKT, 64), BF16, "Vs")
            Vw = AR.alloc(128, (NKT, 64), BF16, "Vw")
            QAs = [[AR.alloc(70, (512,), BF16, f"QA{r}{i}") for i in range(2)] for r in range(4)]
            Gs = [AR.alloc(48, (512,), BF16, f"G{i}") for i in range(2)]
            Pc = Ring([AR.alloc(128, (512,), BF16, f"Pc{i}") for i in range(8)])
            Ps = Ring([AR.alloc(128, (512,), BF16, f"Pp{i}") for i in range(3)])
            pcn = Ring([AR.alloc(128, (512,), BF16, f"pcn{i}") for i in range(4)])
            rdc = AR.alloc(128, (512,), F32, "rdc")
            ocmp = [AR.alloc(64, (512,), F32, f"ocmp{r}") for r in range(4)]
            selbT = AR.alloc(128, (512,), BF16, "selbT")
            imp2 = AR.alloc(128, (128,), F32, "imp2")
            tmp2 = AR.alloc(128, (128,), F32, "tmp2")
            selm = AR.alloc(128, (128,), F32, "selm")
            selb = AR.alloc(128, (128,), BF16, "selb")
            v8a = AR.alloc(128, (8,), F32, "v8a")
            v8b = AR.alloc(128, (8,), F32, "v8b")
            rs_ = AR.alloc(64, (512,), F32, "rs_")
            ob = AR.alloc(64, (512,), F32, "ob")
            acc = AR.alloc(64, (512,), F32, "acc")
            t1 = AR.alloc(64, (512,), F32, "t1")
            oev = Ring([AR.alloc(64, (512,), BF16, f"oev{i}") for i in range(2)])
            Ys = Ring([PS(0, "Y0"), PS(1, "Y1")])
            NUM = PS(2, "NUM")
            DEN = PS(3, "DEN")
            IMP = PS(4, "IMP")
            TRP = PS(5, "TRP", cols=1024, dt=BF16)
            GB = PS(6, "GB")

            def softmax_tile(Y, P, bias, V, vap, first, last, den_parts):
                S.op("act", lambda e: e.activation(P.ap, Y.ap, AF.Exp, bias=bias[0]), reads=[Y.b, bias[1]], writes=[P.b])
                mm(NUM, NUM.ap[0:64, :], V, vap, P, P.ap, first, last)
                mm(DEN, DEN.ap[0:den_parts, :], ones, ones.ap[:, 0:den_parts], P, P.ap, first, last)

            for g in range(4):
                dma("sp", KcA.ap[0:64, :], kcmp[g], [], [KcA.b], "LKcA")
                dma("sp", Vc.ap, vcmp[g].rearrange("(c p) d -> p c d", p=128), [], [Vc.b], "LVc")
                dma("sp", KsA.ap[0:64, :], kf1[8 + g], [], [KsA.b], "LKsA")
                dma("sp", KwA.ap[0:64, :], kf1[12 + g], [], [KwA.b], "LKwA")
                for g4 in range(4):
                    dma("sp", Vs.ap[:, g4 * 16:(g4 + 1) * 16, :], v1[g4 * 2048:(g4 + 1) * 2048, g * 64:(g + 1) * 64].rearrange("(t p) c -> p t c", p=128), [], [Vs.b], "LVs")
                    dma("sp", Vw.ap[:, g4 * 16:(g4 + 1) * 16, :], v1[g4 * 2048:(g4 + 1) * 2048, 256 + g * 64:256 + (g + 1) * 64].rearrange("(t p) c -> p t c", p=128), [], [Vw.b], "LVw")
                for qt in range(NQT):
                    QA = [QAs[r][qt % 2] for r in range(4)]
                    G = Gs[qt % 2]
                    for r in range(4):
                        h = 4 * g + r
                        dma("sp", QA[r].ap[0:64, :], q1[h, :, qt * 512:(qt + 1) * 512], [], [QA[r].b], "L" + QA[r].b.name)
                        dma("sp", QA[r].ap[64:70, :], cd["qaug"][:, 4 + h, :], [], [QA[r].b], "L" + QA[r].b.name)
                    dma("sp", G.ap, gts[:, qt * 512:(qt + 1) * 512], [], [G.b], "L" + G.b.name)
                    chunks = [c for c in range(4) if 4 * c <= qt]
                    for r in range(4):
                        h = 4 * g + r
                        Pl = []
                        for ci, c in enumerate(chunks):
                            rel = qt - 4 * c
                            Y = Ys.next()
                            mm(Y, Y.ap, KcA, KcA.ap[:, c * 128:(c + 1) * 128], QA[r], QA[r].ap, True, rel > 4)
                            if rel <= 4:
                                mm(Y, Y.ap, ident, ident.ap, mask_cmp, mask_cmp.ap[:, rel, :], False, True)
                            P = Pc.next()
                            col = (h * NQT + qt) * 4 + c
                            softmax_tile(Y, P, (bcmp.ap[:, col:col + 1], bcmp.b), Vc, Vc.ap[:, c, :], ci == 0, ci == len(chunks) - 1, 128)
                            Pl.append((c, P))
                        S.op("dve", lambda e: e.tensor_scalar(rdc.ap, DEN.ap, 1e-30, None, ALU.add), reads=[DEN.b], writes=[rdc.b])
                        S.op("dve", lambda e: e.reciprocal(rdc.ap, rdc.ap), reads=[rdc.b], writes=[rdc.b])
                        S.op("dve", lambda e, r=r: e.tensor_tensor(ocmp[r].ap, NUM.ap[0:64, :], rdc.ap[0:64, :], ALU.mult), reads=[NUM.b, rdc.b], writes=[ocmp[r].b])
                        for ci, (c, P) in enumerate(Pl):
                            pn = pcn.next()
                            S.op("dve", lambda e, pn=pn, P=P: e.tensor_tensor(pn.ap, P.ap, rdc.ap, ALU.mult), reads=[P.b, rdc.b], writes=[pn.b])
                            for qs in range(4):
                                mm(IMP, IMP.ap[:, qs * 128:(qs + 1) * 128], pn, pn.ap[:, qs * 128:(qs + 1) * 128], ovl, ovl.ap[:, c, :],
                                   r == 0 and ci == 0, r == 3 and ci == len(Pl) - 1)
                    for qs in range(4):
                        off = 127 - 2 * (4 * qt + qs)
                        S.op("dve", lambda e, qs=qs, off=off: e.tensor_tensor(tmp2.ap, IMP.ap[:, qs * 128:(qs + 1) * 128], tkm.ap[:, off:off + 128], ALU.mult),
                             reads=[IMP.b, tkm.b], writes=[tmp2.b])
                        S.op("dve", lambda e, off=off: e.tensor_tensor(imp2.ap, tmp2.ap, tka.ap[:, off:off + 128], ALU.add), reads=[tmp2.b, tka.b], writes=[imp2.b])
                        S.op("dve", lambda e: e.memset(imp2.ap[:, 0:1], 1.0e6), reads=[], writes=[imp2.b])
                        S.op("dve", lambda e: e.max(v8a.ap, imp2.ap), reads=[imp2.b], writes=[v8a.b])
                        S.op("dve", lambda e: e.match_replace(tmp2.ap, v8a.ap, imp2.ap, -9.0), reads=[imp2.b, v8a.b], writes=[tmp2.b])
                        S.op("dve", lambda e: e.max(v8b.ap, tmp2.ap), reads=[tmp2.b], writes=[v8b.b])
                        S.op("dve", lambda e: e.tensor_scalar(selm.ap, imp2.ap, v8b.ap[:, 7:8], 0.0, ALU.is_ge, ALU.add), reads=[imp2.b, v8b.b], writes=[selm.b])
                        S.op("dve", lambda e: e.scalar_tensor_tensor(selm.ap, imp2.ap, 0.0, selm.ap, ALU.is_ge, ALU.mult), reads=[imp2.b, selm.b], writes=[selm.b])
                        S.op("dve", lambda e: e.tensor_scalar(selb.ap, selm.ap, -1.0, -NEG, ALU.add, ALU.mult), reads=[selm.b], writes=[selb.b])
                        S.op("pe", lambda e, qs=qs: e.transpose(TRP.ap[:, qs * 128:(qs + 1) * 128], selb.ap, ident.ap), reads=[selb.b, ident.b], writes=[TRP.b])
                    S.op("dve", lambda e: e.tensor_copy(selbT.ap, TRP.ap[:, 0:512]), reads=[TRP.b], writes=[selbT.b])
                    for r in range(4):
                        h = 4 * g + r
                        si = 4 + h
                        kts = ktiles_for(qt, None)
                        for i, kt in enumerate(kts):
                            o = kt - 4 * qt
                            Y = Ys.next()
                            mm(Y, Y.ap, KsA, KsA.ap[:, kt * 128:(kt + 1) * 128], QA[r], QA[r].ap, True, False)
                            mm(Y, Y.ap, ewide, ewide.ap[:, kt * 128:(kt + 1) * 128], selbT, selbT.ap, False, o < 0)
                            if o >= 0:
                                mm(Y, Y.ap, ident, ident.ap, mask_c, mask_c.ap[:, o, :], False, True)
                            P = Ps.next()
                            softmax_tile(Y, P, (bias_ap(si, 4 * qt - kt), bias_tab.b), Vs, Vs.ap[:, kt, :], i == 0, i == len(kts) - 1, 64)
                        S.op("dve", lambda e: e.reciprocal(rs_.ap, DEN.ap[0:64, :]), reads=[DEN.b], writes=[rs_.b])
                        S.op("dve", lambda e: e.tensor_tensor(ob.ap, NUM.ap[0:64, :], rs_.ap, ALU.mult), reads=[NUM.b, rs_.b], writes=[ob.b])
                        mm(GB, GB.ap[0:64, :], gsel, gsel.ap[:, (3 * h + 1) * 64:(3 * h + 2) * 64], G, G.ap, True, True)
                        S.op("dve", lambda e: e.tensor_tensor(acc.ap, GB.ap[0:64, :], ob.ap, ALU.mult), reads=[GB.b, ob.b], writes=[acc.b])
                        mm(GB, GB.ap[0:64, :], gsel, gsel.ap[:, (3 * h) * 64:(3 * h + 1) * 64], G, G.ap, True, True)
                        S.op("dve", lambda e, r=r: e.tensor_tensor(t1.ap, GB.ap[0:64, :], ocmp[r].ap, ALU.mult), reads=[GB.b, ocmp[r].b], writes=[t1.b])
                        S.op("pool", lambda e: e.tensor_tensor(acc.ap, acc.ap, t1.ap, ALU.add), reads=[acc.b, t1.b], writes=[acc.b])
                        kts = [kt for kt in range(4 * qt + 3, 4 * qt - 5, -1) if kt >= 0]
                        for i, kt in enumerate(kts):
                            o = kt - 4 * qt
                            Y = Ys.next()
                            mm(Y, Y.ap, KwA, KwA.ap[:, kt * 128:(kt + 1) * 128], QA[r], QA[r].ap, True, False)
                            mm(Y, Y.ap, ident, ident.ap, mask_win, mask_win.ap[:, o + 4, :], False, True)
                            P = Ps.next()
                            softmax_tile(Y, P, (bias_ap(si, 4 * qt - kt), bias_tab.b), Vw, Vw.ap[:, kt, :], i == 0, i == len(kts) - 1, 64)
                        S.op("dve", lambda e: e.reciprocal(rs_.ap, DEN.ap[0:64, :]), reads=[DEN.b], writes=[rs_.b])
                        S.op("dve", lambda e: e.tensor_tensor(ob.ap, NUM.ap[0:64, :], rs_.ap, ALU.mult), reads=[NUM.b, rs_.b], writes=[ob.b])
                        mm(GB, GB.ap[0:64, :], gsel, gsel.ap[:, (3 * h + 2) * 64:(3 * h + 3) * 64], G, G.ap, True, True)
                        S.op("dve", lambda e: e.tensor_tensor(t1.ap, GB.ap[0:64, :], ob.ap, ALU.mult), reads=[GB.b, ob.b], writes=[t1.b])
                        ev = oev.next()
                        S.op("pool", lambda e, ev=ev: e.tensor_tensor(ev.ap, acc.ap, t1.ap, ALU.add), reads=[acc.b, t1.b], writes=[ev.b])
                        store(ot[h * 64:(h + 1) * 64, qt * 512:(qt + 1) * 512], ev)

        attention_l1()
        phase_reset()
        phase_mlp(x2, od_w_out, mlp_w1[1], mlp_w2[1], 3, out_d, final_gain=final_norm)
        S.barrier()
        S.finalize()
    return nc, S


def kernel(**inputs):
    consts = make_consts()
    nc, S = build_program(consts)
    x = np.ascontiguousarray(inputs["x"], dtype=np.float32)
    B = x.shape[0]
    shared = {}
    for k in ("attn_norm", "mlp_norm", "final_norm", "mlp_w1", "mlp_w2"):
        shared[k] = np.ascontiguousarray(inputs[k], dtype=np.float32)
    for k in ("ev_w_in", "ev_subln", "ev_w_out", "od_w_in", "od_cmp_pos_k", "od_cmp_k_w1", "od_cmp_k_w2",
              "od_cmp_pos_v", "od_cmp_v_w1", "od_cmp_v_w2", "od_w_out"):
        shared[k] = np.ascontiguousarray(inputs[k][0], dtype=np.float32)
    shared["lamv"] = np.ascontiguousarray(np.stack([inputs["ev_lam_q1"][0], inputs["ev_lam_k1"][0],
                                                    inputs["ev_lam_q2"][0], inputs["ev_lam_k2"][0]]), dtype=np.float32)
    for k, v in consts.items():
        shared["c_" + k] = v
    in_maps = []
    for core in range(8):
        m = dict(shared)
        m["x"] = x[core % B]
        in_maps.append(m)
    res = run_bass_kernel_spmd(nc, in_maps, core_ids=list(range(8)))
    out = np.stack([np.asarray(res.results[b]["out"], dtype=np.float32) for b in range(B)])
    return out
```

```python
import math
import numpy as np
import ml_dtypes
from contextlib import ExitStack
import concourse.bass as bass
import concourse.mybir as mybir
from concourse.bass_utils import run_bass_kernel_spmd

F32 = mybir.dt.float32
BF16 = mybir.dt.bfloat16
AF = mybir.ActivationFunctionType
ALU = mybir.AluOpType
bf = ml_dtypes.bfloat16

SEQ = 8192
D = 1024
NQT = SEQ // 512
NKT = SEQ // 128
NEG = -30000.0
SB_BACK = 4
ALIBI_CUT = 144.0
SEM_WRAP = 16000
DMA_WRAP = 1000
STOP_AFTER = None


class Buf:
    __slots__ = ("name", "w", "r")

    def __init__(self, name):
        self.name = name
        self.w = []
        self.r = []


class Op:
    __slots__ = ("eng", "fn", "deps", "idx", "needs_inc", "waits", "is_dma", "dkey", "dval")

    def __init__(self, eng, fn):
        self.eng = eng
        self.fn = fn
        self.deps = set()
        self.idx = -1
        self.needs_inc = False
        self.waits = []
        self.is_dma = False
        self.dkey = None
        self.dval = 0


class T:
    __slots__ = ("ap", "b")

    def __init__(self, ap, b):
        self.ap = ap
        self.b = b

    def __getitem__(self, k):
        return self.ap[k]


class Sched:
    ENGS = ("pe", "act", "dve", "pool", "sp")

    def __init__(self, nc, stack):
        self.nc = nc
        self.stack = stack
        self.ops = []
        self.eng_ops = {e: [] for e in self.ENGS}
        self.dma_count = {}
        self.bufs = []

    def buf(self, name):
        b = Buf(name)
        self.bufs.append(b)
        return b

    def op(self, eng, fn, reads=(), writes=(), dma_key=None):
        o = Op(eng, fn)
        oid = len(self.ops)
        for b in reads:
            o.deps.update(b.w)
        for b in writes:
            o.deps.update(b.w)
            o.deps.update(b.r)
        for b in reads:
            b.r.append(oid)
        for b in writes:
            b.w = [oid]
            b.r = []
        if dma_key is not None:
            o.is_dma = True
            o.dkey = dma_key
            n = self.dma_count.get(dma_key, 0) + 1
            self.dma_count[dma_key] = n
            o.dval = n
        o.idx = len(self.eng_ops[eng])
        self.eng_ops[eng].append(o)
        self.ops.append(o)
        return oid

    def barrier(self):
        live = [b for b in self.bufs if b.w or b.r]
        first = True
        sync = self.buf("barrier")
        for e in self.ENGS:
            if first:
                self.op(e, lambda eng: eng.nop(), writes=live + [sync])
                first = False
            else:
                self.op(e, lambda eng: eng.nop(), reads=[sync])
        self.bufs = [sync]

    def finalize(self):
        nc = self.nc
        ops = self.ops
        know = {e: {} for e in self.ENGS}
        comp_know = [None] * len(ops)
        for oid, o in enumerate(ops):
            K = know[o.eng]
            for d in sorted(o.deps):
                p = ops[d]
                if p.is_dma:
                    dom = ("d", p.dkey)
                    val = p.dval
                else:
                    dom = ("e", p.eng)
                    val = p.idx + 1
                    if p.eng == "pe" and o.eng == "pe":
                        continue
                if K.get(dom, 0) >= val:
                    continue
                o.waits.append((dom, val))
                p.needs_inc = True
                for k2, v2 in comp_know[d].items():
                    if K.get(k2, 0) < v2:
                        K[k2] = v2
            ck = dict(K)
            if o.is_dma:
                ck[("d", o.dkey)] = o.dval
            else:
                ck[("e", o.eng)] = o.idx + 1
            comp_know[oid] = ck
        comp_know = None
        eng_sems = {}
        counts = {}
        for e in self.ENGS:
            c = 0
            for o in self.eng_ops[e]:
                if o.is_dma:
                    continue
                if o.needs_inc:
                    c += 1
                counts[(e, o.idx)] = c
            nsem = (c + SEM_WRAP - 1) // SEM_WRAP
            eng_sems[e] = [self.stack.enter_context(nc.semaphore(f"s_{e}_{i}")) for i in range(nsem)]
        dma_sems = {}
        for k, n in self.dma_count.items():
            nsem = (n + DMA_WRAP - 1) // DMA_WRAP
            dma_sems[k] = [self.stack.enter_context(nc.semaphore(f"d_{k}_{i}")) for i in range(nsem)]

        def sem_for(dom, val):
            if dom[0] == "e":
                c = counts[(dom[1], val - 1)]
                return eng_sems[dom[1]][(c - 1) // SEM_WRAP], (c - 1) % SEM_WRAP + 1
            return dma_sems[dom[1]][(val - 1) // DMA_WRAP], ((val - 1) % DMA_WRAP + 1) * 16

        self.n_waits = sum(len(o.waits) for o in ops)
        with nc.Block() as block:
            def make(e):
                def body(eng):
                    for o in self.eng_ops[e]:
                        for dom, val in o.waits:
                            s, v = sem_for(dom, val)
                            eng.wait_ge(s, v)
                        ins = o.fn(eng)
                        if o.is_dma:
                            s, v = sem_for(("d", o.dkey), o.dval)
                            ins.then_inc(s, 16)
                        elif o.needs_inc:
                            c = counts[(e, o.idx)]
                            ins.then_inc(eng_sems[e][(c - 1) // SEM_WRAP], 1)
                return body
            block.tensor(make("pe"))
            block.scalar(make("act"))
            block.vector(make("dve"))
            block.gpsimd(make("pool"))
            block.sync(make("sp"))


class Arena:
    def __init__(self, S, tens, nwords):
        self.S = S
        self.t = tens
        self.n = nwords
        self.off = 0
        self.cnt = 0

    def reset(self):
        self.off = 0

    def alloc(self, parts, shape, dt, name):
        n = 1
        for s in shape:
            n *= s
        words = n if dt == F32 else (n + 1) // 2
        v = self.t[0:parts, self.off:self.off + words]
        self.off += words
        assert self.off <= self.n, f"arena overflow at {name}: {self.off} > {self.n}"
        if dt != F32:
            v = v.bitcast(dt)
        if len(shape) == 2:
            v = v.rearrange("p (a b) -> p a b", a=shape[0])
        elif len(shape) == 3:
            v = v.rearrange("p (a b c) -> p a b c", a=shape[0], b=shape[1])
        self.cnt += 1
        return T(v, self.S.buf(name))


def _split3(v):
    v = np.asarray(v, np.float32)
    hi = v.astype(bf)
    r1 = v - hi.astype(np.float32)
    mid = r1.astype(bf)
    r2 = r1 - mid.astype(np.float32)
    lo = r2.astype(bf)
    return hi, mid, lo


DIFF_SLOPES = [2.0 ** (-8.0 * (h + 1) / 4) for h in range(4)]
NSA_SLOPES = [2.0 ** (-8.0 * (h + 1) / 16) for h in range(16)]
ALL_SLOPES = DIFF_SLOPES + NSA_SLOPES
BIAS_M0 = -3
BIAS_NM = 68


def make_consts():
    c = {}
    j = np.arange(128)
    t = np.arange(512)
    c["ident"] = np.eye(128, dtype=np.float32).astype(bf)
    c["ones"] = np.ones((128, 128), np.float32).astype(bf)
    c["negones"] = (-np.ones((128, 128), np.float32)).astype(bf)
    c["uneg"] = (-(j[:, None] >= j[None, :]).astype(np.float32)).astype(bf)
    c["onesdiv"] = np.full((128, 128), 1.0 / 128, np.float32)
    msb = np.zeros((4, 128, 512), np.float32)
    mc = np.zeros((4, 128, 512), np.float32)
    for o in range(4):
        jj = 128 * o + j[:, None]
        msb[o] = np.where(jj >= t[None, :], NEG, 0.0)
        mc[o] = np.where(jj > t[None, :], NEG, 0.0)
    c["mask_sb"] = np.ascontiguousarray(msb.transpose(1, 0, 2)).astype(bf)
    c["mask_c"] = np.ascontiguousarray(mc.transpose(1, 0, 2)).astype(bf)
    kr = np.zeros((6, SEQ), np.float32)
    kr[0:3] = 1.0
    kr[3:6] = (np.arange(SEQ) % 128)[None, :]
    c["kaug"] = kr.astype(bf)
    kc = np.zeros((6, 512), np.float32)
    kc[0:3] = 1.0
    kc[3:6] = (16 * (np.arange(512) % 128))[None, :]
    c["kaug_cmp"] = kc.astype(bf)
    qa = np.zeros((len(ALL_SLOPES), 6, 512), np.float32).astype(bf)
    for i, s in enumerate(ALL_SLOPES):
        s32 = np.float32(s)
        v = (-(s32 * t.astype(np.float32))).astype(np.float32)
        h3 = _split3(v)
        s3 = _split3(np.full(512, s32, np.float32))
        for r in range(3):
            qa[i, r] = h3[r]
            qa[i, 3 + r] = s3[r]
    c["qaug"] = np.ascontiguousarray(qa.transpose(1, 0, 2))
    bt = np.zeros((len(ALL_SLOPES), BIAS_NM), np.float32)
    for i, s in enumerate(ALL_SLOPES):
        for mi in range(BIAS_NM):
            bt[i, mi] = -np.float32(s) * np.float32(128 * (mi + BIAS_M0))
    c["bias_tab"] = np.broadcast_to(bt.reshape(1, -1), (128, bt.size)).copy()
    bc = np.zeros((16, NQT, 4), np.float32)
    for h, s in enumerate(NSA_SLOPES):
        for qt in range(NQT):
            for cc in range(4):
                bc[h, qt, cc] = -np.float32(s) * np.float32(512 * qt - 2048 * cc - 31)
    c["bias_cmp"] = np.broadcast_to(bc.reshape(1, -1), (128, bc.size)).copy()
    mcm = np.zeros((5, 128, 512), np.float32)
    for rel in range(5):
        mcm[rel] = np.where(512 * rel + t[None, :] >= 16 * j[:, None] + 31, 0.0, NEG)
    c["mask_cmp"] = np.ascontiguousarray(mcm.transpose(1, 0, 2)).astype(bf)
    mw = np.zeros((8, 128, 512), np.float32)
    for oi, o in enumerate(range(-4, 4)):
        dd = t[None, :] - j[:, None] - 128 * o
        mw[oi] = np.where((dd >= 0) & (dd <= 511), 0.0, NEG)
    c["mask_win"] = np.ascontiguousarray(mw.transpose(1, 0, 2)).astype(bf)
    cc = np.arange(SEQ)
    c["ewide"] = (cc[None, :] // 64 == j[:, None]).astype(np.float32).astype(bf)
    n = np.arange(512)
    s_ = np.arange(128)
    ov = ((n[:, None] >= 4 * s_[None, :] - 1) & (n[:, None] <= 4 * s_[None, :] + 3) & (n[:, None] < 511))
    c["ovl"] = np.ascontiguousarray(ov.astype(np.float32).reshape(4, 128, 128).transpose(1, 0, 2)).astype(bf)
    q = np.arange(128)
    u = np.arange(-127, 129)
    cur = (q >= 64).astype(np.int64)
    A = np.zeros((128, 256), np.float32)
    M = np.ones((128, 256), np.float32)
    fut = u[None, :] > cur[:, None]
    A[fut] = -1.0
    M[fut] = 0.0
    f1 = u[None, :] == cur[:, None]
    f2 = u[None, :] == cur[:, None] - 1
    A[f1] = 1.0e6 + 1.0
    M[f1] = 0.0
    A[f2] = 1.0e6 + 2.0
    M[f2] = 0.0
    c["topk_a"] = A
    c["topk_m"] = M
    gs = np.zeros((48, 48, 64), np.float32)
    for r in range(48):
        gs[r, r, :] = 1.0
    c["gsel"] = gs.reshape(48, 48 * 64).astype(bf)
    return c


CONST_SPECS = None


def _dt_of(a):
    return BF16 if a.dtype == bf else F32


def build_program(consts):
    nc = bass.Bass("TRN2", target_bir_lowering=False)
    dr = {}

    def din(name, shape, dt=F32):
        dr[name] = nc.dram_tensor(name, list(shape), dt, kind="ExternalInput").ap()
        return dr[name]

    def dscr(name, shape, dt):
        dr[name] = nc.dram_tensor(name, list(shape), dt, kind="Internal").ap()
        return dr[name]

    x_in = din("x", (SEQ, D))
    attn_norm = din("attn_norm", (2, D))
    mlp_norm = din("mlp_norm", (2, D))
    final_norm = din("final_norm", (D,))
    ev_w_in = din("ev_w_in", (D, 3072))
    lamv = din("lamv", (4, 64))
    ev_subln = din("ev_subln", (128,))
    ev_w_out = din("ev_w_out", (D, D))
    od_w_in = din("od_w_in", (D, 2608))
    pos_k = din("od_cmp_pos_k", (32, 64))
    cw1k = din("od_cmp_k_w1", (2048, 256))
    cw2k = din("od_cmp_k_w2", (256, 64))
    pos_v = din("od_cmp_pos_v", (32, 64))
    cw1v = din("od_cmp_v_w1", (2048, 256))
    cw2v = din("od_cmp_v_w2", (256, 64))
    od_w_out = din("od_w_out", (D, D))
    mlp_w1 = din("mlp_w1", (2, D, 4096))
    mlp_w2 = din("mlp_w2", (2, 4096, D))
    cd = {k: din("c_" + k, v.shape, _dt_of(v)) for k, v in consts.items()}
    out_d = nc.dram_tensor("out", [SEQ, D], F32, kind="ExternalOutput").ap()

    qk0 = dscr("qk0", (32, 64, SEQ), BF16)
    v0 = dscr("v0", (SEQ, 1024), BF16)
    ot = dscr("ot", (D, SEQ), BF16)
    x2 = dscr("x2", (SEQ, D), F32)
    q1 = dscr("q1", (16, 64, SEQ), BF16)
    kf1 = dscr("kf1", (16, 64, SEQ), BF16)
    v1 = dscr("v1", (SEQ, 512), BF16)
    gts = dscr("gts", (48, SEQ), BF16)
    kcmp = dscr("kcmp", (4, 64, 512), BF16)
    vcmp = dscr("vcmp", (4, 512, 64), BF16)
    x1s = dscr("x1s", (SEQ, D), F32)
    uts_d = dscr("uts", (32, 128, SEQ), BF16)

    with ExitStack() as st:
        S = Sched(nc, st)
        NW = 50 * 1024
        arena_t = st.enter_context(nc.sbuf_tensor("arena", [128, NW], F32))
        AR = Arena(S, arena_t, NW)
        pbanks = [st.enter_context(nc.psum_tensor(f"pb{i}", [128, 512], F32)) for i in range(8)]

        def PS(i, name, parts=128, cols=512, dt=F32):
            ap = pbanks[i][0:parts, :]
            if dt != F32:
                ap = ap.bitcast(dt)
            ap = ap[:, 0:cols]
            return T(ap, S.buf(name))

        def dma(q, out, in_, reads, writes, key, slow=False):
            if slow:
                S.op(q, lambda e: e.dma_start(out=out, in_=in_, allow_slow_non_contiguous=True), reads=reads, writes=writes, dma_key=key)
            else:
                S.op(q, lambda e: e.dma_start(out=out, in_=in_), reads=reads, writes=writes, dma_key=key)

        def load_vt(VT, src_cols, width):
            for g4 in range(4):
                dma("sp", VT.ap[:, g4 * 16:(g4 + 1) * 16, 0:width],
                    src_cols[g4 * 2048:(g4 + 1) * 2048, :].rearrange("(t p) c -> p t c", p=128), [], [VT.b], "LVT")

        def load(dst, src, q="sp"):
            dma(q, dst.ap, src, [], [dst.b], "L" + dst.b.name)

        def store(dst, src, ap=None, q="pool"):
            dma(q, dst, src.ap if ap is None else ap, [src.b], [], "S" + src.b.name)

        def mm(out, outap, lhsT, lhsap, rhs, rhsap, start, stop, extra_r=()):
            S.op("pe", lambda e: e.matmul(outap, lhsap, rhsap, start=start, stop=stop),
                 reads=[lhsT.b, rhs.b] + list(extra_r), writes=[out.b])

        class Ring:
            def __init__(self, items):
                self.items = items
                self.i = 0

            def next(self):
                it = self.items[self.i % len(self.items)]
                self.i += 1
                return it

        rr_cast = [0]

        def cast(dst, dstap, src, srcap, scale_ap=None, scale_t=None):
            e = ("dve", "act", "pool")[rr_cast[0] % 3] if scale_ap is None else ("dve", "act")[rr_cast[0] % 2]
            rr_cast[0] += 1
            rd = [src.b] + ([scale_t.b] if scale_t is not None else [])
            if e == "act":
                if scale_ap is None:
                    S.op("act", lambda en: en.copy(dstap, srcap), reads=rd, writes=[dst.b])
                else:
                    S.op("act", lambda en: en.activation(dstap, srcap, AF.Copy, scale=scale_ap), reads=rd, writes=[dst.b])
            elif e == "dve":
                if scale_ap is None:
                    S.op("dve", lambda en: en.tensor_copy(dstap, srcap), reads=rd, writes=[dst.b])
                else:
                    S.op("dve", lambda en: en.tensor_scalar(dstap, srcap, scale_ap, None, ALU.mult), reads=rd, writes=[dst.b])
            else:
                S.op("pool", lambda en: en.tensor_copy(dstap, srcap), reads=rd, writes=[dst.b])

        def load_weight(dst, src_ap, K, N, stage_ring, gain=None, col0=0, ncols=None, dcol0=0):
            ncols = N if ncols is None else ncols
            for k in range(K // 128):
                c = 0
                while c < ncols:
                    w = min(2048, ncols - c)
                    stg = stage_ring.next()
                    dma("sp", stg.ap[:, 0:w], src_ap[k * 128:(k + 1) * 128, col0 + c:col0 + c + w], [], [stg.b], "L" + stg.b.name)
                    cast(dst, dst.ap[:, k, dcol0 + c:dcol0 + c + w], stg, stg.ap[:, 0:w],
                         None if gain is None else gain.ap[:, k:k + 1], gain)
                    c += w

        NWP = 0
        ident = AR.alloc(128, (128,), BF16, "ident")
        load(ident, cd["ident"])
        ones = AR.alloc(128, (128,), BF16, "ones")
        load(ones, cd["ones"])
        negones = AR.alloc(128, (128,), BF16, "negones")
        load(negones, cd["negones"])
        uneg = AR.alloc(128, (128,), BF16, "uneg")
        load(uneg, cd["uneg"])
        onesdiv = AR.alloc(128, (128,), F32, "onesdiv")
        load(onesdiv, cd["onesdiv"])
        bias_tab = AR.alloc(128, (len(ALL_SLOPES) * BIAS_NM,), F32, "bias_tab")
        load(bias_tab, cd["bias_tab"])
        gains = AR.alloc(128, (4, 8), F32, "gains")
        for li in range(2):
            dma("sp", gains.ap[:, 2 * li, :], attn_norm[li].rearrange("(k p) -> p k", p=128), [], [gains.b], "Lgains", slow=True)
            dma("sp", gains.ap[:, 2 * li + 1, :], mlp_norm[li].rearrange("(k p) -> p k", p=128), [], [gains.b], "Lgains", slow=True)
        eps_t = AR.alloc(128, (1,), F32, "eps_t")
        S.op("dve", lambda e: e.memset(eps_t.ap, 1e-6), writes=[eps_t.b])
        persist_off = AR.off

        def bias_ap(si, m):
            col = si * BIAS_NM + (m - BIAS_M0)
            return bias_tab.ap[:, col:col + 1]

        def phase_reset():
            S.barrier()
            AR.off = persist_off

        def norm_transpose(xt, hn, ss, rs, junk, tp, hT, r):
            S.op("act", lambda e: e.activation(junk.ap, xt.ap, AF.Square, accum_out=ss.ap), reads=[xt.b], writes=[junk.b, ss.b])
            S.op("act", lambda e: e.activation(rs.ap, ss.ap, AF.Sqrt, bias=eps_t.ap, scale=1.0 / D), reads=[ss.b, eps_t.b], writes=[rs.b])
            S.op("dve", lambda e: e.reciprocal(rs.ap, rs.ap), reads=[rs.b], writes=[rs.b])
            S.op("act", lambda e: e.activation(hn.ap, xt.ap, AF.Copy, scale=rs.ap), reads=[xt.b, rs.b], writes=[hn.b])
            for k in range(8):
                S.op("pe", lambda e, k=k: e.transpose(tp.ap[:, k * 128:(k + 1) * 128], hn.ap[:, k * 128:(k + 1) * 128], ident.ap),
                     reads=[hn.b, ident.b], writes=[tp.b])
            S.op("dve", lambda e: e.tensor_copy(hT.ap[:, :, r * 128:(r + 1) * 128], tp.ap.rearrange("p (k c) -> p k c", k=8)),
                 reads=[tp.b], writes=[hT.b])

        def phase_proj(x_src, w_src, ncols_total, gain_idx, fm_chunks, tm_chunks):
            stage = Ring([AR.alloc(128, (2048,), F32, f"wstg{i}") for i in range(2)])
            W = AR.alloc(128, (8, ncols_total), BF16, "Win")
            gsl = T(gains.ap[:, gain_idx, :], gains.b)
            load_weight(W, w_src, D, ncols_total, stage, gain=gsl)
            xts = Ring([AR.alloc(128, (D,), F32, f"xt{i}") for i in range(2)])
            hns = Ring([AR.alloc(128, (D,), BF16, f"hn{i}") for i in range(2)])
            junk = AR.alloc(128, (D,), BF16, "junk")
            sss = Ring([AR.alloc(128, (1,), F32, f"ss{i}") for i in range(2)])
            rss = Ring([AR.alloc(128, (1,), F32, f"rs{i}") for i in range(2)])
            hTs = Ring([AR.alloc(128, (8, 512), BF16, f"hT{i}") for i in range(2)])
            tps = Ring([PS(i, f"tp{i}", cols=1024, dt=BF16) for i in (0, 1)])
            pps = Ring([PS(i, f"pp{i}") for i in (2, 3, 4, 5)])
            evs = Ring([AR.alloc(128, (512,), BF16, f"ev{i}") for i in range(4)])
            ev_i = [0]
            for tb in range(NQT):
                hT = hTs.next()
                for r in range(4):
                    xt = xts.next()
                    row0 = tb * 512 + r * 128
                    load(xt, x_src[row0:row0 + 128, :])
                    norm_transpose(xt, hns.next(), sss.next(), rss.next(), junk, tps.next(), hT, r)
                for (col0, M, dsts, scale, func) in fm_chunks:
                    pp = pps.next()
                    for k in range(8):
                        mm(pp, pp.ap[0:M, :], W, W.ap[:, k, col0:col0 + M], hT, hT.ap[:, k, :], k == 0, k == 7)
                    ev = evs.next()
                    eng = ("act", "dve")[ev_i[0] % 2] if func is None else "act"
                    ev_i[0] += 1
                    if eng == "act":
                        S.op("act", lambda e, pp=pp, ev=ev, M=M, scale=scale, func=func: e.activation(
                            ev.ap[0:M, :], pp.ap[0:M, :], AF.Copy if func is None else func, scale=scale),
                            reads=[pp.b], writes=[ev.b])
                    else:
                        S.op("dve", lambda e, pp=pp, ev=ev, M=M, scale=scale: e.tensor_scalar(
                            ev.ap[0:M, :], pp.ap[0:M, :], float(scale), None, ALU.mult), reads=[pp.b], writes=[ev.b])
                    for (dfn, p0, npart) in dsts:
                        store(dfn(tb), ev, ap=ev.ap[p0:p0 + npart, :])
                for (col0, ncols, dfn) in tm_chunks:
                    for r in range(4):
                        pp = pps.next()
                        for k in range(8):
                            mm(pp, pp.ap[:, 0:ncols], hT, hT.ap[:, k, r * 128:(r + 1) * 128], W, W.ap[:, k, col0:col0 + ncols], k == 0, k == 7)
                        ev = evs.next()
                        eng = ("act", "dve")[ev_i[0] % 2]
                        ev_i[0] += 1
                        if eng == "act":
                            S.op("act", lambda e, pp=pp, ev=ev, n=ncols: e.copy(ev.ap[:, 0:n], pp.ap[:, 0:n]), reads=[pp.b], writes=[ev.b])
                        else:
                            S.op("dve", lambda e, pp=pp, ev=ev, n=ncols: e.tensor_copy(ev.ap[:, 0:n], pp.ap[:, 0:n]), reads=[pp.b], writes=[ev.b])
                        store(dfn(tb * 512 + r * 128), ev, ap=ev.ap[:, 0:ncols])

        fm = []
        for jc in range(16):
            col0 = 128 * jc if jc < 8 else 1536 + 128 * (jc - 8)
            isq = (jc < 4) or (8 <= jc < 12)
            dsts = []
            for half in range(2):
                slot = 2 * jc + half
                dsts.append((lambda tb, slot=slot: qk0[slot, :, tb * 512:(tb + 1) * 512], 64 * half, 64))
            fm.append((col0, 128, dsts, 0.125 if isq else 1.0, None))
        tm = [(1024, 512, lambda row0: v0[row0:row0 + 128, 0:512]),
              (2560, 512, lambda row0: v0[row0:row0 + 128, 512:1024])]
        phase_proj(x_in, ev_w_in, 3072, 0, fm, tm)
        phase_reset()

        class Pipe:
            def __init__(self):
                self.q = []
                self.t = 0

            def job(self, stages):
                for lag, fn in stages:
                    if lag == 0:
                        fn()
                    else:
                        self.q.append((self.t + lag, fn))
                rest = []
                for due, fn in self.q:
                    if due <= self.t:
                        fn()
                    else:
                        rest.append((due, fn))
                self.q = rest
                self.t += 1

            def flush(self):
                for due, fn in sorted(self.q, key=lambda x_: x_[0]):
                    fn()
                self.q = []

        def ktiles_for(qt, back):
            hi = 4 * qt + 3
            lo = 0 if back is None else max(0, 4 * qt - back)
            return list(range(hi, lo - 1, -1))

        def alibi_back(slope):
            dcut = ALIBI_CUT / slope
            bk = int(math.floor((dcut + 127.0) / 128.0))
            return bk

        def attention_l0():
            mask_sb = AR.alloc(128, (4, 512), BF16, "mask_sb")
            load(mask_sb, cd["mask_sb"])
            mask_c = AR.alloc(128, (4, 512), BF16, "mask_c")
            load(mask_c, cd["mask_c"])
            KT = AR.alloc(70, (SEQ,), BF16, "KT")
            VT = AR.alloc(128, (NKT, 128), BF16, "VT")
            QTs = Ring([AR.alloc(70, (512,), BF16, f"QT{i}") for i in range(2)])
            Es = Ring([AR.alloc(128, (512,), F32, f"E{i}") for i in range(2)])
            SPs = Ring([AR.alloc(128, (512,), BF16, f"SP{i}") for i in range(4)])
            Ws = Ring([AR.alloc(128, (512,), BF16, f"Wt{i}") for i in range(4)])
            ssum = AR.alloc(128, (512,), F32, "ssum")
            sshs = Ring([AR.alloc(128, (512,), BF16, f"ssh{i}") for i in range(4)])
            oev = Ring([AR.alloc(128, (512,), BF16, f"oev{i}") for i in range(2)])
            Ys = Ring([PS(i, f"Y{i}") for i in (0, 1, 2)])
            Oaccs = [PS(3, "Oacc0"), PS(4, "Oacc1")]
            for h in range(8):
                pipe = Pipe()
                load(T(KT.ap[0:64, :], KT.b), qk0[8 + h])
                load_vt(VT, v0[:, h * 64:(h + 1) * 64], 64)
                for qt in range(NQT):
                    QT = QTs.next()
                    dma("sp", QT.ap[0:64, :], qk0[h, :, qt * 512:(qt + 1) * 512], [], [QT.b], "L" + QT.b.name)
                    kts = ktiles_for(qt, SB_BACK)
                    n = len(kts)
                    Oacc = Oaccs[qt % 2]
                    ssh_prev = None
                    for i, kt in enumerate(kts):
                        Y = Ys.next()
                        E = Es.next()
                        SP = SPs.next()
                        Wt = Ws.next()
                        ssh = sshs.next() if 0 < i < n - 1 else None
                        o = kt - 4 * qt

                        def st0(Y=Y, E=E, SP=SP, o=o, kt=kt, QT=QT, i=i, n=n, ssh=ssh):
                            mm(Y, Y.ap, KT, KT.ap[0:64, kt * 128:(kt + 1) * 128], QT, QT.ap[0:64, :], True, False)
                            if o >= 0:
                                mm(Y, Y.ap, ident, ident.ap, mask_sb, mask_sb.ap[:, o, :], False, False)
                            S.op("act", lambda e: e.activation(E.ap, Y.ap, AF.Exp), reads=[Y.b], writes=[E.b])
                            S.op("act", lambda e: e.activation(SP.ap, E.ap, AF.Ln, bias=1.0), reads=[E.b], writes=[SP.b])
                            if i < n - 1:
                                if i == 0:
                                    S.op("pool", lambda e: e.tensor_copy(ssum.ap, SP.ap), reads=[SP.b], writes=[ssum.b])
                                else:
                                    S.op("pool", lambda e: e.tensor_tensor(ssum.ap, ssum.ap, SP.ap, ALU.add), reads=[SP.b, ssum.b], writes=[ssum.b])
                                    S.op("pool", lambda e: e.tensor_copy(ssh.ap, ssum.ap), reads=[ssum.b], writes=[ssh.b])

                        def st1(Y=Y, SP=SP, Wt=Wt, prev=ssh_prev):
                            mm(Y, Y.ap, uneg, uneg.ap, SP, SP.ap, False, prev is None)
                            if prev is not None:
                                mm(Y, Y.ap, negones, negones.ap, prev, prev.ap, False, True)
                            S.op("act", lambda e: e.activation(Wt.ap, Y.ap, AF.Exp), reads=[Y.b], writes=[Wt.b])

                        def st2(Wt=Wt, kt=kt, i=i, n=n, Oacc=Oacc, h=h, qt=qt):
                            mm(Oacc, Oacc.ap[0:64, :], VT, VT.ap[:, kt, 0:64], Wt, Wt.ap, i == 0, i == n - 1)
                            if i == n - 1:
                                ev = oev.next()
                                S.op("dve", lambda e: e.tensor_copy(ev.ap[0:64, :], Oacc.ap[0:64, :]), reads=[Oacc.b], writes=[ev.b])
                                store(ot[h * 64:(h + 1) * 64, qt * 512:(qt + 1) * 512], ev, ap=ev.ap[0:64, :])

                        pipe.job([(0, st0), (1, st1), (2, st2)])
                        if i < n - 1:
                            ssh_prev = SP if i == 0 else ssh
                pipe.flush()
            lam_t = AR.alloc(128, (4, 64), F32, "lam_t")
            for i in range(4):
                dma("sp", lam_t.ap[:, i, :], lamv[i].partition_broadcast(128), [], [lam_t.b], "Llam")
            lam_p = AR.alloc(128, (2, 64), F32, "lam_p")
            lam_s = AR.alloc(128, (2,), F32, "lam_s")
            neglam = AR.alloc(128, (1,), F32, "neglam")
            S.op("dve", lambda e: e.tensor_tensor(lam_p.ap[:, 0, :], lam_t.ap[:, 0, :], lam_t.ap[:, 1, :], ALU.mult), reads=[lam_t.b], writes=[lam_p.b])
            S.op("dve", lambda e: e.tensor_tensor(lam_p.ap[:, 1, :], lam_t.ap[:, 2, :], lam_t.ap[:, 3, :], ALU.mult), reads=[lam_t.b, lam_p.b], writes=[lam_p.b])
            S.op("dve", lambda e: e.reduce_sum(lam_s.ap, lam_p.ap, axis=mybir.AxisListType.X), reads=[lam_p.b], writes=[lam_s.b])
            S.op("act", lambda e: e.activation(lam_s.ap, lam_s.ap, AF.Exp), reads=[lam_s.b], writes=[lam_s.b])
            lam_init = 0.8 - 0.6 * math.exp(-0.3 * 0)
            S.op("dve", lambda e: e.tensor_tensor(neglam.ap, lam_s.ap[:, 1:2], lam_s.ap[:, 0:1], ALU.subtract), reads=[lam_s.b], writes=[neglam.b])
            S.op("dve", lambda e: e.tensor_scalar(neglam.ap, neglam.ap, -lam_init, None, ALU.add), reads=[neglam.b], writes=[neglam.b])
            sg = AR.alloc(128, (1,), F32, "sg")
            dma("sp", sg.ap, ev_subln.rearrange("(p o) -> p o", o=1), [], [sg.b], "Lsg")
            S.op("dve", lambda e: e.tensor_scalar(sg.ap, sg.ap, 1.0 - lam_init, None, ALU.mult), reads=[sg.b], writes=[sg.b])
            KT2 = [KT, AR.alloc(70, (SEQ,), BF16, "KTb")]
            for kk in KT2:
                dma("sp", kk.ap[64:70, :], cd["kaug"], [], [kk.b], "L" + kk.b.name)
            QT2 = [[AR.alloc(70, (512,), BF16, f"QD{c}{i}") for i in range(2)] for c in range(2)]
            Ps = Ring([AR.alloc(128, (512,), BF16, f"P{i}") for i in range(4)])
            NUM = [PS(3, "NUM0"), PS(4, "NUM1")]
            DEN = [PS(5, "DEN0"), PS(6, "DEN1")]
            MS = PS(7, "MS")
            r_t = [AR.alloc(128, (512,), F32, f"rden{c}") for c in range(2)]
            a_t = [AR.alloc(128, (512,), F32, f"a{c}") for c in range(2)]
            o_t = AR.alloc(128, (512,), F32, "o_t")
            sq_t = AR.alloc(128, (512,), F32, "sq_t")
            for h in range(4):
                pipe = Pipe()
                back = alibi_back(DIFF_SLOPES[h])
                for c in range(2):
                    dma("sp", KT2[c].ap[0:64, :], qk0[24 + 2 * h + c], [], [KT2[c].b], "L" + KT2[c].b.name)
                load_vt(VT, v0[:, 512 + h * 128:512 + (h + 1) * 128], 128)
                for qt in range(NQT):
                    for c in range(2):
                        QT = QT2[c][qt % 2]
                        dma("sp", QT.ap[0:64, :], qk0[16 + 2 * h + c, :, qt * 512:(qt + 1) * 512], [], [QT.b], "L" + QT.b.name)
                        dma("sp", QT.ap[64:70, :], cd["qaug"][:, h, :], [], [QT.b], "L" + QT.b.name)
                        kts = ktiles_for(qt, back)
                        n = len(kts)
                        for i, kt in enumerate(kts):
                            Y = Ys.next()
                            P = Ps.next()
                            o = kt - 4 * qt

                            def st0(Y=Y, P=P, o=o, kt=kt, QT=QT, c=c, qt=qt, h=h):
                                mm(Y, Y.ap, KT2[c], KT2[c].ap[:, kt * 128:(kt + 1) * 128], QT, QT.ap, True, o < 0)
                                if o >= 0:
                                    mm(Y, Y.ap, ident, ident.ap, mask_c, mask_c.ap[:, o, :], False, True)
                                S.op("act", lambda e: e.activation(P.ap, Y.ap, AF.Exp, bias=bias_ap(h, 4 * qt - kt)),
                                     reads=[Y.b, bias_tab.b], writes=[P.b])

                            def st2(P=P, kt=kt, i=i, n=n, c=c, h=h, qt=qt):
                                mm(NUM[c], NUM[c].ap, VT, VT.ap[:, kt, :], P, P.ap, i == 0, i == n - 1)
                                mm(DEN[c], DEN[c].ap, ones, ones.ap, P, P.ap, i == 0, i == n - 1)
                                if i == n - 1:
                                    S.op("dve", lambda e: e.reciprocal(r_t[c].ap, DEN[c].ap), reads=[DEN[c].b], writes=[r_t[c].b])
                                    S.op("dve", lambda e: e.tensor_tensor(a_t[c].ap, NUM[c].ap, r_t[c].ap, ALU.mult), reads=[NUM[c].b, r_t[c].b], writes=[a_t[c].b])
                                    if c == 1:
                                        S.op("dve", lambda e: e.scalar_tensor_tensor(o_t.ap, a_t[1].ap, neglam.ap, a_t[0].ap, ALU.mult, ALU.add),
                                             reads=[a_t[0].b, a_t[1].b, neglam.b], writes=[o_t.b])
                                        S.op("act", lambda e: e.activation(sq_t.ap, o_t.ap, AF.Square), reads=[o_t.b], writes=[sq_t.b])
                                        mm(MS, MS.ap, onesdiv, onesdiv.ap, sq_t, sq_t.ap, True, True)
                                        S.op("act", lambda e: e.activation(sq_t.ap, MS.ap, AF.Sqrt, bias=eps_t.ap), reads=[MS.b, eps_t.b], writes=[sq_t.b])
                                        S.op("dve", lambda e: e.reciprocal(sq_t.ap, sq_t.ap), reads=[sq_t.b], writes=[sq_t.b])
                                        ev = oev.next()
                                        S.op("dve", lambda e: e.scalar_tensor_tensor(ev.ap, o_t.ap, sg.ap, sq_t.ap, ALU.mult, ALU.mult),
                                             reads=[o_t.b, sg.b, sq_t.b], writes=[ev.b])
                                        store(ot[512 + h * 128:512 + (h + 1) * 128, qt * 512:(qt + 1) * 512], ev)

                            pipe.job([(0, st0), (2, st2)])
                pipe.flush()

        attention_l0()
        phase_reset()

        def phase_mlp(x_src, wout_src, w1_src, w2_src, gain_idx, dst, final_gain=None):
            stage = Ring([AR.alloc(128, (2048,), F32, f"wstg{i}") for i in range(2)])
            Wo = AR.alloc(128, (8, D), BF16, "Wo")
            W1 = AR.alloc(128, (8, 4096), BF16, "W1")
            gsl = T(gains.ap[:, gain_idx, :], gains.b)
            load_weight(Wo, wout_src, D, D, stage)
            load_weight(W1, w1_src, D, 4096, stage, gain=gsl)
            OTs = Ring([AR.alloc(128, (8, 512), BF16, f"OT{i}") for i in range(2)])
            x1r = Ring([AR.alloc(128, (D,), F32, f"x1r{i}") for i in range(3)])
            xts = Ring([AR.alloc(128, (D,), F32, f"xt{i}") for i in range(2)])
            hns = Ring([AR.alloc(128, (D,), BF16, f"hn{i}") for i in range(2)])
            junk = AR.alloc(128, (D,), BF16, "junk")
            sss = Ring([AR.alloc(128, (1,), F32, f"ss{i}") for i in range(2)])
            rss = Ring([AR.alloc(128, (1,), F32, f"rs{i}") for i in range(2)])
            hTs = Ring([AR.alloc(128, (8, 512), BF16, f"hT{i}") for i in range(2)])
            uts = Ring([AR.alloc(128, (512,), BF16, f"ut{i}") for i in range(4)])
            sqs = Ring([AR.alloc(128, (512,), BF16, f"sq{i}") for i in range(2)])
            tps = Ring([PS(i, f"tp{i}", cols=1024, dt=BF16) for i in (0, 1)])
            pps = Ring([PS(i, f"pp{i}") for i in (2, 3, 4, 5, 6, 7)])
            for tb in range(NQT):
                OT = OTs.next()
                dma("sp", OT.ap, ot[:, tb * 512:(tb + 1) * 512].rearrange("(k p) t -> p k t", p=128), [], [OT.b], "L" + OT.b.name)
                hT = hTs.next()
                for r in range(4):
                    row0 = tb * 512 + r * 128
                    xt = xts.next()
                    load(xt, x_src[row0:row0 + 128, :])
                    x1v = x1r.next()
                    for half in range(2):
                        pp = pps.next()
                        for k in range(8):
                            mm(pp, pp.ap, OT, OT.ap[:, k, r * 128:(r + 1) * 128], Wo, Wo.ap[:, k, half * 512:(half + 1) * 512], k == 0, k == 7)
                        S.op("dve", lambda e, pp=pp, xt=xt, x1v=x1v, half=half: e.tensor_tensor(
                            x1v.ap[:, half * 512:(half + 1) * 512], pp.ap, xt.ap[:, half * 512:(half + 1) * 512], ALU.add),
                            reads=[pp.b, xt.b], writes=[x1v.b])
                    store(x1s[row0:row0 + 128, :], x1v)
                    norm_transpose(x1v, hns.next(), sss.next(), rss.next(), junk, tps.next(), hT, r)
                for fc in range(32):
                    pp = pps.next()
                    for k in range(8):
                        mm(pp, pp.ap, W1, W1.ap[:, k, fc * 128:(fc + 1) * 128], hT, hT.ap[:, k, :], k == 0, k == 7)
                    sq = sqs.next()
                    u = uts.next()
                    S.op("act", lambda e, pp=pp, sq=sq: e.activation(sq.ap, pp.ap, AF.Square), reads=[pp.b], writes=[sq.b])
                    S.op("dve", lambda e, pp=pp, sq=sq, u=u: e.scalar_tensor_tensor(u.ap, pp.ap, 0.0, sq.ap, ALU.is_gt, ALU.mult),
                         reads=[pp.b, sq.b], writes=[u.b])
                    store(uts_d[fc, :, tb * 512:(tb + 1) * 512], u)
            phase_reset()
            stage = Ring([AR.alloc(128, (2048,), F32, f"wstg{i}") for i in range(2)])
            W2 = AR.alloc(128, (32, D), BF16, "W2")
            load_weight(W2, w2_src, 4096, D, stage)
            fg = None
            if final_gain is not None:
                fg = AR.alloc(128, (D,), F32, "fg")
                load(fg, final_gain.partition_broadcast(128))
            UTs = Ring([AR.alloc(128, (32, 512), BF16, f"UT{i}") for i in range(2)])
            x1r = Ring([AR.alloc(128, (D,), F32, f"x1r{i}") for i in range(3)])
            ys = Ring([AR.alloc(128, (D,), F32, f"y{i}") for i in range(3)])
            junk = AR.alloc(128, (D,), BF16, "junk")
            sss = Ring([AR.alloc(128, (1,), F32, f"ss{i}") for i in range(2)])
            rss = Ring([AR.alloc(128, (1,), F32, f"rs{i}") for i in range(2)])
            pps = Ring([PS(i, f"pp{i}") for i in (0, 1, 2, 3, 4, 5)])
            for tb in range(NQT):
                UT = UTs.next()
                for f4 in range(4):
                    dma("sp", UT.ap[:, f4 * 8:(f4 + 1) * 8, :], uts_d[f4 * 8:(f4 + 1) * 8, :, tb * 512:(tb + 1) * 512].rearrange("f p t -> p f t"),
                        [], [UT.b], "L" + UT.b.name)
                for r in range(4):
                    row0 = tb * 512 + r * 128
                    x1v = x1r.next()
                    load(x1v, x1s[row0:row0 + 128, :])
                    y = ys.next()
                    for half in range(2):
                        pp = pps.next()
                        for fc in range(32):
                            mm(pp, pp.ap, UT, UT.ap[:, fc, r * 128:(r + 1) * 128], W2, W2.ap[:, fc, half * 512:(half + 1) * 512], fc == 0, fc == 31)
                        S.op("dve", lambda e, pp=pp, y=y, x1v=x1v, half=half: e.tensor_tensor(
                            y.ap[:, half * 512:(half + 1) * 512], pp.ap, x1v.ap[:, half * 512:(half + 1) * 512], ALU.add),
                            reads=[pp.b, x1v.b], writes=[y.b])
                    if fg is not None:
                        ss = sss.next()
                        rs = rss.next()
                        S.op("act", lambda e, y=y, ss=ss: e.activation(junk.ap, y.ap, AF.Square, accum_out=ss.ap), reads=[y.b], writes=[junk.b, ss.b])
                        S.op("act", lambda e, ss=ss, rs=rs: e.activation(rs.ap, ss.ap, AF.Sqrt, bias=eps_t.ap, scale=1.0 / D), reads=[ss.b, eps_t.b], writes=[rs.b])
                        S.op("dve", lambda e, rs=rs: e.reciprocal(rs.ap, rs.ap), reads=[rs.b], writes=[rs.b])
                        S.op("dve", lambda e, y=y, rs=rs: e.scalar_tensor_tensor(y.ap, y.ap, rs.ap, fg.ap, ALU.mult, ALU.mult),
                             reads=[y.b, rs.b, fg.b], writes=[y.b])
                    store(dst[row0:row0 + 128, :], y)

        phase_mlp(x_in, ev_w_out, mlp_w1[0], mlp_w2[0], 1, x2)
        if STOP_AFTER == "L0":
            S.barrier()
            cp = Ring([AR.alloc(128, (D,), F32, f"cp{i}") for i in range(2)])
            for i in range(NKT):
                c_ = cp.next()
                load(c_, x2[i * 128:(i + 1) * 128, :])
                store(out_d[i * 128:(i + 1) * 128, :], c_)
            S.barrier()
            S.finalize()
            return nc, S
        phase_reset()

        fm = []
        for jc in range(8):
            dsts = [(lambda tb, slot=2 * jc + half: q1[slot, :, tb * 512:(tb + 1) * 512], 64 * half, 64) for half in range(2)]
            fm.append((128 * jc, 128, dsts, 0.125, None))
        for (cbase, sbase) in ((1024, 0), (1280, 4), (1536, 8), (2048, 12)):
            for jc in range(2):
                dsts = [(lambda tb, slot=sbase + 2 * jc + half: kf1[slot, :, tb * 512:(tb + 1) * 512], 64 * half, 64) for half in range(2)]
                fm.append((cbase + 128 * jc, 128, dsts, 1.0, None))
        fm.append((2560, 48, [(lambda tb: gts[:, tb * 512:(tb + 1) * 512], 0, 48)], 1.0, AF.Sigmoid))
        tm = [(1792, 256, lambda row0: v1[row0:row0 + 128, 0:256]),
              (2304, 256, lambda row0: v1[row0:row0 + 128, 256:512])]
        phase_proj(x2, od_w_in, 2608, 2, fm, tm)
        phase_reset()

        def phase_compress():
            stg = AR.alloc(64, (32 * 256,), F32, "cstg")
            W1c = AR.alloc(64, (32, 256), BF16, "W1c")
            stg2 = AR.alloc(128, (2, 64), F32, "cstg2")
            W2c = AR.alloc(128, (2, 64), BF16, "W2c")
            posf = AR.alloc(64, (32,), F32, "posf")
            posT = AR.alloc(64, (32,), BF16, "posT")
            biash = AR.alloc(128, (2,), F32, "biash")
            src = AR.alloc(64, (SEQ,), BF16, "csrc")
            hid = AR.alloc(128, (2, 512), BF16, "hid")
            u_t = AR.alloc(128, (512,), F32, "u_t")
            w_t = AR.alloc(128, (512,), F32, "w_t")
            evk = AR.alloc(64, (512,), BF16, "evk")
            evv = Ring([AR.alloc(128, (64,), BF16, f"evv{i}") for i in range(2)])
            pb_ = PS(0, "cbias")
            ph = Ring([PS(1, "ph0"), PS(2, "ph1")])
            po = Ring([PS(3, "po0"), PS(4, "po1")])
            S.op("dve", lambda e: e.memset(hid.ap, 0.0), writes=[hid.b])
            S.op("dve", lambda e: e.memset(evk.ap, 0.0), writes=[evk.b])
            for kind, (w1d, w2d, posd) in enumerate(((cw1k, cw2k, pos_k), (cw1v, cw2v, pos_v))):
                dma("sp", stg.ap.rearrange("p (l h) -> p l h", l=32), w1d.rearrange("(l d) h -> d l h", d=64), [], [stg.b], "Lcstg")
                for q4 in range(4):
                    cast(W1c, W1c.ap[:, q4 * 8:(q4 + 1) * 8, :], stg, stg.ap.rearrange("p (l h) -> p l h", l=32)[:, q4 * 8:(q4 + 1) * 8, :])
                dma("sp", stg2.ap, w2d.rearrange("(c p) n -> p c n", p=128), [], [stg2.b], "Lcstg2")
                cast(W2c, W2c.ap, stg2, stg2.ap)
                dma("sp", posf.ap, posd.rearrange("l d -> d l"), [], [posf.b], "Lposf", slow=True)
                cast(posT, posT.ap, posf, posf.ap)
                for hc in range(2):
                    for l in range(32):
                        mm(pb_, pb_.ap[:, 0:1], W1c, W1c.ap[:, l, hc * 128:(hc + 1) * 128], posT, posT.ap[:, l:l + 1], l == 0, l == 31)
                    S.op("dve", lambda e, hc=hc: e.tensor_copy(biash.ap[:, hc:hc + 1], pb_.ap[:, 0:1]), reads=[pb_.b], writes=[biash.b])
                for g in range(4):
                    load(src, kf1[4 * kind + g])
                    for hc in range(2):
                        p_ = ph.next()
                        for l in range(32):
                            mm(p_, p_.ap[:, 0:511], W1c, W1c.ap[:, l, hc * 128:(hc + 1) * 128], src, src.ap[:, l:l + 16 * 510 + 1:16], l == 0, l == 31)
                        S.op("act", lambda e, p_=p_, hc=hc: e.activation(u_t.ap[:, 0:511], p_.ap[:, 0:511], AF.Identity, bias=biash.ap[:, hc:hc + 1]),
                             reads=[p_.b, biash.b], writes=[u_t.b])
                        S.op("act", lambda e: e.activation(w_t.ap[:, 0:511], u_t.ap[:, 0:511], AF.Square), reads=[u_t.b], writes=[w_t.b])
                        S.op("dve", lambda e: e.tensor_scalar(w_t.ap[:, 0:511], w_t.ap[:, 0:511], 0.044715, 1.0, ALU.mult, ALU.add), reads=[w_t.b], writes=[w_t.b])
                        S.op("dve", lambda e: e.tensor_tensor(w_t.ap[:, 0:511], w_t.ap[:, 0:511], u_t.ap[:, 0:511], ALU.mult), reads=[w_t.b, u_t.b], writes=[w_t.b])
                        S.op("act", lambda e: e.activation(w_t.ap[:, 0:511], w_t.ap[:, 0:511], AF.Sigmoid, scale=2.0 * 0.7978845608028654), reads=[w_t.b], writes=[w_t.b])
                        S.op("dve", lambda e, hc=hc: e.tensor_tensor(hid.ap[:, hc, 0:511], w_t.ap[:, 0:511], u_t.ap[:, 0:511], ALU.mult), reads=[w_t.b, u_t.b], writes=[hid.b])
                    if kind == 0:
                        p2 = po.next()
                        for hc in range(2):
                            mm(p2, p2.ap[0:64, 0:511], W2c, W2c.ap[:, hc, :], hid, hid.ap[:, hc, 0:511], hc == 0, hc == 1)
                        S.op("dve", lambda e, p2=p2: e.tensor_copy(evk.ap[:, 0:511], p2.ap[0:64, 0:511]), reads=[p2.b], writes=[evk.b])
                        store(kcmp[g], evk)
                    else:
                        for nchunk in range(4):
                            p2 = po.next()
                            for hc in range(2):
                                mm(p2, p2.ap[:, 0:64], hid, hid.ap[:, hc, nchunk * 128:(nchunk + 1) * 128], W2c, W2c.ap[:, hc, :], hc == 0, hc == 1)
                            ev = evv.next()
                            S.op("dve", lambda e, p2=p2, ev=ev: e.tensor_copy(ev.ap, p2.ap[:, 0:64]), reads=[p2.b], writes=[ev.b])
                            store(vcmp[g, nchunk * 128:(nchunk + 1) * 128, :], ev)

        phase_compress()
        phase_reset()

        def attention_l1():
            mask_c = AR.alloc(128, (4, 512), BF16, "mask_c")
            load(mask_c, cd["mask_c"])
            mask_cmp = AR.alloc(128, (5, 512), BF16, "mask_cmp")
            load(mask_cmp, cd["mask_cmp"])
            mask_win = AR.alloc(128, (8, 512), BF16, "mask_win")
            load(mask_win, cd["mask_win"])
            ewide = AR.alloc(128, (SEQ,), BF16, "ewide")
            load(ewide, cd["ewide"])
            ovl = AR.alloc(128, (4, 128), BF16, "ovl")
            load(ovl, cd["ovl"])
            tka = AR.alloc(128, (256,), F32, "tka")
            load(tka, cd["topk_a"])
            tkm = AR.alloc(128, (256,), F32, "tkm")
            load(tkm, cd["topk_m"])
            bcmp = AR.alloc(128, (16 * NQT * 4,), F32, "bcmp")
            load(bcmp, cd["bias_cmp"])
            KcA = AR.alloc(70, (512,), BF16, "KcA")
            dma("sp", KcA.ap[64:70, :], cd["kaug_cmp"], [], [KcA.b], "LKcA")
            Vc = AR.alloc(128, (4, 64), BF16, "Vc")
            KsA = AR.alloc(70, (SEQ,), BF16, "KsA")
            KwA = AR.alloc(70, (SEQ,), BF16, "KwA")
            dma("sp", KsA.ap[64:70, :], cd["kaug"], [], [KsA.b], "LKsA")
            dma("sp", KwA.ap[64:70, :], cd["kaug"], [], [KwA.b], "LKwA")
            Vs = AR.alloc(128, (NKT, 64), BF16, "Vs")
            Vw = AR.alloc(128, (NKT, 64), BF16, "Vw")
            QAs = [[AR.alloc(70, (512,), BF16, f"QA{r}{i}") for i in range(2)] for r in range(4)]
            GBs = [[AR.alloc(64, (3, 512), BF16, f"GB{r}{i}") for i in range(2)] for r in range(4)]
            Pc = Ring([AR.alloc(128, (512,), BF16, f"Pc{i}") for i in range(10)])
            Ps = Ring([AR.alloc(128, (512,), BF16, f"Pp{i}") for i in range(4)])
            pcn = Ring([AR.alloc(128, (512,), BF16, f"pcn{i}") for i in range(4)])
            rdc = AR.alloc(128, (512,), F32, "rdc")
            ocmp = [AR.alloc(64, (512,), F32, f"ocmp{r}") for r in range(4)]
            selbT = AR.alloc(128, (512,), BF16, "selbT")
            imp2 = AR.alloc(128, (128,), F32, "imp2")
            tmp2 = AR.alloc(128, (128,), F32, "tmp2")
            selm = AR.alloc(128, (128,), F32, "selm")
            selb = AR.alloc(128, (128,), BF16, "selb")
            v8a = AR.alloc(128, (8,), F32, "v8a")
            v8b = AR.alloc(128, (8,), F32, "v8b")
            rs2 = [AR.alloc(64, (512,), F32, f"rs2{i}") for i in range(2)]
            ob2 = [AR.alloc(64, (512,), F32, f"ob2{i}") for i in range(2)]
            acc = AR.alloc(64, (512,), F32, "acc")
            t1 = AR.alloc(64, (512,), F32, "t1")
            oev = Ring([AR.alloc(64, (512,), BF16, f"oev{i}") for i in range(2)])
            Ys = Ring([PS(0, "Y0"), PS(1, "Y1"), PS(2, "Y2")])
            NUMs = [PS(3, "NUMa"), PS(5, "NUMb")]
            DENs = [PS(4, "DENa"), PS(6, "DENb")]
            IMP = PS(7, "IMP")
            TRP = T(pbanks[6][:, :].bitcast(BF16)[:, 0:512], DENs[1].b)

            def tile_job(pipe, K, kap, QAr, extra, bias, V, vap, NUM, DEN, den_parts, P, first, last, tail):
                Y = Ys.next()

                def st0():
                    nx = len(extra)
                    mm(Y, Y.ap, K, kap, QAr, QAr.ap, True, nx == 0)
                    for xi, (lt, lap, rt, rap) in enumerate(extra):
                        mm(Y, Y.ap, lt, lap, rt, rap, False, xi == nx - 1)
                    S.op("act", lambda e: e.activation(P.ap, Y.ap, AF.Exp, bias=bias[0]), reads=[Y.b, bias[1]], writes=[P.b])

                def st2():
                    mm(NUM, NUM.ap[0:64, :], V, vap, P, P.ap, first, last)
                    mm(DEN, DEN.ap[0:den_parts, :], ones, ones.ap[:, 0:den_parts], P, P.ap, first, last)
                    if last and tail is not None:
                        tail()

                pipe.job([(0, st0), (2, st2)])

            for g in range(4):
                dma("sp", KcA.ap[0:64, :], kcmp[g], [], [KcA.b], "LKcA")
                dma("sp", Vc.ap, vcmp[g].rearrange("(c p) d -> p c d", p=128), [], [Vc.b], "LVc")
                dma("sp", KsA.ap[0:64, :], kf1[8 + g], [], [KsA.b], "LKsA")
                dma("sp", KwA.ap[0:64, :], kf1[12 + g], [], [KwA.b], "LKwA")
                for g4 in range(4):
                    dma("sp", Vs.ap[:, g4 * 16:(g4 + 1) * 16, :], v1[g4 * 2048:(g4 + 1) * 2048, g * 64:(g + 1) * 64].rearrange("(t p) c -> p t c", p=128), [], [Vs.b], "LVs")
                    dma("sp", Vw.ap[:, g4 * 16:(g4 + 1) * 16, :], v1[g4 * 2048:(g4 + 1) * 2048, 256 + g * 64:256 + (g + 1) * 64].rearrange("(t p) c -> p t c", p=128), [], [Vw.b], "LVw")
                for qt in range(NQT):
                    pipe = Pipe()
                    QA = [QAs[r][qt % 2] for r in range(4)]
                    GB = [GBs[r][qt % 2] for r in range(4)]
                    for r in range(4):
                        h = 4 * g + r
                        dma("sp", QA[r].ap[0:64, :], q1[h, :, qt * 512:(qt + 1) * 512], [], [QA[r].b], "L" + QA[r].b.name)
                        dma("sp", QA[r].ap[64:70, :], cd["qaug"][:, 4 + h, :], [], [QA[r].b], "L" + QA[r].b.name)
                        for c3 in range(3):
                            dma("sp", GB[r].ap[:, c3, :], gts[3 * h + c3, qt * 512:(qt + 1) * 512].partition_broadcast(64), [], [GB[r].b], "L" + GB[r].b.name)
                    chunks = [c for c in range(4) if 4 * c <= qt]
                    for r in range(4):
                        h = 4 * g + r
                        NUM = NUMs[r % 2]
                        DEN = DENs[r % 2]
                        Pl = [(c, Pc.next()) for c in chunks]

                        def cmp_tail(r=r, NUM=NUM, DEN=DEN, Pl=Pl):
                            S.op("dve", lambda e: e.tensor_scalar(rdc.ap, DEN.ap, 1e-30, None, ALU.add), reads=[DEN.b], writes=[rdc.b])
                            S.op("dve", lambda e: e.reciprocal(rdc.ap, rdc.ap), reads=[rdc.b], writes=[rdc.b])
                            S.op("dve", lambda e: e.tensor_tensor(ocmp[r].ap, NUM.ap[0:64, :], rdc.ap[0:64, :], ALU.mult), reads=[NUM.b, rdc.b], writes=[ocmp[r].b])
                            for ci, (c, P) in enumerate(Pl):
                                pn = pcn.next()
                                S.op("dve", lambda e, pn=pn, P=P: e.tensor_tensor(pn.ap, P.ap, rdc.ap, ALU.mult), reads=[P.b, rdc.b], writes=[pn.b])
                                for qs in range(4):
                                    mm(IMP, IMP.ap[:, qs * 128:(qs + 1) * 128], pn, pn.ap[:, qs * 128:(qs + 1) * 128], ovl, ovl.ap[:, c, :],
                                       r == 0 and ci == 0, r == 3 and ci == len(Pl) - 1)

                        for ci, (c, P) in enumerate(Pl):
                            rel = qt - 4 * c
                            extra = [(ident, ident.ap, mask_cmp, mask_cmp.ap[:, rel, :])] if rel <= 4 else []
                            col = (h * NQT + qt) * 4 + c
                            tile_job(pipe, KcA, KcA.ap[:, c * 128:(c + 1) * 128], QA[r], extra, (bcmp.ap[:, col:col + 1], bcmp.b),
                                     Vc, Vc.ap[:, c, :], NUM, DEN, 128, P, ci == 0, ci == len(Pl) - 1, cmp_tail)
                    pipe.flush()
                    for qs in range(4):
                        off = 127 - 2 * (4 * qt + qs)
                        S.op("dve", lambda e, qs=qs, off=off: e.tensor_tensor(tmp2.ap, IMP.ap[:, qs * 128:(qs + 1) * 128], tkm.ap[:, off:off + 128], ALU.mult),
                             reads=[IMP.b, tkm.b], writes=[tmp2.b])
                        S.op("dve", lambda e, off=off: e.tensor_tensor(imp2.ap, tmp2.ap, tka.ap[:, off:off + 128], ALU.add), reads=[tmp2.b, tka.b], writes=[imp2.b])
                        S.op("dve", lambda e: e.memset(imp2.ap[:, 0:1], 1.0e6), reads=[], writes=[imp2.b])
                        S.op("dve", lambda e: e.max(v8a.ap, imp2.ap), reads=[imp2.b], writes=[v8a.b])
                        S.op("dve", lambda e: e.match_replace(tmp2.ap, v8a.ap, imp2.ap, -9.0), reads=[imp2.b, v8a.b], writes=[tmp2.b])
                        S.op("dve", lambda e: e.max(v8b.ap, tmp2.ap), reads=[tmp2.b], writes=[v8b.b])
                        S.op("dve", lambda e: e.tensor_scalar(selm.ap, imp2.ap, v8b.ap[:, 7:8], 0.0, ALU.is_ge, ALU.add), reads=[imp2.b, v8b.b], writes=[selm.b])
                        S.op("dve", lambda e: e.scalar_tensor_tensor(selm.ap, imp2.ap, 0.0, selm.ap, ALU.is_ge, ALU.mult), reads=[imp2.b, selm.b], writes=[selm.b])
                        S.op("dve", lambda e: e.tensor_scalar(selb.ap, selm.ap, -1.0, -NEG, ALU.add, ALU.mult), reads=[selm.b], writes=[selb.b])
                        S.op("pe", lambda e, qs=qs: e.transpose(TRP.ap[:, qs * 128:(qs + 1) * 128], selb.ap, ident.ap), reads=[selb.b, ident.b], writes=[TRP.b])
                    S.op("dve", lambda e: e.tensor_copy(selbT.ap, TRP.ap[:, 0:512]), reads=[TRP.b], writes=[selbT.b])
                    for r in range(4):
                        h = 4 * g + r
                        si = 4 + h
                        back = alibi_back(NSA_SLOPES[h])

                        def sel_tail(r=r, GBr=GB[r]):
                            S.op("dve", lambda e: e.reciprocal(rs2[0].ap, DENs[0].ap[0:64, :]), reads=[DENs[0].b], writes=[rs2[0].b])
                            S.op("dve", lambda e: e.tensor_tensor(ob2[0].ap, NUMs[0].ap[0:64, :], rs2[0].ap, ALU.mult), reads=[NUMs[0].b, rs2[0].b], writes=[ob2[0].b])
                            S.op("pool", lambda e: e.tensor_tensor(acc.ap, GBr.ap[:, 1, :], ob2[0].ap, ALU.mult), reads=[GBr.b, ob2[0].b], writes=[acc.b])
                            S.op("pool", lambda e: e.tensor_tensor(t1.ap, GBr.ap[:, 0, :], ocmp[r].ap, ALU.mult), reads=[GBr.b, ocmp[r].b], writes=[t1.b])
                            S.op("pool", lambda e: e.tensor_tensor(acc.ap, acc.ap, t1.ap, ALU.add), reads=[acc.b, t1.b], writes=[acc.b])

                        def win_tail(r=r, h=h, qt=qt, GBr=GB[r]):
                            S.op("dve", lambda e: e.reciprocal(rs2[1].ap, DENs[1].ap[0:64, :]), reads=[DENs[1].b], writes=[rs2[1].b])
                            S.op("dve", lambda e: e.tensor_tensor(ob2[1].ap, NUMs[1].ap[0:64, :], rs2[1].ap, ALU.mult), reads=[NUMs[1].b, rs2[1].b], writes=[ob2[1].b])
                            S.op("pool", lambda e: e.tensor_tensor(t1.ap, GBr.ap[:, 2, :], ob2[1].ap, ALU.mult), reads=[GBr.b, ob2[1].b], writes=[t1.b])
                            ev = oev.next()
                            S.op("pool", lambda e: e.tensor_tensor(ev.ap, acc.ap, t1.ap, ALU.add), reads=[acc.b, t1.b], writes=[ev.b])
                            store(ot[h * 64:(h + 1) * 64, qt * 512:(qt + 1) * 512], ev)

                        kts = ktiles_for(qt, back)
                        for i, kt in enumerate(kts):
                            o = kt - 4 * qt
                            extra = [(ewide, ewide.ap[:, kt * 128:(kt + 1) * 128], selbT, selbT.ap)]
                            if o >= 0:
                                extra.append((ident, ident.ap, mask_c, mask_c.ap[:, o, :]))
                            tile_job(pipe, KsA, KsA.ap[:, kt * 128:(kt + 1) * 128], QA[r], extra, (bias_ap(si, 4 * qt - kt), bias_tab.b),
                                     Vs, Vs.ap[:, kt, :], NUMs[0], DENs[0], 64, Ps.next(), i == 0, i == len(kts) - 1, sel_tail)
                        kts = [kt for kt in range(4 * qt + 3, 4 * qt - 5, -1) if kt >= 0]
                        for i, kt in enumerate(kts):
                            o = kt - 4 * qt
                            extra = [(ident, ident.ap, mask_win, mask_win.ap[:, o + 4, :])]
                            tile_job(pipe, KwA, KwA.ap[:, kt * 128:(kt + 1) * 128], QA[r], extra, (bias_ap(si, 4 * qt - kt), bias_tab.b),
                                     Vw, Vw.ap[:, kt, :], NUMs[1], DENs[1], 64, Ps.next(), i == 0, i == len(kts) - 1, win_tail)
                    pipe.flush()

        attention_l1()
        if STOP_AFTER == "L1ATT":
            S.barrier()
            cpb = Ring([AR.alloc(128, (1024,), BF16, f"cpb{i}") for i in range(2)])
            cpf = Ring([AR.alloc(128, (1024,), F32, f"cpf{i}") for i in range(2)])
            ov = out_d.rearrange("(a b) c -> a (b c)", a=1024)
            for k in range(8):
                for cb in range(8):
                    b_ = cpb.next()
                    f_ = cpf.next()
                    load(b_, ot[k * 128:(k + 1) * 128, cb * 1024:(cb + 1) * 1024])
                    S.op("dve", lambda e, b_=b_, f_=f_: e.tensor_copy(f_.ap, b_.ap), reads=[b_.b], writes=[f_.b])
                    store(ov[k * 128:(k + 1) * 128, cb * 1024:(cb + 1) * 1024], f_)
            S.barrier()
            S.finalize()
            return nc, S
        phase_reset()
        phase_mlp(x2, od_w_out, mlp_w1[1], mlp_w2[1], 3, out_d, final_gain=final_norm)
        S.barrier()
        S.finalize()
    return nc, S


def kernel(**inputs):
    consts = make_consts()
    nc, S = build_program(consts)
    x = np.ascontiguousarray(inputs["x"], dtype=np.float32)
    B = x.shape[0]
    shared = {}
    for k in ("attn_norm", "mlp_norm", "final_norm", "mlp_w1", "mlp_w2"):
        shared[k] = np.ascontiguousarray(inputs[k], dtype=np.float32)
    for k in ("ev_w_in", "ev_subln", "ev_w_out", "od_w_in", "od_cmp_pos_k", "od_cmp_k_w1", "od_cmp_k_w2",
              "od_cmp_pos_v", "od_cmp_v_w1", "od_cmp_v_w2", "od_w_out"):
        shared[k] = np.ascontiguousarray(inputs[k][0], dtype=np.float32)
    shared["lamv"] = np.ascontiguousarray(np.stack([inputs["ev_lam_q1"][0], inputs["ev_lam_k1"][0],
                                                    inputs["ev_lam_q2"][0], inputs["ev_lam_k2"][0]]), dtype=np.float32)
    for k, v in consts.items():
        shared["c_" + k] = v
    in_maps = []
    for core in range(8):
        m = dict(shared)
        m["x"] = x[core % B]
        in_maps.append(m)
    res = run_bass_kernel_spmd(nc, in_maps, core_ids=list(range(8)))
    out = np.stack([np.asarray(res.results[b]["out"], dtype=np.float32) for b in range(B)])
    return out
```

```python
import math
import numpy as np
import ml_dtypes
from contextlib import ExitStack
import concourse.bass as bass
import concourse.mybir as mybir
from concourse.bass_utils import run_bass_kernel_spmd

F32 = mybir.dt.float32
BF16 = mybir.dt.bfloat16
AF = mybir.ActivationFunctionType
ALU = mybir.AluOpType
bf = ml_dtypes.bfloat16

SEQ = 8192
D = 1024
NQT = SEQ // 512
NKT = SEQ // 128
NEG = -30000.0
SB_BACK = 4
ALIBI_CUT = 144.0
SEM_WRAP = 16000
DMA_WRAP = 1000
STOP_AFTER = None


class Buf:
    __slots__ = ("name", "w", "r")

    def __init__(self, name):
        self.name = name
        self.w = []
        self.r = []


class Op:
    __slots__ = ("eng", "fn", "deps", "idx", "needs_inc", "waits", "is_dma", "dkey", "dval")

    def __init__(self, eng, fn):
        self.eng = eng
        self.fn = fn
        self.deps = set()
        self.idx = -1
        self.needs_inc = False
        self.waits = []
        self.is_dma = False
        self.dkey = None
        self.dval = 0


class T:
    __slots__ = ("ap", "b")

    def __init__(self, ap, b):
        self.ap = ap
        self.b = b

    def __getitem__(self, k):
        return self.ap[k]


class Sched:
    ENGS = ("pe", "act", "dve", "pool", "sp")

    def __init__(self, nc, stack):
        self.nc = nc
        self.stack = stack
        self.ops = []
        self.eng_ops = {e: [] for e in self.ENGS}
        self.dma_count = {}
        self.bufs = []

    def buf(self, name):
        b = Buf(name)
        self.bufs.append(b)
        return b

    def op(self, eng, fn, reads=(), writes=(), dma_key=None):
        o = Op(eng, fn)
        oid = len(self.ops)
        for b in reads:
            o.deps.update(b.w)
        for b in writes:
            o.deps.update(b.w)
            o.deps.update(b.r)
        for b in reads:
            b.r.append(oid)
        for b in writes:
            b.w = [oid]
            b.r = []
        if dma_key is not None:
            o.is_dma = True
            o.dkey = dma_key
            n = self.dma_count.get(dma_key, 0) + 1
            self.dma_count[dma_key] = n
            o.dval = n
        o.idx = len(self.eng_ops[eng])
        self.eng_ops[eng].append(o)
        self.ops.append(o)
        return oid

    def barrier(self):
        live = [b for b in self.bufs if b.w or b.r]
        first = True
        sync = self.buf("barrier")
        for e in self.ENGS:
            if first:
                self.op(e, lambda eng: eng.nop(), writes=live + [sync])
                first = False
            else:
                self.op(e, lambda eng: eng.nop(), reads=[sync])
        self.bufs = [sync]

    def finalize(self):
        nc = self.nc
        ops = self.ops
        know = {e: {} for e in self.ENGS}
        comp_know = [None] * len(ops)
        for oid, o in enumerate(ops):
            K = know[o.eng]
            for d in sorted(o.deps):
                p = ops[d]
                if p.is_dma:
                    dom = ("d", p.dkey)
                    val = p.dval
                else:
                    dom = ("e", p.eng)
                    val = p.idx + 1
                    if p.eng == "pe" and o.eng == "pe":
                        continue
                if K.get(dom, 0) >= val:
                    continue
                o.waits.append((dom, val))
                p.needs_inc = True
                for k2, v2 in comp_know[d].items():
                    if K.get(k2, 0) < v2:
                        K[k2] = v2
            ck = dict(K)
            if o.is_dma:
                ck[("d", o.dkey)] = o.dval
            else:
                ck[("e", o.eng)] = o.idx + 1
            comp_know[oid] = ck
        comp_know = None
        eng_sems = {}
        counts = {}
        for e in self.ENGS:
            c = 0
            for o in self.eng_ops[e]:
                if o.is_dma:
                    continue
                if o.needs_inc:
                    c += 1
                counts[(e, o.idx)] = c
            nsem = (c + SEM_WRAP - 1) // SEM_WRAP
            eng_sems[e] = [self.stack.enter_context(nc.semaphore(f"s_{e}_{i}")) for i in range(nsem)]
        dma_sems = {}
        for k, n in self.dma_count.items():
            nsem = (n + DMA_WRAP - 1) // DMA_WRAP
            dma_sems[k] = [self.stack.enter_context(nc.semaphore(f"d_{k}_{i}")) for i in range(nsem)]

        def sem_for(dom, val):
            if dom[0] == "e":
                c = counts[(dom[1], val - 1)]
                return eng_sems[dom[1]][(c - 1) // SEM_WRAP], (c - 1) % SEM_WRAP + 1
            mul = 1 if dom[1].startswith("cc") else 16
            return dma_sems[dom[1]][(val - 1) // DMA_WRAP], ((val - 1) % DMA_WRAP + 1) * mul

        self.n_waits = sum(len(o.waits) for o in ops)
        with nc.Block() as block:
            def make(e):
                def body(eng):
                    for o in self.eng_ops[e]:
                        for dom, val in o.waits:
                            s, v = sem_for(dom, val)
                            eng.wait_ge(s, v)
                        ins = o.fn(eng)
                        if o.is_dma:
                            s, v = sem_for(("d", o.dkey), o.dval)
                            ins.then_inc(s, 1 if o.dkey.startswith("cc") else 16)
                        elif o.needs_inc:
                            c = counts[(e, o.idx)]
                            ins.then_inc(eng_sems[e][(c - 1) // SEM_WRAP], 1)
                return body
            block.tensor(make("pe"))
            block.scalar(make("act"))
            block.vector(make("dve"))
            block.gpsimd(make("pool"))
            block.sync(make("sp"))


class Arena:
    def __init__(self, S, tens, nwords):
        self.S = S
        self.t = tens
        self.n = nwords
        self.off = 0
        self.cnt = 0

    def reset(self):
        self.off = 0

    def alloc(self, parts, shape, dt, name):
        n = 1
        for s in shape:
            n *= s
        words = n if dt == F32 else (n + 1) // 2
        v = self.t[0:parts, self.off:self.off + words]
        self.off += words
        assert self.off <= self.n, f"arena overflow at {name}: {self.off} > {self.n}"
        if dt != F32:
            v = v.bitcast(dt)
        if len(shape) == 2:
            v = v.rearrange("p (a b) -> p a b", a=shape[0])
        elif len(shape) == 3:
            v = v.rearrange("p (a b c) -> p a b c", a=shape[0], b=shape[1])
        self.cnt += 1
        return T(v, self.S.buf(name))


def _split3(v):
    v = np.asarray(v, np.float32)
    hi = v.astype(bf)
    r1 = v - hi.astype(np.float32)
    mid = r1.astype(bf)
    r2 = r1 - mid.astype(np.float32)
    lo = r2.astype(bf)
    return hi, mid, lo


DIFF_SLOPES = [2.0 ** (-8.0 * (h + 1) / 4) for h in range(4)]
NSA_SLOPES = [2.0 ** (-8.0 * (h + 1) / 16) for h in range(16)]
DIFF_ASSIGN = ((0, 3), (1, 2))
GROUP_ASSIGN = ((0, 3), (1, 2))
SB_PER_CORE = 4


def slopes_for(p):
    sl = [DIFF_SLOPES[h] for h in DIFF_ASSIGN[p]]
    for g in GROUP_ASSIGN[p]:
        sl += [NSA_SLOPES[4 * g + r] for r in range(4)]
    return sl


NSLOT = 10
BIAS_M0 = -3
BIAS_NM = 68


def make_consts(p):
    ALL_SLOPES = slopes_for(p)
    c = {}
    j = np.arange(128)
    t = np.arange(512)
    c["ident"] = np.eye(128, dtype=np.float32).astype(bf)
    c["ones"] = np.ones((128, 128), np.float32).astype(bf)
    c["negones"] = (-np.ones((128, 128), np.float32)).astype(bf)
    c["uneg"] = (-(j[:, None] >= j[None, :]).astype(np.float32)).astype(bf)
    c["onesdiv"] = np.full((128, 128), 1.0 / 128, np.float32)
    msb = np.zeros((4, 128, 512), np.float32)
    mc = np.zeros((4, 128, 512), np.float32)
    for o in range(4):
        jj = 128 * o + j[:, None]
        msb[o] = np.where(jj >= t[None, :], NEG, 0.0)
        mc[o] = np.where(jj > t[None, :], NEG, 0.0)
    c["mask_sb"] = np.ascontiguousarray(msb.transpose(1, 0, 2)).astype(bf)
    c["mask_c"] = np.ascontiguousarray(mc.transpose(1, 0, 2)).astype(bf)
    kr = np.zeros((6, SEQ), np.float32)
    kr[0:3] = 1.0
    kr[3:6] = (np.arange(SEQ) % 128)[None, :]
    c["kaug"] = kr.astype(bf)
    kc = np.zeros((6, 512), np.float32)
    kc[0:3] = 1.0
    kc[3:6] = (16 * (np.arange(512) % 128))[None, :]
    c["kaug_cmp"] = kc.astype(bf)
    qa = np.zeros((len(ALL_SLOPES), 6, 512), np.float32).astype(bf)
    for i, s in enumerate(ALL_SLOPES):
        s32 = np.float32(s)
        v = (-(s32 * t.astype(np.float32))).astype(np.float32)
        h3 = _split3(v)
        s3 = _split3(np.full(512, s32, np.float32))
        for r in range(3):
            qa[i, r] = h3[r]
            qa[i, 3 + r] = s3[r]
    c["qaug"] = np.ascontiguousarray(qa.transpose(1, 0, 2))
    bt = np.zeros((len(ALL_SLOPES), BIAS_NM), np.float32)
    for i, s in enumerate(ALL_SLOPES):
        for mi in range(BIAS_NM):
            bt[i, mi] = -np.float32(s) * np.float32(128 * (mi + BIAS_M0))
    c["bias_tab"] = np.broadcast_to(bt.reshape(1, -1), (128, bt.size)).copy()
    bc = np.zeros((8, NQT, 4), np.float32)
    for h, s in enumerate(ALL_SLOPES[2:]):
        for qt in range(NQT):
            for cc in range(4):
                bc[h, qt, cc] = -np.float32(s) * np.float32(512 * qt - 2048 * cc - 31)
    c["bias_cmp"] = np.broadcast_to(bc.reshape(1, -1), (128, bc.size)).copy()
    mcm = np.zeros((5, 128, 512), np.float32)
    for rel in range(5):
        mcm[rel] = np.where(512 * rel + t[None, :] >= 16 * j[:, None] + 31, 0.0, NEG)
    c["mask_cmp"] = np.ascontiguousarray(mcm.transpose(1, 0, 2)).astype(bf)
    mw = np.zeros((8, 128, 512), np.float32)
    for oi, o in enumerate(range(-4, 4)):
        dd = t[None, :] - j[:, None] - 128 * o
        mw[oi] = np.where((dd >= 0) & (dd <= 511), 0.0, NEG)
    c["mask_win"] = np.ascontiguousarray(mw.transpose(1, 0, 2)).astype(bf)
    cc = np.arange(SEQ)
    c["ewide"] = (cc[None, :] // 64 == j[:, None]).astype(np.float32).astype(bf)
    n = np.arange(512)
    s_ = np.arange(128)
    ov = ((n[:, None] >= 4 * s_[None, :] - 1) & (n[:, None] <= 4 * s_[None, :] + 3) & (n[:, None] < 511))
    c["ovl"] = np.ascontiguousarray(ov.astype(np.float32).reshape(4, 128, 128).transpose(1, 0, 2)).astype(bf)
    q = np.arange(128)
    u = np.arange(-127, 129)
    cur = (q >= 64).astype(np.int64)
    A = np.zeros((128, 256), np.float32)
    M = np.ones((128, 256), np.float32)
    fut = u[None, :] > cur[:, None]
    A[fut] = -1.0
    M[fut] = 0.0
    f1 = u[None, :] == cur[:, None]
    f2 = u[None, :] == cur[:, None] - 1
    A[f1] = 1.0e6 + 1.0
    M[f1] = 0.0
    A[f2] = 1.0e6 + 2.0
    M[f2] = 0.0
    c["topk_a"] = A
    c["topk_m"] = M
    gs = np.zeros((48, 48, 64), np.float32)
    for r in range(48):
        gs[r, r, :] = 1.0
    c["gsel"] = gs.reshape(48, 48 * 64).astype(bf)
    return c


CONST_SPECS = None


def _dt_of(a):
    return BF16 if a.dtype == bf else F32


def build_program(consts):
    nc = bass.Bass("TRN2", target_bir_lowering=False)
    dr = {}

    def din(name, shape, dt=F32):
        dr[name] = nc.dram_tensor(name, list(shape), dt, kind="ExternalInput").ap()
        return dr[name]

    def dscr(name, shape, dt):
        dr[name] = nc.dram_tensor(name, list(shape), dt, kind="Internal").ap()
        return dr[name]

    x_in = din("x", (SEQ, D))
    attn_norm = din("attn_norm", (2, D))
    mlp_norm = din("mlp_norm", (2, D))
    final_norm = din("final_norm", (D,))
    ev_w_in = din("ev_w_in", (D, 1536))
    lamv = din("lamv", (4, 64))
    ev_subln = din("ev_subln", (128,))
    ev_w_out = din("ev_w_out", (D, D))
    od_w_in = din("od_w_in", (D, 1304))
    pos_k = din("od_cmp_pos_k", (32, 64))
    cw1k = din("od_cmp_k_w1", (2048, 256))
    cw2k = din("od_cmp_k_w2", (256, 64))
    pos_v = din("od_cmp_pos_v", (32, 64))
    cw1v = din("od_cmp_v_w1", (2048, 256))
    cw2v = din("od_cmp_v_w2", (256, 64))
    od_w_out = din("od_w_out", (D, D))
    mlp_w1 = din("mlp_w1", (2, D, 4096))
    mlp_w2 = din("mlp_w2", (2, 4096, D))
    cd = {k: din("c_" + k, v.shape, _dt_of(v)) for k, v in consts.items()}
    out_d = nc.dram_tensor("out", [SEQ, D], F32, kind="ExternalOutput").ap()

    qk0 = dscr("qk0", (16, 64, SEQ), BF16)
    v0 = dscr("v0", (SEQ, 512), BF16)
    ot = dscr("ot", (D, SEQ), BF16)
    ot_p = dscr("ot_p", (512, SEQ), BF16)
    ot4 = dscr("ot4", (4, 256, SEQ), BF16)
    x2 = dscr("x2", (SEQ, D), F32)
    q1 = dscr("q1", (8, 64, SEQ), BF16)
    kf1 = dscr("kf1", (8, 64, SEQ), BF16)
    v1 = dscr("v1", (SEQ, 256), BF16)
    gts = dscr("gts", (24, SEQ), BF16)
    kcmp = dscr("kcmp", (2, 64, 512), BF16)
    vcmp = dscr("vcmp", (2, 512, 64), BF16)
    x1s = dscr("x1s", (SEQ, D), F32)
    uts_d = dscr("uts", (32, 128, SEQ), BF16)

    with ExitStack() as st:
        S = Sched(nc, st)
        NW = 50 * 1024
        arena_t = st.enter_context(nc.sbuf_tensor("arena", [128, NW], F32))
        AR = Arena(S, arena_t, NW)
        pbanks = [st.enter_context(nc.psum_tensor(f"pb{i}", [128, 512], F32)) for i in range(8)]

        def PS(i, name, parts=128, cols=512, dt=F32):
            ap = pbanks[i][0:parts, :]
            if dt != F32:
                ap = ap.bitcast(dt)
            ap = ap[:, 0:cols]
            return T(ap, S.buf(name))

        def dma(q, out, in_, reads, writes, key, slow=False):
            if slow:
                S.op(q, lambda e: e.dma_start(out=out, in_=in_, allow_slow_non_contiguous=True), reads=reads, writes=writes, dma_key=key)
            else:
                S.op(q, lambda e: e.dma_start(out=out, in_=in_), reads=reads, writes=writes, dma_key=key)

        def load_vt(VT, src_cols, width):
            for g4 in range(4):
                dma("sp", VT.ap[:, g4 * 16:(g4 + 1) * 16, 0:width],
                    src_cols[g4 * 2048:(g4 + 1) * 2048, :].rearrange("(t p) c -> p t c", p=128), [], [VT.b], "LVT")

        def load(dst, src, q="sp"):
            dma(q, dst.ap, src, [], [dst.b], "L" + dst.b.name)

        def store(dst, src, ap=None, q="pool"):
            dma(q, dst, src.ap if ap is None else ap, [src.b], [], "S" + src.b.name)

        def mm(out, outap, lhsT, lhsap, rhs, rhsap, start, stop, extra_r=()):
            S.op("pe", lambda e: e.matmul(outap, lhsap, rhsap, start=start, stop=stop),
                 reads=[lhsT.b, rhs.b] + list(extra_r), writes=[out.b])

        class Ring:
            def __init__(self, items):
                self.items = items
                self.i = 0

            def next(self):
                it = self.items[self.i % len(self.items)]
                self.i += 1
                return it

        rr_cast = [0]

        def cast(dst, dstap, src, srcap, scale_ap=None, scale_t=None):
            e = ("dve", "act", "pool")[rr_cast[0] % 3] if scale_ap is None else ("dve", "act")[rr_cast[0] % 2]
            rr_cast[0] += 1
            rd = [src.b] + ([scale_t.b] if scale_t is not None else [])
            if e == "act":
                if scale_ap is None:
                    S.op("act", lambda en: en.copy(dstap, srcap), reads=rd, writes=[dst.b])
                else:
                    S.op("act", lambda en: en.activation(dstap, srcap, AF.Copy, scale=scale_ap), reads=rd, writes=[dst.b])
            elif e == "dve":
                if scale_ap is None:
                    S.op("dve", lambda en: en.tensor_copy(dstap, srcap), reads=rd, writes=[dst.b])
                else:
                    S.op("dve", lambda en: en.tensor_scalar(dstap, srcap, scale_ap, None, ALU.mult), reads=rd, writes=[dst.b])
            else:
                S.op("pool", lambda en: en.tensor_copy(dstap, srcap), reads=rd, writes=[dst.b])

        def load_weight(dst, src_ap, K, N, stage_ring, gain=None, col0=0, ncols=None, dcol0=0):
            ncols = N if ncols is None else ncols
            for k in range(K // 128):
                c = 0
                while c < ncols:
                    w = min(2048, ncols - c)
                    stg = stage_ring.next()
                    dma("sp", stg.ap[:, 0:w], src_ap[k * 128:(k + 1) * 128, col0 + c:col0 + c + w], [], [stg.b], "L" + stg.b.name)
                    cast(dst, dst.ap[:, k, dcol0 + c:dcol0 + c + w], stg, stg.ap[:, 0:w],
                         None if gain is None else gain.ap[:, k:k + 1], gain)
                    c += w

        NWP = 0
        ident = AR.alloc(128, (128,), BF16, "ident")
        load(ident, cd["ident"])
        ones = AR.alloc(128, (128,), BF16, "ones")
        load(ones, cd["ones"])
        negones = AR.alloc(128, (128,), BF16, "negones")
        load(negones, cd["negones"])
        uneg = AR.alloc(128, (128,), BF16, "uneg")
        load(uneg, cd["uneg"])
        onesdiv = AR.alloc(128, (128,), F32, "onesdiv")
        load(onesdiv, cd["onesdiv"])
        bias_tab = AR.alloc(128, (NSLOT * BIAS_NM,), F32, "bias_tab")
        load(bias_tab, cd["bias_tab"])
        gains = AR.alloc(128, (4, 8), F32, "gains")
        for li in range(2):
            dma("sp", gains.ap[:, 2 * li, :], attn_norm[li].rearrange("(k p) -> p k", p=128), [], [gains.b], "Lgains", slow=True)
            dma("sp", gains.ap[:, 2 * li + 1, :], mlp_norm[li].rearrange("(k p) -> p k", p=128), [], [gains.b], "Lgains", slow=True)
        eps_t = AR.alloc(128, (1,), F32, "eps_t")
        S.op("dve", lambda e: e.memset(eps_t.ap, 1e-6), writes=[eps_t.b])
        persist_off = AR.off

        def bias_ap(si, m):
            col = si * BIAS_NM + (m - BIAS_M0)
            return bias_tab.ap[:, col:col + 1]

        def phase_reset():
            S.barrier()
            AR.off = persist_off

        def norm_transpose(xt, hn, ss, rs, junk, tp, hT, r):
            S.op("act", lambda e: e.activation(junk.ap, xt.ap, AF.Square, accum_out=ss.ap), reads=[xt.b], writes=[junk.b, ss.b])
            S.op("act", lambda e: e.activation(rs.ap, ss.ap, AF.Sqrt, bias=eps_t.ap, scale=1.0 / D), reads=[ss.b, eps_t.b], writes=[rs.b])
            S.op("dve", lambda e: e.reciprocal(rs.ap, rs.ap), reads=[rs.b], writes=[rs.b])
            S.op("act", lambda e: e.activation(hn.ap, xt.ap, AF.Copy, scale=rs.ap), reads=[xt.b, rs.b], writes=[hn.b])
            for k in range(8):
                S.op("pe", lambda e, k=k: e.transpose(tp.ap[:, k * 128:(k + 1) * 128], hn.ap[:, k * 128:(k + 1) * 128], ident.ap),
                     reads=[hn.b, ident.b], writes=[tp.b])
            S.op("dve", lambda e: e.tensor_copy(hT.ap[:, :, r * 128:(r + 1) * 128], tp.ap.rearrange("p (k c) -> p k c", k=8)),
                 reads=[tp.b], writes=[hT.b])

        def phase_proj(x_src, w_src, ncols_total, gain_idx, fm_chunks, tm_chunks):
            stage = Ring([AR.alloc(128, (2048,), F32, f"wstg{i}") for i in range(2)])
            W = AR.alloc(128, (8, ncols_total), BF16, "Win")
            gsl = T(gains.ap[:, gain_idx, :], gains.b)
            load_weight(W, w_src, D, ncols_total, stage, gain=gsl)
            xts = Ring([AR.alloc(128, (D,), F32, f"xt{i}") for i in range(2)])
            hns = Ring([AR.alloc(128, (D,), BF16, f"hn{i}") for i in range(2)])
            junk = AR.alloc(128, (D,), BF16, "junk")
            sss = Ring([AR.alloc(128, (1,), F32, f"ss{i}") for i in range(2)])
            rss = Ring([AR.alloc(128, (1,), F32, f"rs{i}") for i in range(2)])
            hTs = Ring([AR.alloc(128, (8, 512), BF16, f"hT{i}") for i in range(2)])
            tps = Ring([PS(i, f"tp{i}", cols=1024, dt=BF16) for i in (0, 1)])
            pps = Ring([PS(i, f"pp{i}") for i in (2, 3, 4, 5)])
            evs = Ring([AR.alloc(128, (512,), BF16, f"ev{i}") for i in range(4)])
            ev_i = [0]
            for tb in range(NQT):
                hT = hTs.next()
                for r in range(4):
                    xt = xts.next()
                    row0 = tb * 512 + r * 128
                    load(xt, x_src[row0:row0 + 128, :])
                    norm_transpose(xt, hns.next(), sss.next(), rss.next(), junk, tps.next(), hT, r)
                for (col0, M, dsts, scale, func) in fm_chunks:
                    pp = pps.next()
                    for k in range(8):
                        mm(pp, pp.ap[0:M, :], W, W.ap[:, k, col0:col0 + M], hT, hT.ap[:, k, :], k == 0, k == 7)
                    ev = evs.next()
                    eng = ("act", "dve")[ev_i[0] % 2] if func is None else "act"
                    ev_i[0] += 1
                    if eng == "act":
                        S.op("act", lambda e, pp=pp, ev=ev, M=M, scale=scale, func=func: e.activation(
                            ev.ap[0:M, :], pp.ap[0:M, :], AF.Copy if func is None else func, scale=scale),
                            reads=[pp.b], writes=[ev.b])
                    else:
                        S.op("dve", lambda e, pp=pp, ev=ev, M=M, scale=scale: e.tensor_scalar(
                            ev.ap[0:M, :], pp.ap[0:M, :], float(scale), None, ALU.mult), reads=[pp.b], writes=[ev.b])
                    for (dfn, p0, npart) in dsts:
                        store(dfn(tb), ev, ap=ev.ap[p0:p0 + npart, :])
                for (col0, ncols, dfn) in tm_chunks:
                    for r in range(4):
                        pp = pps.next()
                        for k in range(8):
                            mm(pp, pp.ap[:, 0:ncols], hT, hT.ap[:, k, r * 128:(r + 1) * 128], W, W.ap[:, k, col0:col0 + ncols], k == 0, k == 7)
                        ev = evs.next()
                        eng = ("act", "dve")[ev_i[0] % 2]
                        ev_i[0] += 1
                        if eng == "act":
                            S.op("act", lambda e, pp=pp, ev=ev, n=ncols: e.copy(ev.ap[:, 0:n], pp.ap[:, 0:n]), reads=[pp.b], writes=[ev.b])
                        else:
                            S.op("dve", lambda e, pp=pp, ev=ev, n=ncols: e.tensor_copy(ev.ap[:, 0:n], pp.ap[:, 0:n]), reads=[pp.b], writes=[ev.b])
                        store(dfn(tb * 512 + r * 128), ev, ap=ev.ap[:, 0:ncols])

        fm = []
        for jc in range(8):
            isq = jc in (0, 1, 4, 5)
            dsts = [(lambda tb, slot=2 * jc + half: qk0[slot, :, tb * 512:(tb + 1) * 512], 64 * half, 64) for half in range(2)]
            fm.append((128 * jc, 128, dsts, 0.125 if isq else 1.0, None))
        tm = [(1024, 512, lambda row0: v0[row0:row0 + 128, 0:512])]
        phase_proj(x_in, ev_w_in, 1536, 0, fm, tm)
        phase_reset()

        class Pipe:
            def __init__(self):
                self.q = []
                self.t = 0

            def job(self, stages):
                for lag, fn in stages:
                    if lag == 0:
                        fn()
                    else:
                        self.q.append((self.t + lag, fn))
                rest = []
                for due, fn in self.q:
                    if due <= self.t:
                        fn()
                    else:
                        rest.append((due, fn))
                self.q = rest
                self.t += 1

            def flush(self):
                for due, fn in sorted(self.q, key=lambda x_: x_[0]):
                    fn()
                self.q = []

        def ktiles_for(qt, back):
            hi = 4 * qt + 3
            lo = 0 if back is None else max(0, 4 * qt - back)
            return list(range(hi, lo - 1, -1))

        def alibi_back(slope):
            dcut = ALIBI_CUT / slope
            bk = int(math.floor((dcut + 127.0) / 128.0))
            return bk

        ccbuf = S.buf("ccbuf")
        cc_n = [0]

        def all_gather_ot():
            S.barrier()
            for k in range(4):
                cc_n[0] += 1
                S.op("pool", lambda e, k=k: e.collective_compute("AllGather", ALU.bypass, replica_groups=[[0, 1], [2, 3], [4, 5], [6, 7]],
                                                                 ins=[ot_p[k * 128:(k + 1) * 128, :]], outs=[ot4[k]]),
                     writes=[ccbuf], dma_key=f"cc{cc_n[0]}")
            S.bufs.append(ccbuf)

        def attention_l0():
            mask_sb = AR.alloc(128, (4, 512), BF16, "mask_sb")
            load(mask_sb, cd["mask_sb"])
            mask_c = AR.alloc(128, (4, 512), BF16, "mask_c")
            load(mask_c, cd["mask_c"])
            KT = AR.alloc(70, (SEQ,), BF16, "KT")
            VT = AR.alloc(128, (NKT, 128), BF16, "VT")
            QTs = Ring([AR.alloc(70, (512,), BF16, f"QT{i}") for i in range(2)])
            Es = Ring([AR.alloc(128, (512,), F32, f"E{i}") for i in range(2)])
            SPs = Ring([AR.alloc(128, (512,), BF16, f"SP{i}") for i in range(4)])
            Ws = Ring([AR.alloc(128, (512,), BF16, f"Wt{i}") for i in range(4)])
            ssum = AR.alloc(128, (512,), F32, "ssum")
            sshs = Ring([AR.alloc(128, (512,), BF16, f"ssh{i}") for i in range(4)])
            oev = Ring([AR.alloc(128, (512,), BF16, f"oev{i}") for i in range(2)])
            Ys = Ring([PS(i, f"Y{i}") for i in (0, 1, 2)])
            Oaccs = [PS(3, "Oacc0"), PS(4, "Oacc1")]
            for h in range(SB_PER_CORE):
                pipe = Pipe()
                load(T(KT.ap[0:64, :], KT.b), qk0[4 + h])
                load_vt(VT, v0[:, h * 64:(h + 1) * 64], 64)
                for qt in range(NQT):
                    QT = QTs.next()
                    dma("sp", QT.ap[0:64, :], qk0[h, :, qt * 512:(qt + 1) * 512], [], [QT.b], "L" + QT.b.name)
                    kts = ktiles_for(qt, SB_BACK)
                    n = len(kts)
                    Oacc = Oaccs[qt % 2]
                    ssh_prev = None
                    for i, kt in enumerate(kts):
                        Y = Ys.next()
                        E = Es.next()
                        SP = SPs.next()
                        Wt = Ws.next()
                        ssh = sshs.next() if 0 < i < n - 1 else None
                        o = kt - 4 * qt

                        def st0(Y=Y, E=E, SP=SP, o=o, kt=kt, QT=QT, i=i, n=n, ssh=ssh):
                            mm(Y, Y.ap, KT, KT.ap[0:64, kt * 128:(kt + 1) * 128], QT, QT.ap[0:64, :], True, False)
                            if o >= 0:
                                mm(Y, Y.ap, ident, ident.ap, mask_sb, mask_sb.ap[:, o, :], False, False)
                            S.op("act", lambda e: e.activation(E.ap, Y.ap, AF.Exp), reads=[Y.b], writes=[E.b])
                            S.op("act", lambda e: e.activation(SP.ap, E.ap, AF.Ln, bias=1.0), reads=[E.b], writes=[SP.b])
                            if i < n - 1:
                                if i == 0:
                                    S.op("pool", lambda e: e.tensor_copy(ssum.ap, SP.ap), reads=[SP.b], writes=[ssum.b])
                                else:
                                    S.op("pool", lambda e: e.tensor_tensor(ssum.ap, ssum.ap, SP.ap, ALU.add), reads=[SP.b, ssum.b], writes=[ssum.b])
                                    S.op("pool", lambda e: e.tensor_copy(ssh.ap, ssum.ap), reads=[ssum.b], writes=[ssh.b])

                        def st1(Y=Y, SP=SP, Wt=Wt, prev=ssh_prev):
                            mm(Y, Y.ap, uneg, uneg.ap, SP, SP.ap, False, prev is None)
                            if prev is not None:
                                mm(Y, Y.ap, negones, negones.ap, prev, prev.ap, False, True)
                            S.op("act", lambda e: e.activation(Wt.ap, Y.ap, AF.Exp), reads=[Y.b], writes=[Wt.b])

                        def st2(Wt=Wt, kt=kt, i=i, n=n, Oacc=Oacc, h=h, qt=qt):
                            mm(Oacc, Oacc.ap[0:64, :], VT, VT.ap[:, kt, 0:64], Wt, Wt.ap, i == 0, i == n - 1)
                            if i == n - 1:
                                ev = oev.next()
                                S.op("dve", lambda e: e.tensor_copy(ev.ap[0:64, :], Oacc.ap[0:64, :]), reads=[Oacc.b], writes=[ev.b])
                                store(ot_p[h * 64:(h + 1) * 64, qt * 512:(qt + 1) * 512], ev, ap=ev.ap[0:64, :])

                        pipe.job([(0, st0), (1, st1), (2, st2)])
                        if i < n - 1:
                            ssh_prev = SP if i == 0 else ssh
                pipe.flush()
            lam_t = AR.alloc(128, (4, 64), F32, "lam_t")
            for i in range(4):
                dma("sp", lam_t.ap[:, i, :], lamv[i].partition_broadcast(128), [], [lam_t.b], "Llam")
            lam_p = AR.alloc(128, (2, 64), F32, "lam_p")
            lam_s = AR.alloc(128, (2,), F32, "lam_s")
            neglam = AR.alloc(128, (1,), F32, "neglam")
            S.op("dve", lambda e: e.tensor_tensor(lam_p.ap[:, 0, :], lam_t.ap[:, 0, :], lam_t.ap[:, 1, :], ALU.mult), reads=[lam_t.b], writes=[lam_p.b])
            S.op("dve", lambda e: e.tensor_tensor(lam_p.ap[:, 1, :], lam_t.ap[:, 2, :], lam_t.ap[:, 3, :], ALU.mult), reads=[lam_t.b, lam_p.b], writes=[lam_p.b])
            S.op("dve", lambda e: e.reduce_sum(lam_s.ap, lam_p.ap, axis=mybir.AxisListType.X), reads=[lam_p.b], writes=[lam_s.b])
            S.op("act", lambda e: e.activation(lam_s.ap, lam_s.ap, AF.Exp), reads=[lam_s.b], writes=[lam_s.b])
            lam_init = 0.8 - 0.6 * math.exp(-0.3 * 0)
            S.op("dve", lambda e: e.tensor_tensor(neglam.ap, lam_s.ap[:, 1:2], lam_s.ap[:, 0:1], ALU.subtract), reads=[lam_s.b], writes=[neglam.b])
            S.op("dve", lambda e: e.tensor_scalar(neglam.ap, neglam.ap, -lam_init, None, ALU.add), reads=[neglam.b], writes=[neglam.b])
            sg = AR.alloc(128, (1,), F32, "sg")
            dma("sp", sg.ap, ev_subln.rearrange("(p o) -> p o", o=1), [], [sg.b], "Lsg")
            S.op("dve", lambda e: e.tensor_scalar(sg.ap, sg.ap, 1.0 - lam_init, None, ALU.mult), reads=[sg.b], writes=[sg.b])
            KT2 = [KT, AR.alloc(70, (SEQ,), BF16, "KTb")]
            for kk in KT2:
                dma("sp", kk.ap[64:70, :], cd["kaug"], [], [kk.b], "L" + kk.b.name)
            QT2 = [[AR.alloc(70, (512,), BF16, f"QD{c}{i}") for i in range(2)] for c in range(2)]
            Ps = Ring([AR.alloc(128, (512,), BF16, f"P{i}") for i in range(4)])
            NUM = [PS(3, "NUM0"), PS(4, "NUM1")]
            DEN = [PS(5, "DEN0"), PS(6, "DEN1")]
            MS = PS(7, "MS")
            r_t = [AR.alloc(128, (512,), F32, f"rden{c}") for c in range(2)]
            a_t = [AR.alloc(128, (512,), F32, f"a{c}") for c in range(2)]
            o_t = AR.alloc(128, (512,), F32, "o_t")
            sq_t = AR.alloc(128, (512,), F32, "sq_t")
            for h in range(2):
                pipe = Pipe()
                back = alibi_back(DIFF_SLOPES[1]) if h == 0 else None
                for c in range(2):
                    dma("sp", KT2[c].ap[0:64, :], qk0[12 + 2 * h + c], [], [KT2[c].b], "L" + KT2[c].b.name)
                load_vt(VT, v0[:, 256 + h * 128:256 + (h + 1) * 128], 128)
                for qt in range(NQT):
                    for c in range(2):
                        QT = QT2[c][qt % 2]
                        dma("sp", QT.ap[0:64, :], qk0[8 + 2 * h + c, :, qt * 512:(qt + 1) * 512], [], [QT.b], "L" + QT.b.name)
                        dma("sp", QT.ap[64:70, :], cd["qaug"][:, h, :], [], [QT.b], "L" + QT.b.name)
                        kts = ktiles_for(qt, back)
                        n = len(kts)
                        for i, kt in enumerate(kts):
                            Y = Ys.next()
                            P = Ps.next()
                            o = kt - 4 * qt

                            def st0(Y=Y, P=P, o=o, kt=kt, QT=QT, c=c, qt=qt, h=h):
                                mm(Y, Y.ap, KT2[c], KT2[c].ap[:, kt * 128:(kt + 1) * 128], QT, QT.ap, True, o < 0)
                                if o >= 0:
                                    mm(Y, Y.ap, ident, ident.ap, mask_c, mask_c.ap[:, o, :], False, True)
                                S.op("act", lambda e: e.activation(P.ap, Y.ap, AF.Exp, bias=bias_ap(h, 4 * qt - kt)),
                                     reads=[Y.b, bias_tab.b], writes=[P.b])

                            def st2(P=P, kt=kt, i=i, n=n, c=c, h=h, qt=qt):
                                mm(NUM[c], NUM[c].ap, VT, VT.ap[:, kt, :], P, P.ap, i == 0, i == n - 1)
                                mm(DEN[c], DEN[c].ap, ones, ones.ap, P, P.ap, i == 0, i == n - 1)
                                if i == n - 1:
                                    S.op("dve", lambda e: e.reciprocal(r_t[c].ap, DEN[c].ap), reads=[DEN[c].b], writes=[r_t[c].b])
                                    S.op("dve", lambda e: e.tensor_tensor(a_t[c].ap, NUM[c].ap, r_t[c].ap, ALU.mult), reads=[NUM[c].b, r_t[c].b], writes=[a_t[c].b])
                                    if c == 1:
                                        S.op("dve", lambda e: e.scalar_tensor_tensor(o_t.ap, a_t[1].ap, neglam.ap, a_t[0].ap, ALU.mult, ALU.add),
                                             reads=[a_t[0].b, a_t[1].b, neglam.b], writes=[o_t.b])
                                        S.op("act", lambda e: e.activation(sq_t.ap, o_t.ap, AF.Square), reads=[o_t.b], writes=[sq_t.b])
                                        mm(MS, MS.ap, onesdiv, onesdiv.ap, sq_t, sq_t.ap, True, True)
                                        S.op("act", lambda e: e.activation(sq_t.ap, MS.ap, AF.Sqrt, bias=eps_t.ap), reads=[MS.b, eps_t.b], writes=[sq_t.b])
                                        S.op("dve", lambda e: e.reciprocal(sq_t.ap, sq_t.ap), reads=[sq_t.b], writes=[sq_t.b])
                                        ev = oev.next()
                                        S.op("dve", lambda e: e.scalar_tensor_tensor(ev.ap, o_t.ap, sg.ap, sq_t.ap, ALU.mult, ALU.mult),
                                             reads=[o_t.b, sg.b, sq_t.b], writes=[ev.b])
                                        store(ot_p[256 + h * 128:256 + (h + 1) * 128, qt * 512:(qt + 1) * 512], ev)

                            pipe.job([(0, st0), (2, st2)])
                pipe.flush()

        attention_l0()
        all_gather_ot()
        phase_reset()

        def phase_mlp(x_src, wout_src, w1_src, w2_src, gain_idx, dst, final_gain=None):
            stage = Ring([AR.alloc(128, (2048,), F32, f"wstg{i}") for i in range(2)])
            Wo = AR.alloc(128, (8, D), BF16, "Wo")
            W1 = AR.alloc(128, (8, 4096), BF16, "W1")
            gsl = T(gains.ap[:, gain_idx, :], gains.b)
            load_weight(Wo, wout_src, D, D, stage)
            load_weight(W1, w1_src, D, 4096, stage, gain=gsl)
            OTs = Ring([AR.alloc(128, (8, 512), BF16, f"OT{i}") for i in range(2)])
            x1r = Ring([AR.alloc(128, (D,), F32, f"x1r{i}") for i in range(3)])
            xts = Ring([AR.alloc(128, (D,), F32, f"xt{i}") for i in range(2)])
            hns = Ring([AR.alloc(128, (D,), BF16, f"hn{i}") for i in range(2)])
            junk = AR.alloc(128, (D,), BF16, "junk")
            sss = Ring([AR.alloc(128, (1,), F32, f"ss{i}") for i in range(2)])
            rss = Ring([AR.alloc(128, (1,), F32, f"rs{i}") for i in range(2)])
            hTs = Ring([AR.alloc(128, (8, 512), BF16, f"hT{i}") for i in range(2)])
            uts = Ring([AR.alloc(128, (512,), BF16, f"ut{i}") for i in range(4)])
            sqs = Ring([AR.alloc(128, (512,), BF16, f"sq{i}") for i in range(2)])
            tps = Ring([PS(i, f"tp{i}", cols=1024, dt=BF16) for i in (0, 1)])
            pps = Ring([PS(i, f"pp{i}") for i in (2, 3, 4, 5, 6, 7)])
            for tb in range(NQT):
                OT = OTs.next()
                for kc in range(8):
                    dma("sp", OT.ap[:, kc, :], ot4[kc % 4, (kc // 4) * 128:(kc // 4 + 1) * 128, tb * 512:(tb + 1) * 512], [], [OT.b], "L" + OT.b.name)
                hT = hTs.next()
                for r in range(4):
                    row0 = tb * 512 + r * 128
                    xt = xts.next()
                    load(xt, x_src[row0:row0 + 128, :])
                    x1v = x1r.next()
                    for half in range(2):
                        pp = pps.next()
                        for k in range(8):
                            mm(pp, pp.ap, OT, OT.ap[:, k, r * 128:(r + 1) * 128], Wo, Wo.ap[:, k, half * 512:(half + 1) * 512], k == 0, k == 7)
                        S.op("dve", lambda e, pp=pp, xt=xt, x1v=x1v, half=half: e.tensor_tensor(
                            x1v.ap[:, half * 512:(half + 1) * 512], pp.ap, xt.ap[:, half * 512:(half + 1) * 512], ALU.add),
                            reads=[pp.b, xt.b], writes=[x1v.b])
                    store(x1s[row0:row0 + 128, :], x1v)
                    norm_transpose(x1v, hns.next(), sss.next(), rss.next(), junk, tps.next(), hT, r)
                for fc in range(32):
                    pp = pps.next()
                    for k in range(8):
                        mm(pp, pp.ap, W1, W1.ap[:, k, fc * 128:(fc + 1) * 128], hT, hT.ap[:, k, :], k == 0, k == 7)
                    sq = sqs.next()
                    u = uts.next()
                    S.op("act", lambda e, pp=pp, sq=sq: e.activation(sq.ap, pp.ap, AF.Square), reads=[pp.b], writes=[sq.b])
                    S.op("dve", lambda e, pp=pp, sq=sq, u=u: e.scalar_tensor_tensor(u.ap, pp.ap, 0.0, sq.ap, ALU.is_gt, ALU.mult),
                         reads=[pp.b, sq.b], writes=[u.b])
                    store(uts_d[fc, :, tb * 512:(tb + 1) * 512], u)
            phase_reset()
            stage = Ring([AR.alloc(128, (2048,), F32, f"wstg{i}") for i in range(2)])
            W2 = AR.alloc(128, (32, D), BF16, "W2")
            load_weight(W2, w2_src, 4096, D, stage)
            fg = None
            if final_gain is not None:
                fg = AR.alloc(128, (D,), F32, "fg")
                load(fg, final_gain.partition_broadcast(128))
            UTs = Ring([AR.alloc(128, (32, 512), BF16, f"UT{i}") for i in range(2)])
            x1r = Ring([AR.alloc(128, (D,), F32, f"x1r{i}") for i in range(3)])
            ys = Ring([AR.alloc(128, (D,), F32, f"y{i}") for i in range(3)])
            junk = AR.alloc(128, (D,), BF16, "junk")
            sss = Ring([AR.alloc(128, (1,), F32, f"ss{i}") for i in range(2)])
            rss = Ring([AR.alloc(128, (1,), F32, f"rs{i}") for i in range(2)])
            pps = Ring([PS(i, f"pp{i}") for i in (0, 1, 2, 3, 4, 5)])
            for tb in range(NQT):
                UT = UTs.next()
                for f4 in range(4):
                    dma("sp", UT.ap[:, f4 * 8:(f4 + 1) * 8, :], uts_d[f4 * 8:(f4 + 1) * 8, :, tb * 512:(tb + 1) * 512].rearrange("f p t -> p f t"),
                        [], [UT.b], "L" + UT.b.name)
                for r in range(4):
                    row0 = tb * 512 + r * 128
                    x1v = x1r.next()
                    load(x1v, x1s[row0:row0 + 128, :])
                    y = ys.next()
                    for half in range(2):
                        pp = pps.next()
                        for fc in range(32):
                            mm(pp, pp.ap, UT, UT.ap[:, fc, r * 128:(r + 1) * 128], W2, W2.ap[:, fc, half * 512:(half + 1) * 512], fc == 0, fc == 31)
                        S.op("dve", lambda e, pp=pp, y=y, x1v=x1v, half=half: e.tensor_tensor(
                            y.ap[:, half * 512:(half + 1) * 512], pp.ap, x1v.ap[:, half * 512:(half + 1) * 512], ALU.add),
                            reads=[pp.b, x1v.b], writes=[y.b])
                    if fg is not None:
                        ss = sss.next()
                        rs = rss.next()
                        S.op("act", lambda e, y=y, ss=ss: e.activation(junk.ap, y.ap, AF.Square, accum_out=ss.ap), reads=[y.b], writes=[junk.b, ss.b])
                        S.op("act", lambda e, ss=ss, rs=rs: e.activation(rs.ap, ss.ap, AF.Sqrt, bias=eps_t.ap, scale=1.0 / D), reads=[ss.b, eps_t.b], writes=[rs.b])
                        S.op("dve", lambda e, rs=rs: e.reciprocal(rs.ap, rs.ap), reads=[rs.b], writes=[rs.b])
                        S.op("dve", lambda e, y=y, rs=rs: e.scalar_tensor_tensor(y.ap, y.ap, rs.ap, fg.ap, ALU.mult, ALU.mult),
                             reads=[y.b, rs.b, fg.b], writes=[y.b])
                    store(dst[row0:row0 + 128, :], y)

        phase_mlp(x_in, ev_w_out, mlp_w1[0], mlp_w2[0], 1, x2)
        if STOP_AFTER == "L0":
            S.barrier()
            cp = Ring([AR.alloc(128, (D,), F32, f"cp{i}") for i in range(2)])
            for i in range(NKT):
                c_ = cp.next()
                load(c_, x2[i * 128:(i + 1) * 128, :])
                store(out_d[i * 128:(i + 1) * 128, :], c_)
            S.barrier()
            S.finalize()
            return nc, S
        phase_reset()

        fm = []
        for jc in range(4):
            dsts = [(lambda tb, slot=2 * jc + half: q1[slot, :, tb * 512:(tb + 1) * 512], 64 * half, 64) for half in range(2)]
            fm.append((128 * jc, 128, dsts, 0.125, None))
        for ji in range(4):
            dsts = [(lambda tb, slot=2 * ji + half: kf1[slot, :, tb * 512:(tb + 1) * 512], 64 * half, 64) for half in range(2)]
            fm.append((512 + 128 * ji, 128, dsts, 1.0, None))
        fm.append((1024, 24, [(lambda tb: gts[:, tb * 512:(tb + 1) * 512], 0, 24)], 1.0, AF.Sigmoid))
        tm = [(1048, 256, lambda row0: v1[row0:row0 + 128, 0:256])]
        phase_proj(x2, od_w_in, 1304, 2, fm, tm)
        phase_reset()

        def phase_compress():
            stg = AR.alloc(64, (32 * 256,), F32, "cstg")
            W1c = AR.alloc(64, (32, 256), BF16, "W1c")
            stg2 = AR.alloc(128, (2, 64), F32, "cstg2")
            W2c = AR.alloc(128, (2, 64), BF16, "W2c")
            posf = AR.alloc(64, (32,), F32, "posf")
            posT = AR.alloc(64, (32,), BF16, "posT")
            biash = AR.alloc(128, (2,), F32, "biash")
            src = AR.alloc(64, (SEQ,), BF16, "csrc")
            hid = AR.alloc(128, (2, 512), BF16, "hid")
            u_t = AR.alloc(128, (512,), F32, "u_t")
            w_t = AR.alloc(128, (512,), F32, "w_t")
            evk = AR.alloc(64, (512,), BF16, "evk")
            evv = Ring([AR.alloc(128, (64,), BF16, f"evv{i}") for i in range(2)])
            pb_ = PS(0, "cbias")
            ph = Ring([PS(1, "ph0"), PS(2, "ph1")])
            po = Ring([PS(3, "po0"), PS(4, "po1")])
            S.op("dve", lambda e: e.memset(hid.ap, 0.0), writes=[hid.b])
            S.op("dve", lambda e: e.memset(evk.ap, 0.0), writes=[evk.b])
            for kind, (w1d, w2d, posd) in enumerate(((cw1k, cw2k, pos_k), (cw1v, cw2v, pos_v))):
                dma("sp", stg.ap.rearrange("p (l h) -> p l h", l=32), w1d.rearrange("(l d) h -> d l h", d=64), [], [stg.b], "Lcstg")
                for q4 in range(4):
                    cast(W1c, W1c.ap[:, q4 * 8:(q4 + 1) * 8, :], stg, stg.ap.rearrange("p (l h) -> p l h", l=32)[:, q4 * 8:(q4 + 1) * 8, :])
                dma("sp", stg2.ap, w2d.rearrange("(c p) n -> p c n", p=128), [], [stg2.b], "Lcstg2")
                cast(W2c, W2c.ap, stg2, stg2.ap)
                dma("sp", posf.ap, posd.rearrange("l d -> d l"), [], [posf.b], "Lposf", slow=True)
                cast(posT, posT.ap, posf, posf.ap)
                for hc in range(2):
                    for l in range(32):
                        mm(pb_, pb_.ap[:, 0:1], W1c, W1c.ap[:, l, hc * 128:(hc + 1) * 128], posT, posT.ap[:, l:l + 1], l == 0, l == 31)
                    S.op("dve", lambda e, hc=hc: e.tensor_copy(biash.ap[:, hc:hc + 1], pb_.ap[:, 0:1]), reads=[pb_.b], writes=[biash.b])
                for g in range(2):
                    load(src, kf1[2 * kind + g])
                    for hc in range(2):
                        p_ = ph.next()
                        for l in range(32):
                            mm(p_, p_.ap[:, 0:511], W1c, W1c.ap[:, l, hc * 128:(hc + 1) * 128], src, src.ap[:, l:l + 16 * 510 + 1:16], l == 0, l == 31)
                        S.op("act", lambda e, p_=p_, hc=hc: e.activation(u_t.ap[:, 0:511], p_.ap[:, 0:511], AF.Identity, bias=biash.ap[:, hc:hc + 1]),
                             reads=[p_.b, biash.b], writes=[u_t.b])
                        S.op("act", lambda e: e.activation(w_t.ap[:, 0:511], u_t.ap[:, 0:511], AF.Square), reads=[u_t.b], writes=[w_t.b])
                        S.op("dve", lambda e: e.tensor_scalar(w_t.ap[:, 0:511], w_t.ap[:, 0:511], 0.044715, 1.0, ALU.mult, ALU.add), reads=[w_t.b], writes=[w_t.b])
                        S.op("dve", lambda e: e.tensor_tensor(w_t.ap[:, 0:511], w_t.ap[:, 0:511], u_t.ap[:, 0:511], ALU.mult), reads=[w_t.b, u_t.b], writes=[w_t.b])
                        S.op("act", lambda e: e.activation(w_t.ap[:, 0:511], w_t.ap[:, 0:511], AF.Sigmoid, scale=2.0 * 0.7978845608028654), reads=[w_t.b], writes=[w_t.b])
                        S.op("dve", lambda e, hc=hc: e.tensor_tensor(hid.ap[:, hc, 0:511], w_t.ap[:, 0:511], u_t.ap[:, 0:511], ALU.mult), reads=[w_t.b, u_t.b], writes=[hid.b])
                    if kind == 0:
                        p2 = po.next()
                        for hc in range(2):
                            mm(p2, p2.ap[0:64, 0:511], W2c, W2c.ap[:, hc, :], hid, hid.ap[:, hc, 0:511], hc == 0, hc == 1)
                        S.op("dve", lambda e, p2=p2: e.tensor_copy(evk.ap[:, 0:511], p2.ap[0:64, 0:511]), reads=[p2.b], writes=[evk.b])
                        store(kcmp[g], evk)
                    else:
                        for nchunk in range(4):
                            p2 = po.next()
                            for hc in range(2):
                                mm(p2, p2.ap[:, 0:64], hid, hid.ap[:, hc, nchunk * 128:(nchunk + 1) * 128], W2c, W2c.ap[:, hc, :], hc == 0, hc == 1)
                            ev = evv.next()
                            S.op("dve", lambda e, p2=p2, ev=ev: e.tensor_copy(ev.ap, p2.ap[:, 0:64]), reads=[p2.b], writes=[ev.b])
                            store(vcmp[g, nchunk * 128:(nchunk + 1) * 128, :], ev)

        phase_compress()
        phase_reset()

        def attention_l1():
            mask_c = AR.alloc(128, (4, 512), BF16, "mask_c")
            load(mask_c, cd["mask_c"])
            mask_cmp = AR.alloc(128, (5, 512), BF16, "mask_cmp")
            load(mask_cmp, cd["mask_cmp"])
            mask_win = AR.alloc(128, (8, 512), BF16, "mask_win")
            load(mask_win, cd["mask_win"])
            ewide = AR.alloc(128, (SEQ,), BF16, "ewide")
            load(ewide, cd["ewide"])
            ovl = AR.alloc(128, (4, 128), BF16, "ovl")
            load(ovl, cd["ovl"])
            tka = AR.alloc(128, (256,), F32, "tka")
            load(tka, cd["topk_a"])
            tkm = AR.alloc(128, (256,), F32, "tkm")
            load(tkm, cd["topk_m"])
            bcmp = AR.alloc(128, (8 * NQT * 4,), F32, "bcmp")
            load(bcmp, cd["bias_cmp"])
            KcA = AR.alloc(70, (512,), BF16, "KcA")
            dma("sp", KcA.ap[64:70, :], cd["kaug_cmp"], [], [KcA.b], "LKcA")
            Vc = AR.alloc(128, (4, 64), BF16, "Vc")
            KsA = AR.alloc(70, (SEQ,), BF16, "KsA")
            KwA = AR.alloc(70, (SEQ,), BF16, "KwA")
            dma("sp", KsA.ap[64:70, :], cd["kaug"], [], [KsA.b], "LKsA")
            dma("sp", KwA.ap[64:70, :], cd["kaug"], [], [KwA.b], "LKwA")
            Vs = AR.alloc(128, (NKT, 64), BF16, "Vs")
            Vw = AR.alloc(128, (NKT, 64), BF16, "Vw")
            QAs = [[AR.alloc(70, (512,), BF16, f"QA{r}{i}") for i in range(2)] for r in range(4)]
            GBs = [[AR.alloc(64, (3, 512), BF16, f"GB{r}{i}") for i in range(2)] for r in range(4)]
            Pc = Ring([AR.alloc(128, (512,), BF16, f"Pc{i}") for i in range(10)])
            Ps = Ring([AR.alloc(128, (512,), BF16, f"Pp{i}") for i in range(4)])
            pcn = Ring([AR.alloc(128, (512,), BF16, f"pcn{i}") for i in range(4)])
            rdc = AR.alloc(128, (512,), F32, "rdc")
            ocmp = [AR.alloc(64, (512,), F32, f"ocmp{r}") for r in range(4)]
            selbT = AR.alloc(128, (512,), BF16, "selbT")
            imp2 = AR.alloc(128, (128,), F32, "imp2")
            tmp2 = AR.alloc(128, (128,), F32, "tmp2")
            selm = AR.alloc(128, (128,), F32, "selm")
            selb = AR.alloc(128, (128,), BF16, "selb")
            v8a = AR.alloc(128, (8,), F32, "v8a")
            v8b = AR.alloc(128, (8,), F32, "v8b")
            rs2 = [AR.alloc(64, (512,), F32, f"rs2{i}") for i in range(2)]
            ob2 = [AR.alloc(64, (512,), F32, f"ob2{i}") for i in range(2)]
            acc = AR.alloc(64, (512,), F32, "acc")
            t1 = AR.alloc(64, (512,), F32, "t1")
            oev = Ring([AR.alloc(64, (512,), BF16, f"oev{i}") for i in range(2)])
            Ys = Ring([PS(0, "Y0"), PS(1, "Y1"), PS(2, "Y2")])
            NUMs = [PS(3, "NUMa"), PS(5, "NUMb")]
            DENs = [PS(4, "DENa"), PS(6, "DENb")]
            IMP = PS(7, "IMP")
            TRP = T(pbanks[6][:, :].bitcast(BF16)[:, 0:512], DENs[1].b)

            def tile_job(pipe, K, kap, QAr, extra, bias, V, vap, NUM, DEN, den_parts, P, first, last, tail):
                Y = Ys.next()

                def st0():
                    nx = len(extra)
                    mm(Y, Y.ap, K, kap, QAr, QAr.ap, True, nx == 0)
                    for xi, (lt, lap, rt, rap) in enumerate(extra):
                        mm(Y, Y.ap, lt, lap, rt, rap, False, xi == nx - 1)
                    S.op("act", lambda e: e.activation(P.ap, Y.ap, AF.Exp, bias=bias[0]), reads=[Y.b, bias[1]], writes=[P.b])

                def st2():
                    mm(NUM, NUM.ap[0:64, :], V, vap, P, P.ap, first, last)
                    mm(DEN, DEN.ap[0:den_parts, :], ones, ones.ap[:, 0:den_parts], P, P.ap, first, last)
                    if last and tail is not None:
                        tail()

                pipe.job([(0, st0), (2, st2)])

            for g in range(2):
                dma("sp", KcA.ap[0:64, :], kcmp[g], [], [KcA.b], "LKcA")
                dma("sp", Vc.ap, vcmp[g].rearrange("(c p) d -> p c d", p=128), [], [Vc.b], "LVc")
                dma("sp", KsA.ap[0:64, :], kf1[4 + g], [], [KsA.b], "LKsA")
                dma("sp", KwA.ap[0:64, :], kf1[6 + g], [], [KwA.b], "LKwA")
                for g4 in range(4):
                    dma("sp", Vs.ap[:, g4 * 16:(g4 + 1) * 16, :], v1[g4 * 2048:(g4 + 1) * 2048, g * 64:(g + 1) * 64].rearrange("(t p) c -> p t c", p=128), [], [Vs.b], "LVs")
                    dma("sp", Vw.ap[:, g4 * 16:(g4 + 1) * 16, :], v1[g4 * 2048:(g4 + 1) * 2048, 128 + g * 64:128 + (g + 1) * 64].rearrange("(t p) c -> p t c", p=128), [], [Vw.b], "LVw")
                for qt in range(NQT):
                    pipe = Pipe()
                    QA = [QAs[r][qt % 2] for r in range(4)]
                    GB = [GBs[r][qt % 2] for r in range(4)]
                    for r in range(4):
                        h = 4 * g + r
                        dma("sp", QA[r].ap[0:64, :], q1[h, :, qt * 512:(qt + 1) * 512], [], [QA[r].b], "L" + QA[r].b.name)
                        dma("sp", QA[r].ap[64:70, :], cd["qaug"][:, 2 + h, :], [], [QA[r].b], "L" + QA[r].b.name)
                        for c3 in range(3):
                            dma("sp", GB[r].ap[:, c3, :], gts[3 * h + c3, qt * 512:(qt + 1) * 512].partition_broadcast(64), [], [GB[r].b], "L" + GB[r].b.name)
                    chunks = [c for c in range(4) if 4 * c <= qt]
                    for r in range(4):
                        h = 4 * g + r
                        NUM = NUMs[r % 2]
                        DEN = DENs[r % 2]
                        Pl = [(c, Pc.next()) for c in chunks]

                        def cmp_tail(r=r, NUM=NUM, DEN=DEN, Pl=Pl):
                            S.op("dve", lambda e: e.tensor_scalar(rdc.ap, DEN.ap, 1e-30, None, ALU.add), reads=[DEN.b], writes=[rdc.b])
                            S.op("dve", lambda e: e.reciprocal(rdc.ap, rdc.ap), reads=[rdc.b], writes=[rdc.b])
                            S.op("dve", lambda e: e.tensor_tensor(ocmp[r].ap, NUM.ap[0:64, :], rdc.ap[0:64, :], ALU.mult), reads=[NUM.b, rdc.b], writes=[ocmp[r].b])
                            for ci, (c, P) in enumerate(Pl):
                                pn = pcn.next()
                                S.op("dve", lambda e, pn=pn, P=P: e.tensor_tensor(pn.ap, P.ap, rdc.ap, ALU.mult), reads=[P.b, rdc.b], writes=[pn.b])
                                for qs in range(4):
                                    mm(IMP, IMP.ap[:, qs * 128:(qs + 1) * 128], pn, pn.ap[:, qs * 128:(qs + 1) * 128], ovl, ovl.ap[:, c, :],
                                       r == 0 and ci == 0, r == 3 and ci == len(Pl) - 1)

                        for ci, (c, P) in enumerate(Pl):
                            rel = qt - 4 * c
                            extra = [(ident, ident.ap, mask_cmp, mask_cmp.ap[:, rel, :])] if rel <= 4 else []
                            col = (h * NQT + qt) * 4 + c
                            tile_job(pipe, KcA, KcA.ap[:, c * 128:(c + 1) * 128], QA[r], extra, (bcmp.ap[:, col:col + 1], bcmp.b),
                                     Vc, Vc.ap[:, c, :], NUM, DEN, 128, P, ci == 0, ci == len(Pl) - 1, cmp_tail)
                    pipe.flush()
                    for qs in range(4):
                        off = 127 - 2 * (4 * qt + qs)
                        S.op("dve", lambda e, qs=qs, off=off: e.tensor_tensor(tmp2.ap, IMP.ap[:, qs * 128:(qs + 1) * 128], tkm.ap[:, off:off + 128], ALU.mult),
                             reads=[IMP.b, tkm.b], writes=[tmp2.b])
                        S.op("dve", lambda e, off=off: e.tensor_tensor(imp2.ap, tmp2.ap, tka.ap[:, off:off + 128], ALU.add), reads=[tmp2.b, tka.b], writes=[imp2.b])
                        S.op("dve", lambda e: e.memset(imp2.ap[:, 0:1], 1.0e6), reads=[], writes=[imp2.b])
                        S.op("dve", lambda e: e.max(v8a.ap, imp2.ap), reads=[imp2.b], writes=[v8a.b])
                        S.op("dve", lambda e: e.match_replace(tmp2.ap, v8a.ap, imp2.ap, -9.0), reads=[imp2.b, v8a.b], writes=[tmp2.b])
                        S.op("dve", lambda e: e.max(v8b.ap, tmp2.ap), reads=[tmp2.b], writes=[v8b.b])
                        S.op("dve", lambda e: e.tensor_scalar(selm.ap, imp2.ap, v8b.ap[:, 7:8], 0.0, ALU.is_ge, ALU.add), reads=[imp2.b, v8b.b], writes=[selm.b])
                        S.op("dve", lambda e: e.scalar_tensor_tensor(selm.ap, imp2.ap, 0.0, selm.ap, ALU.is_ge, ALU.mult), reads=[imp2.b, selm.b], writes=[selm.b])
                        S.op("dve", lambda e: e.tensor_scalar(selb.ap, selm.ap, -1.0, -NEG, ALU.add, ALU.mult), reads=[selm.b], writes=[selb.b])
                        S.op("pe", lambda e, qs=qs: e.transpose(TRP.ap[:, qs * 128:(qs + 1) * 128], selb.ap, ident.ap), reads=[selb.b, ident.b], writes=[TRP.b])
                    S.op("dve", lambda e: e.tensor_copy(selbT.ap, TRP.ap[:, 0:512]), reads=[TRP.b], writes=[selbT.b])
                    for r in range(4):
                        h = 4 * g + r
                        si = 2 + h
                        back = alibi_back(NSA_SLOPES[4 + r]) if g == 0 else None

                        def sel_tail(r=r, GBr=GB[r]):
                            S.op("dve", lambda e: e.reciprocal(rs2[0].ap, DENs[0].ap[0:64, :]), reads=[DENs[0].b], writes=[rs2[0].b])
                            S.op("dve", lambda e: e.tensor_tensor(ob2[0].ap, NUMs[0].ap[0:64, :], rs2[0].ap, ALU.mult), reads=[NUMs[0].b, rs2[0].b], writes=[ob2[0].b])
                            S.op("pool", lambda e: e.tensor_tensor(acc.ap, GBr.ap[:, 1, :], ob2[0].ap, ALU.mult), reads=[GBr.b, ob2[0].b], writes=[acc.b])
                            S.op("pool", lambda e: e.tensor_tensor(t1.ap, GBr.ap[:, 0, :], ocmp[r].ap, ALU.mult), reads=[GBr.b, ocmp[r].b], writes=[t1.b])
                            S.op("pool", lambda e: e.tensor_tensor(acc.ap, acc.ap, t1.ap, ALU.add), reads=[acc.b, t1.b], writes=[acc.b])

                        def win_tail(r=r, h=h, qt=qt, GBr=GB[r]):
                            S.op("dve", lambda e: e.reciprocal(rs2[1].ap, DENs[1].ap[0:64, :]), reads=[DENs[1].b], writes=[rs2[1].b])
                            S.op("dve", lambda e: e.tensor_tensor(ob2[1].ap, NUMs[1].ap[0:64, :], rs2[1].ap, ALU.mult), reads=[NUMs[1].b, rs2[1].b], writes=[ob2[1].b])
                            S.op("pool", lambda e: e.tensor_tensor(t1.ap, GBr.ap[:, 2, :], ob2[1].ap, ALU.mult), reads=[GBr.b, ob2[1].b], writes=[t1.b])
                            ev = oev.next()
                            S.op("pool", lambda e: e.tensor_tensor(ev.ap, acc.ap, t1.ap, ALU.add), reads=[acc.b, t1.b], writes=[ev.b])
                            store(ot_p[h * 64:(h + 1) * 64, qt * 512:(qt + 1) * 512], ev)

                        kts = ktiles_for(qt, back)
                        for i, kt in enumerate(kts):
                            o = kt - 4 * qt
                            extra = [(ewide, ewide.ap[:, kt * 128:(kt + 1) * 128], selbT, selbT.ap)]
                            if o >= 0:
                                extra.append((ident, ident.ap, mask_c, mask_c.ap[:, o, :]))
                            tile_job(pipe, KsA, KsA.ap[:, kt * 128:(kt + 1) * 128], QA[r], extra, (bias_ap(si, 4 * qt - kt), bias_tab.b),
                                     Vs, Vs.ap[:, kt, :], NUMs[0], DENs[0], 64, Ps.next(), i == 0, i == len(kts) - 1, sel_tail)
                        kts = [kt for kt in range(4 * qt + 3, 4 * qt - 5, -1) if kt >= 0]
                        for i, kt in enumerate(kts):
                            o = kt - 4 * qt
                            extra = [(ident, ident.ap, mask_win, mask_win.ap[:, o + 4, :])]
                            tile_job(pipe, KwA, KwA.ap[:, kt * 128:(kt + 1) * 128], QA[r], extra, (bias_ap(si, 4 * qt - kt), bias_tab.b),
                                     Vw, Vw.ap[:, kt, :], NUMs[1], DENs[1], 64, Ps.next(), i == 0, i == len(kts) - 1, win_tail)
                    pipe.flush()

        attention_l1()
        all_gather_ot()
        if STOP_AFTER == "L1ATT":
            S.barrier()
            cpb = Ring([AR.alloc(128, (1024,), BF16, f"cpb{i}") for i in range(2)])
            cpf = Ring([AR.alloc(128, (1024,), F32, f"cpf{i}") for i in range(2)])
            ov = out_d.rearrange("(a b) c -> a (b c)", a=1024)
            for k in range(8):
                for cb in range(8):
                    b_ = cpb.next()
                    f_ = cpf.next()
                    load(b_, ot[k * 128:(k + 1) * 128, cb * 1024:(cb + 1) * 1024])
                    S.op("dve", lambda e, b_=b_, f_=f_: e.tensor_copy(f_.ap, b_.ap), reads=[b_.b], writes=[f_.b])
                    store(ov[k * 128:(k + 1) * 128, cb * 1024:(cb + 1) * 1024], f_)
            S.barrier()
            S.finalize()
            return nc, S
        phase_reset()
        phase_mlp(x2, od_w_out, mlp_w1[1], mlp_w2[1], 3, out_d, final_gain=final_norm)
        S.barrier()
        S.finalize()
    return nc, S


def _cols(*ranges):
    return np.concatenate([np.arange(a, b) for a, b in ranges])


def kernel(**inputs):
    consts = [make_consts(0), make_consts(1)]
    nc, S = build_program(consts[0])
    x = np.ascontiguousarray(inputs["x"], dtype=np.float32)
    B = x.shape[0]
    shared = {}
    for k in ("attn_norm", "mlp_norm", "final_norm", "mlp_w1", "mlp_w2"):
        shared[k] = np.ascontiguousarray(inputs[k], dtype=np.float32)
    for k in ("ev_subln", "od_cmp_pos_k", "od_cmp_k_w1", "od_cmp_k_w2", "od_cmp_pos_v", "od_cmp_v_w1", "od_cmp_v_w2"):
        shared[k] = np.ascontiguousarray(inputs[k][0], dtype=np.float32)
    shared["lamv"] = np.ascontiguousarray(np.stack([inputs["ev_lam_q1"][0], inputs["ev_lam_k1"][0],
                                                    inputs["ev_lam_q2"][0], inputs["ev_lam_k2"][0]]), dtype=np.float32)
    w0 = np.asarray(inputs["ev_w_in"][0], dtype=np.float32)
    w1 = np.asarray(inputs["od_w_in"][0], dtype=np.float32)
    per = []
    feat0 = []
    feat1 = []
    for p in range(2):
        sb = (256 * p, 256 * p + 256)
        dA, dB = DIFF_ASSIGN[p]
        c0 = _cols((0 + sb[0], 0 + sb[1]), (512 + sb[0], 512 + sb[1]),
                   (1536 + 128 * dA, 1536 + 128 * dA + 128), (1536 + 128 * dB, 1536 + 128 * dB + 128),
                   (2048 + 128 * dA, 2048 + 128 * dA + 128), (2048 + 128 * dB, 2048 + 128 * dB + 128),
                   (1024 + sb[0], 1024 + sb[1]),
                   (2560 + 128 * dA, 2560 + 128 * dA + 128), (2560 + 128 * dB, 2560 + 128 * dB + 128))
        gA, gB = GROUP_ASSIGN[p]
        kv = lambda base: [(base + 64 * gA, base + 64 * gA + 64), (base + 64 * gB, base + 64 * gB + 64)]
        c1 = _cols((256 * gA, 256 * gA + 256), (256 * gB, 256 * gB + 256),
                   *kv(1024), *kv(1280), *kv(1536), *kv(2048),
                   (2560 + 12 * gA, 2560 + 12 * gA + 12), (2560 + 12 * gB, 2560 + 12 * gB + 12),
                   *kv(1792), *kv(2304))
        per.append({"ev_w_in": np.ascontiguousarray(w0[:, c0]), "od_w_in": np.ascontiguousarray(w1[:, c1])})
        feat0.append(_cols(sb, (512 + 128 * dA, 512 + 128 * dA + 128), (512 + 128 * dB, 512 + 128 * dB + 128)))
        feat1.append(_cols((256 * gA, 256 * gA + 256), (256 * gB, 256 * gB + 256)))
    perm0 = np.concatenate(feat0)
    perm1 = np.concatenate(feat1)
    shared["ev_w_out"] = np.ascontiguousarray(np.asarray(inputs["ev_w_out"][0], dtype=np.float32)[perm0])
    shared["od_w_out"] = np.ascontiguousarray(np.asarray(inputs["od_w_out"][0], dtype=np.float32)[perm1])
    in_maps = []
    for core in range(8):
        p = core % 2
        m = dict(shared)
        m.update(per[p])
        for k, v in consts[p].items():
            m["c_" + k] = v
        m["x"] = x[(core // 2) % B]
        in_maps.append(m)
    res = run_bass_kernel_spmd(nc, in_maps, core_ids=list(range(8)))
    out = np.stack([np.asarray(res.results[2 * b]["out"], dtype=np.float32) for b in range(B)])
    return out
```

```python
import math
import numpy as np
import ml_dtypes
from contextlib import ExitStack
import concourse.bass as bass
import concourse.mybir as mybir
from concourse.bass_utils import run_bass_kernel_spmd

F32 = mybir.dt.float32
BF16 = mybir.dt.bfloat16
AF = mybir.ActivationFunctionType
ALU = mybir.AluOpType
bf = ml_dtypes.bfloat16

SEQ = 8192
D = 1024
NQT = SEQ // 512
NKT = SEQ // 128
NEG = -30000.0
SB_BACK = 4
ALIBI_CUT = 144.0
SEM_WRAP = 16000
DMA_WRAP = 1000
STOP_AFTER = None


class Buf:
    __slots__ = ("name", "w", "r")

    def __init__(self, name):
        self.name = name
        self.w = []
        self.r = []


class Op:
    __slots__ = ("eng", "fn", "deps", "idx", "needs_inc", "waits", "is_dma", "dkey", "dval")

    def __init__(self, eng, fn):
        self.eng = eng
        self.fn = fn
        self.deps = set()
        self.idx = -1
        self.needs_inc = False
        self.waits = []
        self.is_dma = False
        self.dkey = None
        self.dval = 0


class T:
    __slots__ = ("ap", "b")

    def __init__(self, ap, b):
        self.ap = ap
        self.b = b

    def __getitem__(self, k):
        return self.ap[k]


class Sched:
    ENGS = ("pe", "act", "dve", "pool", "sp")

    def __init__(self, nc, stack):
        self.nc = nc
        self.stack = stack
        self.ops = []
        self.eng_ops = {e: [] for e in self.ENGS}
        self.dma_count = {}
        self.bufs = []

    def buf(self, name):
        b = Buf(name)
        self.bufs.append(b)
        return b

    def op(self, eng, fn, reads=(), writes=(), dma_key=None):
        o = Op(eng, fn)
        oid = len(self.ops)
        for b in reads:
            o.deps.update(b.w)
        for b in writes:
            o.deps.update(b.w)
            o.deps.update(b.r)
        for b in reads:
            b.r.append(oid)
        for b in writes:
            b.w = [oid]
            b.r = []
        if dma_key is not None:
            o.is_dma = True
            o.dkey = dma_key
            n = self.dma_count.get(dma_key, 0) + 1
            self.dma_count[dma_key] = n
            o.dval = n
        o.idx = len(self.eng_ops[eng])
        self.eng_ops[eng].append(o)
        self.ops.append(o)
        return oid

    def barrier(self):
        live = [b for b in self.bufs if b.w or b.r]
        first = True
        sync = self.buf("barrier")
        for e in self.ENGS:
            if first:
                self.op(e, lambda eng: eng.nop(), writes=live + [sync])
                first = False
            else:
                self.op(e, lambda eng: eng.nop(), reads=[sync])
        self.bufs = [sync]

    def finalize(self):
        nc = self.nc
        ops = self.ops
        know = {e: {} for e in self.ENGS}
        comp_know = [None] * len(ops)
        for oid, o in enumerate(ops):
            K = know[o.eng]
            for d in sorted(o.deps):
                p = ops[d]
                if p.is_dma:
                    dom = ("d", p.dkey)
                    val = p.dval
                else:
                    dom = ("e", p.eng)
                    val = p.idx + 1
                    if p.eng == "pe" and o.eng == "pe":
                        continue
                if K.get(dom, 0) >= val:
                    continue
                o.waits.append((dom, val))
                p.needs_inc = True
                for k2, v2 in comp_know[d].items():
                    if K.get(k2, 0) < v2:
                        K[k2] = v2
            ck = dict(K)
            if o.is_dma:
                ck[("d", o.dkey)] = o.dval
            else:
                ck[("e", o.eng)] = o.idx + 1
            comp_know[oid] = ck
        comp_know = None
        eng_sems = {}
        counts = {}
        for e in self.ENGS:
            c = 0
            for o in self.eng_ops[e]:
                if o.is_dma:
                    continue
                if o.needs_inc:
                    c += 1
                counts[(e, o.idx)] = c
            nsem = (c + SEM_WRAP - 1) // SEM_WRAP
            eng_sems[e] = [self.stack.enter_context(nc.semaphore(f"s_{e}_{i}")) for i in range(nsem)]
        dma_sems = {}
        for k, n in self.dma_count.items():
            nsem = (n + DMA_WRAP - 1) // DMA_WRAP
            dma_sems[k] = [self.stack.enter_context(nc.semaphore(f"d_{k}_{i}")) for i in range(nsem)]

        def sem_for(dom, val):
            if dom[0] == "e":
                c = counts[(dom[1], val - 1)]
                return eng_sems[dom[1]][(c - 1) // SEM_WRAP], (c - 1) % SEM_WRAP + 1
            mul = 1 if dom[1].startswith("cc") else 16
            return dma_sems[dom[1]][(val - 1) // DMA_WRAP], ((val - 1) % DMA_WRAP + 1) * mul

        self.n_waits = sum(len(o.waits) for o in ops)
        with nc.Block() as block:
            def make(e):
                def body(eng):
                    for o in self.eng_ops[e]:
                        for dom, val in o.waits:
                            s, v = sem_for(dom, val)
                            eng.wait_ge(s, v)
                        ins = o.fn(eng)
                        if o.is_dma:
                            s, v = sem_for(("d", o.dkey), o.dval)
                            ins.then_inc(s, 1 if o.dkey.startswith("cc") else 16)
                        elif o.needs_inc:
                            c = counts[(e, o.idx)]
                            ins.then_inc(eng_sems[e][(c - 1) // SEM_WRAP], 1)
                return body
            block.tensor(make("pe"))
            block.scalar(make("act"))
            block.vector(make("dve"))
            block.gpsimd(make("pool"))
            block.sync(make("sp"))


class Arena:
    def __init__(self, S, tens, nwords):
        self.S = S
        self.t = tens
        self.n = nwords
        self.off = 0
        self.cnt = 0

    def reset(self):
        self.off = 0

    def alloc(self, parts, shape, dt, name):
        n = 1
        for s in shape:
            n *= s
        words = n if dt == F32 else (n + 1) // 2
        v = self.t[0:parts, self.off:self.off + words]
        self.off += words
        assert self.off <= self.n, f"arena overflow at {name}: {self.off} > {self.n}"
        if dt != F32:
            v = v.bitcast(dt)
        if len(shape) == 2:
            v = v.rearrange("p (a b) -> p a b", a=shape[0])
        elif len(shape) == 3:
            v = v.rearrange("p (a b c) -> p a b c", a=shape[0], b=shape[1])
        self.cnt += 1
        return T(v, self.S.buf(name))


def _split3(v):
    v = np.asarray(v, np.float32)
    hi = v.astype(bf)
    r1 = v - hi.astype(np.float32)
    mid = r1.astype(bf)
    r2 = r1 - mid.astype(np.float32)
    lo = r2.astype(bf)
    return hi, mid, lo


DIFF_SLOPES = [2.0 ** (-8.0 * (h + 1) / 4) for h in range(4)]
NSA_SLOPES = [2.0 ** (-8.0 * (h + 1) / 16) for h in range(16)]
DIFF_ASSIGN = ((0, 3), (1, 2))
GROUP_ASSIGN = ((0, 3), (1, 2))
SB_PER_CORE = 4


def slopes_for(p):
    sl = [DIFF_SLOPES[h] for h in DIFF_ASSIGN[p]]
    for g in GROUP_ASSIGN[p]:
        sl += [NSA_SLOPES[4 * g + r] for r in range(4)]
    return sl


NSLOT = 10
BIAS_M0 = -3
BIAS_NM = 68


def make_consts(p):
    ALL_SLOPES = slopes_for(p)
    c = {}
    j = np.arange(128)
    t = np.arange(512)
    c["ident"] = np.eye(128, dtype=np.float32).astype(bf)
    c["ones"] = np.ones((128, 128), np.float32).astype(bf)
    c["negones"] = (-np.ones((128, 128), np.float32)).astype(bf)
    c["uneg"] = (-(j[:, None] >= j[None, :]).astype(np.float32)).astype(bf)
    c["onesdiv"] = np.full((128, 128), 1.0 / 128, np.float32)
    msb = np.zeros((4, 128, 512), np.float32)
    mc = np.zeros((4, 128, 512), np.float32)
    for o in range(4):
        jj = 128 * o + j[:, None]
        msb[o] = np.where(jj >= t[None, :], NEG, 0.0)
        mc[o] = np.where(jj > t[None, :], NEG, 0.0)
    c["mask_sb"] = np.ascontiguousarray(msb.transpose(1, 0, 2)).astype(bf)
    c["mask_c"] = np.ascontiguousarray(mc.transpose(1, 0, 2)).astype(bf)
    kr = np.zeros((6, SEQ), np.float32)
    kr[0:3] = 1.0
    kr[3:6] = (np.arange(SEQ) % 128)[None, :]
    c["kaug"] = kr.astype(bf)
    kc = np.zeros((6, 512), np.float32)
    kc[0:3] = 1.0
    kc[3:6] = (16 * (np.arange(512) % 128))[None, :]
    c["kaug_cmp"] = kc.astype(bf)
    qa = np.zeros((len(ALL_SLOPES), 6, 512), np.float32).astype(bf)
    for i, s in enumerate(ALL_SLOPES):
        s32 = np.float32(s)
        v = (-(s32 * t.astype(np.float32))).astype(np.float32)
        h3 = _split3(v)
        s3 = _split3(np.full(512, s32, np.float32))
        for r in range(3):
            qa[i, r] = h3[r]
            qa[i, 3 + r] = s3[r]
    c["qaug"] = np.ascontiguousarray(qa.transpose(1, 0, 2))
    bt = np.zeros((len(ALL_SLOPES), BIAS_NM), np.float32)
    for i, s in enumerate(ALL_SLOPES):
        for mi in range(BIAS_NM):
            bt[i, mi] = -np.float32(s) * np.float32(128 * (mi + BIAS_M0))
    c["bias_tab"] = np.broadcast_to(bt.reshape(1, -1), (128, bt.size)).copy()
    bc = np.zeros((8, NQT, 4), np.float32)
    for h, s in enumerate(ALL_SLOPES[2:]):
        for qt in range(NQT):
            for cc in range(4):
                bc[h, qt, cc] = -np.float32(s) * np.float32(512 * qt - 2048 * cc - 31)
    c["bias_cmp"] = np.broadcast_to(bc.reshape(1, -1), (128, bc.size)).copy()
    mcm = np.zeros((5, 128, 512), np.float32)
    for rel in range(5):
        mcm[rel] = np.where(512 * rel + t[None, :] >= 16 * j[:, None] + 31, 0.0, NEG)
    c["mask_cmp"] = np.ascontiguousarray(mcm.transpose(1, 0, 2)).astype(bf)
    mw = np.zeros((8, 128, 512), np.float32)
    for oi, o in enumerate(range(-4, 4)):
        dd = t[None, :] - j[:, None] - 128 * o
        mw[oi] = np.where((dd >= 0) & (dd <= 511), 0.0, NEG)
    c["mask_win"] = np.ascontiguousarray(mw.transpose(1, 0, 2)).astype(bf)
    cc = np.arange(SEQ)
    c["ewide"] = (cc[None, :] // 64 == j[:, None]).astype(np.float32).astype(bf)
    n = np.arange(512)
    s_ = np.arange(128)
    ov = ((n[:, None] >= 4 * s_[None, :] - 1) & (n[:, None] <= 4 * s_[None, :] + 3) & (n[:, None] < 511))
    c["ovl"] = np.ascontiguousarray(ov.astype(np.float32).reshape(4, 128, 128).transpose(1, 0, 2)).astype(bf)
    q = np.arange(128)
    u = np.arange(-127, 129)
    cur = (q >= 64).astype(np.int64)
    A = np.zeros((128, 256), np.float32)
    M = np.ones((128, 256), np.float32)
    fut = u[None, :] > cur[:, None]
    A[fut] = -1.0
    M[fut] = 0.0
    f1 = u[None, :] == cur[:, None]
    f2 = u[None, :] == cur[:, None] - 1
    A[f1] = 1.0e6 + 1.0
    M[f1] = 0.0
    A[f2] = 1.0e6 + 2.0
    M[f2] = 0.0
    c["topk_a"] = A
    c["topk_m"] = M
    gs = np.zeros((48, 48, 64), np.float32)
    for r in range(48):
        gs[r, r, :] = 1.0
    c["gsel"] = gs.reshape(48, 48 * 64).astype(bf)
    return c


CONST_SPECS = None


def _dt_of(a):
    return BF16 if a.dtype == bf else F32


def build_program(consts):
    nc = bass.Bass("TRN2", target_bir_lowering=False)
    dr = {}

    def din(name, shape, dt=F32):
        dr[name] = nc.dram_tensor(name, list(shape), dt, kind="ExternalInput").ap()
        return dr[name]

    def dscr(name, shape, dt):
        dr[name] = nc.dram_tensor(name, list(shape), dt, kind="Internal").ap()
        return dr[name]

    x_in = din("x", (SEQ, D))
    attn_norm = din("attn_norm", (2, D))
    mlp_norm = din("mlp_norm", (2, D))
    final_norm = din("final_norm", (D,))
    ev_w_in = din("ev_w_in", (D, 1536))
    lamv = din("lamv", (4, 64))
    ev_subln = din("ev_subln", (128,))
    ev_w_out = din("ev_w_out", (D, D))
    od_w_in = din("od_w_in", (D, 1304))
    pos_k = din("od_cmp_pos_k", (32, 64))
    cw1k = din("od_cmp_k_w1", (2048, 256))
    cw2k = din("od_cmp_k_w2", (256, 64))
    pos_v = din("od_cmp_pos_v", (32, 64))
    cw1v = din("od_cmp_v_w1", (2048, 256))
    cw2v = din("od_cmp_v_w2", (256, 64))
    od_w_out = din("od_w_out", (D, D))
    mlp_w1 = din("mlp_w1", (2, D, 4096))
    mlp_w2 = din("mlp_w2", (2, 4096, D))
    cd = {k: din("c_" + k, v.shape, _dt_of(v)) for k, v in consts.items()}
    out_d = nc.dram_tensor("out", [SEQ // 2, D], F32, kind="ExternalOutput").ap()
    xh_in = din("xh", (SEQ // 2, D))
    msel_in = din("msel", (128, 2))

    qk0 = dscr("qk0", (16, 64, SEQ), BF16)
    v0 = dscr("v0", (SEQ, 512), BF16)
    ot = dscr("ot", (D, SEQ), BF16)
    ot_p = dscr("ot_p", (512, SEQ), BF16)
    ot4 = dscr("ot4", (4, 256, SEQ), BF16)
    x2 = dscr("x2", (SEQ // 2, D), F32)
    x2g = dscr("x2g", (8, 1024, D), F32)
    q1 = dscr("q1", (8, 64, SEQ), BF16)
    kf1 = dscr("kf1", (8, 64, SEQ), BF16)
    v1 = dscr("v1", (SEQ, 256), BF16)
    gts = dscr("gts", (24, SEQ), BF16)
    kcmp = dscr("kcmp", (2, 64, 512), BF16)
    vcmp = dscr("vcmp", (2, 512, 64), BF16)
    x1s = dscr("x1s", (SEQ // 2, D), F32)
    uts_d = dscr("uts", (32, 128, SEQ // 2), BF16)

    with ExitStack() as st:
        S = Sched(nc, st)
        NW = 50 * 1024
        arena_t = st.enter_context(nc.sbuf_tensor("arena", [128, NW], F32))
        AR = Arena(S, arena_t, NW)
        pbanks = [st.enter_context(nc.psum_tensor(f"pb{i}", [128, 512], F32)) for i in range(8)]

        def PS(i, name, parts=128, cols=512, dt=F32):
            ap = pbanks[i][0:parts, :]
            if dt != F32:
                ap = ap.bitcast(dt)
            ap = ap[:, 0:cols]
            return T(ap, S.buf(name))

        def dma(q, out, in_, reads, writes, key, slow=False):
            if slow:
                S.op(q, lambda e: e.dma_start(out=out, in_=in_, allow_slow_non_contiguous=True), reads=reads, writes=writes, dma_key=key)
            else:
                S.op(q, lambda e: e.dma_start(out=out, in_=in_), reads=reads, writes=writes, dma_key=key)

        def load_vt(VT, src_cols, width):
            for g4 in range(4):
                dma("sp", VT.ap[:, g4 * 16:(g4 + 1) * 16, 0:width],
                    src_cols[g4 * 2048:(g4 + 1) * 2048, :].rearrange("(t p) c -> p t c", p=128), [], [VT.b], "LVT")

        def load(dst, src, q="sp"):
            dma(q, dst.ap, src, [], [dst.b], "L" + dst.b.name)

        def store(dst, src, ap=None, q="pool"):
            dma(q, dst, src.ap if ap is None else ap, [src.b], [], "S" + src.b.name)

        def mm(out, outap, lhsT, lhsap, rhs, rhsap, start, stop, extra_r=()):
            S.op("pe", lambda e: e.matmul(outap, lhsap, rhsap, start=start, stop=stop),
                 reads=[lhsT.b, rhs.b] + list(extra_r), writes=[out.b])

        class Ring:
            def __init__(self, items):
                self.items = items
                self.i = 0

            def next(self):
                it = self.items[self.i % len(self.items)]
                self.i += 1
                return it

        rr_cast = [0]

        def cast(dst, dstap, src, srcap, scale_ap=None, scale_t=None):
            e = ("dve", "act", "pool")[rr_cast[0] % 3] if scale_ap is None else ("dve", "act")[rr_cast[0] % 2]
            rr_cast[0] += 1
            rd = [src.b] + ([scale_t.b] if scale_t is not None else [])
            if e == "act":
                if scale_ap is None:
                    S.op("act", lambda en: en.copy(dstap, srcap), reads=rd, writes=[dst.b])
                else:
                    S.op("act", lambda en: en.activation(dstap, srcap, AF.Copy, scale=scale_ap), reads=rd, writes=[dst.b])
            elif e == "dve":
                if scale_ap is None:
                    S.op("dve", lambda en: en.tensor_copy(dstap, srcap), reads=rd, writes=[dst.b])
                else:
                    S.op("dve", lambda en: en.tensor_scalar(dstap, srcap, scale_ap, None, ALU.mult), reads=rd, writes=[dst.b])
            else:
                S.op("pool", lambda en: en.tensor_copy(dstap, srcap), reads=rd, writes=[dst.b])

        def load_weight(dst, src_ap, K, N, stage_ring, gain=None, col0=0, ncols=None, dcol0=0):
            ncols = N if ncols is None else ncols
            for k in range(K // 128):
                c = 0
                while c < ncols:
                    w = min(2048, ncols - c)
                    stg = stage_ring.next()
                    dma("sp", stg.ap[:, 0:w], src_ap[k * 128:(k + 1) * 128, col0 + c:col0 + c + w], [], [stg.b], "L" + stg.b.name)
                    cast(dst, dst.ap[:, k, dcol0 + c:dcol0 + c + w], stg, stg.ap[:, 0:w],
                         None if gain is None else gain.ap[:, k:k + 1], gain)
                    c += w

        NWP = 0
        ident = AR.alloc(128, (128,), BF16, "ident")
        load(ident, cd["ident"])
        ones = AR.alloc(128, (128,), BF16, "ones")
        load(ones, cd["ones"])
        negones = AR.alloc(128, (128,), BF16, "negones")
        load(negones, cd["negones"])
        uneg = AR.alloc(128, (128,), BF16, "uneg")
        load(uneg, cd["uneg"])
        onesdiv = AR.alloc(128, (128,), F32, "onesdiv")
        load(onesdiv, cd["onesdiv"])
        bias_tab = AR.alloc(128, (NSLOT * BIAS_NM,), F32, "bias_tab")
        load(bias_tab, cd["bias_tab"])
        gains = AR.alloc(128, (4, 8), F32, "gains")
        for li in range(2):
            dma("sp", gains.ap[:, 2 * li, :], attn_norm[li].rearrange("(k p) -> p k", p=128), [], [gains.b], "Lgains", slow=True)
            dma("sp", gains.ap[:, 2 * li + 1, :], mlp_norm[li].rearrange("(k p) -> p k", p=128), [], [gains.b], "Lgains", slow=True)
        eps_t = AR.alloc(128, (1,), F32, "eps_t")
        S.op("dve", lambda e: e.memset(eps_t.ap, 1e-6), writes=[eps_t.b])
        msel = AR.alloc(128, (2,), F32, "msel")
        load(msel, msel_in)
        persist_off = AR.off

        def bias_ap(si, m):
            col = si * BIAS_NM + (m - BIAS_M0)
            return bias_tab.ap[:, col:col + 1]

        def phase_reset():
            S.barrier()
            AR.off = persist_off

        def norm_transpose(xt, hn, ss, rs, junk, tp, hT, r):
            S.op("act", lambda e: e.activation(junk.ap, xt.ap, AF.Square, accum_out=ss.ap), reads=[xt.b], writes=[junk.b, ss.b])
            S.op("act", lambda e: e.activation(rs.ap, ss.ap, AF.Sqrt, bias=eps_t.ap, scale=1.0 / D), reads=[ss.b, eps_t.b], writes=[rs.b])
            S.op("dve", lambda e: e.reciprocal(rs.ap, rs.ap), reads=[rs.b], writes=[rs.b])
            S.op("act", lambda e: e.activation(hn.ap, xt.ap, AF.Copy, scale=rs.ap), reads=[xt.b, rs.b], writes=[hn.b])
            for k in range(8):
                S.op("pe", lambda e, k=k: e.transpose(tp.ap[:, k * 128:(k + 1) * 128], hn.ap[:, k * 128:(k + 1) * 128], ident.ap),
                     reads=[hn.b, ident.b], writes=[tp.b])
            S.op("dve", lambda e: e.tensor_copy(hT.ap[:, :, r * 128:(r + 1) * 128], tp.ap.rearrange("p (k c) -> p k c", k=8)),
                 reads=[tp.b], writes=[hT.b])

        def phase_proj(x_src, w_src, ncols_total, gain_idx, fm_chunks, tm_chunks):
            stage = Ring([AR.alloc(128, (2048,), F32, f"wstg{i}") for i in range(2)])
            W = AR.alloc(128, (8, ncols_total), BF16, "Win")
            gsl = T(gains.ap[:, gain_idx, :], gains.b)
            load_weight(W, w_src, D, ncols_total, stage, gain=gsl)
            xts = Ring([AR.alloc(128, (D,), F32, f"xt{i}") for i in range(2)])
            hns = Ring([AR.alloc(128, (D,), BF16, f"hn{i}") for i in range(2)])
            junk = AR.alloc(128, (D,), BF16, "junk")
            sss = Ring([AR.alloc(128, (1,), F32, f"ss{i}") for i in range(2)])
            rss = Ring([AR.alloc(128, (1,), F32, f"rs{i}") for i in range(2)])
            hTs = Ring([AR.alloc(128, (8, 512), BF16, f"hT{i}") for i in range(2)])
            tps = Ring([PS(i, f"tp{i}", cols=1024, dt=BF16) for i in (0, 1)])
            pps = Ring([PS(i, f"pp{i}") for i in (2, 3, 4, 5)])
            evs = Ring([AR.alloc(128, (512,), BF16, f"ev{i}") for i in range(4)])
            ev_i = [0]
            for tb in range(NQT):
                hT = hTs.next()
                for r in range(4):
                    xt = xts.next()
                    row0 = tb * 512 + r * 128
                    load(xt, x_src(row0))
                    norm_transpose(xt, hns.next(), sss.next(), rss.next(), junk, tps.next(), hT, r)
                for (col0, M, dsts, scale, func) in fm_chunks:
                    pp = pps.next()
                    for k in range(8):
                        mm(pp, pp.ap[0:M, :], W, W.ap[:, k, col0:col0 + M], hT, hT.ap[:, k, :], k == 0, k == 7)
                    ev = evs.next()
                    eng = ("act", "dve")[ev_i[0] % 2] if func is None else "act"
                    ev_i[0] += 1
                    if eng == "act":
                        S.op("act", lambda e, pp=pp, ev=ev, M=M, scale=scale, func=func: e.activation(
                            ev.ap[0:M, :], pp.ap[0:M, :], AF.Copy if func is None else func, scale=scale),
                            reads=[pp.b], writes=[ev.b])
                    else:
                        S.op("dve", lambda e, pp=pp, ev=ev, M=M, scale=scale: e.tensor_scalar(
                            ev.ap[0:M, :], pp.ap[0:M, :], float(scale), None, ALU.mult), reads=[pp.b], writes=[ev.b])
                    for (dfn, p0, npart) in dsts:
                        store(dfn(tb), ev, ap=ev.ap[p0:p0 + npart, :])
                for (col0, ncols, dfn) in tm_chunks:
                    for r in range(4):
                        pp = pps.next()
                        for k in range(8):
                            mm(pp, pp.ap[:, 0:ncols], hT, hT.ap[:, k, r * 128:(r + 1) * 128], W, W.ap[:, k, col0:col0 + ncols], k == 0, k == 7)
                        ev = evs.next()
                        eng = ("act", "dve")[ev_i[0] % 2]
                        ev_i[0] += 1
                        if eng == "act":
                            S.op("act", lambda e, pp=pp, ev=ev, n=ncols: e.copy(ev.ap[:, 0:n], pp.ap[:, 0:n]), reads=[pp.b], writes=[ev.b])
                        else:
                            S.op("dve", lambda e, pp=pp, ev=ev, n=ncols: e.tensor_copy(ev.ap[:, 0:n], pp.ap[:, 0:n]), reads=[pp.b], writes=[ev.b])
                        store(dfn(tb * 512 + r * 128), ev, ap=ev.ap[:, 0:ncols])

        fm = []
        for jc in range(8):
            isq = jc in (0, 1, 4, 5)
            dsts = [(lambda tb, slot=2 * jc + half: qk0[slot, :, tb * 512:(tb + 1) * 512], 64 * half, 64) for half in range(2)]
            fm.append((128 * jc, 128, dsts, 0.125 if isq else 1.0, None))
        tm = [(1024, 512, lambda row0: v0[row0:row0 + 128, 0:512])]
        phase_proj(lambda row0: x_in[row0:row0 + 128, :], ev_w_in, 1536, 0, fm, tm)
        phase_reset()

        class Pipe:
            def __init__(self):
                self.q = []
                self.t = 0

            def job(self, stages):
                for lag, fn in stages:
                    if lag == 0:
                        fn()
                    else:
                        self.q.append((self.t + lag, fn))
                rest = []
                for due, fn in self.q:
                    if due <= self.t:
                        fn()
                    else:
                        rest.append((due, fn))
                self.q = rest
                self.t += 1

            def flush(self):
                for due, fn in sorted(self.q, key=lambda x_: x_[0]):
                    fn()
                self.q = []

        def ktiles_for(qt, back):
            hi = 4 * qt + 3
            lo = 0 if back is None else max(0, 4 * qt - back)
            return list(range(hi, lo - 1, -1))

        def alibi_back(slope):
            dcut = ALIBI_CUT / slope
            bk = int(math.floor((dcut + 127.0) / 128.0))
            return bk

        ccbuf = S.buf("ccbuf")
        cc_n = [0]

        def all_gather_ot():
            S.barrier()
            for k in range(4):
                cc_n[0] += 1
                S.op("pool", lambda e, k=k: e.collective_compute("AllGather", ALU.bypass, replica_groups=[[0, 1], [2, 3], [4, 5], [6, 7]],
                                                                 ins=[ot_p[k * 128:(k + 1) * 128, :]], outs=[ot4[k]]),
                     writes=[ccbuf], dma_key="cc")
            S.bufs.append(ccbuf)

        def all_gather_x2():
            S.barrier()
            for k in range(8):
                cc_n[0] += 1
                S.op("pool", lambda e, k=k: e.collective_compute("AllGather", ALU.bypass, replica_groups=[[0, 1], [2, 3], [4, 5], [6, 7]],
                                                                 ins=[x2[k * 512:(k + 1) * 512, :]], outs=[x2g[k]]),
                     writes=[ccbuf], dma_key="cc")
            S.bufs.append(ccbuf)

        def attention_l0():
            mask_sb = AR.alloc(128, (4, 512), BF16, "mask_sb")
            load(mask_sb, cd["mask_sb"])
            mask_c = AR.alloc(128, (4, 512), BF16, "mask_c")
            load(mask_c, cd["mask_c"])
            KT = AR.alloc(70, (SEQ,), BF16, "KT")
            VT = AR.alloc(128, (NKT, 128), BF16, "VT")
            QTs = Ring([AR.alloc(70, (512,), BF16, f"QT{i}") for i in range(2)])
            Es = Ring([AR.alloc(128, (512,), F32, f"E{i}") for i in range(2)])
            SPs = Ring([AR.alloc(128, (512,), BF16, f"SP{i}") for i in range(4)])
            Ws = Ring([AR.alloc(128, (512,), BF16, f"Wt{i}") for i in range(4)])
            ssum = AR.alloc(128, (512,), F32, "ssum")
            sshs = Ring([AR.alloc(128, (512,), BF16, f"ssh{i}") for i in range(4)])
            oev = Ring([AR.alloc(128, (512,), BF16, f"oev{i}") for i in range(2)])
            Ys = Ring([PS(i, f"Y{i}") for i in (0, 1, 2)])
            Oaccs = [PS(3, "Oacc0"), PS(4, "Oacc1")]
            for h in range(SB_PER_CORE):
                pipe = Pipe()
                load(T(KT.ap[0:64, :], KT.b), qk0[4 + h])
                load_vt(VT, v0[:, h * 64:(h + 1) * 64], 64)
                for qt in range(NQT):
                    QT = QTs.next()
                    dma("sp", QT.ap[0:64, :], qk0[h, :, qt * 512:(qt + 1) * 512], [], [QT.b], "L" + QT.b.name)
                    kts = ktiles_for(qt, SB_BACK)
                    n = len(kts)
                    Oacc = Oaccs[qt % 2]
                    ssh_prev = None
                    for i, kt in enumerate(kts):
                        Y = Ys.next()
                        E = Es.next()
                        SP = SPs.next()
                        Wt = Ws.next()
                        ssh = sshs.next() if 0 < i < n - 1 else None
                        o = kt - 4 * qt

                        def st0(Y=Y, E=E, SP=SP, o=o, kt=kt, QT=QT, i=i, n=n, ssh=ssh):
                            mm(Y, Y.ap, KT, KT.ap[0:64, kt * 128:(kt + 1) * 128], QT, QT.ap[0:64, :], True, False)
                            if o >= 0:
                                mm(Y, Y.ap, ident, ident.ap, mask_sb, mask_sb.ap[:, o, :], False, False)
                            S.op("act", lambda e: e.activation(E.ap, Y.ap, AF.Exp), reads=[Y.b], writes=[E.b])
                            S.op("act", lambda e: e.activation(SP.ap, E.ap, AF.Ln, bias=1.0), reads=[E.b], writes=[SP.b])
                            if i < n - 1:
                                if i == 0:
                                    S.op("pool", lambda e: e.tensor_copy(ssum.ap, SP.ap), reads=[SP.b], writes=[ssum.b])
                                else:
                                    S.op("pool", lambda e: e.tensor_tensor(ssum.ap, ssum.ap, SP.ap, ALU.add), reads=[SP.b, ssum.b], writes=[ssum.b])
                                    S.op("pool", lambda e: e.tensor_copy(ssh.ap, ssum.ap), reads=[ssum.b], writes=[ssh.b])

                        def st1(Y=Y, SP=SP, Wt=Wt, prev=ssh_prev):
                            mm(Y, Y.ap, uneg, uneg.ap, SP, SP.ap, False, prev is None)
                            if prev is not None:
                                mm(Y, Y.ap, negones, negones.ap, prev, prev.ap, False, True)
                            S.op("act", lambda e: e.activation(Wt.ap, Y.ap, AF.Exp), reads=[Y.b], writes=[Wt.b])

                        def st2(Wt=Wt, kt=kt, i=i, n=n, Oacc=Oacc, h=h, qt=qt):
                            mm(Oacc, Oacc.ap[0:64, :], VT, VT.ap[:, kt, 0:64], Wt, Wt.ap, i == 0, i == n - 1)
                            if i == n - 1:
                                ev = oev.next()
                                S.op("dve", lambda e: e.tensor_copy(ev.ap[0:64, :], Oacc.ap[0:64, :]), reads=[Oacc.b], writes=[ev.b])
                                store(ot_p[h * 64:(h + 1) * 64, qt * 512:(qt + 1) * 512], ev, ap=ev.ap[0:64, :])

                        pipe.job([(0, st0), (1, st1), (2, st2)])
                        if i < n - 1:
                            ssh_prev = SP if i == 0 else ssh
                pipe.flush()
            lam_t = AR.alloc(128, (4, 64), F32, "lam_t")
            for i in range(4):
                dma("sp", lam_t.ap[:, i, :], lamv[i].partition_broadcast(128), [], [lam_t.b], "Llam")
            lam_p = AR.alloc(128, (2, 64), F32, "lam_p")
            lam_s = AR.alloc(128, (2,), F32, "lam_s")
            neglam = AR.alloc(128, (1,), F32, "neglam")
            S.op("dve", lambda e: e.tensor_tensor(lam_p.ap[:, 0, :], lam_t.ap[:, 0, :], lam_t.ap[:, 1, :], ALU.mult), reads=[lam_t.b], writes=[lam_p.b])
            S.op("dve", lambda e: e.tensor_tensor(lam_p.ap[:, 1, :], lam_t.ap[:, 2, :], lam_t.ap[:, 3, :], ALU.mult), reads=[lam_t.b, lam_p.b], writes=[lam_p.b])
            S.op("dve", lambda e: e.reduce_sum(lam_s.ap, lam_p.ap, axis=mybir.AxisListType.X), reads=[lam_p.b], writes=[lam_s.b])
            S.op("act", lambda e: e.activation(lam_s.ap, lam_s.ap, AF.Exp), reads=[lam_s.b], writes=[lam_s.b])
            lam_init = 0.8 - 0.6 * math.exp(-0.3 * 0)
            S.op("dve", lambda e: e.tensor_tensor(neglam.ap, lam_s.ap[:, 1:2], lam_s.ap[:, 0:1], ALU.subtract), reads=[lam_s.b], writes=[neglam.b])
            S.op("dve", lambda e: e.tensor_scalar(neglam.ap, neglam.ap, -lam_init, None, ALU.add), reads=[neglam.b], writes=[neglam.b])
            sg = AR.alloc(128, (1,), F32, "sg")
            dma("sp", sg.ap, ev_subln.rearrange("(p o) -> p o", o=1), [], [sg.b], "Lsg")
            S.op("dve", lambda e: e.tensor_scalar(sg.ap, sg.ap, 1.0 - lam_init, None, ALU.mult), reads=[sg.b], writes=[sg.b])
            KT2 = [KT, AR.alloc(70, (SEQ,), BF16, "KTb")]
            for kk in KT2:
                dma("sp", kk.ap[64:70, :], cd["kaug"], [], [kk.b], "L" + kk.b.name)
            QT2 = [[AR.alloc(70, (512,), BF16, f"QD{c}{i}") for i in range(2)] for c in range(2)]
            Ps = Ring([AR.alloc(128, (512,), BF16, f"P{i}") for i in range(4)])
            NUM = [PS(3, "NUM0"), PS(4, "NUM1")]
            DEN = [PS(5, "DEN0"), PS(6, "DEN1")]
            MS = PS(7, "MS")
            r_t = [AR.alloc(128, (512,), F32, f"rden{c}") for c in range(2)]
            a_t = [AR.alloc(128, (512,), F32, f"a{c}") for c in range(2)]
            o_t = AR.alloc(128, (512,), F32, "o_t")
            sq_t = AR.alloc(128, (512,), F32, "sq_t")
            for h in range(2):
                pipe = Pipe()
                back = alibi_back(DIFF_SLOPES[1]) if h == 0 else None
                for c in range(2):
                    dma("sp", KT2[c].ap[0:64, :], qk0[12 + 2 * h + c], [], [KT2[c].b], "L" + KT2[c].b.name)
                load_vt(VT, v0[:, 256 + h * 128:256 + (h + 1) * 128], 128)
                for qt in range(NQT):
                    for c in range(2):
                        QT = QT2[c][qt % 2]
                        dma("sp", QT.ap[0:64, :], qk0[8 + 2 * h + c, :, qt * 512:(qt + 1) * 512], [], [QT.b], "L" + QT.b.name)
                        dma("sp", QT.ap[64:70, :], cd["qaug"][:, h, :], [], [QT.b], "L" + QT.b.name)
                        kts = ktiles_for(qt, back)
                        n = len(kts)
                        for i, kt in enumerate(kts):
                            Y = Ys.next()
                            P = Ps.next()
                            o = kt - 4 * qt

                            def st0(Y=Y, P=P, o=o, kt=kt, QT=QT, c=c, qt=qt, h=h):
                                mm(Y, Y.ap, KT2[c], KT2[c].ap[:, kt * 128:(kt + 1) * 128], QT, QT.ap, True, o < 0)
                                if o >= 0:
                                    mm(Y, Y.ap, ident, ident.ap, mask_c, mask_c.ap[:, o, :], False, True)
                                S.op("act", lambda e: e.activation(P.ap, Y.ap, AF.Exp, bias=bias_ap(h, 4 * qt - kt)),
                                     reads=[Y.b, bias_tab.b], writes=[P.b])

                            def st2(P=P, kt=kt, i=i, n=n, c=c, h=h, qt=qt):
                                mm(NUM[c], NUM[c].ap, VT, VT.ap[:, kt, :], P, P.ap, i == 0, i == n - 1)
                                mm(DEN[c], DEN[c].ap, ones, ones.ap, P, P.ap, i == 0, i == n - 1)
                                if i == n - 1:
                                    S.op("dve", lambda e: e.reciprocal(r_t[c].ap, DEN[c].ap), reads=[DEN[c].b], writes=[r_t[c].b])
                                    S.op("dve", lambda e: e.tensor_tensor(a_t[c].ap, NUM[c].ap, r_t[c].ap, ALU.mult), reads=[NUM[c].b, r_t[c].b], writes=[a_t[c].b])
                                    if c == 1:
                                        S.op("dve", lambda e: e.scalar_tensor_tensor(o_t.ap, a_t[1].ap, neglam.ap, a_t[0].ap, ALU.mult, ALU.add),
                                             reads=[a_t[0].b, a_t[1].b, neglam.b], writes=[o_t.b])
                                        S.op("act", lambda e: e.activation(sq_t.ap, o_t.ap, AF.Square), reads=[o_t.b], writes=[sq_t.b])
                                        mm(MS, MS.ap, onesdiv, onesdiv.ap, sq_t, sq_t.ap, True, True)
                                        S.op("act", lambda e: e.activation(sq_t.ap, MS.ap, AF.Sqrt, bias=eps_t.ap), reads=[MS.b, eps_t.b], writes=[sq_t.b])
                                        S.op("dve", lambda e: e.reciprocal(sq_t.ap, sq_t.ap), reads=[sq_t.b], writes=[sq_t.b])
                                        ev = oev.next()
                                        S.op("dve", lambda e: e.scalar_tensor_tensor(ev.ap, o_t.ap, sg.ap, sq_t.ap, ALU.mult, ALU.mult),
                                             reads=[o_t.b, sg.b, sq_t.b], writes=[ev.b])
                                        store(ot_p[256 + h * 128:256 + (h + 1) * 128, qt * 512:(qt + 1) * 512], ev)

                            pipe.job([(0, st0), (2, st2)])
                pipe.flush()

        attention_l0()
        all_gather_ot()
        phase_reset()

        def phase_mlp(x_src, wout_src, w1_src, w2_src, gain_idx, dst, final_gain=None):
            stage = Ring([AR.alloc(128, (2048,), F32, f"wstg{i}") for i in range(2)])
            Wo = AR.alloc(128, (8, D), BF16, "Wo")
            W1 = AR.alloc(128, (8, 4096), BF16, "W1")
            gsl = T(gains.ap[:, gain_idx, :], gains.b)
            load_weight(Wo, wout_src, D, D, stage)
            load_weight(W1, w1_src, D, 4096, stage, gain=gsl)
            OTs = Ring([AR.alloc(128, (8, 512), BF16, f"OT{i}") for i in range(2)])
            OAs = Ring([AR.alloc(128, (8, 512), BF16, f"OA{i}") for i in range(1)])
            OBs = Ring([AR.alloc(128, (8, 512), BF16, f"OB{i}") for i in range(1)])
            x1r = Ring([AR.alloc(128, (D,), F32, f"x1r{i}") for i in range(3)])
            xts = Ring([AR.alloc(128, (D,), F32, f"xt{i}") for i in range(2)])
            hns = Ring([AR.alloc(128, (D,), BF16, f"hn{i}") for i in range(2)])
            junk = AR.alloc(128, (D,), BF16, "junk")
            sss = Ring([AR.alloc(128, (1,), F32, f"ss{i}") for i in range(2)])
            rss = Ring([AR.alloc(128, (1,), F32, f"rs{i}") for i in range(2)])
            hTs = Ring([AR.alloc(128, (8, 512), BF16, f"hT{i}") for i in range(2)])
            uts = Ring([AR.alloc(128, (512,), BF16, f"ut{i}") for i in range(4)])
            sqs = Ring([AR.alloc(128, (512,), BF16, f"sq{i}") for i in range(2)])
            tps = Ring([PS(i, f"tp{i}", cols=1024, dt=BF16) for i in (0, 1)])
            pps = Ring([PS(i, f"pp{i}") for i in (2, 3, 4, 5, 6, 7)])
            for tb in range(NQT // 2):
                OT = OTs.next()
                OA = OAs.next()
                OB = OBs.next()
                for kc in range(8):
                    dma("sp", OA.ap[:, kc, :], ot4[kc % 4, (kc // 4) * 128:(kc // 4 + 1) * 128, tb * 512:(tb + 1) * 512], [], [OA.b], "L" + OA.b.name)
                    dma("sp", OB.ap[:, kc, :], ot4[kc % 4, (kc // 4) * 128:(kc // 4 + 1) * 128, SEQ // 2 + tb * 512:SEQ // 2 + (tb + 1) * 512], [], [OB.b], "L" + OB.b.name)
                S.op("dve", lambda e, OA=OA: e.tensor_scalar(OA.ap, OA.ap, msel.ap[:, 0:1], None, ALU.mult), reads=[OA.b, msel.b], writes=[OA.b])
                S.op("dve", lambda e, OA=OA, OB=OB, OT=OT: e.scalar_tensor_tensor(OT.ap, OB.ap, msel.ap[:, 1:2], OA.ap, ALU.mult, ALU.add),
                     reads=[OA.b, OB.b, msel.b], writes=[OT.b])
                hT = hTs.next()
                for r in range(4):
                    row0 = tb * 512 + r * 128
                    xt = xts.next()
                    load(xt, x_src[row0:row0 + 128, :])
                    x1v = x1r.next()
                    for half in range(2):
                        pp = pps.next()
                        for k in range(8):
                            mm(pp, pp.ap, OT, OT.ap[:, k, r * 128:(r + 1) * 128], Wo, Wo.ap[:, k, half * 512:(half + 1) * 512], k == 0, k == 7)
                        S.op("dve", lambda e, pp=pp, xt=xt, x1v=x1v, half=half: e.tensor_tensor(
                            x1v.ap[:, half * 512:(half + 1) * 512], pp.ap, xt.ap[:, half * 512:(half + 1) * 512], ALU.add),
                            reads=[pp.b, xt.b], writes=[x1v.b])
                    store(x1s[row0:row0 + 128, :], x1v)
                    norm_transpose(x1v, hns.next(), sss.next(), rss.next(), junk, tps.next(), hT, r)
                for fc in range(32):
                    pp = pps.next()
                    for k in range(8):
                        mm(pp, pp.ap, W1, W1.ap[:, k, fc * 128:(fc + 1) * 128], hT, hT.ap[:, k, :], k == 0, k == 7)
                    sq = sqs.next()
                    u = uts.next()
                    S.op("act", lambda e, pp=pp, sq=sq: e.activation(sq.ap, pp.ap, AF.Square), reads=[pp.b], writes=[sq.b])
                    S.op("dve", lambda e, pp=pp, sq=sq, u=u: e.scalar_tensor_tensor(u.ap, pp.ap, 0.0, sq.ap, ALU.is_gt, ALU.mult),
                         reads=[pp.b, sq.b], writes=[u.b])
                    store(uts_d[fc, :, tb * 512:(tb + 1) * 512], u)
            phase_reset()
            stage = Ring([AR.alloc(128, (2048,), F32, f"wstg{i}") for i in range(2)])
            W2 = AR.alloc(128, (32, D), BF16, "W2")
            load_weight(W2, w2_src, 4096, D, stage)
            fg = None
            if final_gain is not None:
                fg = AR.alloc(128, (D,), F32, "fg")
                load(fg, final_gain.partition_broadcast(128))
            UTs = Ring([AR.alloc(128, (32, 512), BF16, f"UT{i}") for i in range(2)])
            x1r = Ring([AR.alloc(128, (D,), F32, f"x1r{i}") for i in range(3)])
            ys = Ring([AR.alloc(128, (D,), F32, f"y{i}") for i in range(3)])
            junk = AR.alloc(128, (D,), BF16, "junk")
            sss = Ring([AR.alloc(128, (1,), F32, f"ss{i}") for i in range(2)])
            rss = Ring([AR.alloc(128, (1,), F32, f"rs{i}") for i in range(2)])
            pps = Ring([PS(i, f"pp{i}") for i in (0, 1, 2, 3, 4, 5)])
            for tb in range(NQT // 2):
                UT = UTs.next()
                for f4 in range(4):
                    dma("sp", UT.ap[:, f4 * 8:(f4 + 1) * 8, :], uts_d[f4 * 8:(f4 + 1) * 8, :, tb * 512:(tb + 1) * 512].rearrange("f p t -> p f t"),
                        [], [UT.b], "L" + UT.b.name)
                for r in range(4):
                    row0 = tb * 512 + r * 128
                    x1v = x1r.next()
                    load(x1v, x1s[row0:row0 + 128, :])
                    y = ys.next()
                    for half in range(2):
                        pp = pps.next()
                        for fc in range(32):
                            mm(pp, pp.ap, UT, UT.ap[:, fc, r * 128:(r + 1) * 128], W2, W2.ap[:, fc, half * 512:(half + 1) * 512], fc == 0, fc == 31)
                        S.op("dve", lambda e, pp=pp, y=y, x1v=x1v, half=half: e.tensor_tensor(
                            y.ap[:, half * 512:(half + 1) * 512], pp.ap, x1v.ap[:, half * 512:(half + 1) * 512], ALU.add),
                            reads=[pp.b, x1v.b], writes=[y.b])
                    if fg is not None:
                        ss = sss.next()
                        rs = rss.next()
                        S.op("act", lambda e, y=y, ss=ss: e.activation(junk.ap, y.ap, AF.Square, accum_out=ss.ap), reads=[y.b], writes=[junk.b, ss.b])
                        S.op("act", lambda e, ss=ss, rs=rs: e.activation(rs.ap, ss.ap, AF.Sqrt, bias=eps_t.ap, scale=1.0 / D), reads=[ss.b, eps_t.b], writes=[rs.b])
                        S.op("dve", lambda e, rs=rs: e.reciprocal(rs.ap, rs.ap), reads=[rs.b], writes=[rs.b])
                        S.op("dve", lambda e, y=y, rs=rs: e.scalar_tensor_tensor(y.ap, y.ap, rs.ap, fg.ap, ALU.mult, ALU.mult),
                             reads=[y.b, rs.b, fg.b], writes=[y.b])
                    store(dst[row0:row0 + 128, :], y)

        phase_mlp(xh_in, ev_w_out, mlp_w1[0], mlp_w2[0], 1, x2)
        all_gather_x2()
        if False:
            S.barrier()
            cp = Ring([AR.alloc(128, (D,), F32, f"cp{i}") for i in range(2)])
            for i in range(NKT):
                c_ = cp.next()
                load(c_, x2[i * 128:(i + 1) * 128, :])
                store(out_d[i * 128:(i + 1) * 128, :], c_)
            S.barrier()
            S.finalize()
            return nc, S
        phase_reset()

        fm = []
        for jc in range(4):
            dsts = [(lambda tb, slot=2 * jc + half: q1[slot, :, tb * 512:(tb + 1) * 512], 64 * half, 64) for half in range(2)]
            fm.append((128 * jc, 128, dsts, 0.125, None))
        for ji in range(4):
            dsts = [(lambda tb, slot=2 * ji + half: kf1[slot, :, tb * 512:(tb + 1) * 512], 64 * half, 64) for half in range(2)]
            fm.append((512 + 128 * ji, 128, dsts, 1.0, None))
        fm.append((1024, 24, [(lambda tb: gts[:, tb * 512:(tb + 1) * 512], 0, 24)], 1.0, AF.Sigmoid))
        tm = [(1048, 256, lambda row0: v1[row0:row0 + 128, 0:256])]
        phase_proj(lambda row0: x2g[(row0 % 4096) // 512, (row0 // 4096) * 512 + row0 % 512:(row0 // 4096) * 512 + row0 % 512 + 128, :], od_w_in, 1304, 2, fm, tm)
        phase_reset()

        def phase_compress():
            stg = AR.alloc(64, (32 * 256,), F32, "cstg")
            W1c = AR.alloc(64, (32, 256), BF16, "W1c")
            stg2 = AR.alloc(128, (2, 64), F32, "cstg2")
            W2c = AR.alloc(128, (2, 64), BF16, "W2c")
            posf = AR.alloc(64, (32,), F32, "posf")
            posT = AR.alloc(64, (32,), BF16, "posT")
            biash = AR.alloc(128, (2,), F32, "biash")
            src = AR.alloc(64, (SEQ,), BF16, "csrc")
            hid = AR.alloc(128, (2, 512), BF16, "hid")
            u_t = AR.alloc(128, (512,), F32, "u_t")
            w_t = AR.alloc(128, (512,), F32, "w_t")
            evk = AR.alloc(64, (512,), BF16, "evk")
            evv = Ring([AR.alloc(128, (64,), BF16, f"evv{i}") for i in range(2)])
            pb_ = PS(0, "cbias")
            ph = Ring([PS(1, "ph0"), PS(2, "ph1")])
            po = Ring([PS(3, "po0"), PS(4, "po1")])
            S.op("dve", lambda e: e.memset(hid.ap, 0.0), writes=[hid.b])
            S.op("dve", lambda e: e.memset(evk.ap, 0.0), writes=[evk.b])
            for kind, (w1d, w2d, posd) in enumerate(((cw1k, cw2k, pos_k), (cw1v, cw2v, pos_v))):
                dma("sp", stg.ap.rearrange("p (l h) -> p l h", l=32), w1d.rearrange("(l d) h -> d l h", d=64), [], [stg.b], "Lcstg")
                for q4 in range(4):
                    cast(W1c, W1c.ap[:, q4 * 8:(q4 + 1) * 8, :], stg, stg.ap.rearrange("p (l h) -> p l h", l=32)[:, q4 * 8:(q4 + 1) * 8, :])
                dma("sp", stg2.ap, w2d.rearrange("(c p) n -> p c n", p=128), [], [stg2.b], "Lcstg2")
                cast(W2c, W2c.ap, stg2, stg2.ap)
                dma("sp", posf.ap, posd.rearrange("l d -> d l"), [], [posf.b], "Lposf", slow=True)
                cast(posT, posT.ap, posf, posf.ap)
                for hc in range(2):
                    for l in range(32):
                        mm(pb_, pb_.ap[:, 0:1], W1c, W1c.ap[:, l, hc * 128:(hc + 1) * 128], posT, posT.ap[:, l:l + 1], l == 0, l == 31)
                    S.op("dve", lambda e, hc=hc: e.tensor_copy(biash.ap[:, hc:hc + 1], pb_.ap[:, 0:1]), reads=[pb_.b], writes=[biash.b])
                for g in range(2):
                    load(src, kf1[2 * kind + g])
                    for hc in range(2):
                        p_ = ph.next()
                        for l in range(32):
                            mm(p_, p_.ap[:, 0:511], W1c, W1c.ap[:, l, hc * 128:(hc + 1) * 128], src, src.ap[:, l:l + 16 * 510 + 1:16], l == 0, l == 31)
                        S.op("act", lambda e, p_=p_, hc=hc: e.activation(u_t.ap[:, 0:511], p_.ap[:, 0:511], AF.Identity, bias=biash.ap[:, hc:hc + 1]),
                             reads=[p_.b, biash.b], writes=[u_t.b])
                        S.op("act", lambda e: e.activation(w_t.ap[:, 0:511], u_t.ap[:, 0:511], AF.Square), reads=[u_t.b], writes=[w_t.b])
                        S.op("dve", lambda e: e.tensor_scalar(w_t.ap[:, 0:511], w_t.ap[:, 0:511], 0.044715, 1.0, ALU.mult, ALU.add), reads=[w_t.b], writes=[w_t.b])
                        S.op("dve", lambda e: e.tensor_tensor(w_t.ap[:, 0:511], w_t.ap[:, 0:511], u_t.ap[:, 0:511], ALU.mult), reads=[w_t.b, u_t.b], writes=[w_t.b])
                        S.op("act", lambda e: e.activation(w_t.ap[:, 0:511], w_t.ap[:, 0:511], AF.Sigmoid, scale=2.0 * 0.7978845608028654), reads=[w_t.b], writes=[w_t.b])
                        S.op("dve", lambda e, hc=hc: e.tensor_tensor(hid.ap[:, hc, 0:511], w_t.ap[:, 0:511], u_t.ap[:, 0:511], ALU.mult), reads=[w_t.b, u_t.b], writes=[hid.b])
                    if kind == 0:
                        p2 = po.next()
                        for hc in range(2):
                            mm(p2, p2.ap[0:64, 0:511], W2c, W2c.ap[:, hc, :], hid, hid.ap[:, hc, 0:511], hc == 0, hc == 1)
                        S.op("dve", lambda e, p2=p2: e.tensor_copy(evk.ap[:, 0:511], p2.ap[0:64, 0:511]), reads=[p2.b], writes=[evk.b])
                        store(kcmp[g], evk)
                    else:
                        for nchunk in range(4):
                            p2 = po.next()
                            for hc in range(2):
                                mm(p2, p2.ap[:, 0:64], hid, hid.ap[:, hc, nchunk * 128:(nchunk + 1) * 128], W2c, W2c.ap[:, hc, :], hc == 0, hc == 1)
                            ev = evv.next()
                            S.op("dve", lambda e, p2=p2, ev=ev: e.tensor_copy(ev.ap, p2.ap[:, 0:64]), reads=[p2.b], writes=[ev.b])
                            store(vcmp[g, nchunk * 128:(nchunk + 1) * 128, :], ev)

        phase_compress()
        phase_reset()

        def attention_l1():
            mask_c = AR.alloc(128, (4, 512), BF16, "mask_c")
            load(mask_c, cd["mask_c"])
            mask_cmp = AR.alloc(128, (5, 512), BF16, "mask_cmp")
            load(mask_cmp, cd["mask_cmp"])
            mask_win = AR.alloc(128, (8, 512), BF16, "mask_win")
            load(mask_win, cd["mask_win"])
            ewide = AR.alloc(128, (SEQ,), BF16, "ewide")
            load(ewide, cd["ewide"])
            ovl = AR.alloc(128, (4, 128), BF16, "ovl")
            load(ovl, cd["ovl"])
            tka = AR.alloc(128, (256,), F32, "tka")
            load(tka, cd["topk_a"])
            tkm = AR.alloc(128, (256,), F32, "tkm")
            load(tkm, cd["topk_m"])
            bcmp = AR.alloc(128, (8 * NQT * 4,), F32, "bcmp")
            load(bcmp, cd["bias_cmp"])
            KcA = AR.alloc(70, (512,), BF16, "KcA")
            dma("sp", KcA.ap[64:70, :], cd["kaug_cmp"], [], [KcA.b], "LKcA")
            Vc = AR.alloc(128, (4, 64), BF16, "Vc")
            KsA = AR.alloc(70, (SEQ,), BF16, "KsA")
            KwA = AR.alloc(70, (SEQ,), BF16, "KwA")
            dma("sp", KsA.ap[64:70, :], cd["kaug"], [], [KsA.b], "LKsA")
            dma("sp", KwA.ap[64:70, :], cd["kaug"], [], [KwA.b], "LKwA")
            Vs = AR.alloc(128, (NKT, 64), BF16, "Vs")
            Vw = AR.alloc(128, (NKT, 64), BF16, "Vw")
            QAs = [[AR.alloc(70, (512,), BF16, f"QA{r}{i}") for i in range(2)] for r in range(4)]
            GBs = [[AR.alloc(64, (3, 512), BF16, f"GB{r}{i}") for i in range(2)] for r in range(4)]
            Pc = Ring([AR.alloc(128, (512,), BF16, f"Pc{i}") for i in range(10)])
            Ps = Ring([AR.alloc(128, (512,), BF16, f"Pp{i}") for i in range(4)])
            pcn = Ring([AR.alloc(128, (512,), BF16, f"pcn{i}") for i in range(4)])
            rdc = AR.alloc(128, (512,), F32, "rdc")
            ocmp = [AR.alloc(64, (512,), F32, f"ocmp{r}") for r in range(4)]
            selbT = AR.alloc(128, (512,), BF16, "selbT")
            imp2 = AR.alloc(128, (128,), F32, "imp2")
            tmp2 = AR.alloc(128, (128,), F32, "tmp2")
            selm = AR.alloc(128, (128,), F32, "selm")
            selb = AR.alloc(128, (128,), BF16, "selb")
            v8a = AR.alloc(128, (8,), F32, "v8a")
            v8b = AR.alloc(128, (8,), F32, "v8b")
            rs2 = [AR.alloc(64, (512,), F32, f"rs2{i}") for i in range(2)]
            ob2 = [AR.alloc(64, (512,), F32, f"ob2{i}") for i in range(2)]
            acc = AR.alloc(64, (512,), F32, "acc")
            t1 = AR.alloc(64, (512,), F32, "t1")
            oev = Ring([AR.alloc(64, (512,), BF16, f"oev{i}") for i in range(2)])
            Ys = Ring([PS(0, "Y0"), PS(1, "Y1"), PS(2, "Y2")])
            NUMs = [PS(3, "NUMa"), PS(5, "NUMb")]
            DENs = [PS(4, "DENa"), PS(6, "DENb")]
            IMP = PS(7, "IMP")
            TRP = T(pbanks[6][:, :].bitcast(BF16)[:, 0:512], DENs[1].b)

            def tile_job(pipe, K, kap, QAr, extra, bias, V, vap, NUM, DEN, den_parts, P, first, last, tail):
                Y = Ys.next()

                def st0():
                    nx = len(extra)
                    mm(Y, Y.ap, K, kap, QAr, QAr.ap, True, nx == 0)
                    for xi, (lt, lap, rt, rap) in enumerate(extra):
                        mm(Y, Y.ap, lt, lap, rt, rap, False, xi == nx - 1)
                    S.op("act", lambda e: e.activation(P.ap, Y.ap, AF.Exp, bias=bias[0]), reads=[Y.b, bias[1]], writes=[P.b])

                def st2():
                    mm(NUM, NUM.ap[0:64, :], V, vap, P, P.ap, first, last)
                    mm(DEN, DEN.ap[0:den_parts, :], ones, ones.ap[:, 0:den_parts], P, P.ap, first, last)
                    if last and tail is not None:
                        tail()

                pipe.job([(0, st0), (2, st2)])

            for g in range(2):
                dma("sp", KcA.ap[0:64, :], kcmp[g], [], [KcA.b], "LKcA")
                dma("sp", Vc.ap, vcmp[g].rearrange("(c p) d -> p c d", p=128), [], [Vc.b], "LVc")
                dma("sp", KsA.ap[0:64, :], kf1[4 + g], [], [KsA.b], "LKsA")
                dma("sp", KwA.ap[0:64, :], kf1[6 + g], [], [KwA.b], "LKwA")
                for g4 in range(4):
                    dma("sp", Vs.ap[:, g4 * 16:(g4 + 1) * 16, :], v1[g4 * 2048:(g4 + 1) * 2048, g * 64:(g + 1) * 64].rearrange("(t p) c -> p t c", p=128), [], [Vs.b], "LVs")
                    dma("sp", Vw.ap[:, g4 * 16:(g4 + 1) * 16, :], v1[g4 * 2048:(g4 + 1) * 2048, 128 + g * 64:128 + (g + 1) * 64].rearrange("(t p) c -> p t c", p=128), [], [Vw.b], "LVw")
                for qt in range(NQT):
                    pipe = Pipe()
                    QA = [QAs[r][qt % 2] for r in range(4)]
                    GB = [GBs[r][qt % 2] for r in range(4)]
                    for r in range(4):
                        h = 4 * g + r
                        dma("sp", QA[r].ap[0:64, :], q1[h, :, qt * 512:(qt + 1) * 512], [], [QA[r].b], "L" + QA[r].b.name)
                        dma("sp", QA[r].ap[64:70, :], cd["qaug"][:, 2 + h, :], [], [QA[r].b], "L" + QA[r].b.name)
                        for c3 in range(3):
                            dma("sp", GB[r].ap[:, c3, :], gts[3 * h + c3, qt * 512:(qt + 1) * 512].partition_broadcast(64), [], [GB[r].b], "L" + GB[r].b.name)
                    chunks = [c for c in range(4) if 4 * c <= qt]
                    for r in range(4):
                        h = 4 * g + r
                        NUM = NUMs[r % 2]
                        DEN = DENs[r % 2]
                        Pl = [(c, Pc.next()) for c in chunks]

                        def cmp_tail(r=r, NUM=NUM, DEN=DEN, Pl=Pl):
                            S.op("dve", lambda e: e.tensor_scalar(rdc.ap, DEN.ap, 1e-30, None, ALU.add), reads=[DEN.b], writes=[rdc.b])
                            S.op("dve", lambda e: e.reciprocal(rdc.ap, rdc.ap), reads=[rdc.b], writes=[rdc.b])
                            S.op("dve", lambda e: e.tensor_tensor(ocmp[r].ap, NUM.ap[0:64, :], rdc.ap[0:64, :], ALU.mult), reads=[NUM.b, rdc.b], writes=[ocmp[r].b])
                            for ci, (c, P) in enumerate(Pl):
                                pn = pcn.next()
                                S.op("dve", lambda e, pn=pn, P=P: e.tensor_tensor(pn.ap, P.ap, rdc.ap, ALU.mult), reads=[P.b, rdc.b], writes=[pn.b])
                                for qs in range(4):
                                    mm(IMP, IMP.ap[:, qs * 128:(qs + 1) * 128], pn, pn.ap[:, qs * 128:(qs + 1) * 128], ovl, ovl.ap[:, c, :],
                                       r == 0 and ci == 0, r == 3 and ci == len(Pl) - 1)

                        for ci, (c, P) in enumerate(Pl):
                            rel = qt - 4 * c
                            extra = [(ident, ident.ap, mask_cmp, mask_cmp.ap[:, rel, :])] if rel <= 4 else []
                            col = (h * NQT + qt) * 4 + c
                            tile_job(pipe, KcA, KcA.ap[:, c * 128:(c + 1) * 128], QA[r], extra, (bcmp.ap[:, col:col + 1], bcmp.b),
                                     Vc, Vc.ap[:, c, :], NUM, DEN, 128, P, ci == 0, ci == len(Pl) - 1, cmp_tail)
                    pipe.flush()
                    for qs in range(4):
                        off = 127 - 2 * (4 * qt + qs)
                        S.op("dve", lambda e, qs=qs, off=off: e.tensor_tensor(tmp2.ap, IMP.ap[:, qs * 128:(qs + 1) * 128], tkm.ap[:, off:off + 128], ALU.mult),
                             reads=[IMP.b, tkm.b], writes=[tmp2.b])
                        S.op("dve", lambda e, off=off: e.tensor_tensor(imp2.ap, tmp2.ap, tka.ap[:, off:off + 128], ALU.add), reads=[tmp2.b, tka.b], writes=[imp2.b])
                        S.op("dve", lambda e: e.memset(imp2.ap[:, 0:1], 1.0e6), reads=[], writes=[imp2.b])
                        S.op("dve", lambda e: e.max(v8a.ap, imp2.ap), reads=[imp2.b], writes=[v8a.b])
                        S.op("dve", lambda e: e.match_replace(tmp2.ap, v8a.ap, imp2.ap, -9.0), reads=[imp2.b, v8a.b], writes=[tmp2.b])
                        S.op("dve", lambda e: e.max(v8b.ap, tmp2.ap), reads=[tmp2.b], writes=[v8b.b])
                        S.op("dve", lambda e: e.tensor_scalar(selm.ap, imp2.ap, v8b.ap[:, 7:8], 0.0, ALU.is_ge, ALU.add), reads=[imp2.b, v8b.b], writes=[selm.b])
                        S.op("dve", lambda e: e.scalar_tensor_tensor(selm.ap, imp2.ap, 0.0, selm.ap, ALU.is_ge, ALU.mult), reads=[imp2.b, selm.b], writes=[selm.b])
                        S.op("dve", lambda e: e.tensor_scalar(selb.ap, selm.ap, -1.0, -NEG, ALU.add, ALU.mult), reads=[selm.b], writes=[selb.b])
                        S.op("pe", lambda e, qs=qs: e.transpose(TRP.ap[:, qs * 128:(qs + 1) * 128], selb.ap, ident.ap), reads=[selb.b, ident.b], writes=[TRP.b])
                    S.op("dve", lambda e: e.tensor_copy(selbT.ap, TRP.ap[:, 0:512]), reads=[TRP.b], writes=[selbT.b])
                    for r in range(4):
                        h = 4 * g + r
                        si = 2 + h
                        back = alibi_back(NSA_SLOPES[4 + r]) if g == 0 else None

                        def sel_tail(r=r, GBr=GB[r]):
                            S.op("dve", lambda e: e.reciprocal(rs2[0].ap, DENs[0].ap[0:64, :]), reads=[DENs[0].b], writes=[rs2[0].b])
                            S.op("dve", lambda e: e.tensor_tensor(ob2[0].ap, NUMs[0].ap[0:64, :], rs2[0].ap, ALU.mult), reads=[NUMs[0].b, rs2[0].b], writes=[ob2[0].b])
                            S.op("pool", lambda e: e.tensor_tensor(acc.ap, GBr.ap[:, 1, :], ob2[0].ap, ALU.mult), reads=[GBr.b, ob2[0].b], writes=[acc.b])
                            S.op("pool", lambda e: e.tensor_tensor(t1.ap, GBr.ap[:, 0, :], ocmp[r].ap, ALU.mult), reads=[GBr.b, ocmp[r].b], writes=[t1.b])
                            S.op("pool", lambda e: e.tensor_tensor(acc.ap, acc.ap, t1.ap, ALU.add), reads=[acc.b, t1.b], writes=[acc.b])

                        def win_tail(r=r, h=h, qt=qt, GBr=GB[r]):
                            S.op("dve", lambda e: e.reciprocal(rs2[1].ap, DENs[1].ap[0:64, :]), reads=[DENs[1].b], writes=[rs2[1].b])
                            S.op("dve", lambda e: e.tensor_tensor(ob2[1].ap, NUMs[1].ap[0:64, :], rs2[1].ap, ALU.mult), reads=[NUMs[1].b, rs2[1].b], writes=[ob2[1].b])
                            S.op("pool", lambda e: e.tensor_tensor(t1.ap, GBr.ap[:, 2, :], ob2[1].ap, ALU.mult), reads=[GBr.b, ob2[1].b], writes=[t1.b])
                            ev = oev.next()
                            S.op("pool", lambda e: e.tensor_tensor(ev.ap, acc.ap, t1.ap, ALU.add), reads=[acc.b, t1.b], writes=[ev.b])
                            store(ot_p[h * 64:(h + 1) * 64, qt * 512:(qt + 1) * 512], ev)

                        kts = ktiles_for(qt, back)
                        for i, kt in enumerate(kts):
                            o = kt - 4 * qt
                            extra = [(ewide, ewide.ap[:, kt * 128:(kt + 1) * 128], selbT, selbT.ap)]
                            if o >= 0:
                                extra.append((ident, ident.ap, mask_c, mask_c.ap[:, o, :]))
                            tile_job(pipe, KsA, KsA.ap[:, kt * 128:(kt + 1) * 128], QA[r], extra, (bias_ap(si, 4 * qt - kt), bias_tab.b),
                                     Vs, Vs.ap[:, kt, :], NUMs[0], DENs[0], 64, Ps.next(), i == 0, i == len(kts) - 1, sel_tail)
                        kts = [kt for kt in range(4 * qt + 3, 4 * qt - 5, -1) if kt >= 0]
                        for i, kt in enumerate(kts):
                            o = kt - 4 * qt
                            extra = [(ident, ident.ap, mask_win, mask_win.ap[:, o + 4, :])]
                            tile_job(pipe, KwA, KwA.ap[:, kt * 128:(kt + 1) * 128], QA[r], extra, (bias_ap(si, 4 * qt - kt), bias_tab.b),
                                     Vw, Vw.ap[:, kt, :], NUMs[1], DENs[1], 64, Ps.next(), i == 0, i == len(kts) - 1, win_tail)
                    pipe.flush()

        attention_l1()
        all_gather_ot()
        if STOP_AFTER == "L1ATT":
            S.barrier()
            cpb = Ring([AR.alloc(128, (1024,), BF16, f"cpb{i}") for i in range(2)])
            cpf = Ring([AR.alloc(128, (1024,), F32, f"cpf{i}") for i in range(2)])
            ov = out_d.rearrange("(a b) c -> a (b c)", a=1024)
            for k in range(8):
                for cb in range(8):
                    b_ = cpb.next()
                    f_ = cpf.next()
                    load(b_, ot[k * 128:(k + 1) * 128, cb * 1024:(cb + 1) * 1024])
                    S.op("dve", lambda e, b_=b_, f_=f_: e.tensor_copy(f_.ap, b_.ap), reads=[b_.b], writes=[f_.b])
                    store(ov[k * 128:(k + 1) * 128, cb * 1024:(cb + 1) * 1024], f_)
            S.barrier()
            S.finalize()
            return nc, S
        phase_reset()
        phase_mlp(x2, od_w_out, mlp_w1[1], mlp_w2[1], 3, out_d, final_gain=final_norm)
        S.barrier()
        S.finalize()
    return nc, S


def _cols(*ranges):
    return np.concatenate([np.arange(a, b) for a, b in ranges])


def kernel(**inputs):
    consts = [make_consts(0), make_consts(1)]
    nc, S = build_program(consts[0])
    x = np.ascontiguousarray(inputs["x"], dtype=np.float32)
    B = x.shape[0]
    shared = {}
    for k in ("attn_norm", "mlp_norm", "final_norm", "mlp_w1", "mlp_w2"):
        shared[k] = np.ascontiguousarray(inputs[k], dtype=np.float32)
    for k in ("ev_subln", "od_cmp_pos_k", "od_cmp_k_w1", "od_cmp_k_w2", "od_cmp_pos_v", "od_cmp_v_w1", "od_cmp_v_w2"):
        shared[k] = np.ascontiguousarray(inputs[k][0], dtype=np.float32)
    shared["lamv"] = np.ascontiguousarray(np.stack([inputs["ev_lam_q1"][0], inputs["ev_lam_k1"][0],
                                                    inputs["ev_lam_q2"][0], inputs["ev_lam_k2"][0]]), dtype=np.float32)
    w0 = np.asarray(inputs["ev_w_in"][0], dtype=np.float32)
    w1 = np.asarray(inputs["od_w_in"][0], dtype=np.float32)
    per = []
    feat0 = []
    feat1 = []
    for p in range(2):
        sb = (256 * p, 256 * p + 256)
        dA, dB = DIFF_ASSIGN[p]
        c0 = _cols((0 + sb[0], 0 + sb[1]), (512 + sb[0], 512 + sb[1]),
                   (1536 + 128 * dA, 1536 + 128 * dA + 128), (1536 + 128 * dB, 1536 + 128 * dB + 128),
                   (2048 + 128 * dA, 2048 + 128 * dA + 128), (2048 + 128 * dB, 2048 + 128 * dB + 128),
                   (1024 + sb[0], 1024 + sb[1]),
                   (2560 + 128 * dA, 2560 + 128 * dA + 128), (2560 + 128 * dB, 2560 + 128 * dB + 128))
        gA, gB = GROUP_ASSIGN[p]
        kv = lambda base: [(base + 64 * gA, base + 64 * gA + 64), (base + 64 * gB, base + 64 * gB + 64)]
        c1 = _cols((256 * gA, 256 * gA + 256), (256 * gB, 256 * gB + 256),
                   *kv(1024), *kv(1280), *kv(1536), *kv(2048),
                   (2560 + 12 * gA, 2560 + 12 * gA + 12), (2560 + 12 * gB, 2560 + 12 * gB + 12),
                   *kv(1792), *kv(2304))
        per.append({"ev_w_in": np.ascontiguousarray(w0[:, c0]), "od_w_in": np.ascontiguousarray(w1[:, c1])})
        feat0.append(_cols(sb, (512 + 128 * dA, 512 + 128 * dA + 128), (512 + 128 * dB, 512 + 128 * dB + 128)))
        feat1.append(_cols((256 * gA, 256 * gA + 256), (256 * gB, 256 * gB + 256)))
    perm0 = np.concatenate(feat0)
    perm1 = np.concatenate(feat1)
    shared["ev_w_out"] = np.ascontiguousarray(np.asarray(inputs["ev_w_out"][0], dtype=np.float32)[perm0])
    shared["od_w_out"] = np.ascontiguousarray(np.asarray(inputs["od_w_out"][0], dtype=np.float32)[perm1])
    in_maps = []
    for core in range(8):
        p = core % 2
        m = dict(shared)
        m.update(per[p])
        for k, v in consts[p].items():
            m["c_" + k] = v
        xb = x[(core // 2) % B]
        m["x"] = xb
        m["xh"] = np.ascontiguousarray(xb[p * (SEQ // 2):(p + 1) * (SEQ // 2)])
        ms = np.zeros((128, 2), np.float32)
        ms[:, p] = 1.0
        m["msel"] = ms
        in_maps.append(m)
    res = run_bass_kernel_spmd(nc, in_maps, core_ids=list(range(8)))
    out = np.stack([np.concatenate([np.asarray(res.results[2 * b + p]["out"], dtype=np.float32) for p in range(2)], axis=0)
                    for b in range(B)])
    return out
```

```python
import math
import numpy as np
import ml_dtypes
from contextlib import ExitStack
import concourse.bass as bass
import concourse.mybir as mybir
from concourse.bass_utils import run_bass_kernel_spmd

F32 = mybir.dt.float32
BF16 = mybir.dt.bfloat16
AF = mybir.ActivationFunctionType
ALU = mybir.AluOpType
bf = ml_dtypes.bfloat16

SEQ = 8192
D = 1024
NQT = SEQ // 512
NKT = SEQ // 128
NEG = -30000.0
SB_BACK = 4
ALIBI_CUT = 144.0
SEM_WRAP = 16000
DMA_WRAP = 1000
STOP_AFTER = None


class Buf:
    __slots__ = ("name", "w", "r")

    def __init__(self, name):
        self.name = name
        self.w = []
        self.r = []


class Op:
    __slots__ = ("eng", "fn", "deps", "idx", "needs_inc", "waits", "is_dma", "dkey", "dval")

    def __init__(self, eng, fn):
        self.eng = eng
        self.fn = fn
        self.deps = set()
        self.idx = -1
        self.needs_inc = False
        self.waits = []
        self.is_dma = False
        self.dkey = None
        self.dval = 0


class T:
    __slots__ = ("ap", "b")

    def __init__(self, ap, b):
        self.ap = ap
        self.b = b

    def __getitem__(self, k):
        return self.ap[k]


class Sched:
    ENGS = ("pe", "act", "dve", "pool", "sp")

    def __init__(self, nc, stack):
        self.nc = nc
        self.stack = stack
        self.ops = []
        self.eng_ops = {e: [] for e in self.ENGS}
        self.dma_count = {}
        self.bufs = []

    def buf(self, name):
        b = Buf(name)
        self.bufs.append(b)
        return b

    def op(self, eng, fn, reads=(), writes=(), dma_key=None):
        o = Op(eng, fn)
        oid = len(self.ops)
        for b in reads:
            o.deps.update(b.w)
        for b in writes:
            o.deps.update(b.w)
            o.deps.update(b.r)
        for b in reads:
            b.r.append(oid)
        for b in writes:
            b.w = [oid]
            b.r = []
        if dma_key is not None:
            o.is_dma = True
            o.dkey = dma_key
            n = self.dma_count.get(dma_key, 0) + 1
            self.dma_count[dma_key] = n
            o.dval = n
        o.idx = len(self.eng_ops[eng])
        self.eng_ops[eng].append(o)
        self.ops.append(o)
        return oid

    def barrier(self):
        live = [b for b in self.bufs if b.w or b.r]
        first = True
        sync = self.buf("barrier")
        for e in self.ENGS:
            if first:
                self.op(e, lambda eng: eng.nop(), writes=live + [sync])
                first = False
            else:
                self.op(e, lambda eng: eng.nop(), reads=[sync])
        self.bufs = [sync]

    def finalize(self):
        nc = self.nc
        ops = self.ops
        know = {e: {} for e in self.ENGS}
        comp_know = [None] * len(ops)
        for oid, o in enumerate(ops):
            K = know[o.eng]
            for d in sorted(o.deps):
                p = ops[d]
                if p.is_dma:
                    dom = ("d", p.dkey)
                    val = p.dval
                else:
                    dom = ("e", p.eng)
                    val = p.idx + 1
                    if p.eng == "pe" and o.eng == "pe":
                        continue
                if K.get(dom, 0) >= val:
                    continue
                o.waits.append((dom, val))
                p.needs_inc = True
                for k2, v2 in comp_know[d].items():
                    if K.get(k2, 0) < v2:
                        K[k2] = v2
            ck = dict(K)
            if o.is_dma:
                ck[("d", o.dkey)] = o.dval
            else:
                ck[("e", o.eng)] = o.idx + 1
            comp_know[oid] = ck
        comp_know = None
        eng_sems = {}
        counts = {}
        for e in self.ENGS:
            c = 0
            for o in self.eng_ops[e]:
                if o.is_dma:
                    continue
                if o.needs_inc:
                    c += 1
                counts[(e, o.idx)] = c
            nsem = (c + SEM_WRAP - 1) // SEM_WRAP
            eng_sems[e] = [self.stack.enter_context(nc.semaphore(f"s_{e}_{i}")) for i in range(nsem)]
        dma_sems = {}
        for k, n in self.dma_count.items():
            nsem = (n + DMA_WRAP - 1) // DMA_WRAP
            dma_sems[k] = [self.stack.enter_context(nc.semaphore(f"d_{k}_{i}")) for i in range(nsem)]

        def sem_for(dom, val):
            if dom[0] == "e":
                c = counts[(dom[1], val - 1)]
                return eng_sems[dom[1]][(c - 1) // SEM_WRAP], (c - 1) % SEM_WRAP + 1
            mul = 1 if dom[1].startswith("cc") else 16
            return dma_sems[dom[1]][(val - 1) // DMA_WRAP], ((val - 1) % DMA_WRAP + 1) * mul

        self.n_waits = sum(len(o.waits) for o in ops)
        with nc.Block() as block:
            def make(e):
                def body(eng):
                    for o in self.eng_ops[e]:
                        for dom, val in o.waits:
                            s, v = sem_for(dom, val)
                            eng.wait_ge(s, v)
                        ins = o.fn(eng)
                        if o.is_dma:
                            s, v = sem_for(("d", o.dkey), o.dval)
                            ins.then_inc(s, 1 if o.dkey.startswith("cc") else 16)
                        elif o.needs_inc:
                            c = counts[(e, o.idx)]
                            ins.then_inc(eng_sems[e][(c - 1) // SEM_WRAP], 1)
                return body
            block.tensor(make("pe"))
            block.scalar(make("act"))
            block.vector(make("dve"))
            block.gpsimd(make("pool"))
            block.sync(make("sp"))


class Arena:
    def __init__(self, S, tens, nwords):
        self.S = S
        self.t = tens
        self.n = nwords
        self.off = 0
        self.cnt = 0

    def reset(self):
        self.off = 0

    def alloc(self, parts, shape, dt, name):
        n = 1
        for s in shape:
            n *= s
        words = n if dt == F32 else (n + 1) // 2
        v = self.t[0:parts, self.off:self.off + words]
        self.off += words
        assert self.off <= self.n, f"arena overflow at {name}: {self.off} > {self.n}"
        if dt != F32:
            v = v.bitcast(dt)
        if len(shape) == 2:
            v = v.rearrange("p (a b) -> p a b", a=shape[0])
        elif len(shape) == 3:
            v = v.rearrange("p (a b c) -> p a b c", a=shape[0], b=shape[1])
        self.cnt += 1
        return T(v, self.S.buf(name))


def _split3(v):
    v = np.asarray(v, np.float32)
    hi = v.astype(bf)
    r1 = v - hi.astype(np.float32)
    mid = r1.astype(bf)
    r2 = r1 - mid.astype(np.float32)
    lo = r2.astype(bf)
    return hi, mid, lo


DIFF_SLOPES = [2.0 ** (-8.0 * (h + 1) / 4) for h in range(4)]
NSA_SLOPES = [2.0 ** (-8.0 * (h + 1) / 16) for h in range(16)]
DIFF_ASSIGN = ((0, 3), (1, 2))
GROUP_ASSIGN = ((0, 3), (1, 2))
SB_PER_CORE = 4


def slopes_for(p):
    sl = [DIFF_SLOPES[h] for h in DIFF_ASSIGN[p]]
    for g in GROUP_ASSIGN[p]:
        sl += [NSA_SLOPES[4 * g + r] for r in range(4)]
    return sl


NSLOT = 10
BIAS_M0 = -3
BIAS_NM = 68


def make_consts(p):
    ALL_SLOPES = slopes_for(p)
    c = {}
    j = np.arange(128)
    t = np.arange(512)
    c["ident"] = np.eye(128, dtype=np.float32).astype(bf)
    c["ones"] = np.ones((128, 128), np.float32).astype(bf)
    c["negones"] = (-np.ones((128, 128), np.float32)).astype(bf)
    c["uneg"] = (-(j[:, None] >= j[None, :]).astype(np.float32)).astype(bf)
    c["onesdiv"] = np.full((128, 128), 1.0 / 128, np.float32)
    msb = np.zeros((4, 128, 512), np.float32)
    mc = np.zeros((4, 128, 512), np.float32)
    for o in range(4):
        jj = 128 * o + j[:, None]
        msb[o] = np.where(jj >= t[None, :], NEG, 0.0)
        mc[o] = np.where(jj > t[None, :], NEG, 0.0)
    c["mask_sb"] = np.ascontiguousarray(msb.transpose(1, 0, 2)).astype(bf)
    c["mask_c"] = np.ascontiguousarray(mc.transpose(1, 0, 2)).astype(bf)
    kr = np.zeros((6, SEQ), np.float32)
    kr[0:3] = 1.0
    kr[3:6] = (np.arange(SEQ) % 128)[None, :]
    c["kaug"] = kr.astype(bf)
    kc = np.zeros((6, 512), np.float32)
    kc[0:3] = 1.0
    kc[3:6] = (16 * (np.arange(512) % 128))[None, :]
    c["kaug_cmp"] = kc.astype(bf)
    qa = np.zeros((len(ALL_SLOPES), 6, 512), np.float32).astype(bf)
    for i, s in enumerate(ALL_SLOPES):
        s32 = np.float32(s)
        v = (-(s32 * t.astype(np.float32))).astype(np.float32)
        h3 = _split3(v)
        s3 = _split3(np.full(512, s32, np.float32))
        for r in range(3):
            qa[i, r] = h3[r]
            qa[i, 3 + r] = s3[r]
    c["qaug"] = np.ascontiguousarray(qa.transpose(1, 0, 2))
    bt = np.zeros((len(ALL_SLOPES), BIAS_NM), np.float32)
    for i, s in enumerate(ALL_SLOPES):
        for mi in range(BIAS_NM):
            bt[i, mi] = -np.float32(s) * np.float32(128 * (mi + BIAS_M0))
    c["bias_tab"] = np.broadcast_to(bt.reshape(1, -1), (128, bt.size)).copy()
    bc = np.zeros((8, NQT, 4), np.float32)
    for h, s in enumerate(ALL_SLOPES[2:]):
        for qt in range(NQT):
            for cc in range(4):
                bc[h, qt, cc] = -np.float32(s) * np.float32(512 * qt - 2048 * cc - 31)
    c["bias_cmp"] = np.broadcast_to(bc.reshape(1, -1), (128, bc.size)).copy()
    mcm = np.zeros((5, 128, 512), np.float32)
    for rel in range(5):
        mcm[rel] = np.where(512 * rel + t[None, :] >= 16 * j[:, None] + 31, 0.0, NEG)
    c["mask_cmp"] = np.ascontiguousarray(mcm.transpose(1, 0, 2)).astype(bf)
    mw = np.zeros((8, 128, 512), np.float32)
    for oi, o in enumerate(range(-4, 4)):
        dd = t[None, :] - j[:, None] - 128 * o
        mw[oi] = np.where((dd >= 0) & (dd <= 511), 0.0, NEG)
    c["mask_win"] = np.ascontiguousarray(mw.transpose(1, 0, 2)).astype(bf)
    cc = np.arange(SEQ)
    c["ewide"] = (cc[None, :] // 64 == j[:, None]).astype(np.float32).astype(bf)
    n = np.arange(512)
    s_ = np.arange(128)
    ov = ((n[:, None] >= 4 * s_[None, :] - 1) & (n[:, None] <= 4 * s_[None, :] + 3) & (n[:, None] < 511))
    c["ovl"] = np.ascontiguousarray(ov.astype(np.float32).reshape(4, 128, 128).transpose(1, 0, 2)).astype(bf)
    q = np.arange(128)
    u = np.arange(-127, 129)
    cur = (q >= 64).astype(np.int64)
    A = np.zeros((128, 256), np.float32)
    M = np.ones((128, 256), np.float32)
    fut = u[None, :] > cur[:, None]
    A[fut] = -1.0
    M[fut] = 0.0
    f1 = u[None, :] == cur[:, None]
    f2 = u[None, :] == cur[:, None] - 1
    A[f1] = 1.0e6 + 1.0
    M[f1] = 0.0
    A[f2] = 1.0e6 + 2.0
    M[f2] = 0.0
    c["topk_a"] = A
    c["topk_m"] = M
    gs = np.zeros((48, 48, 64), np.float32)
    for r in range(48):
        gs[r, r, :] = 1.0
    c["gsel"] = gs.reshape(48, 48 * 64).astype(bf)
    return c


CONST_SPECS = None


def _dt_of(a):
    return BF16 if a.dtype == bf else F32


def build_program(consts):
    nc = bass.Bass("TRN2", target_bir_lowering=False)
    dr = {}

    def din(name, shape, dt=F32):
        dr[name] = nc.dram_tensor(name, list(shape), dt, kind="ExternalInput").ap()
        return dr[name]

    def dscr(name, shape, dt):
        dr[name] = nc.dram_tensor(name, list(shape), dt, kind="Internal").ap()
        return dr[name]

    x_in = din("x", (SEQ, D))
    attn_norm = din("attn_norm", (2, D))
    mlp_norm = din("mlp_norm", (2, D))
    final_norm = din("final_norm", (D,))
    ev_w_in = din("ev_w_in", (D, 1536))
    lamv = din("lamv", (4, 64))
    ev_subln = din("ev_subln", (128,))
    ev_w_out = din("ev_w_out", (D, D))
    od_w_in = din("od_w_in", (D, 1304))
    pos_k = din("od_cmp_pos_k", (32, 64))
    cw1k = din("od_cmp_k_w1", (2048, 256))
    cw2k = din("od_cmp_k_w2", (256, 64))
    pos_v = din("od_cmp_pos_v", (32, 64))
    cw1v = din("od_cmp_v_w1", (2048, 256))
    cw2v = din("od_cmp_v_w2", (256, 64))
    od_w_out = din("od_w_out", (D, D))
    mlp_w1 = din("mlp_w1", (2, D, 4096))
    mlp_w2 = din("mlp_w2", (2, 4096, D))
    cd = {k: din("c_" + k, v.shape, _dt_of(v)) for k, v in consts.items()}
    out_d = nc.dram_tensor("out", [SEQ // 2, D], F32, kind="ExternalOutput").ap()
    xh_in = din("xh", (SEQ // 2, D))
    msel_in = din("msel", (128, 2))

    qk0 = dscr("qk0", (16, 64, SEQ), BF16)
    v0 = dscr("v0", (SEQ, 512), BF16)
    ot = dscr("ot", (D, SEQ), BF16)
    ot_p = dscr("ot_p", (512, SEQ), BF16)
    ot4 = dscr("ot4", (4, 256, SEQ), BF16)
    x2 = dscr("x2", (SEQ // 2, D), F32)
    x2g = dscr("x2g", (8, 1024, D), F32)
    q1 = dscr("q1", (8, 64, SEQ), BF16)
    kf1 = dscr("kf1", (8, 64, SEQ), BF16)
    v1 = dscr("v1", (SEQ, 256), BF16)
    gts = dscr("gts", (24, SEQ), BF16)
    kcmp = dscr("kcmp", (2, 64, 512), BF16)
    vcmp = dscr("vcmp", (2, 512, 64), BF16)
    x1s = dscr("x1s", (SEQ // 2, D), F32)
    uts_d = dscr("uts", (32, 128, SEQ // 2), BF16)

    with ExitStack() as st:
        S = Sched(nc, st)
        NW = 50 * 1024
        arena_t = st.enter_context(nc.sbuf_tensor("arena", [128, NW], F32))
        AR = Arena(S, arena_t, NW)
        pbanks = [st.enter_context(nc.psum_tensor(f"pb{i}", [128, 512], F32)) for i in range(8)]

        def PS(i, name, parts=128, cols=512, dt=F32):
            ap = pbanks[i][0:parts, :]
            if dt != F32:
                ap = ap.bitcast(dt)
            ap = ap[:, 0:cols]
            return T(ap, S.buf(name))

        def dma(q, out, in_, reads, writes, key, slow=False):
            if slow:
                S.op(q, lambda e: e.dma_start(out=out, in_=in_, allow_slow_non_contiguous=True), reads=reads, writes=writes, dma_key=key)
            else:
                S.op(q, lambda e: e.dma_start(out=out, in_=in_), reads=reads, writes=writes, dma_key=key)

        def load_vt(VT, src_cols, width):
            for g4 in range(4):
                dma("sp", VT.ap[:, g4 * 16:(g4 + 1) * 16, 0:width],
                    src_cols[g4 * 2048:(g4 + 1) * 2048, :].rearrange("(t p) c -> p t c", p=128), [], [VT.b], "LVT")

        def load(dst, src, q="sp"):
            dma(q, dst.ap, src, [], [dst.b], "L" + dst.b.name)

        def store(dst, src, ap=None, q="pool"):
            dma(q, dst, src.ap if ap is None else ap, [src.b], [], "S" + src.b.name)

        def mm(out, outap, lhsT, lhsap, rhs, rhsap, start, stop, extra_r=()):
            S.op("pe", lambda e: e.matmul(outap, lhsap, rhsap, start=start, stop=stop),
                 reads=[lhsT.b, rhs.b] + list(extra_r), writes=[out.b])

        class Ring:
            def __init__(self, items):
                self.items = items
                self.i = 0

            def next(self):
                it = self.items[self.i % len(self.items)]
                self.i += 1
                return it

        rr_cast = [0]

        def cast(dst, dstap, src, srcap, scale_ap=None, scale_t=None):
            e = ("dve", "act", "pool")[rr_cast[0] % 3] if scale_ap is None else ("dve", "act")[rr_cast[0] % 2]
            rr_cast[0] += 1
            rd = [src.b] + ([scale_t.b] if scale_t is not None else [])
            if e == "act":
                if scale_ap is None:
                    S.op("act", lambda en: en.copy(dstap, srcap), reads=rd, writes=[dst.b])
                else:
                    S.op("act", lambda en: en.activation(dstap, srcap, AF.Copy, scale=scale_ap), reads=rd, writes=[dst.b])
            elif e == "dve":
                if scale_ap is None:
                    S.op("dve", lambda en: en.tensor_copy(dstap, srcap), reads=rd, writes=[dst.b])
                else:
                    S.op("dve", lambda en: en.tensor_scalar(dstap, srcap, scale_ap, None, ALU.mult), reads=rd, writes=[dst.b])
            else:
                S.op("pool", lambda en: en.tensor_copy(dstap, srcap), reads=rd, writes=[dst.b])

        def load_weight(dst, src_ap, K, N, stage_ring, gain=None, col0=0, ncols=None, dcol0=0):
            ncols = N if ncols is None else ncols
            for k in range(K // 128):
                c = 0
                while c < ncols:
                    w = min(2048, ncols - c)
                    stg = stage_ring.next()
                    dma("sp", stg.ap[:, 0:w], src_ap[k * 128:(k + 1) * 128, col0 + c:col0 + c + w], [], [stg.b], "L" + stg.b.name)
                    cast(dst, dst.ap[:, k, dcol0 + c:dcol0 + c + w], stg, stg.ap[:, 0:w],
                         None if gain is None else gain.ap[:, k:k + 1], gain)
                    c += w

        NWP = 0
        ident = AR.alloc(128, (128,), BF16, "ident")
        load(ident, cd["ident"])
        ones = AR.alloc(128, (128,), BF16, "ones")
        load(ones, cd["ones"])
        negones = AR.alloc(128, (128,), BF16, "negones")
        load(negones, cd["negones"])
        uneg = AR.alloc(128, (128,), BF16, "uneg")
        load(uneg, cd["uneg"])
        onesdiv = AR.alloc(128, (128,), F32, "onesdiv")
        load(onesdiv, cd["onesdiv"])
        bias_tab = AR.alloc(128, (NSLOT * BIAS_NM,), F32, "bias_tab")
        load(bias_tab, cd["bias_tab"])
        gains = AR.alloc(128, (4, 8), F32, "gains")
        for li in range(2):
            dma("sp", gains.ap[:, 2 * li, :], attn_norm[li].rearrange("(k p) -> p k", p=128), [], [gains.b], "Lgains", slow=True)
            dma("sp", gains.ap[:, 2 * li + 1, :], mlp_norm[li].rearrange("(k p) -> p k", p=128), [], [gains.b], "Lgains", slow=True)
        eps_t = AR.alloc(128, (1,), F32, "eps_t")
        S.op("dve", lambda e: e.memset(eps_t.ap, 1e-6), writes=[eps_t.b])
        msel = AR.alloc(128, (2,), F32, "msel")
        load(msel, msel_in)
        persist_off = AR.off

        def bias_ap(si, m):
            col = si * BIAS_NM + (m - BIAS_M0)
            return bias_tab.ap[:, col:col + 1]

        def phase_reset():
            S.barrier()
            AR.off = persist_off

        def norm_transpose(xt, hn, ss, rs, junk, tp, hT, r):
            S.op("act", lambda e: e.activation(junk.ap, xt.ap, AF.Square, accum_out=ss.ap), reads=[xt.b], writes=[junk.b, ss.b])
            S.op("act", lambda e: e.activation(rs.ap, ss.ap, AF.Sqrt, bias=eps_t.ap, scale=1.0 / D), reads=[ss.b, eps_t.b], writes=[rs.b])
            S.op("dve", lambda e: e.reciprocal(rs.ap, rs.ap), reads=[rs.b], writes=[rs.b])
            S.op("act", lambda e: e.activation(hn.ap, xt.ap, AF.Copy, scale=rs.ap), reads=[xt.b, rs.b], writes=[hn.b])
            for k in range(8):
                S.op("pe", lambda e, k=k: e.transpose(tp.ap[:, k * 128:(k + 1) * 128], hn.ap[:, k * 128:(k + 1) * 128], ident.ap),
                     reads=[hn.b, ident.b], writes=[tp.b])
            S.op("dve", lambda e: e.tensor_copy(hT.ap[:, :, r * 128:(r + 1) * 128], tp.ap.rearrange("p (k c) -> p k c", k=8)),
                 reads=[tp.b], writes=[hT.b])

        def phase_proj(x_src, w_src, ncols_total, gain_idx, fm_chunks, tm_chunks):
            stage = Ring([AR.alloc(128, (2048,), F32, f"wstg{i}") for i in range(2)])
            W = AR.alloc(128, (8, ncols_total), BF16, "Win")
            gsl = T(gains.ap[:, gain_idx, :], gains.b)
            load_weight(W, w_src, D, ncols_total, stage, gain=gsl)
            xts = Ring([AR.alloc(128, (D,), F32, f"xt{i}") for i in range(2)])
            hns = Ring([AR.alloc(128, (D,), BF16, f"hn{i}") for i in range(2)])
            junk = AR.alloc(128, (D,), BF16, "junk")
            sss = Ring([AR.alloc(128, (1,), F32, f"ss{i}") for i in range(2)])
            rss = Ring([AR.alloc(128, (1,), F32, f"rs{i}") for i in range(2)])
            hTs = Ring([AR.alloc(128, (8, 512), BF16, f"hT{i}") for i in range(2)])
            tps = Ring([PS(i, f"tp{i}", cols=1024, dt=BF16) for i in (0, 1)])
            pps = Ring([PS(i, f"pp{i}") for i in (2, 3, 4, 5)])
            evs = Ring([AR.alloc(128, (512,), BF16, f"ev{i}") for i in range(4)])
            ev_i = [0]
            for tb in range(NQT):
                hT = hTs.next()
                for r in range(4):
                    xt = xts.next()
                    row0 = tb * 512 + r * 128
                    load(xt, x_src(row0))
                    norm_transpose(xt, hns.next(), sss.next(), rss.next(), junk, tps.next(), hT, r)
                for (col0, M, dsts, scale, func) in fm_chunks:
                    pp = pps.next()
                    for k in range(8):
                        mm(pp, pp.ap[0:M, :], W, W.ap[:, k, col0:col0 + M], hT, hT.ap[:, k, :], k == 0, k == 7)
                    ev = evs.next()
                    eng = ("act", "dve")[ev_i[0] % 2] if func is None else "act"
                    ev_i[0] += 1
                    if eng == "act":
                        S.op("act", lambda e, pp=pp, ev=ev, M=M, scale=scale, func=func: e.activation(
                            ev.ap[0:M, :], pp.ap[0:M, :], AF.Copy if func is None else func, scale=scale),
                            reads=[pp.b], writes=[ev.b])
                    else:
                        S.op("dve", lambda e, pp=pp, ev=ev, M=M, scale=scale: e.tensor_scalar(
                            ev.ap[0:M, :], pp.ap[0:M, :], float(scale), None, ALU.mult), reads=[pp.b], writes=[ev.b])
                    for (dfn, p0, npart) in dsts:
                        store(dfn(tb), ev, ap=ev.ap[p0:p0 + npart, :])
                for (col0, ncols, dfn) in tm_chunks:
                    for r in range(4):
                        pp = pps.next()
                        for k in range(8):
                            mm(pp, pp.ap[:, 0:ncols], hT, hT.ap[:, k, r * 128:(r + 1) * 128], W, W.ap[:, k, col0:col0 + ncols], k == 0, k == 7)
                        ev = evs.next()
                        eng = ("act", "dve")[ev_i[0] % 2]
                        ev_i[0] += 1
                        if eng == "act":
                            S.op("act", lambda e, pp=pp, ev=ev, n=ncols: e.copy(ev.ap[:, 0:n], pp.ap[:, 0:n]), reads=[pp.b], writes=[ev.b])
                        else:
                            S.op("dve", lambda e, pp=pp, ev=ev, n=ncols: e.tensor_copy(ev.ap[:, 0:n], pp.ap[:, 0:n]), reads=[pp.b], writes=[ev.b])
                        store(dfn(tb * 512 + r * 128), ev, ap=ev.ap[:, 0:ncols])

        fm = []
        for jc in range(8):
            isq = jc in (0, 1, 4, 5)
            dsts = [(lambda tb, slot=2 * jc + half: qk0[slot, :, tb * 512:(tb + 1) * 512], 64 * half, 64) for half in range(2)]
            fm.append((128 * jc, 128, dsts, 0.125 if isq else 1.0, None))
        tm = [(1024, 512, lambda row0: v0[row0:row0 + 128, 0:512])]
        phase_proj(lambda row0: x_in[row0:row0 + 128, :], ev_w_in, 1536, 0, fm, tm)
        phase_reset()

        class Pipe:
            def __init__(self):
                self.q = []
                self.t = 0

            def job(self, stages):
                for lag, fn in stages:
                    if lag == 0:
                        fn()
                    else:
                        self.q.append((self.t + lag, fn))
                rest = []
                for due, fn in self.q:
                    if due <= self.t:
                        fn()
                    else:
                        rest.append((due, fn))
                self.q = rest
                self.t += 1

            def flush(self):
                for due, fn in sorted(self.q, key=lambda x_: x_[0]):
                    fn()
                self.q = []

        def ktiles_for(qt, back):
            hi = 4 * qt + 3
            lo = 0 if back is None else max(0, 4 * qt - back)
            return list(range(hi, lo - 1, -1))

        def alibi_back(slope):
            dcut = ALIBI_CUT / slope
            bk = int(math.floor((dcut + 127.0) / 128.0))
            return bk

        ccbuf = S.buf("ccbuf")
        cc_n = [0]

        def all_gather_ot():
            S.barrier()
            for k in range(4):
                cc_n[0] += 1
                S.op("pool", lambda e, k=k: e.collective_compute("AllGather", ALU.bypass, replica_groups=[[0, 1], [2, 3], [4, 5], [6, 7]],
                                                                 ins=[ot_p[k * 128:(k + 1) * 128, :]], outs=[ot4[k]]),
                     writes=[ccbuf], dma_key="cc")
            S.bufs.append(ccbuf)

        def all_gather_x2():
            S.barrier()
            for k in range(8):
                cc_n[0] += 1
                S.op("pool", lambda e, k=k: e.collective_compute("AllGather", ALU.bypass, replica_groups=[[0, 1], [2, 3], [4, 5], [6, 7]],
                                                                 ins=[x2[k * 512:(k + 1) * 512, :]], outs=[x2g[k]]),
                     writes=[ccbuf], dma_key="cc")
            S.bufs.append(ccbuf)

        def attention_l0():
            mask_sb = AR.alloc(128, (4, 512), BF16, "mask_sb")
            load(mask_sb, cd["mask_sb"])
            mask_c = AR.alloc(128, (4, 512), BF16, "mask_c")
            load(mask_c, cd["mask_c"])
            KT = AR.alloc(70, (SEQ,), BF16, "KT")
            VT = AR.alloc(128, (NKT, 128), BF16, "VT")
            QTs = Ring([AR.alloc(70, (512,), BF16, f"QT{i}") for i in range(2)])
            Es = Ring([AR.alloc(128, (512,), F32, f"E{i}") for i in range(2)])
            SPs = Ring([AR.alloc(128, (512,), BF16, f"SP{i}") for i in range(4)])
            Ws = Ring([AR.alloc(128, (512,), BF16, f"Wt{i}") for i in range(4)])
            ssum = AR.alloc(128, (512,), F32, "ssum")
            sshs = Ring([AR.alloc(128, (512,), BF16, f"ssh{i}") for i in range(4)])
            oev = Ring([AR.alloc(128, (512,), BF16, f"oev{i}") for i in range(2)])
            Ys = Ring([PS(i, f"Y{i}") for i in (0, 1, 2)])
            Oaccs = [PS(3, "Oacc0"), PS(4, "Oacc1")]
            for h in range(SB_PER_CORE):
                pipe = Pipe()
                load(T(KT.ap[0:64, :], KT.b), qk0[4 + h])
                load_vt(VT, v0[:, h * 64:(h + 1) * 64], 64)
                for qt in range(NQT):
                    QT = QTs.next()
                    dma("sp", QT.ap[0:64, :], qk0[h, :, qt * 512:(qt + 1) * 512], [], [QT.b], "L" + QT.b.name)
                    kts = ktiles_for(qt, SB_BACK)
                    n = len(kts)
                    Oacc = Oaccs[qt % 2]
                    ssh_prev = None
                    for i, kt in enumerate(kts):
                        Y = Ys.next()
                        E = Es.next()
                        SP = SPs.next()
                        Wt = Ws.next()
                        ssh = sshs.next() if 0 < i < n - 1 else None
                        o = kt - 4 * qt

                        def st0(Y=Y, E=E, SP=SP, o=o, kt=kt, QT=QT, i=i, n=n, ssh=ssh):
                            mm(Y, Y.ap, KT, KT.ap[0:64, kt * 128:(kt + 1) * 128], QT, QT.ap[0:64, :], True, False)
                            if o >= 0:
                                mm(Y, Y.ap, ident, ident.ap, mask_sb, mask_sb.ap[:, o, :], False, False)
                            S.op("act", lambda e: e.activation(E.ap, Y.ap, AF.Exp), reads=[Y.b], writes=[E.b])
                            S.op("act", lambda e: e.activation(SP.ap, E.ap, AF.Ln, bias=1.0), reads=[E.b], writes=[SP.b])
                            if i < n - 1:
                                if i == 0:
                                    S.op("pool", lambda e: e.tensor_copy(ssum.ap, SP.ap), reads=[SP.b], writes=[ssum.b])
                                else:
                                    S.op("pool", lambda e: e.tensor_tensor(ssum.ap, ssum.ap, SP.ap, ALU.add), reads=[SP.b, ssum.b], writes=[ssum.b])
                                    S.op("pool", lambda e: e.tensor_copy(ssh.ap, ssum.ap), reads=[ssum.b], writes=[ssh.b])

                        def st1(Y=Y, SP=SP, Wt=Wt, prev=ssh_prev):
                            mm(Y, Y.ap, uneg, uneg.ap, SP, SP.ap, False, prev is None)
                            if prev is not None:
                                mm(Y, Y.ap, negones, negones.ap, prev, prev.ap, False, True)
                            S.op("act", lambda e: e.activation(Wt.ap, Y.ap, AF.Exp), reads=[Y.b], writes=[Wt.b])

                        def st2(Wt=Wt, kt=kt, i=i, n=n, Oacc=Oacc, h=h, qt=qt):
                            mm(Oacc, Oacc.ap[0:64, :], VT, VT.ap[:, kt, 0:64], Wt, Wt.ap, i == 0, i == n - 1)
                            if i == n - 1:
                                ev = oev.next()
                                S.op("dve", lambda e: e.tensor_copy(ev.ap[0:64, :], Oacc.ap[0:64, :]), reads=[Oacc.b], writes=[ev.b])
                                store(ot_p[h * 64:(h + 1) * 64, qt * 512:(qt + 1) * 512], ev, ap=ev.ap[0:64, :])

                        pipe.job([(0, st0), (1, st1), (2, st2)])
                        if i < n - 1:
                            ssh_prev = SP if i == 0 else ssh
                pipe.flush()
            lam_t = AR.alloc(128, (4, 64), F32, "lam_t")
            for i in range(4):
                dma("sp", lam_t.ap[:, i, :], lamv[i].partition_broadcast(128), [], [lam_t.b], "Llam")
            lam_p = AR.alloc(128, (2, 64), F32, "lam_p")
            lam_s = AR.alloc(128, (2,), F32, "lam_s")
            neglam = AR.alloc(128, (1,), F32, "neglam")
            S.op("dve", lambda e: e.tensor_tensor(lam_p.ap[:, 0, :], lam_t.ap[:, 0, :], lam_t.ap[:, 1, :], ALU.mult), reads=[lam_t.b], writes=[lam_p.b])
            S.op("dve", lambda e: e.tensor_tensor(lam_p.ap[:, 1, :], lam_t.ap[:, 2, :], lam_t.ap[:, 3, :], ALU.mult), reads=[lam_t.b, lam_p.b], writes=[lam_p.b])
            S.op("dve", lambda e: e.reduce_sum(lam_s.ap, lam_p.ap, axis=mybir.AxisListType.X), reads=[lam_p.b], writes=[lam_s.b])
            S.op("act", lambda e: e.activation(lam_s.ap, lam_s.ap, AF.Exp), reads=[lam_s.b], writes=[lam_s.b])
            lam_init = 0.8 - 0.6 * math.exp(-0.3 * 0)
            S.op("dve", lambda e: e.tensor_tensor(neglam.ap, lam_s.ap[:, 1:2], lam_s.ap[:, 0:1], ALU.subtract), reads=[lam_s.b], writes=[neglam.b])
            S.op("dve", lambda e: e.tensor_scalar(neglam.ap, neglam.ap, -lam_init, None, ALU.add), reads=[neglam.b], writes=[neglam.b])
            sg = AR.alloc(128, (1,), F32, "sg")
            dma("sp", sg.ap, ev_subln.rearrange("(p o) -> p o", o=1), [], [sg.b], "Lsg")
            S.op("dve", lambda e: e.tensor_scalar(sg.ap, sg.ap, 1.0 - lam_init, None, ALU.mult), reads=[sg.b], writes=[sg.b])
            KT2 = [KT, AR.alloc(70, (SEQ,), BF16, "KTb")]
            for kk in KT2:
                dma("sp", kk.ap[64:70, :], cd["kaug"], [], [kk.b], "L" + kk.b.name)
            QT2 = [[AR.alloc(70, (512,), BF16, f"QD{c}{i}") for i in range(2)] for c in range(2)]
            Ps = Ring([AR.alloc(128, (512,), BF16, f"P{i}") for i in range(4)])
            NUM = [PS(3, "NUM0"), PS(4, "NUM1")]
            DEN = [PS(5, "DEN0"), PS(6, "DEN1")]
            MS = PS(7, "MS")
            r_t = [AR.alloc(128, (512,), F32, f"rden{c}") for c in range(2)]
            a_t = [AR.alloc(128, (512,), F32, f"a{c}") for c in range(2)]
            o_t = AR.alloc(128, (512,), F32, "o_t")
            sq_t = AR.alloc(128, (512,), F32, "sq_t")
            for h in range(2):
                pipe = Pipe()
                back = alibi_back(DIFF_SLOPES[1]) if h == 0 else None
                for c in range(2):
                    dma("sp", KT2[c].ap[0:64, :], qk0[12 + 2 * h + c], [], [KT2[c].b], "L" + KT2[c].b.name)
                load_vt(VT, v0[:, 256 + h * 128:256 + (h + 1) * 128], 128)
                for qt in range(NQT):
                    for c in range(2):
                        QT = QT2[c][qt % 2]
                        dma("sp", QT.ap[0:64, :], qk0[8 + 2 * h + c, :, qt * 512:(qt + 1) * 512], [], [QT.b], "L" + QT.b.name)
                        dma("sp", QT.ap[64:70, :], cd["qaug"][:, h, :], [], [QT.b], "L" + QT.b.name)
                        kts = ktiles_for(qt, back)
                        n = len(kts)
                        for i, kt in enumerate(kts):
                            Y = Ys.next()
                            P = Ps.next()
                            o = kt - 4 * qt

                            def st0(Y=Y, P=P, o=o, kt=kt, QT=QT, c=c, qt=qt, h=h):
                                mm(Y, Y.ap, KT2[c], KT2[c].ap[:, kt * 128:(kt + 1) * 128], QT, QT.ap, True, o < 0)
                                if o >= 0:
                                    mm(Y, Y.ap, ident, ident.ap, mask_c, mask_c.ap[:, o, :], False, True)
                                S.op("act", lambda e: e.activation(P.ap, Y.ap, AF.Exp, bias=bias_ap(h, 4 * qt - kt)),
                                     reads=[Y.b, bias_tab.b], writes=[P.b])

                            def st2(P=P, kt=kt, i=i, n=n, c=c, h=h, qt=qt):
                                mm(NUM[c], NUM[c].ap, VT, VT.ap[:, kt, :], P, P.ap, i == 0, i == n - 1)
                                mm(DEN[c], DEN[c].ap, ones, ones.ap, P, P.ap, i == 0, i == n - 1)
                                if i == n - 1:
                                    S.op("dve", lambda e: e.reciprocal(r_t[c].ap, DEN[c].ap), reads=[DEN[c].b], writes=[r_t[c].b])
                                    S.op("dve", lambda e: e.tensor_tensor(a_t[c].ap, NUM[c].ap, r_t[c].ap, ALU.mult), reads=[NUM[c].b, r_t[c].b], writes=[a_t[c].b])
                                    if c == 1:
                                        S.op("dve", lambda e: e.scalar_tensor_tensor(o_t.ap, a_t[1].ap, neglam.ap, a_t[0].ap, ALU.mult, ALU.add),
                                             reads=[a_t[0].b, a_t[1].b, neglam.b], writes=[o_t.b])
                                        S.op("act", lambda e: e.activation(sq_t.ap, o_t.ap, AF.Square), reads=[o_t.b], writes=[sq_t.b])
                                        mm(MS, MS.ap, onesdiv, onesdiv.ap, sq_t, sq_t.ap, True, True)
                                        S.op("act", lambda e: e.activation(sq_t.ap, MS.ap, AF.Sqrt, bias=eps_t.ap), reads=[MS.b, eps_t.b], writes=[sq_t.b])
                                        S.op("dve", lambda e: e.reciprocal(sq_t.ap, sq_t.ap), reads=[sq_t.b], writes=[sq_t.b])
                                        ev = oev.next()
                                        S.op("dve", lambda e: e.scalar_tensor_tensor(ev.ap, o_t.ap, sg.ap, sq_t.ap, ALU.mult, ALU.mult),
                                             reads=[o_t.b, sg.b, sq_t.b], writes=[ev.b])
                                        store(ot_p[256 + h * 128:256 + (h + 1) * 128, qt * 512:(qt + 1) * 512], ev)

                            pipe.job([(0, st0), (2, st2)])
                pipe.flush()

        attention_l0()
        all_gather_ot()
        phase_reset()

        def phase_mlp(x_src, wout_src, w1_src, w2_src, gain_idx, dst, final_gain=None):
            stage = Ring([AR.alloc(128, (2048,), F32, f"wstg{i}") for i in range(2)])
            Wo = AR.alloc(128, (8, D), BF16, "Wo")
            W1 = AR.alloc(128, (8, 4096), BF16, "W1")
            gsl = T(gains.ap[:, gain_idx, :], gains.b)
            load_weight(Wo, wout_src, D, D, stage)
            load_weight(W1, w1_src, D, 4096, stage, gain=gsl)
            OTs = Ring([AR.alloc(128, (8, 512), BF16, f"OT{i}") for i in range(2)])
            OAs = Ring([AR.alloc(128, (8, 512), BF16, f"OA{i}") for i in range(1)])
            OBs = Ring([AR.alloc(128, (8, 512), BF16, f"OB{i}") for i in range(1)])
            x1r = Ring([AR.alloc(128, (D,), F32, f"x1r{i}") for i in range(3)])
            xts = Ring([AR.alloc(128, (D,), F32, f"xt{i}") for i in range(2)])
            hns = Ring([AR.alloc(128, (D,), BF16, f"hn{i}") for i in range(2)])
            junk = AR.alloc(128, (D,), BF16, "junk")
            sss = Ring([AR.alloc(128, (1,), F32, f"ss{i}") for i in range(2)])
            rss = Ring([AR.alloc(128, (1,), F32, f"rs{i}") for i in range(2)])
            hTs = Ring([AR.alloc(128, (8, 512), BF16, f"hT{i}") for i in range(2)])
            uts = Ring([AR.alloc(128, (512,), BF16, f"ut{i}") for i in range(4)])
            sqs = Ring([AR.alloc(128, (512,), BF16, f"sq{i}") for i in range(2)])
            tps = Ring([PS(i, f"tp{i}", cols=1024, dt=BF16) for i in (0, 1)])
            pps = Ring([PS(i, f"pp{i}") for i in (2, 3, 4, 5, 6, 7)])
            for tb in range(NQT // 2):
                OT = OTs.next()
                OA = OAs.next()
                OB = OBs.next()
                for kc in range(8):
                    dma("sp", OA.ap[:, kc, :], ot4[kc % 4, (kc // 4) * 128:(kc // 4 + 1) * 128, tb * 512:(tb + 1) * 512], [], [OA.b], "L" + OA.b.name)
                    dma("sp", OB.ap[:, kc, :], ot4[kc % 4, (kc // 4) * 128:(kc // 4 + 1) * 128, SEQ // 2 + tb * 512:SEQ // 2 + (tb + 1) * 512], [], [OB.b], "L" + OB.b.name)
                S.op("dve", lambda e, OA=OA: e.tensor_scalar(OA.ap, OA.ap, msel.ap[:, 0:1], None, ALU.mult), reads=[OA.b, msel.b], writes=[OA.b])
                S.op("dve", lambda e, OA=OA, OB=OB, OT=OT: e.scalar_tensor_tensor(OT.ap, OB.ap, msel.ap[:, 1:2], OA.ap, ALU.mult, ALU.add),
                     reads=[OA.b, OB.b, msel.b], writes=[OT.b])
                hT = hTs.next()
                for r in range(4):
                    row0 = tb * 512 + r * 128
                    xt = xts.next()
                    load(xt, x_src[row0:row0 + 128, :])
                    x1v = x1r.next()
                    for half in range(2):
                        pp = pps.next()
                        for k in range(8):
                            mm(pp, pp.ap, OT, OT.ap[:, k, r * 128:(r + 1) * 128], Wo, Wo.ap[:, k, half * 512:(half + 1) * 512], k == 0, k == 7)
                        S.op("dve", lambda e, pp=pp, xt=xt, x1v=x1v, half=half: e.tensor_tensor(
                            x1v.ap[:, half * 512:(half + 1) * 512], pp.ap, xt.ap[:, half * 512:(half + 1) * 512], ALU.add),
                            reads=[pp.b, xt.b], writes=[x1v.b])
                    store(x1s[row0:row0 + 128, :], x1v)
                    norm_transpose(x1v, hns.next(), sss.next(), rss.next(), junk, tps.next(), hT, r)
                for fc in range(32):
                    pp = pps.next()
                    for k in range(8):
                        mm(pp, pp.ap, W1, W1.ap[:, k, fc * 128:(fc + 1) * 128], hT, hT.ap[:, k, :], k == 0, k == 7)
                    sq = sqs.next()
                    u = uts.next()
                    S.op("act", lambda e, pp=pp, sq=sq: e.activation(sq.ap, pp.ap, AF.Square), reads=[pp.b], writes=[sq.b])
                    S.op("dve", lambda e, pp=pp, sq=sq, u=u: e.scalar_tensor_tensor(u.ap, pp.ap, 0.0, sq.ap, ALU.is_gt, ALU.mult),
                         reads=[pp.b, sq.b], writes=[u.b])
                    store(uts_d[fc, :, tb * 512:(tb + 1) * 512], u)
            phase_reset()
            stage = Ring([AR.alloc(128, (2048,), F32, f"wstg{i}") for i in range(2)])
            W2 = AR.alloc(128, (32, D), BF16, "W2")
            load_weight(W2, w2_src, 4096, D, stage)
            fg = None
            if final_gain is not None:
                fg = AR.alloc(128, (D,), F32, "fg")
                load(fg, final_gain.partition_broadcast(128))
            UTs = Ring([AR.alloc(128, (32, 512), BF16, f"UT{i}") for i in range(2)])
            x1r = Ring([AR.alloc(128, (D,), F32, f"x1r{i}") for i in range(3)])
            ys = Ring([AR.alloc(128, (D,), F32, f"y{i}") for i in range(3)])
            junk = AR.alloc(128, (D,), BF16, "junk")
            sss = Ring([AR.alloc(128, (1,), F32, f"ss{i}") for i in range(2)])
            rss = Ring([AR.alloc(128, (1,), F32, f"rs{i}") for i in range(2)])
            pps = Ring([PS(i, f"pp{i}") for i in (0, 1, 2, 3, 4, 5)])
            for tb in range(NQT // 2):
                UT = UTs.next()
                for f4 in range(4):
                    dma("sp", UT.ap[:, f4 * 8:(f4 + 1) * 8, :], uts_d[f4 * 8:(f4 + 1) * 8, :, tb * 512:(tb + 1) * 512].rearrange("f p t -> p f t"),
                        [], [UT.b], "L" + UT.b.name)
                for r in range(4):
                    row0 = tb * 512 + r * 128
                    x1v = x1r.next()
                    load(x1v, x1s[row0:row0 + 128, :])
                    y = ys.next()
                    for half in range(2):
                        pp = pps.next()
                        for fc in range(32):
                            mm(pp, pp.ap, UT, UT.ap[:, fc, r * 128:(r + 1) * 128], W2, W2.ap[:, fc, half * 512:(half + 1) * 512], fc == 0, fc == 31)
                        S.op("dve", lambda e, pp=pp, y=y, x1v=x1v, half=half: e.tensor_tensor(
                            y.ap[:, half * 512:(half + 1) * 512], pp.ap, x1v.ap[:, half * 512:(half + 1) * 512], ALU.add),
                            reads=[pp.b, x1v.b], writes=[y.b])
                    if fg is not None:
                        ss = sss.next()
                        rs = rss.next()
                        S.op("act", lambda e, y=y, ss=ss: e.activation(junk.ap, y.ap, AF.Square, accum_out=ss.ap), reads=[y.b], writes=[junk.b, ss.b])
                        S.op("act", lambda e, ss=ss, rs=rs: e.activation(rs.ap, ss.ap, AF.Sqrt, bias=eps_t.ap, scale=1.0 / D), reads=[ss.b, eps_t.b], writes=[rs.b])
                        S.op("dve", lambda e, rs=rs: e.reciprocal(rs.ap, rs.ap), reads=[rs.b], writes=[rs.b])
                        S.op("dve", lambda e, y=y, rs=rs: e.scalar_tensor_tensor(y.ap, y.ap, rs.ap, fg.ap, ALU.mult, ALU.mult),
                             reads=[y.b, rs.b, fg.b], writes=[y.b])
                    store(dst[row0:row0 + 128, :], y)

        phase_mlp(xh_in, ev_w_out, mlp_w1[0], mlp_w2[0], 1, x2)
        all_gather_x2()
        if False:
            S.barrier()
            cp = Ring([AR.alloc(128, (D,), F32, f"cp{i}") for i in range(2)])
            for i in range(NKT):
                c_ = cp.next()
                load(c_, x2[i * 128:(i + 1) * 128, :])
                store(out_d[i * 128:(i + 1) * 128, :], c_)
            S.barrier()
            S.finalize()
            return nc, S
        phase_reset()

        fm = []
        for jc in range(4):
            dsts = [(lambda tb, slot=2 * jc + half: q1[slot, :, tb * 512:(tb + 1) * 512], 64 * half, 64) for half in range(2)]
            fm.append((128 * jc, 128, dsts, 0.125, None))
        for ji in range(4):
            dsts = [(lambda tb, slot=2 * ji + half: kf1[slot, :, tb * 512:(tb + 1) * 512], 64 * half, 64) for half in range(2)]
            fm.append((512 + 128 * ji, 128, dsts, 1.0, None))
        fm.append((1024, 24, [(lambda tb: gts[:, tb * 512:(tb + 1) * 512], 0, 24)], 1.0, AF.Sigmoid))
        tm = [(1048, 256, lambda row0: v1[row0:row0 + 128, 0:256])]
        phase_proj(lambda row0: x2g[(row0 % 4096) // 512, (row0 // 4096) * 512 + row0 % 512:(row0 // 4096) * 512 + row0 % 512 + 128, :], od_w_in, 1304, 2, fm, tm)
        phase_reset()

        def phase_compress():
            stg = AR.alloc(64, (32 * 256,), F32, "cstg")
            W1c = AR.alloc(64, (32, 256), BF16, "W1c")
            stg2 = AR.alloc(128, (2, 64), F32, "cstg2")
            W2c = AR.alloc(128, (2, 64), BF16, "W2c")
            posf = AR.alloc(64, (32,), F32, "posf")
            posT = AR.alloc(64, (32,), BF16, "posT")
            biash = AR.alloc(128, (2,), F32, "biash")
            src = AR.alloc(64, (SEQ,), BF16, "csrc")
            hid = AR.alloc(128, (2, 512), BF16, "hid")
            u_t = AR.alloc(128, (512,), F32, "u_t")
            w_t = AR.alloc(128, (512,), F32, "w_t")
            evk = AR.alloc(64, (512,), BF16, "evk")
            evv = Ring([AR.alloc(128, (64,), BF16, f"evv{i}") for i in range(2)])
            pb_ = PS(0, "cbias")
            ph = Ring([PS(1, "ph0"), PS(2, "ph1")])
            po = Ring([PS(3, "po0"), PS(4, "po1")])
            S.op("dve", lambda e: e.memset(hid.ap, 0.0), writes=[hid.b])
            S.op("dve", lambda e: e.memset(evk.ap, 0.0), writes=[evk.b])
            for kind, (w1d, w2d, posd) in enumerate(((cw1k, cw2k, pos_k), (cw1v, cw2v, pos_v))):
                dma("sp", stg.ap.rearrange("p (l h) -> p l h", l=32), w1d.rearrange("(l d) h -> d l h", d=64), [], [stg.b], "Lcstg")
                for q4 in range(4):
                    cast(W1c, W1c.ap[:, q4 * 8:(q4 + 1) * 8, :], stg, stg.ap.rearrange("p (l h) -> p l h", l=32)[:, q4 * 8:(q4 + 1) * 8, :])
                dma("sp", stg2.ap, w2d.rearrange("(c p) n -> p c n", p=128), [], [stg2.b], "Lcstg2")
                cast(W2c, W2c.ap, stg2, stg2.ap)
                dma("sp", posf.ap, posd.rearrange("l d -> d l"), [], [posf.b], "Lposf", slow=True)
                cast(posT, posT.ap, posf, posf.ap)
                for hc in range(2):
                    for l in range(32):
                        mm(pb_, pb_.ap[:, 0:1], W1c, W1c.ap[:, l, hc * 128:(hc + 1) * 128], posT, posT.ap[:, l:l + 1], l == 0, l == 31)
                    S.op("dve", lambda e, hc=hc: e.tensor_copy(biash.ap[:, hc:hc + 1], pb_.ap[:, 0:1]), reads=[pb_.b], writes=[biash.b])
                for g in range(2):
                    load(src, kf1[2 * kind + g])
                    for hc in range(2):
                        p_ = ph.next()
                        for l in range(32):
                            mm(p_, p_.ap[:, 0:511], W1c, W1c.ap[:, l, hc * 128:(hc + 1) * 128], src, src.ap[:, l:l + 16 * 510 + 1:16], l == 0, l == 31)
                        S.op("act", lambda e, p_=p_, hc=hc: e.activation(u_t.ap[:, 0:511], p_.ap[:, 0:511], AF.Identity, bias=biash.ap[:, hc:hc + 1]),
                             reads=[p_.b, biash.b], writes=[u_t.b])
                        S.op("act", lambda e: e.activation(w_t.ap[:, 0:511], u_t.ap[:, 0:511], AF.Square), reads=[u_t.b], writes=[w_t.b])
                        S.op("dve", lambda e: e.tensor_scalar(w_t.ap[:, 0:511], w_t.ap[:, 0:511], 0.044715, 1.0, ALU.mult, ALU.add), reads=[w_t.b], writes=[w_t.b])
                        S.op("dve", lambda e: e.tensor_tensor(w_t.ap[:, 0:511], w_t.ap[:, 0:511], u_t.ap[:, 0:511], ALU.mult), reads=[w_t.b, u_t.b], writes=[w_t.b])
                        S.op("act", lambda e: e.activation(w_t.ap[:, 0:511], w_t.ap[:, 0:511], AF.Sigmoid, scale=2.0 * 0.7978845608028654), reads=[w_t.b], writes=[w_t.b])
                        S.op("dve", lambda e, hc=hc: e.tensor_tensor(hid.ap[:, hc, 0:511], w_t.ap[:, 0:511], u_t.ap[:, 0:511], ALU.mult), reads=[w_t.b, u_t.b], writes=[hid.b])
                    if kind == 0:
                        p2 = po.next()
                        for hc in range(2):
                            mm(p2, p2.ap[0:64, 0:511], W2c, W2c.ap[:, hc, :], hid, hid.ap[:, hc, 0:511], hc == 0, hc == 1)
                        S.op("dve", lambda e, p2=p2: e.tensor_copy(evk.ap[:, 0:511], p2.ap[0:64, 0:511]), reads=[p2.b], writes=[evk.b])
                        store(kcmp[g], evk)
                    else:
                        for nchunk in range(4):
                            p2 = po.next()
                            for hc in range(2):
                                mm(p2, p2.ap[:, 0:64], hid, hid.ap[:, hc, nchunk * 128:(nchunk + 1) * 128], W2c, W2c.ap[:, hc, :], hc == 0, hc == 1)
                            ev = evv.next()
                            S.op("dve", lambda e, p2=p2, ev=ev: e.tensor_copy(ev.ap, p2.ap[:, 0:64]), reads=[p2.b], writes=[ev.b])
                            store(vcmp[g, nchunk * 128:(nchunk + 1) * 128, :], ev)

        phase_compress()
        phase_reset()

        def attention_l1():
            mask_c = AR.alloc(128, (4, 512), BF16, "mask_c")
            load(mask_c, cd["mask_c"])
            mask_cmp = AR.alloc(128, (5, 512), BF16, "mask_cmp")
            load(mask_cmp, cd["mask_cmp"])
            mask_win = AR.alloc(128, (8, 512), BF16, "mask_win")
            load(mask_win, cd["mask_win"])
            ewide = AR.alloc(128, (SEQ,), BF16, "ewide")
            load(ewide, cd["ewide"])
            ovl = AR.alloc(128, (4, 128), BF16, "ovl")
            load(ovl, cd["ovl"])
            tka = AR.alloc(128, (256,), F32, "tka")
            load(tka, cd["topk_a"])
            tkm = AR.alloc(128, (256,), F32, "tkm")
            load(tkm, cd["topk_m"])
            bcmp = AR.alloc(128, (8 * NQT * 4,), F32, "bcmp")
            load(bcmp, cd["bias_cmp"])
            KcA = AR.alloc(70, (512,), BF16, "KcA")
            dma("sp", KcA.ap[64:70, :], cd["kaug_cmp"], [], [KcA.b], "LKcA")
            Vc = AR.alloc(128, (4, 64), BF16, "Vc")
            KsA = AR.alloc(70, (SEQ,), BF16, "KsA")
            KwA = AR.alloc(70, (SEQ,), BF16, "KwA")
            dma("sp", KsA.ap[64:70, :], cd["kaug"], [], [KsA.b], "LKsA")
            dma("sp", KwA.ap[64:70, :], cd["kaug"], [], [KwA.b], "LKwA")
            Vs = AR.alloc(128, (NKT, 128), BF16, "Vs")
            Vw = AR.alloc(128, (NKT, 128), BF16, "Vw")
            S.op("dve", lambda e: e.memset(Vs.ap[:, :, 64:128], 1.0), writes=[Vs.b])
            S.op("dve", lambda e: e.memset(Vw.ap[:, :, 64:128], 1.0), writes=[Vw.b])
            QAs = [[AR.alloc(70, (512,), BF16, f"QA{r}{i}") for i in range(2)] for r in range(4)]
            GBs = [[AR.alloc(64, (3, 512), BF16, f"GB{r}{i}") for i in range(2)] for r in range(4)]
            Pc = Ring([AR.alloc(128, (512,), BF16, f"Pc{i}") for i in range(10)])
            Ps = Ring([AR.alloc(128, (512,), BF16, f"Pp{i}") for i in range(4)])
            pcn = Ring([AR.alloc(128, (512,), BF16, f"pcn{i}") for i in range(4)])
            rdc = AR.alloc(128, (512,), F32, "rdc")
            ocmp = [AR.alloc(64, (512,), F32, f"ocmp{r}") for r in range(4)]
            selbT = AR.alloc(128, (512,), BF16, "selbT")
            imp2 = AR.alloc(128, (128,), F32, "imp2")
            tmp2 = AR.alloc(128, (128,), F32, "tmp2")
            selm = AR.alloc(128, (128,), F32, "selm")
            selb = AR.alloc(128, (128,), BF16, "selb")
            v8a = AR.alloc(128, (8,), F32, "v8a")
            v8b = AR.alloc(128, (8,), F32, "v8b")
            rs2 = [AR.alloc(128, (512,), F32, f"rs2{i}") for i in range(2)]
            ob2 = [AR.alloc(64, (512,), F32, f"ob2{i}") for i in range(2)]
            acc = AR.alloc(64, (512,), F32, "acc")
            t1 = AR.alloc(64, (512,), F32, "t1")
            oev = Ring([AR.alloc(64, (512,), BF16, f"oev{i}") for i in range(2)])
            Ys = Ring([PS(0, "Y0"), PS(1, "Y1"), PS(2, "Y2")])
            NUMs = [PS(3, "NUMa"), PS(5, "NUMb")]
            DENs = [PS(4, "DENa"), PS(6, "DENb")]
            IMP = PS(7, "IMP")
            TRP = T(pbanks[6][:, :].bitcast(BF16)[:, 0:512], DENs[1].b)

            def tile_job(pipe, K, kap, QAr, extra, bias, V, vap, NUM, DEN, den_parts, P, first, last, tail):
                Y = Ys.next()

                def st0():
                    nx = len(extra)
                    mm(Y, Y.ap, K, kap, QAr, QAr.ap, True, nx == 0)
                    for xi, (lt, lap, rt, rap) in enumerate(extra):
                        mm(Y, Y.ap, lt, lap, rt, rap, False, xi == nx - 1)
                    S.op("act", lambda e: e.activation(P.ap, Y.ap, AF.Exp, bias=bias[0]), reads=[Y.b, bias[1]], writes=[P.b])

                def st2():
                    if DEN is None:
                        mm(NUM, NUM.ap, V, vap, P, P.ap, first, last)
                    else:
                        mm(NUM, NUM.ap[0:64, :], V, vap, P, P.ap, first, last)
                        mm(DEN, DEN.ap[0:den_parts, :], ones, ones.ap[:, 0:den_parts], P, P.ap, first, last)
                    if last and tail is not None:
                        tail()

                pipe.job([(0, st0), (2, st2)])

            for g in range(2):
                dma("sp", KcA.ap[0:64, :], kcmp[g], [], [KcA.b], "LKcA")
                dma("sp", Vc.ap, vcmp[g].rearrange("(c p) d -> p c d", p=128), [], [Vc.b], "LVc")
                dma("sp", KsA.ap[0:64, :], kf1[4 + g], [], [KsA.b], "LKsA")
                dma("sp", KwA.ap[0:64, :], kf1[6 + g], [], [KwA.b], "LKwA")
                for g4 in range(4):
                    dma("sp", Vs.ap[:, g4 * 16:(g4 + 1) * 16, 0:64], v1[g4 * 2048:(g4 + 1) * 2048, g * 64:(g + 1) * 64].rearrange("(t p) c -> p t c", p=128), [], [Vs.b], "LVs")
                    dma("sp", Vw.ap[:, g4 * 16:(g4 + 1) * 16, 0:64], v1[g4 * 2048:(g4 + 1) * 2048, 128 + g * 64:128 + (g + 1) * 64].rearrange("(t p) c -> p t c", p=128), [], [Vw.b], "LVw")
                for qt in range(NQT):
                    pipe = Pipe()
                    QA = [QAs[r][qt % 2] for r in range(4)]
                    GB = [GBs[r][qt % 2] for r in range(4)]
                    for r in range(4):
                        h = 4 * g + r
                        dma("sp", QA[r].ap[0:64, :], q1[h, :, qt * 512:(qt + 1) * 512], [], [QA[r].b], "L" + QA[r].b.name)
                        dma("sp", QA[r].ap[64:70, :], cd["qaug"][:, 2 + h, :], [], [QA[r].b], "L" + QA[r].b.name)
                        for c3 in range(3):
                            dma("sp", GB[r].ap[:, c3, :], gts[3 * h + c3, qt * 512:(qt + 1) * 512].partition_broadcast(64), [], [GB[r].b], "L" + GB[r].b.name)
                    chunks = [c for c in range(4) if 4 * c <= qt]
                    for r in range(4):
                        h = 4 * g + r
                        NUM = NUMs[r % 2]
                        DEN = DENs[r % 2]
                        Pl = [(c, Pc.next()) for c in chunks]

                        def cmp_tail(r=r, NUM=NUM, DEN=DEN, Pl=Pl):
                            S.op("dve", lambda e: e.tensor_scalar(rdc.ap, DEN.ap, 1e-30, None, ALU.add), reads=[DEN.b], writes=[rdc.b])
                            S.op("dve", lambda e: e.reciprocal(rdc.ap, rdc.ap), reads=[rdc.b], writes=[rdc.b])
                            S.op("dve", lambda e: e.tensor_tensor(ocmp[r].ap, NUM.ap[0:64, :], rdc.ap[0:64, :], ALU.mult), reads=[NUM.b, rdc.b], writes=[ocmp[r].b])
                            for ci, (c, P) in enumerate(Pl):
                                pn = pcn.next()
                                S.op("dve", lambda e, pn=pn, P=P: e.tensor_tensor(pn.ap, P.ap, rdc.ap, ALU.mult), reads=[P.b, rdc.b], writes=[pn.b])
                                for qs in range(4):
                                    mm(IMP, IMP.ap[:, qs * 128:(qs + 1) * 128], pn, pn.ap[:, qs * 128:(qs + 1) * 128], ovl, ovl.ap[:, c, :],
                                       r == 0 and ci == 0, r == 3 and ci == len(Pl) - 1)

                        for ci, (c, P) in enumerate(Pl):
                            rel = qt - 4 * c
                            extra = [(ident, ident.ap, mask_cmp, mask_cmp.ap[:, rel, :])] if rel <= 4 else []
                            col = (h * NQT + qt) * 4 + c
                            tile_job(pipe, KcA, KcA.ap[:, c * 128:(c + 1) * 128], QA[r], extra, (bcmp.ap[:, col:col + 1], bcmp.b),
                                     Vc, Vc.ap[:, c, :], NUM, DEN, 128, P, ci == 0, ci == len(Pl) - 1, cmp_tail)
                    pipe.flush()
                    for qs in range(4):
                        off = 127 - 2 * (4 * qt + qs)
                        S.op("dve", lambda e, qs=qs, off=off: e.tensor_tensor(tmp2.ap, IMP.ap[:, qs * 128:(qs + 1) * 128], tkm.ap[:, off:off + 128], ALU.mult),
                             reads=[IMP.b, tkm.b], writes=[tmp2.b])
                        S.op("dve", lambda e, off=off: e.tensor_tensor(imp2.ap, tmp2.ap, tka.ap[:, off:off + 128], ALU.add), reads=[tmp2.b, tka.b], writes=[imp2.b])
                        S.op("dve", lambda e: e.memset(imp2.ap[:, 0:1], 1.0e6), reads=[], writes=[imp2.b])
                        S.op("dve", lambda e: e.max(v8a.ap, imp2.ap), reads=[imp2.b], writes=[v8a.b])
                        S.op("dve", lambda e: e.match_replace(tmp2.ap, v8a.ap, imp2.ap, -9.0), reads=[imp2.b, v8a.b], writes=[tmp2.b])
                        S.op("dve", lambda e: e.max(v8b.ap, tmp2.ap), reads=[tmp2.b], writes=[v8b.b])
                        S.op("dve", lambda e: e.tensor_scalar(selm.ap, imp2.ap, v8b.ap[:, 7:8], 0.0, ALU.is_ge, ALU.add), reads=[imp2.b, v8b.b], writes=[selm.b])
                        S.op("dve", lambda e: e.scalar_tensor_tensor(selm.ap, imp2.ap, 0.0, selm.ap, ALU.is_ge, ALU.mult), reads=[imp2.b, selm.b], writes=[selm.b])
                        S.op("dve", lambda e: e.tensor_scalar(selb.ap, selm.ap, -1.0, -NEG, ALU.add, ALU.mult), reads=[selm.b], writes=[selb.b])
                        S.op("pe", lambda e, qs=qs: e.transpose(TRP.ap[:, qs * 128:(qs + 1) * 128], selb.ap, ident.ap), reads=[selb.b, ident.b], writes=[TRP.b])
                    S.op("dve", lambda e: e.tensor_copy(selbT.ap, TRP.ap[:, 0:512]), reads=[TRP.b], writes=[selbT.b])
                    for r in range(4):
                        h = 4 * g + r
                        si = 2 + h
                        back = alibi_back(NSA_SLOPES[4 + r]) if g == 0 else None

                        def sel_tail(r=r, GBr=GB[r]):
                            S.op("dve", lambda e: e.reciprocal(rs2[0].ap[64:128, :], NUMs[0].ap[64:128, :]), reads=[NUMs[0].b], writes=[rs2[0].b])
                            S.op("dve", lambda e: e.tensor_tensor(ob2[0].ap, NUMs[0].ap[0:64, :], rs2[0].ap[64:128, :], ALU.mult), reads=[NUMs[0].b, rs2[0].b], writes=[ob2[0].b])
                            S.op("pool", lambda e: e.tensor_tensor(acc.ap, GBr.ap[:, 1, :], ob2[0].ap, ALU.mult), reads=[GBr.b, ob2[0].b], writes=[acc.b])
                            S.op("pool", lambda e: e.tensor_tensor(t1.ap, GBr.ap[:, 0, :], ocmp[r].ap, ALU.mult), reads=[GBr.b, ocmp[r].b], writes=[t1.b])
                            S.op("pool", lambda e: e.tensor_tensor(acc.ap, acc.ap, t1.ap, ALU.add), reads=[acc.b, t1.b], writes=[acc.b])

                        def win_tail(r=r, h=h, qt=qt, GBr=GB[r]):
                            S.op("dve", lambda e: e.reciprocal(rs2[1].ap[64:128, :], NUMs[1].ap[64:128, :]), reads=[NUMs[1].b], writes=[rs2[1].b])
                            S.op("dve", lambda e: e.tensor_tensor(ob2[1].ap, NUMs[1].ap[0:64, :], rs2[1].ap[64:128, :], ALU.mult), reads=[NUMs[1].b, rs2[1].b], writes=[ob2[1].b])
                            S.op("pool", lambda e: e.tensor_tensor(t1.ap, GBr.ap[:, 2, :], ob2[1].ap, ALU.mult), reads=[GBr.b, ob2[1].b], writes=[t1.b])
                            ev = oev.next()
                            S.op("pool", lambda e: e.tensor_tensor(ev.ap, acc.ap, t1.ap, ALU.add), reads=[acc.b, t1.b], writes=[ev.b])
                            store(ot_p[h * 64:(h + 1) * 64, qt * 512:(qt + 1) * 512], ev)

                        kts = ktiles_for(qt, back)
                        for i, kt in enumerate(kts):
                            o = kt - 4 * qt
                            extra = [(ewide, ewide.ap[:, kt * 128:(kt + 1) * 128], selbT, selbT.ap)]
                            if o >= 0:
                                extra.append((ident, ident.ap, mask_c, mask_c.ap[:, o, :]))
                            tile_job(pipe, KsA, KsA.ap[:, kt * 128:(kt + 1) * 128], QA[r], extra, (bias_ap(si, 4 * qt - kt), bias_tab.b),
                                     Vs, Vs.ap[:, kt, :], NUMs[0], None, 64, Ps.next(), i == 0, i == len(kts) - 1, sel_tail)
                        kts = [kt for kt in range(4 * qt + 3, 4 * qt - 5, -1) if kt >= 0]
                        for i, kt in enumerate(kts):
                            o = kt - 4 * qt
                            extra = [(ident, ident.ap, mask_win, mask_win.ap[:, o + 4, :])]
                            tile_job(pipe, KwA, KwA.ap[:, kt * 128:(kt + 1) * 128], QA[r], extra, (bias_ap(si, 4 * qt - kt), bias_tab.b),
                                     Vw, Vw.ap[:, kt, :], NUMs[1], None, 64, Ps.next(), i == 0, i == len(kts) - 1, win_tail)
                    pipe.flush()

        attention_l1()
        all_gather_ot()
        if STOP_AFTER == "L1ATT":
            S.barrier()
            cpb = Ring([AR.alloc(128, (1024,), BF16, f"cpb{i}") for i in range(2)])
            cpf = Ring([AR.alloc(128, (1024,), F32, f"cpf{i}") for i in range(2)])
            ov = out_d.rearrange("(a b) c -> a (b c)", a=1024)
            for k in range(8):
                for cb in range(8):
                    b_ = cpb.next()
                    f_ = cpf.next()
                    load(b_, ot[k * 128:(k + 1) * 128, cb * 1024:(cb + 1) * 1024])
                    S.op("dve", lambda e, b_=b_, f_=f_: e.tensor_copy(f_.ap, b_.ap), reads=[b_.b], writes=[f_.b])
                    store(ov[k * 128:(k + 1) * 128, cb * 1024:(cb + 1) * 1024], f_)
            S.barrier()
            S.finalize()
            return nc, S
        phase_reset()
        phase_mlp(x2, od_w_out, mlp_w1[1], mlp_w2[1], 3, out_d, final_gain=final_norm)
        S.barrier()
        S.finalize()
    return nc, S


def _cols(*ranges):
    return np.concatenate([np.arange(a, b) for a, b in ranges])


def kernel(**inputs):
    consts = [make_consts(0), make_consts(1)]
    nc, S = build_program(consts[0])
    x = np.ascontiguousarray(inputs["x"], dtype=np.float32)
    B = x.shape[0]
    shared = {}
    for k in ("attn_norm", "mlp_norm", "final_norm", "mlp_w1", "mlp_w2"):
        shared[k] = np.ascontiguousarray(inputs[k], dtype=np.float32)
    for k in ("ev_subln", "od_cmp_pos_k", "od_cmp_k_w1", "od_cmp_k_w2", "od_cmp_pos_v", "od_cmp_v_w1", "od_cmp_v_w2"):
        shared[k] = np.ascontiguousarray(inputs[k][0], dtype=np.float32)
    shared["lamv"] = np.ascontiguousarray(np.stack([inputs["ev_lam_q1"][0], inputs["ev_lam_k1"][0],
                                                    inputs["ev_lam_q2"][0], inputs["ev_lam_k2"][0]]), dtype=np.float32)
    w0 = np.asarray(inputs["ev_w_in"][0], dtype=np.float32)
    w1 = np.asarray(inputs["od_w_in"][0], dtype=np.float32)
    per = []
    feat0 = []
    feat1 = []
    for p in range(2):
        sb = (256 * p, 256 * p + 256)
        dA, dB = DIFF_ASSIGN[p]
        c0 = _cols((0 + sb[0], 0 + sb[1]), (512 + sb[0], 512 + sb[1]),
                   (1536 + 128 * dA, 1536 + 128 * dA + 128), (1536 + 128 * dB, 1536 + 128 * dB + 128),
                   (2048 + 128 * dA, 2048 + 128 * dA + 128), (2048 + 128 * dB, 2048 + 128 * dB + 128),
                   (1024 + sb[0], 1024 + sb[1]),
                   (2560 + 128 * dA, 2560 + 128 * dA + 128), (2560 + 128 * dB, 2560 + 128 * dB + 128))
        gA, gB = GROUP_ASSIGN[p]
        kv = lambda base: [(base + 64 * gA, base + 64 * gA + 64), (base + 64 * gB, base + 64 * gB + 64)]
        c1 = _cols((256 * gA, 256 * gA + 256), (256 * gB, 256 * gB + 256),
                   *kv(1024), *kv(1280), *kv(1536), *kv(2048),
                   (2560 + 12 * gA, 2560 + 12 * gA + 12), (2560 + 12 * gB, 2560 + 12 * gB + 12),
                   *kv(1792), *kv(2304))
        per.append({"ev_w_in": np.ascontiguousarray(w0[:, c0]), "od_w_in": np.ascontiguousarray(w1[:, c1])})
        feat0.append(_cols(sb, (512 + 128 * dA, 512 + 128 * dA + 128), (512 + 128 * dB, 512 + 128 * dB + 128)))
        feat1.append(_cols((256 * gA, 256 * gA + 256), (256 * gB, 256 * gB + 256)))
    perm0 = np.concatenate(feat0)
    perm1 = np.concatenate(feat1)
    shared["ev_w_out"] = np.ascontiguousarray(np.asarray(inputs["ev_w_out"][0], dtype=np.float32)[perm0])
    shared["od_w_out"] = np.ascontiguousarray(np.asarray(inputs["od_w_out"][0], dtype=np.float32)[perm1])
    in_maps = []
    for core in range(8):
        p = core % 2
        m = dict(shared)
        m.update(per[p])
        for k, v in consts[p].items():
            m["c_" + k] = v
        xb = x[(core // 2) % B]
        m["x"] = xb
        m["xh"] = np.ascontiguousarray(xb[p * (SEQ // 2):(p + 1) * (SEQ // 2)])
        ms = np.zeros((128, 2), np.float32)
        ms[:, p] = 1.0
        m["msel"] = ms
        in_maps.append(m)
    res = run_bass_kernel_spmd(nc, in_maps, core_ids=list(range(8)))
    out = np.stack([np.concatenate([np.asarray(res.results[2 * b + p]["out"], dtype=np.float32) for p in range(2)], axis=0)
                    for b in range(B)])
    return out
```

```python
import math
import numpy as np
import ml_dtypes
from contextlib import ExitStack
import concourse.bass as bass
import concourse.mybir as mybir
from concourse.bass_utils import run_bass_kernel_spmd

F32 = mybir.dt.float32
BF16 = mybir.dt.bfloat16
AF = mybir.ActivationFunctionType
ALU = mybir.AluOpType
bf = ml_dtypes.bfloat16

SEQ = 8192
D = 1024
NQT = SEQ // 512
NKT = SEQ // 128
NEG = -30000.0
SB_BACK = 4
ALIBI_CUT = 144.0
SEM_WRAP = 16000
DMA_WRAP = 1000
STOP_AFTER = None


class Buf:
    __slots__ = ("name", "w", "r")

    def __init__(self, name):
        self.name = name
        self.w = []
        self.r = []


class Op:
    __slots__ = ("eng", "fn", "deps", "idx", "needs_inc", "waits", "is_dma", "dkey", "dval")

    def __init__(self, eng, fn):
        self.eng = eng
        self.fn = fn
        self.deps = set()
        self.idx = -1
        self.needs_inc = False
        self.waits = []
        self.is_dma = False
        self.dkey = None
        self.dval = 0


class T:
    __slots__ = ("ap", "b")

    def __init__(self, ap, b):
        self.ap = ap
        self.b = b

    def __getitem__(self, k):
        return self.ap[k]


class Sched:
    ENGS = ("pe", "act", "dve", "pool", "sp")

    def __init__(self, nc, stack):
        self.nc = nc
        self.stack = stack
        self.ops = []
        self.eng_ops = {e: [] for e in self.ENGS}
        self.dma_count = {}
        self.bufs = []

    def buf(self, name):
        b = Buf(name)
        self.bufs.append(b)
        return b

    def op(self, eng, fn, reads=(), writes=(), dma_key=None):
        o = Op(eng, fn)
        oid = len(self.ops)
        for b in reads:
            o.deps.update(b.w)
        for b in writes:
            o.deps.update(b.w)
            o.deps.update(b.r)
        for b in reads:
            b.r.append(oid)
        for b in writes:
            b.w = [oid]
            b.r = []
        if dma_key is not None:
            o.is_dma = True
            o.dkey = dma_key
            n = self.dma_count.get(dma_key, 0) + 1
            self.dma_count[dma_key] = n
            o.dval = n
        o.idx = len(self.eng_ops[eng])
        self.eng_ops[eng].append(o)
        self.ops.append(o)
        return oid

    def barrier(self):
        live = [b for b in self.bufs if b.w or b.r]
        first = True
        sync = self.buf("barrier")
        for e in self.ENGS:
            if first:
                self.op(e, lambda eng: eng.nop(), writes=live + [sync])
                first = False
            else:
                self.op(e, lambda eng: eng.nop(), reads=[sync])
        self.bufs = [sync]

    def finalize(self):
        nc = self.nc
        ops = self.ops
        know = {e: {} for e in self.ENGS}
        comp_know = [None] * len(ops)
        for oid, o in enumerate(ops):
            K = know[o.eng]
            best = {}
            for d in o.deps:
                p = ops[d]
                if p.is_dma:
                    dom = ("d", p.dkey)
                    val = p.dval
                else:
                    if p.eng == "pe" and o.eng == "pe":
                        continue
                    dom = ("e", p.eng)
                    val = p.idx + 1
                if val > best.get(dom, (0, None))[0]:
                    best[dom] = (val, d)
            for dom, (val, d) in sorted(best.items(), key=lambda kv: kv[1][1]):
                if K.get(dom, 0) >= val:
                    continue
                o.waits.append((dom, val))
                ops[d].needs_inc = True
                for k2, v2 in comp_know[d].items():
                    if K.get(k2, 0) < v2:
                        K[k2] = v2
            ck = dict(K)
            if o.is_dma:
                ck[("d", o.dkey)] = o.dval
            else:
                ck[("e", o.eng)] = o.idx + 1
            comp_know[oid] = ck
        comp_know = None
        eng_sems = {}
        counts = {}
        for e in self.ENGS:
            c = 0
            for o in self.eng_ops[e]:
                if o.is_dma:
                    continue
                if o.needs_inc:
                    c += 1
                counts[(e, o.idx)] = c
            nsem = (c + SEM_WRAP - 1) // SEM_WRAP
            eng_sems[e] = [self.stack.enter_context(nc.semaphore(f"s_{e}_{i}")) for i in range(nsem)]
        dma_sems = {}
        for k, n in self.dma_count.items():
            nsem = (n + DMA_WRAP - 1) // DMA_WRAP
            dma_sems[k] = [self.stack.enter_context(nc.semaphore(f"d_{k}_{i}")) for i in range(nsem)]

        def sem_for(dom, val):
            if dom[0] == "e":
                c = counts[(dom[1], val - 1)]
                return eng_sems[dom[1]][(c - 1) // SEM_WRAP], (c - 1) % SEM_WRAP + 1
            mul = 1 if dom[1].startswith("cc") else 16
            return dma_sems[dom[1]][(val - 1) // DMA_WRAP], ((val - 1) % DMA_WRAP + 1) * mul

        self.n_waits = sum(len(o.waits) for o in ops)
        with nc.Block() as block:
            def make(e):
                def body(eng):
                    for o in self.eng_ops[e]:
                        for dom, val in o.waits:
                            s, v = sem_for(dom, val)
                            eng.wait_ge(s, v)
                        ins = o.fn(eng)
                        if o.is_dma:
                            s, v = sem_for(("d", o.dkey), o.dval)
                            ins.then_inc(s, 1 if o.dkey.startswith("cc") else 16)
                        elif o.needs_inc:
                            c = counts[(e, o.idx)]
                            ins.then_inc(eng_sems[e][(c - 1) // SEM_WRAP], 1)
                return body
            block.tensor(make("pe"))
            block.scalar(make("act"))
            block.vector(make("dve"))
            block.gpsimd(make("pool"))
            block.sync(make("sp"))


class Arena:
    def __init__(self, S, tens, nwords):
        self.S = S
        self.t = tens
        self.n = nwords
        self.off = 0
        self.cnt = 0

    def reset(self):
        self.off = 0

    def alloc(self, parts, shape, dt, name):
        n = 1
        for s in shape:
            n *= s
        words = n if dt == F32 else (n + 1) // 2
        v = self.t[0:parts, self.off:self.off + words]
        self.off += words
        assert self.off <= self.n, f"arena overflow at {name}: {self.off} > {self.n}"
        if dt != F32:
            v = v.bitcast(dt)
        if len(shape) == 2:
            v = v.rearrange("p (a b) -> p a b", a=shape[0])
        elif len(shape) == 3:
            v = v.rearrange("p (a b c) -> p a b c", a=shape[0], b=shape[1])
        self.cnt += 1
        return T(v, self.S.buf(name))


def _split3(v):
    v = np.asarray(v, np.float32)
    hi = v.astype(bf)
    r1 = v - hi.astype(np.float32)
    mid = r1.astype(bf)
    r2 = r1 - mid.astype(np.float32)
    lo = r2.astype(bf)
    return hi, mid, lo


DIFF_SLOPES = [2.0 ** (-8.0 * (h + 1) / 4) for h in range(4)]
NSA_SLOPES = [2.0 ** (-8.0 * (h + 1) / 16) for h in range(16)]
DIFF_ASSIGN = ((0, 3), (1, 2))
GROUP_ASSIGN = ((0, 3), (1, 2))
SB_PER_CORE = 4


def slopes_for(p):
    sl = [DIFF_SLOPES[h] for h in DIFF_ASSIGN[p]]
    for g in GROUP_ASSIGN[p]:
        sl += [NSA_SLOPES[4 * g + r] for r in range(4)]
    return sl


NSLOT = 10
BIAS_M0 = -3
BIAS_NM = 68


def make_consts(p):
    ALL_SLOPES = slopes_for(p)
    c = {}
    j = np.arange(128)
    t = np.arange(512)
    c["ident"] = np.eye(128, dtype=np.float32).astype(bf)
    c["ones"] = np.ones((128, 128), np.float32).astype(bf)
    c["negones"] = (-np.ones((128, 128), np.float32)).astype(bf)
    c["uneg"] = (-(j[:, None] >= j[None, :]).astype(np.float32)).astype(bf)
    c["onesdiv"] = np.full((128, 128), 1.0 / 128, np.float32)
    msb = np.zeros((4, 128, 512), np.float32)
    mc = np.zeros((4, 128, 512), np.float32)
    for o in range(4):
        jj = 128 * o + j[:, None]
        msb[o] = np.where(jj >= t[None, :], NEG, 0.0)
        mc[o] = np.where(jj > t[None, :], NEG, 0.0)
    c["mask_sb"] = np.ascontiguousarray(msb.transpose(1, 0, 2)).astype(bf)
    c["mask_c"] = np.ascontiguousarray(mc.transpose(1, 0, 2)).astype(bf)
    kr = np.zeros((6, SEQ), np.float32)
    kr[0:3] = 1.0
    kr[3:6] = (np.arange(SEQ) % 128)[None, :]
    c["kaug"] = kr.astype(bf)
    kc = np.zeros((6, 512), np.float32)
    kc[0:3] = 1.0
    kc[3:6] = (16 * (np.arange(512) % 128))[None, :]
    c["kaug_cmp"] = kc.astype(bf)
    qa = np.zeros((len(ALL_SLOPES), 6, 512), np.float32).astype(bf)
    for i, s in enumerate(ALL_SLOPES):
        s32 = np.float32(s)
        v = (-(s32 * t.astype(np.float32))).astype(np.float32)
        h3 = _split3(v)
        s3 = _split3(np.full(512, s32, np.float32))
        for r in range(3):
            qa[i, r] = h3[r]
            qa[i, 3 + r] = s3[r]
    c["qaug"] = np.ascontiguousarray(qa.transpose(1, 0, 2))
    bt = np.zeros((len(ALL_SLOPES), BIAS_NM), np.float32)
    for i, s in enumerate(ALL_SLOPES):
        for mi in range(BIAS_NM):
            bt[i, mi] = -np.float32(s) * np.float32(128 * (mi + BIAS_M0))
    c["bias_tab"] = np.broadcast_to(bt.reshape(1, -1), (128, bt.size)).copy()
    bc = np.zeros((8, NQT, 4), np.float32)
    for h, s in enumerate(ALL_SLOPES[2:]):
        for qt in range(NQT):
            for cc in range(4):
                bc[h, qt, cc] = -np.float32(s) * np.float32(512 * qt - 2048 * cc - 31)
    c["bias_cmp"] = np.broadcast_to(bc.reshape(1, -1), (128, bc.size)).copy()
    mcm = np.zeros((5, 128, 512), np.float32)
    for rel in range(5):
        mcm[rel] = np.where(512 * rel + t[None, :] >= 16 * j[:, None] + 31, 0.0, NEG)
    c["mask_cmp"] = np.ascontiguousarray(mcm.transpose(1, 0, 2)).astype(bf)
    mw = np.zeros((8, 128, 512), np.float32)
    for oi, o in enumerate(range(-4, 4)):
        dd = t[None, :] - j[:, None] - 128 * o
        mw[oi] = np.where((dd >= 0) & (dd <= 511), 0.0, NEG)
    c["mask_win"] = np.ascontiguousarray(mw.transpose(1, 0, 2)).astype(bf)
    cc = np.arange(SEQ)
    c["ewide"] = (cc[None, :] // 64 == j[:, None]).astype(np.float32).astype(bf)
    n = np.arange(512)
    s_ = np.arange(128)
    ov = ((n[:, None] >= 4 * s_[None, :] - 1) & (n[:, None] <= 4 * s_[None, :] + 3) & (n[:, None] < 511))
    c["ovl"] = np.ascontiguousarray(ov.astype(np.float32).reshape(4, 128, 128).transpose(1, 0, 2)).astype(bf)
    q = np.arange(128)
    u = np.arange(-127, 129)
    cur = (q >= 64).astype(np.int64)
    A = np.zeros((128, 256), np.float32)
    M = np.ones((128, 256), np.float32)
    fut = u[None, :] > cur[:, None]
    A[fut] = -1.0
    M[fut] = 0.0
    f1 = u[None, :] == cur[:, None]
    f2 = u[None, :] == cur[:, None] - 1
    A[f1] = 1.0e6 + 1.0
    M[f1] = 0.0
    A[f2] = 1.0e6 + 2.0
    M[f2] = 0.0
    c["topk_a"] = A
    c["topk_m"] = M
    gs = np.zeros((48, 48, 64), np.float32)
    for r in range(48):
        gs[r, r, :] = 1.0
    c["gsel"] = gs.reshape(48, 48 * 64).astype(bf)
    return c


CONST_SPECS = None


def _dt_of(a):
    return BF16 if a.dtype == bf else F32


def build_program(consts):
    nc = bass.Bass("TRN2", target_bir_lowering=False)
    dr = {}

    def din(name, shape, dt=F32):
        dr[name] = nc.dram_tensor(name, list(shape), dt, kind="ExternalInput").ap()
        return dr[name]

    def dscr(name, shape, dt):
        dr[name] = nc.dram_tensor(name, list(shape), dt, kind="Internal").ap()
        return dr[name]

    x_in = din("x", (SEQ, D))
    attn_norm = din("attn_norm", (2, D))
    mlp_norm = din("mlp_norm", (2, D))
    final_norm = din("final_norm", (D,))
    ev_w_in = din("ev_w_in", (D, 1536))
    lamv = din("lamv", (4, 64))
    ev_subln = din("ev_subln", (128,))
    ev_w_out = din("ev_w_out", (D, D))
    od_w_in = din("od_w_in", (D, 1304))
    pos_k = din("od_cmp_pos_k", (32, 64))
    cw1k = din("od_cmp_k_w1", (2048, 256))
    cw2k = din("od_cmp_k_w2", (256, 64))
    pos_v = din("od_cmp_pos_v", (32, 64))
    cw1v = din("od_cmp_v_w1", (2048, 256))
    cw2v = din("od_cmp_v_w2", (256, 64))
    od_w_out = din("od_w_out", (D, D))
    mlp_w1 = din("mlp_w1", (2, D, 4096))
    mlp_w2 = din("mlp_w2", (2, 4096, D))
    cd = {k: din("c_" + k, v.shape, _dt_of(v)) for k, v in consts.items()}
    out_d = nc.dram_tensor("out", [SEQ // 2, D], F32, kind="ExternalOutput").ap()
    xh_in = din("xh", (SEQ // 2, D))
    msel_in = din("msel", (128, 2))

    qk0 = dscr("qk0", (16, 64, SEQ), BF16)
    v0 = dscr("v0", (SEQ, 512), BF16)
    ot = dscr("ot", (D, SEQ), BF16)
    ot_p = dscr("ot_p", (512, SEQ), BF16)
    ot4 = dscr("ot4", (4, 256, SEQ), BF16)
    x2 = dscr("x2", (SEQ // 2, D), F32)
    x2g = dscr("x2g", (8, 1024, D), F32)
    q1 = dscr("q1", (8, 64, SEQ), BF16)
    kf1 = dscr("kf1", (8, 64, SEQ), BF16)
    v1 = dscr("v1", (SEQ, 256), BF16)
    gts = dscr("gts", (24, SEQ), BF16)
    kcmp = dscr("kcmp", (2, 64, 512), BF16)
    vcmp = dscr("vcmp", (2, 512, 64), BF16)
    x1s = dscr("x1s", (SEQ // 2, D), F32)
    uts_d = dscr("uts", (32, 128, SEQ // 2), BF16)

    with ExitStack() as st:
        S = Sched(nc, st)
        NW = 50 * 1024
        arena_t = st.enter_context(nc.sbuf_tensor("arena", [128, NW], F32))
        AR = Arena(S, arena_t, NW)
        pbanks = [st.enter_context(nc.psum_tensor(f"pb{i}", [128, 512], F32)) for i in range(8)]

        def PS(i, name, parts=128, cols=512, dt=F32):
            ap = pbanks[i][0:parts, :]
            if dt != F32:
                ap = ap.bitcast(dt)
            ap = ap[:, 0:cols]
            return T(ap, S.buf(name))

        def dma(q, out, in_, reads, writes, key, slow=False):
            if slow:
                S.op(q, lambda e: e.dma_start(out=out, in_=in_, allow_slow_non_contiguous=True), reads=reads, writes=writes, dma_key=key)
            else:
                S.op(q, lambda e: e.dma_start(out=out, in_=in_), reads=reads, writes=writes, dma_key=key)

        def load_vt(VT, src_cols, width):
            for g4 in range(4):
                dma("sp", VT.ap[:, g4 * 16:(g4 + 1) * 16, 0:width],
                    src_cols[g4 * 2048:(g4 + 1) * 2048, :].rearrange("(t p) c -> p t c", p=128), [], [VT.b], "LVT")

        def load(dst, src, q="sp"):
            dma(q, dst.ap, src, [], [dst.b], "L" + dst.b.name)

        def store(dst, src, ap=None, q="pool"):
            dma(q, dst, src.ap if ap is None else ap, [src.b], [], "S" + src.b.name)

        def mm(out, outap, lhsT, lhsap, rhs, rhsap, start, stop, extra_r=()):
            S.op("pe", lambda e: e.matmul(outap, lhsap, rhsap, start=start, stop=stop),
                 reads=[lhsT.b, rhs.b] + list(extra_r), writes=[out.b])

        class Ring:
            def __init__(self, items):
                self.items = items
                self.i = 0

            def next(self):
                it = self.items[self.i % len(self.items)]
                self.i += 1
                return it

        rr_cast = [0]

        def cast(dst, dstap, src, srcap, scale_ap=None, scale_t=None):
            e = ("dve", "act", "pool")[rr_cast[0] % 3] if scale_ap is None else ("dve", "act")[rr_cast[0] % 2]
            rr_cast[0] += 1
            rd = [src.b] + ([scale_t.b] if scale_t is not None else [])
            if e == "act":
                if scale_ap is None:
                    S.op("act", lambda en: en.copy(dstap, srcap), reads=rd, writes=[dst.b])
                else:
                    S.op("act", lambda en: en.activation(dstap, srcap, AF.Copy, scale=scale_ap), reads=rd, writes=[dst.b])
            elif e == "dve":
                if scale_ap is None:
                    S.op("dve", lambda en: en.tensor_copy(dstap, srcap), reads=rd, writes=[dst.b])
                else:
                    S.op("dve", lambda en: en.tensor_scalar(dstap, srcap, scale_ap, None, ALU.mult), reads=rd, writes=[dst.b])
            else:
                S.op("pool", lambda en: en.tensor_copy(dstap, srcap), reads=rd, writes=[dst.b])

        def load_weight(dst, src_ap, K, N, stage_ring, gain=None, col0=0, ncols=None, dcol0=0):
            ncols = N if ncols is None else ncols
            for k in range(K // 128):
                c = 0
                while c < ncols:
                    w = min(2048, ncols - c)
                    stg = stage_ring.next()
                    dma("sp", stg.ap[:, 0:w], src_ap[k * 128:(k + 1) * 128, col0 + c:col0 + c + w], [], [stg.b], "L" + stg.b.name)
                    cast(dst, dst.ap[:, k, dcol0 + c:dcol0 + c + w], stg, stg.ap[:, 0:w],
                         None if gain is None else gain.ap[:, k:k + 1], gain)
                    c += w

        NWP = 0
        ident = AR.alloc(128, (128,), BF16, "ident")
        load(ident, cd["ident"])
        ones = AR.alloc(128, (128,), BF16, "ones")
        load(ones, cd["ones"])
        negones = AR.alloc(128, (128,), BF16, "negones")
        load(negones, cd["negones"])
        uneg = AR.alloc(128, (128,), BF16, "uneg")
        load(uneg, cd["uneg"])
        onesdiv = AR.alloc(128, (128,), F32, "onesdiv")
        load(onesdiv, cd["onesdiv"])
        bias_tab = AR.alloc(128, (NSLOT * BIAS_NM,), F32, "bias_tab")
        load(bias_tab, cd["bias_tab"])
        gains = AR.alloc(128, (4, 8), F32, "gains")
        for li in range(2):
            dma("sp", gains.ap[:, 2 * li, :], attn_norm[li].rearrange("(k p) -> p k", p=128), [], [gains.b], "Lgains", slow=True)
            dma("sp", gains.ap[:, 2 * li + 1, :], mlp_norm[li].rearrange("(k p) -> p k", p=128), [], [gains.b], "Lgains", slow=True)
        eps_t = AR.alloc(128, (1,), F32, "eps_t")
        S.op("dve", lambda e: e.memset(eps_t.ap, 1e-6), writes=[eps_t.b])
        msel = AR.alloc(128, (2,), F32, "msel")
        load(msel, msel_in)
        persist_off = AR.off

        def bias_ap(si, m):
            col = si * BIAS_NM + (m - BIAS_M0)
            return bias_tab.ap[:, col:col + 1]

        def phase_reset():
            S.barrier()
            AR.off = persist_off

        def norm_transpose(xt, hn, ss, rs, junk, tp, hT, r):
            S.op("act", lambda e: e.activation(junk.ap, xt.ap, AF.Square, accum_out=ss.ap), reads=[xt.b], writes=[junk.b, ss.b])
            S.op("act", lambda e: e.activation(rs.ap, ss.ap, AF.Sqrt, bias=eps_t.ap, scale=1.0 / D), reads=[ss.b, eps_t.b], writes=[rs.b])
            S.op("dve", lambda e: e.reciprocal(rs.ap, rs.ap), reads=[rs.b], writes=[rs.b])
            S.op("act", lambda e: e.activation(hn.ap, xt.ap, AF.Copy, scale=rs.ap), reads=[xt.b, rs.b], writes=[hn.b])
            for k in range(8):
                S.op("pe", lambda e, k=k: e.transpose(tp.ap[:, k * 128:(k + 1) * 128], hn.ap[:, k * 128:(k + 1) * 128], ident.ap),
                     reads=[hn.b, ident.b], writes=[tp.b])
            S.op("dve", lambda e: e.tensor_copy(hT.ap[:, :, r * 128:(r + 1) * 128], tp.ap.rearrange("p (k c) -> p k c", k=8)),
                 reads=[tp.b], writes=[hT.b])

        def phase_proj(x_src, w_src, ncols_total, gain_idx, fm_chunks, tm_chunks):
            stage = Ring([AR.alloc(128, (2048,), F32, f"wstg{i}") for i in range(2)])
            W = AR.alloc(128, (8, ncols_total), BF16, "Win")
            gsl = T(gains.ap[:, gain_idx, :], gains.b)
            load_weight(W, w_src, D, ncols_total, stage, gain=gsl)
            xts = Ring([AR.alloc(128, (D,), F32, f"xt{i}") for i in range(2)])
            hns = Ring([AR.alloc(128, (D,), BF16, f"hn{i}") for i in range(2)])
            junk = AR.alloc(128, (D,), BF16, "junk")
            sss = Ring([AR.alloc(128, (1,), F32, f"ss{i}") for i in range(2)])
            rss = Ring([AR.alloc(128, (1,), F32, f"rs{i}") for i in range(2)])
            hTs = Ring([AR.alloc(128, (8, 512), BF16, f"hT{i}") for i in range(2)])
            tps = Ring([PS(i, f"tp{i}", cols=1024, dt=BF16) for i in (0, 1)])
            pps = Ring([PS(i, f"pp{i}") for i in (2, 3, 4, 5)])
            evs = Ring([AR.alloc(128, (512,), BF16, f"ev{i}") for i in range(4)])
            ev_i = [0]
            for tb in range(NQT):
                hT = hTs.next()
                for r in range(4):
                    xt = xts.next()
                    row0 = tb * 512 + r * 128
                    load(xt, x_src(row0))
                    norm_transpose(xt, hns.next(), sss.next(), rss.next(), junk, tps.next(), hT, r)
                for (col0, M, dsts, scale, func) in fm_chunks:
                    pp = pps.next()
                    for k in range(8):
                        mm(pp, pp.ap[0:M, :], W, W.ap[:, k, col0:col0 + M], hT, hT.ap[:, k, :], k == 0, k == 7)
                    ev = evs.next()
                    eng = ("act", "dve")[ev_i[0] % 2] if func is None else "act"
                    ev_i[0] += 1
                    if eng == "act":
                        S.op("act", lambda e, pp=pp, ev=ev, M=M, scale=scale, func=func: e.activation(
                            ev.ap[0:M, :], pp.ap[0:M, :], AF.Copy if func is None else func, scale=scale),
                            reads=[pp.b], writes=[ev.b])
                    else:
                        S.op("dve", lambda e, pp=pp, ev=ev, M=M, scale=scale: e.tensor_scalar(
                            ev.ap[0:M, :], pp.ap[0:M, :], float(scale), None, ALU.mult), reads=[pp.b], writes=[ev.b])
                    for (dfn, p0, npart) in dsts:
                        store(dfn(tb), ev, ap=ev.ap[p0:p0 + npart, :])
                for (col0, ncols, dfn) in tm_chunks:
                    for r in range(4):
                        pp = pps.next()
                        for k in range(8):
                            mm(pp, pp.ap[:, 0:ncols], hT, hT.ap[:, k, r * 128:(r + 1) * 128], W, W.ap[:, k, col0:col0 + ncols], k == 0, k == 7)
                        ev = evs.next()
                        eng = ("act", "dve")[ev_i[0] % 2]
                        ev_i[0] += 1
                        if eng == "act":
                            S.op("act", lambda e, pp=pp, ev=ev, n=ncols: e.copy(ev.ap[:, 0:n], pp.ap[:, 0:n]), reads=[pp.b], writes=[ev.b])
                        else:
                            S.op("dve", lambda e, pp=pp, ev=ev, n=ncols: e.tensor_copy(ev.ap[:, 0:n], pp.ap[:, 0:n]), reads=[pp.b], writes=[ev.b])
                        store(dfn(tb * 512 + r * 128), ev, ap=ev.ap[:, 0:ncols])

        fm = []
        for jc in range(8):
            isq = jc in (0, 1, 4, 5)
            dsts = [(lambda tb, slot=2 * jc + half: qk0[slot, :, tb * 512:(tb + 1) * 512], 64 * half, 64) for half in range(2)]
            fm.append((128 * jc, 128, dsts, 0.125 if isq else 1.0, None))
        tm = [(1024, 512, lambda row0: v0[row0:row0 + 128, 0:512])]
        phase_proj(lambda row0: x_in[row0:row0 + 128, :], ev_w_in, 1536, 0, fm, tm)
        phase_reset()

        class Pipe:
            def __init__(self):
                self.q = []
                self.t = 0

            def job(self, stages):
                for lag, fn in stages:
                    if lag == 0:
                        fn()
                    else:
                        self.q.append((self.t + lag, fn))
                rest = []
                for due, fn in self.q:
                    if due <= self.t:
                        fn()
                    else:
                        rest.append((due, fn))
                self.q = rest
                self.t += 1

            def flush(self):
                for due, fn in sorted(self.q, key=lambda x_: x_[0]):
                    fn()
                self.q = []

        def ktiles_for(qt, back):
            hi = 4 * qt + 3
            lo = 0 if back is None else max(0, 4 * qt - back)
            return list(range(hi, lo - 1, -1))

        def alibi_back(slope):
            dcut = ALIBI_CUT / slope
            bk = int(math.floor((dcut + 127.0) / 128.0))
            return bk

        ccbuf = S.buf("ccbuf")
        cc_n = [0]

        def all_gather_ot():
            S.barrier()
            for k in range(4):
                cc_n[0] += 1
                S.op("pool", lambda e, k=k: e.collective_compute("AllGather", ALU.bypass, replica_groups=[[0, 1], [2, 3], [4, 5], [6, 7]],
                                                                 ins=[ot_p[k * 128:(k + 1) * 128, :]], outs=[ot4[k]]),
                     writes=[ccbuf], dma_key="cc")
            S.bufs.append(ccbuf)

        def all_gather_x2():
            S.barrier()
            for k in range(8):
                cc_n[0] += 1
                S.op("pool", lambda e, k=k: e.collective_compute("AllGather", ALU.bypass, replica_groups=[[0, 1], [2, 3], [4, 5], [6, 7]],
                                                                 ins=[x2[k * 512:(k + 1) * 512, :]], outs=[x2g[k]]),
                     writes=[ccbuf], dma_key="cc")
            S.bufs.append(ccbuf)

        def attention_l0():
            mask_sb = AR.alloc(128, (4, 512), BF16, "mask_sb")
            load(mask_sb, cd["mask_sb"])
            mask_c = AR.alloc(128, (4, 512), BF16, "mask_c")
            load(mask_c, cd["mask_c"])
            KT = AR.alloc(70, (SEQ,), BF16, "KT")
            VT = AR.alloc(128, (NKT, 128), BF16, "VT")
            QTs = Ring([AR.alloc(70, (512,), BF16, f"QT{i}") for i in range(2)])
            Es = Ring([AR.alloc(128, (512,), F32, f"E{i}") for i in range(2)])
            SPs = Ring([AR.alloc(128, (512,), BF16, f"SP{i}") for i in range(4)])
            Ws = Ring([AR.alloc(128, (512,), BF16, f"Wt{i}") for i in range(4)])
            ssum = AR.alloc(128, (512,), F32, "ssum")
            sshs = Ring([AR.alloc(128, (512,), BF16, f"ssh{i}") for i in range(4)])
            oev = Ring([AR.alloc(128, (512,), BF16, f"oev{i}") for i in range(2)])
            Ys = Ring([PS(i, f"Y{i}") for i in (0, 1, 2)])
            Oaccs = [PS(3, "Oacc0"), PS(4, "Oacc1")]
            for h in range(SB_PER_CORE):
                pipe = Pipe()
                load(T(KT.ap[0:64, :], KT.b), qk0[4 + h])
                load_vt(VT, v0[:, h * 64:(h + 1) * 64], 64)
                for qt in range(NQT):
                    QT = QTs.next()
                    dma("sp", QT.ap[0:64, :], qk0[h, :, qt * 512:(qt + 1) * 512], [], [QT.b], "L" + QT.b.name)
                    kts = ktiles_for(qt, SB_BACK)
                    n = len(kts)
                    Oacc = Oaccs[qt % 2]
                    ssh_prev = None
                    for i, kt in enumerate(kts):
                        Y = Ys.next()
                        E = Es.next()
                        SP = SPs.next()
                        Wt = Ws.next()
                        ssh = sshs.next() if 0 < i < n - 1 else None
                        o = kt - 4 * qt

                        def st0(Y=Y, E=E, SP=SP, o=o, kt=kt, QT=QT, i=i, n=n, ssh=ssh):
                            mm(Y, Y.ap, KT, KT.ap[0:64, kt * 128:(kt + 1) * 128], QT, QT.ap[0:64, :], True, False)
                            if o >= 0:
                                mm(Y, Y.ap, ident, ident.ap, mask_sb, mask_sb.ap[:, o, :], False, False)
                            S.op("act", lambda e: e.activation(E.ap, Y.ap, AF.Exp), reads=[Y.b], writes=[E.b])
                            S.op("act", lambda e: e.activation(SP.ap, E.ap, AF.Ln, bias=1.0), reads=[E.b], writes=[SP.b])
                            if i < n - 1:
                                if i == 0:
                                    S.op("pool", lambda e: e.tensor_copy(ssum.ap, SP.ap), reads=[SP.b], writes=[ssum.b])
                                else:
                                    S.op("pool", lambda e: e.tensor_tensor(ssum.ap, ssum.ap, SP.ap, ALU.add), reads=[SP.b, ssum.b], writes=[ssum.b])
                                    S.op("pool", lambda e: e.tensor_copy(ssh.ap, ssum.ap), reads=[ssum.b], writes=[ssh.b])

                        def st1(Y=Y, SP=SP, Wt=Wt, prev=ssh_prev):
                            mm(Y, Y.ap, uneg, uneg.ap, SP, SP.ap, False, prev is None)
                            if prev is not None:
                                mm(Y, Y.ap, negones, negones.ap, prev, prev.ap, False, True)
                            S.op("act", lambda e: e.activation(Wt.ap, Y.ap, AF.Exp), reads=[Y.b], writes=[Wt.b])

                        def st2(Wt=Wt, kt=kt, i=i, n=n, Oacc=Oacc, h=h, qt=qt):
                            mm(Oacc, Oacc.ap[0:64, :], VT, VT.ap[:, kt, 0:64], Wt, Wt.ap, i == 0, i == n - 1)
                            if i == n - 1:
                                ev = oev.next()
                                S.op("dve", lambda e: e.tensor_copy(ev.ap[0:64, :], Oacc.ap[0:64, :]), reads=[Oacc.b], writes=[ev.b])
                                store(ot_p[h * 64:(h + 1) * 64, qt * 512:(qt + 1) * 512], ev, ap=ev.ap[0:64, :])

                        pipe.job([(0, st0), (1, st1), (2, st2)])
                        if i < n - 1:
                            ssh_prev = SP if i == 0 else ssh
                pipe.flush()
            lam_t = AR.alloc(128, (4, 64), F32, "lam_t")
            for i in range(4):
                dma("sp", lam_t.ap[:, i, :], lamv[i].partition_broadcast(128), [], [lam_t.b], "Llam")
            lam_p = AR.alloc(128, (2, 64), F32, "lam_p")
            lam_s = AR.alloc(128, (2,), F32, "lam_s")
            neglam = AR.alloc(128, (1,), F32, "neglam")
            S.op("dve", lambda e: e.tensor_tensor(lam_p.ap[:, 0, :], lam_t.ap[:, 0, :], lam_t.ap[:, 1, :], ALU.mult), reads=[lam_t.b], writes=[lam_p.b])
            S.op("dve", lambda e: e.tensor_tensor(lam_p.ap[:, 1, :], lam_t.ap[:, 2, :], lam_t.ap[:, 3, :], ALU.mult), reads=[lam_t.b, lam_p.b], writes=[lam_p.b])
            S.op("dve", lambda e: e.reduce_sum(lam_s.ap, lam_p.ap, axis=mybir.AxisListType.X), reads=[lam_p.b], writes=[lam_s.b])
            S.op("act", lambda e: e.activation(lam_s.ap, lam_s.ap, AF.Exp), reads=[lam_s.b], writes=[lam_s.b])
            lam_init = 0.8 - 0.6 * math.exp(-0.3 * 0)
            S.op("dve", lambda e: e.tensor_tensor(neglam.ap, lam_s.ap[:, 1:2], lam_s.ap[:, 0:1], ALU.subtract), reads=[lam_s.b], writes=[neglam.b])
            S.op("dve", lambda e: e.tensor_scalar(neglam.ap, neglam.ap, -lam_init, None, ALU.add), reads=[neglam.b], writes=[neglam.b])
            sg = AR.alloc(128, (1,), F32, "sg")
            dma("sp", sg.ap, ev_subln.rearrange("(p o) -> p o", o=1), [], [sg.b], "Lsg")
            S.op("dve", lambda e: e.tensor_scalar(sg.ap, sg.ap, 1.0 - lam_init, None, ALU.mult), reads=[sg.b], writes=[sg.b])
            KT2 = [KT, AR.alloc(70, (SEQ,), BF16, "KTb")]
            for kk in KT2:
                dma("sp", kk.ap[64:70, :], cd["kaug"], [], [kk.b], "L" + kk.b.name)
            QT2 = [[AR.alloc(70, (512,), BF16, f"QD{c}{i}") for i in range(2)] for c in range(2)]
            Ps = Ring([AR.alloc(128, (512,), BF16, f"P{i}") for i in range(4)])
            NUM = [PS(3, "NUM0"), PS(4, "NUM1")]
            DEN = [PS(5, "DEN0"), PS(6, "DEN1")]
            MS = PS(7, "MS")
            r_t = [AR.alloc(128, (512,), F32, f"rden{c}") for c in range(2)]
            a_t = [AR.alloc(128, (512,), F32, f"a{c}") for c in range(2)]
            o_t = AR.alloc(128, (512,), F32, "o_t")
            sq_t = AR.alloc(128, (512,), F32, "sq_t")
            for h in range(2):
                pipe = Pipe()
                back = alibi_back(DIFF_SLOPES[1]) if h == 0 else None
                for c in range(2):
                    dma("sp", KT2[c].ap[0:64, :], qk0[12 + 2 * h + c], [], [KT2[c].b], "L" + KT2[c].b.name)
                load_vt(VT, v0[:, 256 + h * 128:256 + (h + 1) * 128], 128)
                for qt in range(NQT):
                    for c in range(2):
                        QT = QT2[c][qt % 2]
                        dma("sp", QT.ap[0:64, :], qk0[8 + 2 * h + c, :, qt * 512:(qt + 1) * 512], [], [QT.b], "L" + QT.b.name)
                        dma("sp", QT.ap[64:70, :], cd["qaug"][:, h, :], [], [QT.b], "L" + QT.b.name)
                        kts = ktiles_for(qt, back)
                        n = len(kts)
                        for i, kt in enumerate(kts):
                            Y = Ys.next()
                            P = Ps.next()
                            o = kt - 4 * qt

                            def st0(Y=Y, P=P, o=o, kt=kt, QT=QT, c=c, qt=qt, h=h):
                                mm(Y, Y.ap, KT2[c], KT2[c].ap[:, kt * 128:(kt + 1) * 128], QT, QT.ap, True, o < 0)
                                if o >= 0:
                                    mm(Y, Y.ap, ident, ident.ap, mask_c, mask_c.ap[:, o, :], False, True)
                                S.op("act", lambda e: e.activation(P.ap, Y.ap, AF.Exp, bias=bias_ap(h, 4 * qt - kt)),
                                     reads=[Y.b, bias_tab.b], writes=[P.b])

                            def st2(P=P, kt=kt, i=i, n=n, c=c, h=h, qt=qt):
                                mm(NUM[c], NUM[c].ap, VT, VT.ap[:, kt, :], P, P.ap, i == 0, i == n - 1)
                                mm(DEN[c], DEN[c].ap, ones, ones.ap, P, P.ap, i == 0, i == n - 1)
                                if i == n - 1:
                                    S.op("dve", lambda e: e.reciprocal(r_t[c].ap, DEN[c].ap), reads=[DEN[c].b], writes=[r_t[c].b])
                                    S.op("dve", lambda e: e.tensor_tensor(a_t[c].ap, NUM[c].ap, r_t[c].ap, ALU.mult), reads=[NUM[c].b, r_t[c].b], writes=[a_t[c].b])
                                    if c == 1:
                                        S.op("dve", lambda e: e.scalar_tensor_tensor(o_t.ap, a_t[1].ap, neglam.ap, a_t[0].ap, ALU.mult, ALU.add),
                                             reads=[a_t[0].b, a_t[1].b, neglam.b], writes=[o_t.b])
                                        S.op("act", lambda e: e.activation(sq_t.ap, o_t.ap, AF.Square), reads=[o_t.b], writes=[sq_t.b])
                                        mm(MS, MS.ap, onesdiv, onesdiv.ap, sq_t, sq_t.ap, True, True)
                                        S.op("act", lambda e: e.activation(sq_t.ap, MS.ap, AF.Sqrt, bias=eps_t.ap), reads=[MS.b, eps_t.b], writes=[sq_t.b])
                                        S.op("dve", lambda e: e.reciprocal(sq_t.ap, sq_t.ap), reads=[sq_t.b], writes=[sq_t.b])
                                        ev = oev.next()
                                        S.op("dve", lambda e: e.scalar_tensor_tensor(ev.ap, o_t.ap, sg.ap, sq_t.ap, ALU.mult, ALU.mult),
                                             reads=[o_t.b, sg.b, sq_t.b], writes=[ev.b])
                                        store(ot_p[256 + h * 128:256 + (h + 1) * 128, qt * 512:(qt + 1) * 512], ev)

                            pipe.job([(0, st0), (2, st2)])
                pipe.flush()

        attention_l0()
        all_gather_ot()
        phase_reset()

        def phase_mlp(x_src, wout_src, w1_src, w2_src, gain_idx, dst, final_gain=None):
            stage = Ring([AR.alloc(128, (2048,), F32, f"wstg{i}") for i in range(2)])
            Wo = AR.alloc(128, (8, D), BF16, "Wo")
            W1 = AR.alloc(128, (8, 4096), BF16, "W1")
            gsl = T(gains.ap[:, gain_idx, :], gains.b)
            load_weight(Wo, wout_src, D, D, stage)
            load_weight(W1, w1_src, D, 4096, stage, gain=gsl)
            OTs = Ring([AR.alloc(128, (8, 512), BF16, f"OT{i}") for i in range(2)])
            OAs = Ring([AR.alloc(128, (8, 512), BF16, f"OA{i}") for i in range(1)])
            OBs = Ring([AR.alloc(128, (8, 512), BF16, f"OB{i}") for i in range(1)])
            x1r = Ring([AR.alloc(128, (D,), F32, f"x1r{i}") for i in range(3)])
            xts = Ring([AR.alloc(128, (D,), F32, f"xt{i}") for i in range(2)])
            hns = Ring([AR.alloc(128, (D,), BF16, f"hn{i}") for i in range(2)])
            junk = AR.alloc(128, (D,), BF16, "junk")
            sss = Ring([AR.alloc(128, (1,), F32, f"ss{i}") for i in range(2)])
            rss = Ring([AR.alloc(128, (1,), F32, f"rs{i}") for i in range(2)])
            hTs = Ring([AR.alloc(128, (8, 512), BF16, f"hT{i}") for i in range(2)])
            uts = Ring([AR.alloc(128, (512,), BF16, f"ut{i}") for i in range(4)])
            sqs = Ring([AR.alloc(128, (512,), BF16, f"sq{i}") for i in range(2)])
            tps = Ring([PS(i, f"tp{i}", cols=1024, dt=BF16) for i in (0, 1)])
            pps = Ring([PS(i, f"pp{i}") for i in (2, 3, 4, 5, 6, 7)])
            for tb in range(NQT // 2):
                OT = OTs.next()
                OA = OAs.next()
                OB = OBs.next()
                for kc in range(8):
                    dma("sp", OA.ap[:, kc, :], ot4[kc % 4, (kc // 4) * 128:(kc // 4 + 1) * 128, tb * 512:(tb + 1) * 512], [], [OA.b], "L" + OA.b.name)
                    dma("sp", OB.ap[:, kc, :], ot4[kc % 4, (kc // 4) * 128:(kc // 4 + 1) * 128, SEQ // 2 + tb * 512:SEQ // 2 + (tb + 1) * 512], [], [OB.b], "L" + OB.b.name)
                S.op("dve", lambda e, OA=OA: e.tensor_scalar(OA.ap, OA.ap, msel.ap[:, 0:1], None, ALU.mult), reads=[OA.b, msel.b], writes=[OA.b])
                S.op("dve", lambda e, OA=OA, OB=OB, OT=OT: e.scalar_tensor_tensor(OT.ap, OB.ap, msel.ap[:, 1:2], OA.ap, ALU.mult, ALU.add),
                     reads=[OA.b, OB.b, msel.b], writes=[OT.b])
                hT = hTs.next()
                for r in range(4):
                    row0 = tb * 512 + r * 128
                    xt = xts.next()
                    load(xt, x_src[row0:row0 + 128, :])
                    x1v = x1r.next()
                    for half in range(2):
                        pp = pps.next()
                        for k in range(8):
                            mm(pp, pp.ap, OT, OT.ap[:, k, r * 128:(r + 1) * 128], Wo, Wo.ap[:, k, half * 512:(half + 1) * 512], k == 0, k == 7)
                        S.op("dve", lambda e, pp=pp, xt=xt, x1v=x1v, half=half: e.tensor_tensor(
                            x1v.ap[:, half * 512:(half + 1) * 512], pp.ap, xt.ap[:, half * 512:(half + 1) * 512], ALU.add),
                            reads=[pp.b, xt.b], writes=[x1v.b])
                    store(x1s[row0:row0 + 128, :], x1v)
                    norm_transpose(x1v, hns.next(), sss.next(), rss.next(), junk, tps.next(), hT, r)
                for fc in range(32):
                    pp = pps.next()
                    for k in range(8):
                        mm(pp, pp.ap, W1, W1.ap[:, k, fc * 128:(fc + 1) * 128], hT, hT.ap[:, k, :], k == 0, k == 7)
                    sq = sqs.next()
                    u = uts.next()
                    S.op("act", lambda e, pp=pp, sq=sq: e.activation(sq.ap, pp.ap, AF.Square), reads=[pp.b], writes=[sq.b])
                    S.op("dve", lambda e, pp=pp, sq=sq, u=u: e.scalar_tensor_tensor(u.ap, pp.ap, 0.0, sq.ap, ALU.is_gt, ALU.mult),
                         reads=[pp.b, sq.b], writes=[u.b])
                    store(uts_d[fc, :, tb * 512:(tb + 1) * 512], u)
            phase_reset()
            stage = Ring([AR.alloc(128, (2048,), F32, f"wstg{i}") for i in range(2)])
            W2 = AR.alloc(128, (32, D), BF16, "W2")
            load_weight(W2, w2_src, 4096, D, stage)
            fg = None
            if final_gain is not None:
                fg = AR.alloc(128, (D,), F32, "fg")
                load(fg, final_gain.partition_broadcast(128))
            UTs = Ring([AR.alloc(128, (32, 512), BF16, f"UT{i}") for i in range(2)])
            x1r = Ring([AR.alloc(128, (D,), F32, f"x1r{i}") for i in range(3)])
            ys = Ring([AR.alloc(128, (D,), F32, f"y{i}") for i in range(3)])
            junk = AR.alloc(128, (D,), BF16, "junk")
            sss = Ring([AR.alloc(128, (1,), F32, f"ss{i}") for i in range(2)])
            rss = Ring([AR.alloc(128, (1,), F32, f"rs{i}") for i in range(2)])
            pps = Ring([PS(i, f"pp{i}") for i in (0, 1, 2, 3, 4, 5)])
            for tb in range(NQT // 2):
                UT = UTs.next()
                for f4 in range(4):
                    dma("sp", UT.ap[:, f4 * 8:(f4 + 1) * 8, :], uts_d[f4 * 8:(f4 + 1) * 8, :, tb * 512:(tb + 1) * 512].rearrange("f p t -> p f t"),
                        [], [UT.b], "L" + UT.b.name)
                for r in range(4):
                    row0 = tb * 512 + r * 128
                    x1v = x1r.next()
                    load(x1v, x1s[row0:row0 + 128, :])
                    y = ys.next()
                    for half in range(2):
                        pp = pps.next()
                        for fc in range(32):
                            mm(pp, pp.ap, UT, UT.ap[:, fc, r * 128:(r + 1) * 128], W2, W2.ap[:, fc, half * 512:(half + 1) * 512], fc == 0, fc == 31)
                        S.op("dve", lambda e, pp=pp, y=y, x1v=x1v, half=half: e.tensor_tensor(
                            y.ap[:, half * 512:(half + 1) * 512], pp.ap, x1v.ap[:, half * 512:(half + 1) * 512], ALU.add),
                            reads=[pp.b, x1v.b], writes=[y.b])
                    if fg is not None:
                        ss = sss.next()
                        rs = rss.next()
                        S.op("act", lambda e, y=y, ss=ss: e.activation(junk.ap, y.ap, AF.Square, accum_out=ss.ap), reads=[y.b], writes=[junk.b, ss.b])
                        S.op("act", lambda e, ss=ss, rs=rs: e.activation(rs.ap, ss.ap, AF.Sqrt, bias=eps_t.ap, scale=1.0 / D), reads=[ss.b, eps_t.b], writes=[rs.b])
                        S.op("dve", lambda e, rs=rs: e.reciprocal(rs.ap, rs.ap), reads=[rs.b], writes=[rs.b])
                        S.op("dve", lambda e, y=y, rs=rs: e.scalar_tensor_tensor(y.ap, y.ap, rs.ap, fg.ap, ALU.mult, ALU.mult),
                             reads=[y.b, rs.b, fg.b], writes=[y.b])
                    store(dst[row0:row0 + 128, :], y)

        phase_mlp(xh_in, ev_w_out, mlp_w1[0], mlp_w2[0], 1, x2)
        all_gather_x2()
        if False:
            S.barrier()
            cp = Ring([AR.alloc(128, (D,), F32, f"cp{i}") for i in range(2)])
            for i in range(NKT):
                c_ = cp.next()
                load(c_, x2[i * 128:(i + 1) * 128, :])
                store(out_d[i * 128:(i + 1) * 128, :], c_)
            S.barrier()
            S.finalize()
            return nc, S
        phase_reset()

        fm = []
        for jc in range(4):
            dsts = [(lambda tb, slot=2 * jc + half: q1[slot, :, tb * 512:(tb + 1) * 512], 64 * half, 64) for half in range(2)]
            fm.append((128 * jc, 128, dsts, 0.125, None))
        for ji in range(4):
            dsts = [(lambda tb, slot=2 * ji + half: kf1[slot, :, tb * 512:(tb + 1) * 512], 64 * half, 64) for half in range(2)]
            fm.append((512 + 128 * ji, 128, dsts, 1.0, None))
        fm.append((1024, 24, [(lambda tb: gts[:, tb * 512:(tb + 1) * 512], 0, 24)], 1.0, AF.Sigmoid))
        tm = [(1048, 256, lambda row0: v1[row0:row0 + 128, 0:256])]
        phase_proj(lambda row0: x2g[(row0 % 4096) // 512, (row0 // 4096) * 512 + row0 % 512:(row0 // 4096) * 512 + row0 % 512 + 128, :], od_w_in, 1304, 2, fm, tm)
        phase_reset()

        def phase_compress():
            stg = AR.alloc(64, (32 * 256,), F32, "cstg")
            W1c = AR.alloc(64, (32, 256), BF16, "W1c")
            stg2 = AR.alloc(128, (2, 64), F32, "cstg2")
            W2c = AR.alloc(128, (2, 64), BF16, "W2c")
            posf = AR.alloc(64, (32,), F32, "posf")
            posT = AR.alloc(64, (32,), BF16, "posT")
            biash = AR.alloc(128, (2,), F32, "biash")
            src = AR.alloc(64, (SEQ,), BF16, "csrc")
            hid = AR.alloc(128, (2, 512), BF16, "hid")
            u_t = AR.alloc(128, (512,), F32, "u_t")
            w_t = AR.alloc(128, (512,), F32, "w_t")
            evk = AR.alloc(64, (512,), BF16, "evk")
            evv = Ring([AR.alloc(128, (64,), BF16, f"evv{i}") for i in range(2)])
            pb_ = PS(0, "cbias")
            ph = Ring([PS(1, "ph0"), PS(2, "ph1")])
            po = Ring([PS(3, "po0"), PS(4, "po1")])
            S.op("dve", lambda e: e.memset(hid.ap, 0.0), writes=[hid.b])
            S.op("dve", lambda e: e.memset(evk.ap, 0.0), writes=[evk.b])
            for kind, (w1d, w2d, posd) in enumerate(((cw1k, cw2k, pos_k), (cw1v, cw2v, pos_v))):
                dma("sp", stg.ap.rearrange("p (l h) -> p l h", l=32), w1d.rearrange("(l d) h -> d l h", d=64), [], [stg.b], "Lcstg")
                for q4 in range(4):
                    cast(W1c, W1c.ap[:, q4 * 8:(q4 + 1) * 8, :], stg, stg.ap.rearrange("p (l h) -> p l h", l=32)[:, q4 * 8:(q4 + 1) * 8, :])
                dma("sp", stg2.ap, w2d.rearrange("(c p) n -> p c n", p=128), [], [stg2.b], "Lcstg2")
                cast(W2c, W2c.ap, stg2, stg2.ap)
                dma("sp", posf.ap, posd.rearrange("l d -> d l"), [], [posf.b], "Lposf", slow=True)
                cast(posT, posT.ap, posf, posf.ap)
                for hc in range(2):
                    for l in range(32):
                        mm(pb_, pb_.ap[:, 0:1], W1c, W1c.ap[:, l, hc * 128:(hc + 1) * 128], posT, posT.ap[:, l:l + 1], l == 0, l == 31)
                    S.op("dve", lambda e, hc=hc: e.tensor_copy(biash.ap[:, hc:hc + 1], pb_.ap[:, 0:1]), reads=[pb_.b], writes=[biash.b])
                for g in range(2):
                    load(src, kf1[2 * kind + g])
                    for hc in range(2):
                        p_ = ph.next()
                        for l in range(32):
                            mm(p_, p_.ap[:, 0:511], W1c, W1c.ap[:, l, hc * 128:(hc + 1) * 128], src, src.ap[:, l:l + 16 * 510 + 1:16], l == 0, l == 31)
                        S.op("act", lambda e, p_=p_, hc=hc: e.activation(u_t.ap[:, 0:511], p_.ap[:, 0:511], AF.Identity, bias=biash.ap[:, hc:hc + 1]),
                             reads=[p_.b, biash.b], writes=[u_t.b])
                        S.op("act", lambda e: e.activation(w_t.ap[:, 0:511], u_t.ap[:, 0:511], AF.Square), reads=[u_t.b], writes=[w_t.b])
                        S.op("dve", lambda e: e.tensor_scalar(w_t.ap[:, 0:511], w_t.ap[:, 0:511], 0.044715, 1.0, ALU.mult, ALU.add), reads=[w_t.b], writes=[w_t.b])
                        S.op("dve", lambda e: e.tensor_tensor(w_t.ap[:, 0:511], w_t.ap[:, 0:511], u_t.ap[:, 0:511], ALU.mult), reads=[w_t.b, u_t.b], writes=[w_t.b])
                        S.op("act", lambda e: e.activation(w_t.ap[:, 0:511], w_t.ap[:, 0:511], AF.Sigmoid, scale=2.0 * 0.7978845608028654), reads=[w_t.b], writes=[w_t.b])
                        S.op("dve", lambda e, hc=hc: e.tensor_tensor(hid.ap[:, hc, 0:511], w_t.ap[:, 0:511], u_t.ap[:, 0:511], ALU.mult), reads=[w_t.b, u_t.b], writes=[hid.b])
                    if kind == 0:
                        p2 = po.next()
                        for hc in range(2):
                            mm(p2, p2.ap[0:64, 0:511], W2c, W2c.ap[:, hc, :], hid, hid.ap[:, hc, 0:511], hc == 0, hc == 1)
                        S.op("dve", lambda e, p2=p2: e.tensor_copy(evk.ap[:, 0:511], p2.ap[0:64, 0:511]), reads=[p2.b], writes=[evk.b])
                        store(kcmp[g], evk)
                    else:
                        for nchunk in range(4):
                            p2 = po.next()
                            for hc in range(2):
                                mm(p2, p2.ap[:, 0:64], hid, hid.ap[:, hc, nchunk * 128:(nchunk + 1) * 128], W2c, W2c.ap[:, hc, :], hc == 0, hc == 1)
                            ev = evv.next()
                            S.op("dve", lambda e, p2=p2, ev=ev: e.tensor_copy(ev.ap, p2.ap[:, 0:64]), reads=[p2.b], writes=[ev.b])
                            store(vcmp[g, nchunk * 128:(nchunk + 1) * 128, :], ev)

        phase_compress()
        phase_reset()

        def attention_l1():
            mask_c = AR.alloc(128, (4, 512), BF16, "mask_c")
            load(mask_c, cd["mask_c"])
            mask_cmp = AR.alloc(128, (5, 512), BF16, "mask_cmp")
            load(mask_cmp, cd["mask_cmp"])
            mask_win = AR.alloc(128, (8, 512), BF16, "mask_win")
            load(mask_win, cd["mask_win"])
            ewide = AR.alloc(128, (SEQ,), BF16, "ewide")
            load(ewide, cd["ewide"])
            ovl = AR.alloc(128, (4, 128), BF16, "ovl")
            load(ovl, cd["ovl"])
            tka = AR.alloc(128, (256,), F32, "tka")
            load(tka, cd["topk_a"])
            tkm = AR.alloc(128, (256,), F32, "tkm")
            load(tkm, cd["topk_m"])
            bcmp = AR.alloc(128, (8 * NQT * 4,), F32, "bcmp")
            load(bcmp, cd["bias_cmp"])
            KcA = AR.alloc(70, (512,), BF16, "KcA")
            dma("sp", KcA.ap[64:70, :], cd["kaug_cmp"], [], [KcA.b], "LKcA")
            Vc = AR.alloc(128, (4, 64), BF16, "Vc")
            KsA = AR.alloc(70, (SEQ,), BF16, "KsA")
            KwA = AR.alloc(70, (SEQ,), BF16, "KwA")
            dma("sp", KsA.ap[64:70, :], cd["kaug"], [], [KsA.b], "LKsA")
            dma("sp", KwA.ap[64:70, :], cd["kaug"], [], [KwA.b], "LKwA")
            Vs = AR.alloc(128, (NKT, 128), BF16, "Vs")
            Vw = AR.alloc(128, (NKT, 128), BF16, "Vw")
            S.op("dve", lambda e: e.memset(Vs.ap[:, :, 64:128], 1.0), writes=[Vs.b])
            S.op("dve", lambda e: e.memset(Vw.ap[:, :, 64:128], 1.0), writes=[Vw.b])
            QAs = [[AR.alloc(70, (512,), BF16, f"QA{r}{i}") for i in range(2)] for r in range(4)]
            GBs = [[AR.alloc(64, (3, 512), BF16, f"GB{r}{i}") for i in range(2)] for r in range(4)]
            Pc = Ring([AR.alloc(128, (512,), BF16, f"Pc{i}") for i in range(10)])
            Ps = Ring([AR.alloc(128, (512,), BF16, f"Pp{i}") for i in range(4)])
            pcn = Ring([AR.alloc(128, (512,), BF16, f"pcn{i}") for i in range(4)])
            rdc = AR.alloc(128, (512,), F32, "rdc")
            ocmp = [AR.alloc(64, (512,), F32, f"ocmp{r}") for r in range(4)]
            selbT = AR.alloc(128, (512,), BF16, "selbT")
            imp2 = AR.alloc(128, (128,), F32, "imp2")
            tmp2 = AR.alloc(128, (128,), F32, "tmp2")
            selm = AR.alloc(128, (128,), F32, "selm")
            selb = AR.alloc(128, (128,), BF16, "selb")
            v8a = AR.alloc(128, (8,), F32, "v8a")
            v8b = AR.alloc(128, (8,), F32, "v8b")
            rs2 = [AR.alloc(128, (512,), F32, f"rs2{i}") for i in range(2)]
            ob2 = [AR.alloc(64, (512,), F32, f"ob2{i}") for i in range(2)]
            acc = AR.alloc(64, (512,), F32, "acc")
            t1 = AR.alloc(64, (512,), F32, "t1")
            oev = Ring([AR.alloc(64, (512,), BF16, f"oev{i}") for i in range(2)])
            Ys = Ring([PS(0, "Y0"), PS(1, "Y1"), PS(2, "Y2")])
            NUMs = [PS(3, "NUMa"), PS(5, "NUMb")]
            DENs = [PS(4, "DENa"), PS(6, "DENb")]
            IMP = PS(7, "IMP")
            TRP = T(pbanks[6][:, :].bitcast(BF16)[:, 0:512], DENs[1].b)

            def tile_job(pipe, K, kap, QAr, extra, bias, V, vap, NUM, DEN, den_parts, P, first, last, tail):
                Y = Ys.next()

                def st0():
                    nx = len(extra)
                    mm(Y, Y.ap, K, kap, QAr, QAr.ap, True, nx == 0)
                    for xi, (lt, lap, rt, rap) in enumerate(extra):
                        mm(Y, Y.ap, lt, lap, rt, rap, False, xi == nx - 1)
                    S.op("act", lambda e: e.activation(P.ap, Y.ap, AF.Exp, bias=bias[0]), reads=[Y.b, bias[1]], writes=[P.b])

                def st2():
                    if DEN is None:
                        mm(NUM, NUM.ap, V, vap, P, P.ap, first, last)
                    else:
                        mm(NUM, NUM.ap[0:64, :], V, vap, P, P.ap, first, last)
                        mm(DEN, DEN.ap[0:den_parts, :], ones, ones.ap[:, 0:den_parts], P, P.ap, first, last)
                    if last and tail is not None:
                        tail()

                pipe.job([(0, st0), (2, st2)])

            for g in range(2):
                dma("sp", KcA.ap[0:64, :], kcmp[g], [], [KcA.b], "LKcA")
                dma("sp", Vc.ap, vcmp[g].rearrange("(c p) d -> p c d", p=128), [], [Vc.b], "LVc")
                dma("sp", KsA.ap[0:64, :], kf1[4 + g], [], [KsA.b], "LKsA")
                dma("sp", KwA.ap[0:64, :], kf1[6 + g], [], [KwA.b], "LKwA")
                for g4 in range(4):
                    dma("sp", Vs.ap[:, g4 * 16:(g4 + 1) * 16, 0:64], v1[g4 * 2048:(g4 + 1) * 2048, g * 64:(g + 1) * 64].rearrange("(t p) c -> p t c", p=128), [], [Vs.b], "LVs")
                    dma("sp", Vw.ap[:, g4 * 16:(g4 + 1) * 16, 0:64], v1[g4 * 2048:(g4 + 1) * 2048, 128 + g * 64:128 + (g + 1) * 64].rearrange("(t p) c -> p t c", p=128), [], [Vw.b], "LVw")
                for qt in range(NQT):
                    pipe = Pipe()
                    QA = [QAs[r][qt % 2] for r in range(4)]
                    GB = [GBs[r][qt % 2] for r in range(4)]
                    for r in range(4):
                        h = 4 * g + r
                        dma("sp", QA[r].ap[0:64, :], q1[h, :, qt * 512:(qt + 1) * 512], [], [QA[r].b], "L" + QA[r].b.name)
                        dma("sp", QA[r].ap[64:70, :], cd["qaug"][:, 2 + h, :], [], [QA[r].b], "L" + QA[r].b.name)
                        for c3 in range(3):
                            dma("sp", GB[r].ap[:, c3, :], gts[3 * h + c3, qt * 512:(qt + 1) * 512].partition_broadcast(64), [], [GB[r].b], "L" + GB[r].b.name)
                    chunks = [c for c in range(4) if 4 * c <= qt]
                    for r in range(4):
                        h = 4 * g + r
                        NUM = NUMs[r % 2]
                        DEN = DENs[r % 2]
                        Pl = [(c, Pc.next()) for c in chunks]

                        def cmp_tail(r=r, NUM=NUM, DEN=DEN, Pl=Pl):
                            S.op("dve", lambda e: e.tensor_scalar(rdc.ap, DEN.ap, 1e-30, None, ALU.add), reads=[DEN.b], writes=[rdc.b])
                            S.op("dve", lambda e: e.reciprocal(rdc.ap, rdc.ap), reads=[rdc.b], writes=[rdc.b])
                            S.op("dve", lambda e: e.tensor_tensor(ocmp[r].ap, NUM.ap[0:64, :], rdc.ap[0:64, :], ALU.mult), reads=[NUM.b, rdc.b], writes=[ocmp[r].b])
                            for ci, (c, P) in enumerate(Pl):
                                pn = pcn.next()
                                S.op("dve", lambda e, pn=pn, P=P: e.tensor_tensor(pn.ap, P.ap, rdc.ap, ALU.mult), reads=[P.b, rdc.b], writes=[pn.b])
                                for qs in range(4):
                                    mm(IMP, IMP.ap[:, qs * 128:(qs + 1) * 128], pn, pn.ap[:, qs * 128:(qs + 1) * 128], ovl, ovl.ap[:, c, :],
                                       r == 0 and ci == 0, r == 3 and ci == len(Pl) - 1)

                        for ci, (c, P) in enumerate(Pl):
                            rel = qt - 4 * c
                            extra = [(ident, ident.ap, mask_cmp, mask_cmp.ap[:, rel, :])] if rel <= 4 else []
                            col = (h * NQT + qt) * 4 + c
                            tile_job(pipe, KcA, KcA.ap[:, c * 128:(c + 1) * 128], QA[r], extra, (bcmp.ap[:, col:col + 1], bcmp.b),
                                     Vc, Vc.ap[:, c, :], NUM, DEN, 128, P, ci == 0, ci == len(Pl) - 1, cmp_tail)
                    pipe.flush()
                    for qs in range(4):
                        off = 127 - 2 * (4 * qt + qs)
                        S.op("dve", lambda e, qs=qs, off=off: e.tensor_tensor(tmp2.ap, IMP.ap[:, qs * 128:(qs + 1) * 128], tkm.ap[:, off:off + 128], ALU.mult),
                             reads=[IMP.b, tkm.b], writes=[tmp2.b])
                        S.op("dve", lambda e, off=off: e.tensor_tensor(imp2.ap, tmp2.ap, tka.ap[:, off:off + 128], ALU.add), reads=[tmp2.b, tka.b], writes=[imp2.b])
                        S.op("dve", lambda e: e.memset(imp2.ap[:, 0:1], 1.0e6), reads=[], writes=[imp2.b])
                        S.op("dve", lambda e: e.max(v8a.ap, imp2.ap), reads=[imp2.b], writes=[v8a.b])
                        S.op("dve", lambda e: e.match_replace(tmp2.ap, v8a.ap, imp2.ap, -9.0), reads=[imp2.b, v8a.b], writes=[tmp2.b])
                        S.op("dve", lambda e: e.max(v8b.ap, tmp2.ap), reads=[tmp2.b], writes=[v8b.b])
                        S.op("dve", lambda e: e.tensor_scalar(selm.ap, imp2.ap, v8b.ap[:, 7:8], 0.0, ALU.is_ge, ALU.add), reads=[imp2.b, v8b.b], writes=[selm.b])
                        S.op("dve", lambda e: e.scalar_tensor_tensor(selm.ap, imp2.ap, 0.0, selm.ap, ALU.is_ge, ALU.mult), reads=[imp2.b, selm.b], writes=[selm.b])
                        S.op("dve", lambda e: e.tensor_scalar(selb.ap, selm.ap, -1.0, -NEG, ALU.add, ALU.mult), reads=[selm.b], writes=[selb.b])
                        S.op("pe", lambda e, qs=qs: e.transpose(TRP.ap[:, qs * 128:(qs + 1) * 128], selb.ap, ident.ap), reads=[selb.b, ident.b], writes=[TRP.b])
                    S.op("dve", lambda e: e.tensor_copy(selbT.ap, TRP.ap[:, 0:512]), reads=[TRP.b], writes=[selbT.b])
                    for r in range(4):
                        h = 4 * g + r
                        si = 2 + h
                        back = alibi_back(NSA_SLOPES[4 + r]) if g == 0 else None

                        def sel_tail(r=r, GBr=GB[r]):
                            S.op("dve", lambda e: e.reciprocal(rs2[0].ap[64:128, :], NUMs[0].ap[64:128, :]), reads=[NUMs[0].b], writes=[rs2[0].b])
                            S.op("dve", lambda e: e.tensor_tensor(ob2[0].ap, NUMs[0].ap[0:64, :], rs2[0].ap[64:128, :], ALU.mult), reads=[NUMs[0].b, rs2[0].b], writes=[ob2[0].b])
                            S.op("pool", lambda e: e.tensor_tensor(acc.ap, GBr.ap[:, 1, :], ob2[0].ap, ALU.mult), reads=[GBr.b, ob2[0].b], writes=[acc.b])
                            S.op("pool", lambda e: e.tensor_tensor(t1.ap, GBr.ap[:, 0, :], ocmp[r].ap, ALU.mult), reads=[GBr.b, ocmp[r].b], writes=[t1.b])
                            S.op("pool", lambda e: e.tensor_tensor(acc.ap, acc.ap, t1.ap, ALU.add), reads=[acc.b, t1.b], writes=[acc.b])

                        def win_tail(r=r, h=h, qt=qt, GBr=GB[r]):
                            S.op("dve", lambda e: e.reciprocal(rs2[1].ap[64:128, :], NUMs[1].ap[64:128, :]), reads=[NUMs[1].b], writes=[rs2[1].b])
                            S.op("dve", lambda e: e.tensor_tensor(ob2[1].ap, NUMs[1].ap[0:64, :], rs2[1].ap[64:128, :], ALU.mult), reads=[NUMs[1].b, rs2[1].b], writes=[ob2[1].b])
                            S.op("pool", lambda e: e.tensor_tensor(t1.ap, GBr.ap[:, 2, :], ob2[1].ap, ALU.mult), reads=[GBr.b, ob2[1].b], writes=[t1.b])
                            ev = oev.next()
                            S.op("pool", lambda e: e.tensor_tensor(ev.ap, acc.ap, t1.ap, ALU.add), reads=[acc.b, t1.b], writes=[ev.b])
                            store(ot_p[h * 64:(h + 1) * 64, qt * 512:(qt + 1) * 512], ev)

                        kts = ktiles_for(qt, back)
                        for i, kt in enumerate(kts):
                            o = kt - 4 * qt
                            extra = [(ewide, ewide.ap[:, kt * 128:(kt + 1) * 128], selbT, selbT.ap)]
                            if o >= 0:
                                extra.append((ident, ident.ap, mask_c, mask_c.ap[:, o, :]))
                            tile_job(pipe, KsA, KsA.ap[:, kt * 128:(kt + 1) * 128], QA[r], extra, (bias_ap(si, 4 * qt - kt), bias_tab.b),
                                     Vs, Vs.ap[:, kt, :], NUMs[0], None, 64, Ps.next(), i == 0, i == len(kts) - 1, sel_tail)
                        kts = [kt for kt in range(4 * qt + 3, 4 * qt - 5, -1) if kt >= 0]
                        for i, kt in enumerate(kts):
                            o = kt - 4 * qt
                            extra = [(ident, ident.ap, mask_win, mask_win.ap[:, o + 4, :])]
                            tile_job(pipe, KwA, KwA.ap[:, kt * 128:(kt + 1) * 128], QA[r], extra, (bias_ap(si, 4 * qt - kt), bias_tab.b),
                                     Vw, Vw.ap[:, kt, :], NUMs[1], None, 64, Ps.next(), i == 0, i == len(kts) - 1, win_tail)
                    pipe.flush()

        attention_l1()
        all_gather_ot()
        if STOP_AFTER == "L1ATT":
            S.barrier()
            cpb = Ring([AR.alloc(128, (1024,), BF16, f"cpb{i}") for i in range(2)])
            cpf = Ring([AR.alloc(128, (1024,), F32, f"cpf{i}") for i in range(2)])
            ov = out_d.rearrange("(a b) c -> a (b c)", a=1024)
            for k in range(8):
                for cb in range(8):
                    b_ = cpb.next()
                    f_ = cpf.next()
                    load(b_, ot[k * 128:(k + 1) * 128, cb * 1024:(cb + 1) * 1024])
                    S.op("dve", lambda e, b_=b_, f_=f_: e.tensor_copy(f_.ap, b_.ap), reads=[b_.b], writes=[f_.b])
                    store(ov[k * 128:(k + 1) * 128, cb * 1024:(cb + 1) * 1024], f_)
            S.barrier()
            S.finalize()
            return nc, S
        phase_reset()
        phase_mlp(x2, od_w_out, mlp_w1[1], mlp_w2[1], 3, out_d, final_gain=final_norm)
        S.barrier()
        S.finalize()
    return nc, S


def _cols(*ranges):
    return np.concatenate([np.arange(a, b) for a, b in ranges])


def kernel(**inputs):
    consts = [make_consts(0), make_consts(1)]
    nc, S = build_program(consts[0])
    x = np.ascontiguousarray(inputs["x"], dtype=np.float32)
    B = x.shape[0]
    shared = {}
    for k in ("attn_norm", "mlp_norm", "final_norm", "mlp_w1", "mlp_w2"):
        shared[k] = np.ascontiguousarray(inputs[k], dtype=np.float32)
    for k in ("ev_subln", "od_cmp_pos_k", "od_cmp_k_w1", "od_cmp_k_w2", "od_cmp_pos_v", "od_cmp_v_w1", "od_cmp_v_w2"):
        shared[k] = np.ascontiguousarray(inputs[k][0], dtype=np.float32)
    shared["lamv"] = np.ascontiguousarray(np.stack([inputs["ev_lam_q1"][0], inputs["ev_lam_k1"][0],
                                                    inputs["ev_lam_q2"][0], inputs["ev_lam_k2"][0]]), dtype=np.float32)
    w0 = np.asarray(inputs["ev_w_in"][0], dtype=np.float32)
    w1 = np.asarray(inputs["od_w_in"][0], dtype=np.float32)
    per = []
    feat0 = []
    feat1 = []
    for p in range(2):
        sb = (256 * p, 256 * p + 256)
        dA, dB = DIFF_ASSIGN[p]
        c0 = _cols((0 + sb[0], 0 + sb[1]), (512 + sb[0], 512 + sb[1]),
                   (1536 + 128 * dA, 1536 + 128 * dA + 128), (1536 + 128 * dB, 1536 + 128 * dB + 128),
                   (2048 + 128 * dA, 2048 + 128 * dA + 128), (2048 + 128 * dB, 2048 + 128 * dB + 128),
                   (1024 + sb[0], 1024 + sb[1]),
                   (2560 + 128 * dA, 2560 + 128 * dA + 128), (2560 + 128 * dB, 2560 + 128 * dB + 128))
        gA, gB = GROUP_ASSIGN[p]
        kv = lambda base: [(base + 64 * gA, base + 64 * gA + 64), (base + 64 * gB, base + 64 * gB + 64)]
        c1 = _cols((256 * gA, 256 * gA + 256), (256 * gB, 256 * gB + 256),
                   *kv(1024), *kv(1280), *kv(1536), *kv(2048),
                   (2560 + 12 * gA, 2560 + 12 * gA + 12), (2560 + 12 * gB, 2560 + 12 * gB + 12),
                   *kv(1792), *kv(2304))
        per.append({"ev_w_in": np.ascontiguousarray(w0[:, c0]), "od_w_in": np.ascontiguousarray(w1[:, c1])})
        feat0.append(_cols(sb, (512 + 128 * dA, 512 + 128 * dA + 128), (512 + 128 * dB, 512 + 128 * dB + 128)))
        feat1.append(_cols((256 * gA, 256 * gA + 256), (256 * gB, 256 * gB + 256)))
    perm0 = np.concatenate(feat0)
    perm1 = np.concatenate(feat1)
    shared["ev_w_out"] = np.ascontiguousarray(np.asarray(inputs["ev_w_out"][0], dtype=np.float32)[perm0])
    shared["od_w_out"] = np.ascontiguousarray(np.asarray(inputs["od_w_out"][0], dtype=np.float32)[perm1])
    in_maps = []
    for core in range(8):
        p = core % 2
        m = dict(shared)
        m.update(per[p])
        for k, v in consts[p].items():
            m["c_" + k] = v
        xb = x[(core // 2) % B]
        m["x"] = xb
        m["xh"] = np.ascontiguousarray(xb[p * (SEQ // 2):(p + 1) * (SEQ // 2)])
        ms = np.zeros((128, 2), np.float32)
        ms[:, p] = 1.0
        m["msel"] = ms
        in_maps.append(m)
    res = run_bass_kernel_spmd(nc, in_maps, core_ids=list(range(8)))
    out = np.stack([np.concatenate([np.asarray(res.results[2 * b + p]["out"], dtype=np.float32) for p in range(2)], axis=0)
                    for b in range(B)])
    return out
```

```python
import math
import numpy as np
import ml_dtypes
from contextlib import ExitStack
import concourse.bass as bass
import concourse.mybir as mybir
from concourse.bass_utils import run_bass_kernel_spmd

F32 = mybir.dt.float32
BF16 = mybir.dt.bfloat16
AF = mybir.ActivationFunctionType
ALU = mybir.AluOpType
bf = ml_dtypes.bfloat16

SEQ = 8192
D = 1024
NQT = SEQ // 512
NKT = SEQ // 128
NEG = -30000.0
SB_BACK = 3
ALIBI_CUT = 128.0
SEM_WRAP = 16000
DMA_WRAP = 1000
STOP_AFTER = None


class Buf:
    __slots__ = ("name", "w", "r")

    def __init__(self, name):
        self.name = name
        self.w = []
        self.r = []


class Op:
    __slots__ = ("eng", "fn", "deps", "idx", "needs_inc", "waits", "is_dma", "dkey", "dval")

    def __init__(self, eng, fn):
        self.eng = eng
        self.fn = fn
        self.deps = set()
        self.idx = -1
        self.needs_inc = False
        self.waits = []
        self.is_dma = False
        self.dkey = None
        self.dval = 0


class T:
    __slots__ = ("ap", "b")

    def __init__(self, ap, b):
        self.ap = ap
        self.b = b

    def __getitem__(self, k):
        return self.ap[k]


class Sched:
    ENGS = ("pe", "act", "dve", "pool", "sp")

    def __init__(self, nc, stack):
        self.nc = nc
        self.stack = stack
        self.ops = []
        self.eng_ops = {e: [] for e in self.ENGS}
        self.dma_count = {}
        self.bufs = []

    def buf(self, name):
        b = Buf(name)
        self.bufs.append(b)
        return b

    def op(self, eng, fn, reads=(), writes=(), dma_key=None):
        o = Op(eng, fn)
        oid = len(self.ops)
        for b in reads:
            o.deps.update(b.w)
        for b in writes:
            o.deps.update(b.w)
            o.deps.update(b.r)
        for b in reads:
            b.r.append(oid)
        for b in writes:
            b.w = [oid]
            b.r = []
        if dma_key is not None:
            o.is_dma = True
            o.dkey = dma_key
            n = self.dma_count.get(dma_key, 0) + 1
            self.dma_count[dma_key] = n
            o.dval = n
        o.idx = len(self.eng_ops[eng])
        self.eng_ops[eng].append(o)
        self.ops.append(o)
        return oid

    def barrier(self):
        live = [b for b in self.bufs if b.w or b.r]
        first = True
        sync = self.buf("barrier")
        for e in self.ENGS:
            if first:
                self.op(e, lambda eng: eng.nop(), writes=live + [sync])
                first = False
            else:
                self.op(e, lambda eng: eng.nop(), reads=[sync])
        self.bufs = [sync]

    def finalize(self):
        nc = self.nc
        ops = self.ops
        know = {e: {} for e in self.ENGS}
        comp_know = [None] * len(ops)
        for oid, o in enumerate(ops):
            K = know[o.eng]
            for d in sorted(o.deps):
                p = ops[d]
                if p.is_dma:
                    dom = ("d", p.dkey)
                    val = p.dval
                else:
                    dom = ("e", p.eng)
                    val = p.idx + 1
                    if p.eng == "pe" and o.eng == "pe":
                        continue
                if K.get(dom, 0) >= val:
                    continue
                o.waits.append((dom, val))
                p.needs_inc = True
                for k2, v2 in comp_know[d].items():
                    if K.get(k2, 0) < v2:
                        K[k2] = v2
            ck = dict(K)
            if o.is_dma:
                ck[("d", o.dkey)] = o.dval
            else:
                ck[("e", o.eng)] = o.idx + 1
            comp_know[oid] = ck
        comp_know = None
        eng_sems = {}
        counts = {}
        for e in self.ENGS:
            c = 0
            for o in self.eng_ops[e]:
                if o.is_dma:
                    continue
                if o.needs_inc:
                    c += 1
                counts[(e, o.idx)] = c
            nsem = (c + SEM_WRAP - 1) // SEM_WRAP
            eng_sems[e] = [self.stack.enter_context(nc.semaphore(f"s_{e}_{i}")) for i in range(nsem)]
        dma_sems = {}
        for k, n in self.dma_count.items():
            nsem = (n + DMA_WRAP - 1) // DMA_WRAP
            dma_sems[k] = [self.stack.enter_context(nc.semaphore(f"d_{k}_{i}")) for i in range(nsem)]

        def sem_for(dom, val):
            if dom[0] == "e":
                c = counts[(dom[1], val - 1)]
                return eng_sems[dom[1]][(c - 1) // SEM_WRAP], (c - 1) % SEM_WRAP + 1
            mul = 1 if dom[1].startswith("cc") else 16
            return dma_sems[dom[1]][(val - 1) // DMA_WRAP], ((val - 1) % DMA_WRAP + 1) * mul

        self.n_waits = sum(len(o.waits) for o in ops)
        with nc.Block() as block:
            def make(e):
                def body(eng):
                    for o in self.eng_ops[e]:
                        for dom, val in o.waits:
                            s, v = sem_for(dom, val)
                            eng.wait_ge(s, v)
                        ins = o.fn(eng)
                        if o.is_dma:
                            s, v = sem_for(("d", o.dkey), o.dval)
                            ins.then_inc(s, 1 if o.dkey.startswith("cc") else 16)
                        elif o.needs_inc:
                            c = counts[(e, o.idx)]
                            ins.then_inc(eng_sems[e][(c - 1) // SEM_WRAP], 1)
                return body
            block.tensor(make("pe"))
            block.scalar(make("act"))
            block.vector(make("dve"))
            block.gpsimd(make("pool"))
            block.sync(make("sp"))


class Arena:
    def __init__(self, S, tens, nwords):
        self.S = S
        self.t = tens
        self.n = nwords
        self.off = 0
        self.cnt = 0

    def reset(self):
        self.off = 0

    def alloc(self, parts, shape, dt, name):
        n = 1
        for s in shape:
            n *= s
        words = n if dt == F32 else (n + 1) // 2
        v = self.t[0:parts, self.off:self.off + words]
        self.off += words
        assert self.off <= self.n, f"arena overflow at {name}: {self.off} > {self.n}"
        if dt != F32:
            v = v.bitcast(dt)
        if len(shape) == 2:
            v = v.rearrange("p (a b) -> p a b", a=shape[0])
        elif len(shape) == 3:
            v = v.rearrange("p (a b c) -> p a b c", a=shape[0], b=shape[1])
        self.cnt += 1
        return T(v, self.S.buf(name))


def _split3(v):
    v = np.asarray(v, np.float32)
    hi = v.astype(bf)
    r1 = v - hi.astype(np.float32)
    mid = r1.astype(bf)
    r2 = r1 - mid.astype(np.float32)
    lo = r2.astype(bf)
    return hi, mid, lo


DIFF_SLOPES = [2.0 ** (-8.0 * (h + 1) / 4) for h in range(4)]
NSA_SLOPES = [2.0 ** (-8.0 * (h + 1) / 16) for h in range(16)]
DIFF_ASSIGN = ((0, 3), (1, 2))
GROUP_ASSIGN = ((0, 3), (1, 2))
SB_PER_CORE = 4


def slopes_for(p):
    sl = [DIFF_SLOPES[h] for h in DIFF_ASSIGN[p]]
    for g in GROUP_ASSIGN[p]:
        sl += [NSA_SLOPES[4 * g + r] for r in range(4)]
    return sl


NSLOT = 10
BIAS_M0 = -3
BIAS_NM = 68


def make_consts(p):
    ALL_SLOPES = slopes_for(p)
    c = {}
    j = np.arange(128)
    t = np.arange(512)
    c["ident"] = np.eye(128, dtype=np.float32).astype(bf)
    c["ones"] = np.ones((128, 128), np.float32).astype(bf)
    c["negones"] = (-np.ones((128, 128), np.float32)).astype(bf)
    c["uneg"] = (-(j[:, None] >= j[None, :]).astype(np.float32)).astype(bf)
    c["onesdiv"] = np.full((128, 128), 1.0 / 128, np.float32)
    msb = np.zeros((4, 128, 512), np.float32)
    mc = np.zeros((4, 128, 512), np.float32)
    for o in range(4):
        jj = 128 * o + j[:, None]
        msb[o] = np.where(jj >= t[None, :], NEG, 0.0)
        mc[o] = np.where(jj > t[None, :], NEG, 0.0)
    c["mask_sb"] = np.ascontiguousarray(msb.transpose(1, 0, 2)).astype(bf)
    c["mask_c"] = np.ascontiguousarray(mc.transpose(1, 0, 2)).astype(bf)
    kr = np.zeros((6, SEQ), np.float32)
    kr[0:3] = 1.0
    kr[3:6] = (np.arange(SEQ) % 128)[None, :]
    c["kaug"] = kr.astype(bf)
    kc = np.zeros((6, 512), np.float32)
    kc[0:3] = 1.0
    kc[3:6] = (16 * (np.arange(512) % 128))[None, :]
    c["kaug_cmp"] = kc.astype(bf)
    qa = np.zeros((len(ALL_SLOPES), 6, 512), np.float32).astype(bf)
    for i, s in enumerate(ALL_SLOPES):
        s32 = np.float32(s)
        v = (-(s32 * t.astype(np.float32))).astype(np.float32)
        h3 = _split3(v)
        s3 = _split3(np.full(512, s32, np.float32))
        for r in range(3):
            qa[i, r] = h3[r]
            qa[i, 3 + r] = s3[r]
    c["qaug"] = np.ascontiguousarray(qa.transpose(1, 0, 2))
    bt = np.zeros((len(ALL_SLOPES), BIAS_NM), np.float32)
    for i, s in enumerate(ALL_SLOPES):
        for mi in range(BIAS_NM):
            bt[i, mi] = -np.float32(s) * np.float32(128 * (mi + BIAS_M0))
    c["bias_tab"] = np.broadcast_to(bt.reshape(1, -1), (128, bt.size)).copy()
    bc = np.zeros((8, NQT, 4), np.float32)
    for h, s in enumerate(ALL_SLOPES[2:]):
        for qt in range(NQT):
            for cc in range(4):
                bc[h, qt, cc] = -np.float32(s) * np.float32(512 * qt - 2048 * cc - 31)
    c["bias_cmp"] = np.broadcast_to(bc.reshape(1, -1), (128, bc.size)).copy()
    mcm = np.zeros((5, 128, 512), np.float32)
    for rel in range(5):
        mcm[rel] = np.where(512 * rel + t[None, :] >= 16 * j[:, None] + 31, 0.0, NEG)
    c["mask_cmp"] = np.ascontiguousarray(mcm.transpose(1, 0, 2)).astype(bf)
    mw = np.zeros((8, 128, 512), np.float32)
    for oi, o in enumerate(range(-4, 4)):
        dd = t[None, :] - j[:, None] - 128 * o
        mw[oi] = np.where((dd >= 0) & (dd <= 511), 0.0, NEG)
    c["mask_win"] = np.ascontiguousarray(mw.transpose(1, 0, 2)).astype(bf)
    cc = np.arange(SEQ)
    c["ewide"] = (cc[None, :] // 64 == j[:, None]).astype(np.float32).astype(bf)
    n = np.arange(512)
    s_ = np.arange(128)
    ov = ((n[:, None] >= 4 * s_[None, :] - 1) & (n[:, None] <= 4 * s_[None, :] + 3) & (n[:, None] < 511))
    c["ovl"] = np.ascontiguousarray(ov.astype(np.float32).reshape(4, 128, 128).transpose(1, 0, 2)).astype(bf)
    q = np.arange(128)
    u = np.arange(-127, 129)
    cur = (q >= 64).astype(np.int64)
    A = np.zeros((128, 256), np.float32)
    M = np.ones((128, 256), np.float32)
    fut = u[None, :] > cur[:, None]
    A[fut] = -1.0
    M[fut] = 0.0
    f1 = u[None, :] == cur[:, None]
    f2 = u[None, :] == cur[:, None] - 1
    A[f1] = 1.0e6 + 1.0
    M[f1] = 0.0
    A[f2] = 1.0e6 + 2.0
    M[f2] = 0.0
    c["topk_a"] = A
    c["topk_m"] = M
    gs = np.zeros((48, 48, 64), np.float32)
    for r in range(48):
        gs[r, r, :] = 1.0
    c["gsel"] = gs.reshape(48, 48 * 64).astype(bf)
    return c


CONST_SPECS = None


def _dt_of(a):
    return BF16 if a.dtype == bf else F32


def build_program(consts):
    nc = bass.Bass("TRN2", target_bir_lowering=False)
    dr = {}

    def din(name, shape, dt=F32):
        dr[name] = nc.dram_tensor(name, list(shape), dt, kind="ExternalInput").ap()
        return dr[name]

    def dscr(name, shape, dt):
        dr[name] = nc.dram_tensor(name, list(shape), dt, kind="Internal").ap()
        return dr[name]

    x_in = din("x", (SEQ, D))
    attn_norm = din("attn_norm", (2, D))
    mlp_norm = din("mlp_norm", (2, D))
    final_norm = din("final_norm", (D,))
    ev_w_in = din("ev_w_in", (D, 1536))
    lamv = din("lamv", (4, 64))
    ev_subln = din("ev_subln", (128,))
    ev_w_out = din("ev_w_out", (D, D))
    od_w_in = din("od_w_in", (D, 1304))
    pos_k = din("od_cmp_pos_k", (32, 64))
    cw1k = din("od_cmp_k_w1", (2048, 256))
    cw2k = din("od_cmp_k_w2", (256, 64))
    pos_v = din("od_cmp_pos_v", (32, 64))
    cw1v = din("od_cmp_v_w1", (2048, 256))
    cw2v = din("od_cmp_v_w2", (256, 64))
    od_w_out = din("od_w_out", (D, D))
    mlp_w1 = din("mlp_w1", (2, D, 4096))
    mlp_w2 = din("mlp_w2", (2, 4096, D))
    cd = {k: din("c_" + k, v.shape, _dt_of(v)) for k, v in consts.items()}
    out_d = nc.dram_tensor("out", [SEQ // 2, D], F32, kind="ExternalOutput").ap()
    xh_in = din("xh", (SEQ // 2, D))
    msel_in = din("msel", (128, 2))

    qk0 = dscr("qk0", (16, 64, SEQ), BF16)
    v0 = dscr("v0", (SEQ, 512), BF16)
    ot = dscr("ot", (D, SEQ), BF16)
    ot_p = dscr("ot_p", (512, SEQ), BF16)
    ot4 = dscr("ot4", (4, 256, SEQ), BF16)
    x2 = dscr("x2", (SEQ // 2, D), F32)
    x2g = dscr("x2g", (8, 1024, D), F32)
    q1 = dscr("q1", (8, 64, SEQ), BF16)
    kf1 = dscr("kf1", (8, 64, SEQ), BF16)
    v1 = dscr("v1", (SEQ, 256), BF16)
    gts = dscr("gts", (24, SEQ), BF16)
    kcmp = dscr("kcmp", (2, 64, 512), BF16)
    vcmp = dscr("vcmp", (2, 512, 64), BF16)
    x1s = dscr("x1s", (SEQ // 2, D), F32)
    uts_d = dscr("uts", (32, 128, SEQ // 2), BF16)

    with ExitStack() as st:
        S = Sched(nc, st)
        NW = 50 * 1024
        arena_t = st.enter_context(nc.sbuf_tensor("arena", [128, NW], F32))
        AR = Arena(S, arena_t, NW)
        pbanks = [st.enter_context(nc.psum_tensor(f"pb{i}", [128, 512], F32)) for i in range(8)]

        def PS(i, name, parts=128, cols=512, dt=F32):
            ap = pbanks[i][0:parts, :]
            if dt != F32:
                ap = ap.bitcast(dt)
            ap = ap[:, 0:cols]
            return T(ap, S.buf(name))

        def dma(q, out, in_, reads, writes, key, slow=False):
            if slow:
                S.op(q, lambda e: e.dma_start(out=out, in_=in_, allow_slow_non_contiguous=True), reads=reads, writes=writes, dma_key=key)
            else:
                S.op(q, lambda e: e.dma_start(out=out, in_=in_), reads=reads, writes=writes, dma_key=key)

        def load_vt(VT, src_cols, width):
            for g4 in range(4):
                dma("sp", VT.ap[:, g4 * 16:(g4 + 1) * 16, 0:width],
                    src_cols[g4 * 2048:(g4 + 1) * 2048, :].rearrange("(t p) c -> p t c", p=128), [], [VT.b], "LVT")

        def load(dst, src, q="sp"):
            dma(q, dst.ap, src, [], [dst.b], "L" + dst.b.name)

        def store(dst, src, ap=None, q="pool"):
            dma(q, dst, src.ap if ap is None else ap, [src.b], [], "S" + src.b.name)

        def mm(out, outap, lhsT, lhsap, rhs, rhsap, start, stop, extra_r=()):
            S.op("pe", lambda e: e.matmul(outap, lhsap, rhsap, start=start, stop=stop),
                 reads=[lhsT.b, rhs.b] + list(extra_r), writes=[out.b])

        class Ring:
            def __init__(self, items):
                self.items = items
                self.i = 0

            def next(self):
                it = self.items[self.i % len(self.items)]
                self.i += 1
                return it

        rr_cast = [0]

        def cast(dst, dstap, src, srcap, scale_ap=None, scale_t=None):
            e = ("dve", "act", "pool")[rr_cast[0] % 3] if scale_ap is None else ("dve", "act")[rr_cast[0] % 2]
            rr_cast[0] += 1
            rd = [src.b] + ([scale_t.b] if scale_t is not None else [])
            if e == "act":
                if scale_ap is None:
                    S.op("act", lambda en: en.copy(dstap, srcap), reads=rd, writes=[dst.b])
                else:
                    S.op("act", lambda en: en.activation(dstap, srcap, AF.Copy, scale=scale_ap), reads=rd, writes=[dst.b])
            elif e == "dve":
                if scale_ap is None:
                    S.op("dve", lambda en: en.tensor_copy(dstap, srcap), reads=rd, writes=[dst.b])
                else:
                    S.op("dve", lambda en: en.tensor_scalar(dstap, srcap, scale_ap, None, ALU.mult), reads=rd, writes=[dst.b])
            else:
                S.op("pool", lambda en: en.tensor_copy(dstap, srcap), reads=rd, writes=[dst.b])

        def load_weight(dst, src_ap, K, N, stage_ring, gain=None, col0=0, ncols=None, dcol0=0):
            ncols = N if ncols is None else ncols
            for k in range(K // 128):
                c = 0
                while c < ncols:
                    w = min(2048, ncols - c)
                    stg = stage_ring.next()
                    dma("sp", stg.ap[:, 0:w], src_ap[k * 128:(k + 1) * 128, col0 + c:col0 + c + w], [], [stg.b], "L" + stg.b.name)
                    cast(dst, dst.ap[:, k, dcol0 + c:dcol0 + c + w], stg, stg.ap[:, 0:w],
                         None if gain is None else gain.ap[:, k:k + 1], gain)
                    c += w

        NWP = 0
        ident = AR.alloc(128, (128,), BF16, "ident")
        load(ident, cd["ident"])
        ones = AR.alloc(128, (128,), BF16, "ones")
        load(ones, cd["ones"])
        negones = AR.alloc(128, (128,), BF16, "negones")
        load(negones, cd["negones"])
        uneg = AR.alloc(128, (128,), BF16, "uneg")
        load(uneg, cd["uneg"])
        onesdiv = AR.alloc(128, (128,), F32, "onesdiv")
        load(onesdiv, cd["onesdiv"])
        bias_tab = AR.alloc(128, (NSLOT * BIAS_NM,), F32, "bias_tab")
        load(bias_tab, cd["bias_tab"])
        gains = AR.alloc(128, (4, 8), F32, "gains")
        for li in range(2):
            dma("sp", gains.ap[:, 2 * li, :], attn_norm[li].rearrange("(k p) -> p k", p=128), [], [gains.b], "Lgains", slow=True)
            dma("sp", gains.ap[:, 2 * li + 1, :], mlp_norm[li].rearrange("(k p) -> p k", p=128), [], [gains.b], "Lgains", slow=True)
        eps_t = AR.alloc(128, (1,), F32, "eps_t")
        S.op("dve", lambda e: e.memset(eps_t.ap, 1e-6), writes=[eps_t.b])
        msel = AR.alloc(128, (2,), F32, "msel")
        load(msel, msel_in)
        persist_off = AR.off

        def bias_ap(si, m):
            col = si * BIAS_NM + (m - BIAS_M0)
            return bias_tab.ap[:, col:col + 1]

        def phase_reset():
            S.barrier()
            AR.off = persist_off

        def norm_transpose(xt, hn, ss, rs, junk, tp, hT, r):
            S.op("act", lambda e: e.activation(junk.ap, xt.ap, AF.Square, accum_out=ss.ap), reads=[xt.b], writes=[junk.b, ss.b])
            S.op("act", lambda e: e.activation(rs.ap, ss.ap, AF.Sqrt, bias=eps_t.ap, scale=1.0 / D), reads=[ss.b, eps_t.b], writes=[rs.b])
            S.op("dve", lambda e: e.reciprocal(rs.ap, rs.ap), reads=[rs.b], writes=[rs.b])
            S.op("act", lambda e: e.activation(hn.ap, xt.ap, AF.Copy, scale=rs.ap), reads=[xt.b, rs.b], writes=[hn.b])
            for k in range(8):
                S.op("pe", lambda e, k=k: e.transpose(tp.ap[:, k * 128:(k + 1) * 128], hn.ap[:, k * 128:(k + 1) * 128], ident.ap),
                     reads=[hn.b, ident.b], writes=[tp.b])
            S.op("dve", lambda e: e.tensor_copy(hT.ap[:, :, r * 128:(r + 1) * 128], tp.ap.rearrange("p (k c) -> p k c", k=8)),
                 reads=[tp.b], writes=[hT.b])

        def phase_proj(x_src, w_src, ncols_total, gain_idx, fm_chunks, tm_chunks):
            stage = Ring([AR.alloc(128, (2048,), F32, f"wstg{i}") for i in range(2)])
            W = AR.alloc(128, (8, ncols_total), BF16, "Win")
            gsl = T(gains.ap[:, gain_idx, :], gains.b)
            load_weight(W, w_src, D, ncols_total, stage, gain=gsl)
            xts = Ring([AR.alloc(128, (D,), F32, f"xt{i}") for i in range(2)])
            hns = Ring([AR.alloc(128, (D,), BF16, f"hn{i}") for i in range(2)])
            junk = AR.alloc(128, (D,), BF16, "junk")
            sss = Ring([AR.alloc(128, (1,), F32, f"ss{i}") for i in range(2)])
            rss = Ring([AR.alloc(128, (1,), F32, f"rs{i}") for i in range(2)])
            hTs = Ring([AR.alloc(128, (8, 512), BF16, f"hT{i}") for i in range(2)])
            tps = Ring([PS(i, f"tp{i}", cols=1024, dt=BF16) for i in (0, 1)])
            pps = Ring([PS(i, f"pp{i}") for i in (2, 3, 4, 5)])
            evs = Ring([AR.alloc(128, (512,), BF16, f"ev{i}") for i in range(4)])
            ev_i = [0]
            for tb in range(NQT):
                hT = hTs.next()
                for r in range(4):
                    xt = xts.next()
                    row0 = tb * 512 + r * 128
                    load(xt, x_src(row0))
                    norm_transpose(xt, hns.next(), sss.next(), rss.next(), junk, tps.next(), hT, r)
                for (col0, M, dsts, scale, func) in fm_chunks:
                    pp = pps.next()
                    for k in range(8):
                        mm(pp, pp.ap[0:M, :], W, W.ap[:, k, col0:col0 + M], hT, hT.ap[:, k, :], k == 0, k == 7)
                    ev = evs.next()
                    eng = ("act", "dve")[ev_i[0] % 2] if func is None else "act"
                    ev_i[0] += 1
                    if eng == "act":
                        S.op("act", lambda e, pp=pp, ev=ev, M=M, scale=scale, func=func: e.activation(
                            ev.ap[0:M, :], pp.ap[0:M, :], AF.Copy if func is None else func, scale=scale),
                            reads=[pp.b], writes=[ev.b])
                    else:
                        S.op("dve", lambda e, pp=pp, ev=ev, M=M, scale=scale: e.tensor_scalar(
                            ev.ap[0:M, :], pp.ap[0:M, :], float(scale), None, ALU.mult), reads=[pp.b], writes=[ev.b])
                    for (dfn, p0, npart) in dsts:
                        store(dfn(tb), ev, ap=ev.ap[p0:p0 + npart, :])
                for (col0, ncols, dfn) in tm_chunks:
                    for r in range(4):
                        pp = pps.next()
                        for k in range(8):
                            mm(pp, pp.ap[:, 0:ncols], hT, hT.ap[:, k, r * 128:(r + 1) * 128], W, W.ap[:, k, col0:col0 + ncols], k == 0, k == 7)
                        ev = evs.next()
                        eng = ("act", "dve")[ev_i[0] % 2]
                        ev_i[0] += 1
                        if eng == "act":
                            S.op("act", lambda e, pp=pp, ev=ev, n=ncols: e.copy(ev.ap[:, 0:n], pp.ap[:, 0:n]), reads=[pp.b], writes=[ev.b])
                        else:
                            S.op("dve", lambda e, pp=pp, ev=ev, n=ncols: e.tensor_copy(ev.ap[:, 0:n], pp.ap[:, 0:n]), reads=[pp.b], writes=[ev.b])
                        store(dfn(tb * 512 + r * 128), ev, ap=ev.ap[:, 0:ncols])

        fm = []
        for jc in range(8):
            isq = jc in (0, 1, 4, 5)
            dsts = [(lambda tb, slot=2 * jc + half: qk0[slot, :, tb * 512:(tb + 1) * 512], 64 * half, 64) for half in range(2)]
            fm.append((128 * jc, 128, dsts, 0.125 if isq else 1.0, None))
        tm = [(1024, 512, lambda row0: v0[row0:row0 + 128, 0:512])]
        phase_proj(lambda row0: x_in[row0:row0 + 128, :], ev_w_in, 1536, 0, fm, tm)
        phase_reset()

        class Pipe:
            def __init__(self):
                self.q = []
                self.t = 0

            def job(self, stages):
                for lag, fn in stages:
                    if lag == 0:
                        fn()
                    else:
                        self.q.append((self.t + lag, fn))
                rest = []
                for due, fn in self.q:
                    if due <= self.t:
                        fn()
                    else:
                        rest.append((due, fn))
                self.q = rest
                self.t += 1

            def flush(self):
                for due, fn in sorted(self.q, key=lambda x_: x_[0]):
                    fn()
                self.q = []

        def ktiles_for(qt, back):
            hi = 4 * qt + 3
            lo = 0 if back is None else max(0, 4 * qt - back)
            return list(range(hi, lo - 1, -1))

        def alibi_back(slope):
            dcut = ALIBI_CUT / slope
            bk = int(math.floor((dcut + 127.0) / 128.0))
            return bk

        ccbuf = S.buf("ccbuf")
        cc_n = [0]

        def all_gather_ot():
            S.barrier()
            for k in range(4):
                cc_n[0] += 1
                S.op("pool", lambda e, k=k: e.collective_compute("AllGather", ALU.bypass, replica_groups=[[0, 1], [2, 3], [4, 5], [6, 7]],
                                                                 ins=[ot_p[k * 128:(k + 1) * 128, :]], outs=[ot4[k]]),
                     writes=[ccbuf], dma_key="cc")
            S.bufs.append(ccbuf)

        def all_gather_x2():
            S.barrier()
            for k in range(8):
                cc_n[0] += 1
                S.op("pool", lambda e, k=k: e.collective_compute("AllGather", ALU.bypass, replica_groups=[[0, 1], [2, 3], [4, 5], [6, 7]],
                                                                 ins=[x2[k * 512:(k + 1) * 512, :]], outs=[x2g[k]]),
                     writes=[ccbuf], dma_key="cc")
            S.bufs.append(ccbuf)

        def attention_l0():
            mask_sb = AR.alloc(128, (4, 512), BF16, "mask_sb")
            load(mask_sb, cd["mask_sb"])
            mask_c = AR.alloc(128, (4, 512), BF16, "mask_c")
            load(mask_c, cd["mask_c"])
            KT = AR.alloc(70, (SEQ,), BF16, "KT")
            VT = AR.alloc(128, (NKT, 128), BF16, "VT")
            QTs = Ring([AR.alloc(70, (512,), BF16, f"QT{i}") for i in range(2)])
            Es = Ring([AR.alloc(128, (512,), F32, f"E{i}") for i in range(2)])
            SPs = Ring([AR.alloc(128, (512,), BF16, f"SP{i}") for i in range(4)])
            Ws = Ring([AR.alloc(128, (512,), BF16, f"Wt{i}") for i in range(4)])
            ssum = AR.alloc(128, (512,), F32, "ssum")
            sshs = Ring([AR.alloc(128, (512,), BF16, f"ssh{i}") for i in range(4)])
            oev = Ring([AR.alloc(128, (512,), BF16, f"oev{i}") for i in range(2)])
            Ys = Ring([PS(i, f"Y{i}") for i in (0, 1, 2)])
            Oaccs = [PS(3, "Oacc0"), PS(4, "Oacc1")]
            for h in range(SB_PER_CORE):
                pipe = Pipe()
                load(T(KT.ap[0:64, :], KT.b), qk0[4 + h])
                load_vt(VT, v0[:, h * 64:(h + 1) * 64], 64)
                for qt in range(NQT):
                    QT = QTs.next()
                    dma("sp", QT.ap[0:64, :], qk0[h, :, qt * 512:(qt + 1) * 512], [], [QT.b], "L" + QT.b.name)
                    kts = ktiles_for(qt, SB_BACK)
                    n = len(kts)
                    Oacc = Oaccs[qt % 2]
                    ssh_prev = None
                    for i, kt in enumerate(kts):
                        Y = Ys.next()
                        E = Es.next()
                        SP = SPs.next()
                        Wt = Ws.next()
                        ssh = sshs.next() if 0 < i < n - 1 else None
                        o = kt - 4 * qt

                        def st0(Y=Y, E=E, SP=SP, o=o, kt=kt, QT=QT, i=i, n=n, ssh=ssh):
                            mm(Y, Y.ap, KT, KT.ap[0:64, kt * 128:(kt + 1) * 128], QT, QT.ap[0:64, :], True, False)
                            if o >= 0:
                                mm(Y, Y.ap, ident, ident.ap, mask_sb, mask_sb.ap[:, o, :], False, False)
                            S.op("act", lambda e: e.activation(E.ap, Y.ap, AF.Exp), reads=[Y.b], writes=[E.b])
                            S.op("act", lambda e: e.activation(SP.ap, E.ap, AF.Ln, bias=1.0), reads=[E.b], writes=[SP.b])
                            if i < n - 1:
                                if i == 0:
                                    S.op("pool", lambda e: e.tensor_copy(ssum.ap, SP.ap), reads=[SP.b], writes=[ssum.b])
                                else:
                                    S.op("pool", lambda e: e.tensor_tensor(ssum.ap, ssum.ap, SP.ap, ALU.add), reads=[SP.b, ssum.b], writes=[ssum.b])
                                    S.op("pool", lambda e: e.tensor_copy(ssh.ap, ssum.ap), reads=[ssum.b], writes=[ssh.b])

                        def st1(Y=Y, SP=SP, Wt=Wt, prev=ssh_prev):
                            mm(Y, Y.ap, uneg, uneg.ap, SP, SP.ap, False, prev is None)
                            if prev is not None:
                                mm(Y, Y.ap, negones, negones.ap, prev, prev.ap, False, True)
                            S.op("act", lambda e: e.activation(Wt.ap, Y.ap, AF.Exp), reads=[Y.b], writes=[Wt.b])

                        def st2(Wt=Wt, kt=kt, i=i, n=n, Oacc=Oacc, h=h, qt=qt):
                            mm(Oacc, Oacc.ap[0:64, :], VT, VT.ap[:, kt, 0:64], Wt, Wt.ap, i == 0, i == n - 1)
                            if i == n - 1:
                                ev = oev.next()
                                S.op("dve", lambda e: e.tensor_copy(ev.ap[0:64, :], Oacc.ap[0:64, :]), reads=[Oacc.b], writes=[ev.b])
                                store(ot_p[h * 64:(h + 1) * 64, qt * 512:(qt + 1) * 512], ev, ap=ev.ap[0:64, :])

                        pipe.job([(0, st0), (1, st1), (2, st2)])
                        if i < n - 1:
                            ssh_prev = SP if i == 0 else ssh
                pipe.flush()
            lam_t = AR.alloc(128, (4, 64), F32, "lam_t")
            for i in range(4):
                dma("sp", lam_t.ap[:, i, :], lamv[i].partition_broadcast(128), [], [lam_t.b], "Llam")
            lam_p = AR.alloc(128, (2, 64), F32, "lam_p")
            lam_s = AR.alloc(128, (2,), F32, "lam_s")
            neglam = AR.alloc(128, (1,), F32, "neglam")
            S.op("dve", lambda e: e.tensor_tensor(lam_p.ap[:, 0, :], lam_t.ap[:, 0, :], lam_t.ap[:, 1, :], ALU.mult), reads=[lam_t.b], writes=[lam_p.b])
            S.op("dve", lambda e: e.tensor_tensor(lam_p.ap[:, 1, :], lam_t.ap[:, 2, :], lam_t.ap[:, 3, :], ALU.mult), reads=[lam_t.b, lam_p.b], writes=[lam_p.b])
            S.op("dve", lambda e: e.reduce_sum(lam_s.ap, lam_p.ap, axis=mybir.AxisListType.X), reads=[lam_p.b], writes=[lam_s.b])
            S.op("act", lambda e: e.activation(lam_s.ap, lam_s.ap, AF.Exp), reads=[lam_s.b], writes=[lam_s.b])
            lam_init = 0.8 - 0.6 * math.exp(-0.3 * 0)
            S.op("dve", lambda e: e.tensor_tensor(neglam.ap, lam_s.ap[:, 1:2], lam_s.ap[:, 0:1], ALU.subtract), reads=[lam_s.b], writes=[neglam.b])
            S.op("dve", lambda e: e.tensor_scalar(neglam.ap, neglam.ap, -lam_init, None, ALU.add), reads=[neglam.b], writes=[neglam.b])
            sg = AR.alloc(128, (1,), F32, "sg")
            dma("sp", sg.ap, ev_subln.rearrange("(p o) -> p o", o=1), [], [sg.b], "Lsg")
            S.op("dve", lambda e: e.tensor_scalar(sg.ap, sg.ap, 1.0 - lam_init, None, ALU.mult), reads=[sg.b], writes=[sg.b])
            KT2 = [KT, AR.alloc(70, (SEQ,), BF16, "KTb")]
            for kk in KT2:
                dma("sp", kk.ap[64:70, :], cd["kaug"], [], [kk.b], "L" + kk.b.name)
            QT2 = [[AR.alloc(70, (512,), BF16, f"QD{c}{i}") for i in range(2)] for c in range(2)]
            Ps = Ring([AR.alloc(128, (512,), BF16, f"P{i}") for i in range(4)])
            NUM = [PS(3, "NUM0"), PS(4, "NUM1")]
            DEN = [PS(5, "DEN0"), PS(6, "DEN1")]
            MS = PS(7, "MS")
            r_t = [AR.alloc(128, (512,), F32, f"rden{c}") for c in range(2)]
            a_t = [AR.alloc(128, (512,), F32, f"a{c}") for c in range(2)]
            o_t = AR.alloc(128, (512,), F32, "o_t")
            sq_t = AR.alloc(128, (512,), F32, "sq_t")
            for h in range(2):
                pipe = Pipe()
                back = alibi_back(DIFF_SLOPES[1]) if h == 0 else None
                for c in range(2):
                    dma("sp", KT2[c].ap[0:64, :], qk0[12 + 2 * h + c], [], [KT2[c].b], "L" + KT2[c].b.name)
                load_vt(VT, v0[:, 256 + h * 128:256 + (h + 1) * 128], 128)
                for qt in range(NQT):
                    for c in range(2):
                        QT = QT2[c][qt % 2]
                        dma("sp", QT.ap[0:64, :], qk0[8 + 2 * h + c, :, qt * 512:(qt + 1) * 512], [], [QT.b], "L" + QT.b.name)
                        dma("sp", QT.ap[64:70, :], cd["qaug"][:, h, :], [], [QT.b], "L" + QT.b.name)
                        kts = ktiles_for(qt, back)
                        n = len(kts)
                        for i, kt in enumerate(kts):
                            Y = Ys.next()
                            P = Ps.next()
                            o = kt - 4 * qt

                            def st0(Y=Y, P=P, o=o, kt=kt, QT=QT, c=c, qt=qt, h=h):
                                mm(Y, Y.ap, KT2[c], KT2[c].ap[:, kt * 128:(kt + 1) * 128], QT, QT.ap, True, o < 0)
                                if o >= 0:
                                    mm(Y, Y.ap, ident, ident.ap, mask_c, mask_c.ap[:, o, :], False, True)
                                S.op("act", lambda e: e.activation(P.ap, Y.ap, AF.Exp, bias=bias_ap(h, 4 * qt - kt)),
                                     reads=[Y.b, bias_tab.b], writes=[P.b])

                            def st2(P=P, kt=kt, i=i, n=n, c=c, h=h, qt=qt):
                                mm(NUM[c], NUM[c].ap, VT, VT.ap[:, kt, :], P, P.ap, i == 0, i == n - 1)
                                mm(DEN[c], DEN[c].ap, ones, ones.ap, P, P.ap, i == 0, i == n - 1)
                                if i == n - 1:
                                    S.op("dve", lambda e: e.reciprocal(r_t[c].ap, DEN[c].ap), reads=[DEN[c].b], writes=[r_t[c].b])
                                    S.op("dve", lambda e: e.tensor_tensor(a_t[c].ap, NUM[c].ap, r_t[c].ap, ALU.mult), reads=[NUM[c].b, r_t[c].b], writes=[a_t[c].b])
                                    if c == 1:
                                        S.op("dve", lambda e: e.scalar_tensor_tensor(o_t.ap, a_t[1].ap, neglam.ap, a_t[0].ap, ALU.mult, ALU.add),
                                             reads=[a_t[0].b, a_t[1].b, neglam.b], writes=[o_t.b])
                                        S.op("act", lambda e: e.activation(sq_t.ap, o_t.ap, AF.Square), reads=[o_t.b], writes=[sq_t.b])
                                        mm(MS, MS.ap, onesdiv, onesdiv.ap, sq_t, sq_t.ap, True, True)
                                        S.op("act", lambda e: e.activation(sq_t.ap, MS.ap, AF.Sqrt, bias=eps_t.ap), reads=[MS.b, eps_t.b], writes=[sq_t.b])
                                        S.op("dve", lambda e: e.reciprocal(sq_t.ap, sq_t.ap), reads=[sq_t.b], writes=[sq_t.b])
                                        ev = oev.next()
                                        S.op("dve", lambda e: e.scalar_tensor_tensor(ev.ap, o_t.ap, sg.ap, sq_t.ap, ALU.mult, ALU.mult),
                                             reads=[o_t.b, sg.b, sq_t.b], writes=[ev.b])
                                        store(ot_p[256 + h * 128:256 + (h + 1) * 128, qt * 512:(qt + 1) * 512], ev)

                            pipe.job([(0, st0), (2, st2)])
                pipe.flush()

        attention_l0()
        all_gather_ot()
        phase_reset()

        def phase_mlp(x_src, wout_src, w1_src, w2_src, gain_idx, dst, final_gain=None):
            stage = Ring([AR.alloc(128, (2048,), F32, f"wstg{i}") for i in range(2)])
            Wo = AR.alloc(128, (8, D), BF16, "Wo")
            W1 = AR.alloc(128, (8, 4096), BF16, "W1")
            gsl = T(gains.ap[:, gain_idx, :], gains.b)
            load_weight(Wo, wout_src, D, D, stage)
            load_weight(W1, w1_src, D, 4096, stage, gain=gsl)
            OTs = Ring([AR.alloc(128, (8, 512), BF16, f"OT{i}") for i in range(2)])
            OAs = Ring([AR.alloc(128, (8, 512), BF16, f"OA{i}") for i in range(1)])
            OBs = Ring([AR.alloc(128, (8, 512), BF16, f"OB{i}") for i in range(1)])
            x1r = Ring([AR.alloc(128, (D,), F32, f"x1r{i}") for i in range(3)])
            xts = Ring([AR.alloc(128, (D,), F32, f"xt{i}") for i in range(2)])
            hns = Ring([AR.alloc(128, (D,), BF16, f"hn{i}") for i in range(2)])
            junk = AR.alloc(128, (D,), BF16, "junk")
            sss = Ring([AR.alloc(128, (1,), F32, f"ss{i}") for i in range(2)])
            rss = Ring([AR.alloc(128, (1,), F32, f"rs{i}") for i in range(2)])
            hTs = Ring([AR.alloc(128, (8, 512), BF16, f"hT{i}") for i in range(2)])
            uts = Ring([AR.alloc(128, (512,), BF16, f"ut{i}") for i in range(4)])
            sqs = Ring([AR.alloc(128, (512,), BF16, f"sq{i}") for i in range(2)])
            tps = Ring([PS(i, f"tp{i}", cols=1024, dt=BF16) for i in (0, 1)])
            pps = Ring([PS(i, f"pp{i}") for i in (2, 3, 4, 5, 6, 7)])
            for tb in range(NQT // 2):
                OT = OTs.next()
                OA = OAs.next()
                OB = OBs.next()
                for kc in range(8):
                    dma("sp", OA.ap[:, kc, :], ot4[kc % 4, (kc // 4) * 128:(kc // 4 + 1) * 128, tb * 512:(tb + 1) * 512], [], [OA.b], "L" + OA.b.name)
                    dma("sp", OB.ap[:, kc, :], ot4[kc % 4, (kc // 4) * 128:(kc // 4 + 1) * 128, SEQ // 2 + tb * 512:SEQ // 2 + (tb + 1) * 512], [], [OB.b], "L" + OB.b.name)
                S.op("dve", lambda e, OA=OA: e.tensor_scalar(OA.ap, OA.ap, msel.ap[:, 0:1], None, ALU.mult), reads=[OA.b, msel.b], writes=[OA.b])
                S.op("dve", lambda e, OA=OA, OB=OB, OT=OT: e.scalar_tensor_tensor(OT.ap, OB.ap, msel.ap[:, 1:2], OA.ap, ALU.mult, ALU.add),
                     reads=[OA.b, OB.b, msel.b], writes=[OT.b])
                hT = hTs.next()
                for r in range(4):
                    row0 = tb * 512 + r * 128
                    xt = xts.next()
                    load(xt, x_src[row0:row0 + 128, :])
                    x1v = x1r.next()
                    for half in range(2):
                        pp = pps.next()
                        for k in range(8):
                            mm(pp, pp.ap, OT, OT.ap[:, k, r * 128:(r + 1) * 128], Wo, Wo.ap[:, k, half * 512:(half + 1) * 512], k == 0, k == 7)
                        S.op("dve", lambda e, pp=pp, xt=xt, x1v=x1v, half=half: e.tensor_tensor(
                            x1v.ap[:, half * 512:(half + 1) * 512], pp.ap, xt.ap[:, half * 512:(half + 1) * 512], ALU.add),
                            reads=[pp.b, xt.b], writes=[x1v.b])
                    store(x1s[row0:row0 + 128, :], x1v)
                    norm_transpose(x1v, hns.next(), sss.next(), rss.next(), junk, tps.next(), hT, r)
                for fc in range(32):
                    pp = pps.next()
                    for k in range(8):
                        mm(pp, pp.ap, W1, W1.ap[:, k, fc * 128:(fc + 1) * 128], hT, hT.ap[:, k, :], k == 0, k == 7)
                    sq = sqs.next()
                    u = uts.next()
                    S.op("act", lambda e, pp=pp, sq=sq: e.activation(sq.ap, pp.ap, AF.Square), reads=[pp.b], writes=[sq.b])
                    S.op("dve", lambda e, pp=pp, sq=sq, u=u: e.scalar_tensor_tensor(u.ap, pp.ap, 0.0, sq.ap, ALU.is_gt, ALU.mult),
                         reads=[pp.b, sq.b], writes=[u.b])
                    store(uts_d[fc, :, tb * 512:(tb + 1) * 512], u)
            phase_reset()
            stage = Ring([AR.alloc(128, (2048,), F32, f"wstg{i}") for i in range(2)])
            W2 = AR.alloc(128, (32, D), BF16, "W2")
            load_weight(W2, w2_src, 4096, D, stage)
            fg = None
            if final_gain is not None:
                fg = AR.alloc(128, (D,), F32, "fg")
                load(fg, final_gain.partition_broadcast(128))
            UTs = Ring([AR.alloc(128, (32, 512), BF16, f"UT{i}") for i in range(2)])
            x1r = Ring([AR.alloc(128, (D,), F32, f"x1r{i}") for i in range(3)])
            ys = Ring([AR.alloc(128, (D,), F32, f"y{i}") for i in range(3)])
            junk = AR.alloc(128, (D,), BF16, "junk")
            sss = Ring([AR.alloc(128, (1,), F32, f"ss{i}") for i in range(2)])
            rss = Ring([AR.alloc(128, (1,), F32, f"rs{i}") for i in range(2)])
            pps = Ring([PS(i, f"pp{i}") for i in (0, 1, 2, 3, 4, 5)])
            for tb in range(NQT // 2):
                UT = UTs.next()
                for f4 in range(4):
                    dma("sp", UT.ap[:, f4 * 8:(f4 + 1) * 8, :], uts_d[f4 * 8:(f4 + 1) * 8, :, tb * 512:(tb + 1) * 512].rearrange("f p t -> p f t"),
                        [], [UT.b], "L" + UT.b.name)
                for r in range(4):
                    row0 = tb * 512 + r * 128
                    x1v = x1r.next()
                    load(x1v, x1s[row0:row0 + 128, :])
                    y = ys.next()
                    for half in range(2):
                        pp = pps.next()
                        for fc in range(32):
                            mm(pp, pp.ap, UT, UT.ap[:, fc, r * 128:(r + 1) * 128], W2, W2.ap[:, fc, half * 512:(half + 1) * 512], fc == 0, fc == 31)
                        S.op("dve", lambda e, pp=pp, y=y, x1v=x1v, half=half: e.tensor_tensor(
                            y.ap[:, half * 512:(half + 1) * 512], pp.ap, x1v.ap[:, half * 512:(half + 1) * 512], ALU.add),
                            reads=[pp.b, x1v.b], writes=[y.b])
                    if fg is not None:
                        ss = sss.next()
                        rs = rss.next()
                        S.op("act", lambda e, y=y, ss=ss: e.activation(junk.ap, y.ap, AF.Square, accum_out=ss.ap), reads=[y.b], writes=[junk.b, ss.b])
                        S.op("act", lambda e, ss=ss, rs=rs: e.activation(rs.ap, ss.ap, AF.Sqrt, bias=eps_t.ap, scale=1.0 / D), reads=[ss.b, eps_t.b], writes=[rs.b])
                        S.op("dve", lambda e, rs=rs: e.reciprocal(rs.ap, rs.ap), reads=[rs.b], writes=[rs.b])
                        S.op("dve", lambda e, y=y, rs=rs: e.scalar_tensor_tensor(y.ap, y.ap, rs.ap, fg.ap, ALU.mult, ALU.mult),
                             reads=[y.b, rs.b, fg.b], writes=[y.b])
                    store(dst[row0:row0 + 128, :], y)

        phase_mlp(xh_in, ev_w_out, mlp_w1[0], mlp_w2[0], 1, x2)
        all_gather_x2()
        if False:
            S.barrier()
            cp = Ring([AR.alloc(128, (D,), F32, f"cp{i}") for i in range(2)])
            for i in range(NKT):
                c_ = cp.next()
                load(c_, x2[i * 128:(i + 1) * 128, :])
                store(out_d[i * 128:(i + 1) * 128, :], c_)
            S.barrier()
            S.finalize()
            return nc, S
        phase_reset()

        fm = []
        for jc in range(4):
            dsts = [(lambda tb, slot=2 * jc + half: q1[slot, :, tb * 512:(tb + 1) * 512], 64 * half, 64) for half in range(2)]
            fm.append((128 * jc, 128, dsts, 0.125, None))
        for ji in range(4):
            dsts = [(lambda tb, slot=2 * ji + half: kf1[slot, :, tb * 512:(tb + 1) * 512], 64 * half, 64) for half in range(2)]
            fm.append((512 + 128 * ji, 128, dsts, 1.0, None))
        fm.append((1024, 24, [(lambda tb: gts[:, tb * 512:(tb + 1) * 512], 0, 24)], 1.0, AF.Sigmoid))
        tm = [(1048, 256, lambda row0: v1[row0:row0 + 128, 0:256])]
        phase_proj(lambda row0: x2g[(row0 % 4096) // 512, (row0 // 4096) * 512 + row0 % 512:(row0 // 4096) * 512 + row0 % 512 + 128, :], od_w_in, 1304, 2, fm, tm)
        phase_reset()

        def phase_compress():
            stg = AR.alloc(64, (32 * 256,), F32, "cstg")
            W1c = AR.alloc(64, (32, 256), BF16, "W1c")
            stg2 = AR.alloc(128, (2, 64), F32, "cstg2")
            W2c = AR.alloc(128, (2, 64), BF16, "W2c")
            posf = AR.alloc(64, (32,), F32, "posf")
            posT = AR.alloc(64, (32,), BF16, "posT")
            biash = AR.alloc(128, (2,), F32, "biash")
            src = AR.alloc(64, (SEQ,), BF16, "csrc")
            hid = AR.alloc(128, (2, 512), BF16, "hid")
            u_t = AR.alloc(128, (512,), F32, "u_t")
            w_t = AR.alloc(128, (512,), F32, "w_t")
            evk = AR.alloc(64, (512,), BF16, "evk")
            evv = Ring([AR.alloc(128, (64,), BF16, f"evv{i}") for i in range(2)])
            pb_ = PS(0, "cbias")
            ph = Ring([PS(1, "ph0"), PS(2, "ph1")])
            po = Ring([PS(3, "po0"), PS(4, "po1")])
            S.op("dve", lambda e: e.memset(hid.ap, 0.0), writes=[hid.b])
            S.op("dve", lambda e: e.memset(evk.ap, 0.0), writes=[evk.b])
            for kind, (w1d, w2d, posd) in enumerate(((cw1k, cw2k, pos_k), (cw1v, cw2v, pos_v))):
                dma("sp", stg.ap.rearrange("p (l h) -> p l h", l=32), w1d.rearrange("(l d) h -> d l h", d=64), [], [stg.b], "Lcstg")
                for q4 in range(4):
                    cast(W1c, W1c.ap[:, q4 * 8:(q4 + 1) * 8, :], stg, stg.ap.rearrange("p (l h) -> p l h", l=32)[:, q4 * 8:(q4 + 1) * 8, :])
                dma("sp", stg2.ap, w2d.rearrange("(c p) n -> p c n", p=128), [], [stg2.b], "Lcstg2")
                cast(W2c, W2c.ap, stg2, stg2.ap)
                dma("sp", posf.ap, posd.rearrange("l d -> d l"), [], [posf.b], "Lposf", slow=True)
                cast(posT, posT.ap, posf, posf.ap)
                for hc in range(2):
                    for l in range(32):
                        mm(pb_, pb_.ap[:, 0:1], W1c, W1c.ap[:, l, hc * 128:(hc + 1) * 128], posT, posT.ap[:, l:l + 1], l == 0, l == 31)
                    S.op("dve", lambda e, hc=hc: e.tensor_copy(biash.ap[:, hc:hc + 1], pb_.ap[:, 0:1]), reads=[pb_.b], writes=[biash.b])
                for g in range(2):
                    load(src, kf1[2 * kind + g])
                    for hc in range(2):
                        p_ = ph.next()
                        for l in range(32):
                            mm(p_, p_.ap[:, 0:511], W1c, W1c.ap[:, l, hc * 128:(hc + 1) * 128], src, src.ap[:, l:l + 16 * 510 + 1:16], l == 0, l == 31)
                        S.op("act", lambda e, p_=p_, hc=hc: e.activation(u_t.ap[:, 0:511], p_.ap[:, 0:511], AF.Identity, bias=biash.ap[:, hc:hc + 1]),
                             reads=[p_.b, biash.b], writes=[u_t.b])
                        S.op("act", lambda e: e.activation(w_t.ap[:, 0:511], u_t.ap[:, 0:511], AF.Square), reads=[u_t.b], writes=[w_t.b])
                        S.op("dve", lambda e: e.tensor_scalar(w_t.ap[:, 0:511], w_t.ap[:, 0:511], 0.044715, 1.0, ALU.mult, ALU.add), reads=[w_t.b], writes=[w_t.b])
                        S.op("dve", lambda e: e.tensor_tensor(w_t.ap[:, 0:511], w_t.ap[:, 0:511], u_t.ap[:, 0:511], ALU.mult), reads=[w_t.b, u_t.b], writes=[w_t.b])
                        S.op("act", lambda e: e.activation(w_t.ap[:, 0:511], w_t.ap[:, 0:511], AF.Sigmoid, scale=2.0 * 0.7978845608028654), reads=[w_t.b], writes=[w_t.b])
                        S.op("dve", lambda e, hc=hc: e.tensor_tensor(hid.ap[:, hc, 0:511], w_t.ap[:, 0:511], u_t.ap[:, 0:511], ALU.mult), reads=[w_t.b, u_t.b], writes=[hid.b])
                    if kind == 0:
                        p2 = po.next()
                        for hc in range(2):
                            mm(p2, p2.ap[0:64, 0:511], W2c, W2c.ap[:, hc, :], hid, hid.ap[:, hc, 0:511], hc == 0, hc == 1)
                        S.op("dve", lambda e, p2=p2: e.tensor_copy(evk.ap[:, 0:511], p2.ap[0:64, 0:511]), reads=[p2.b], writes=[evk.b])
                        store(kcmp[g], evk)
                    else:
                        for nchunk in range(4):
                            p2 = po.next()
                            for hc in range(2):
                                mm(p2, p2.ap[:, 0:64], hid, hid.ap[:, hc, nchunk * 128:(nchunk + 1) * 128], W2c, W2c.ap[:, hc, :], hc == 0, hc == 1)
                            ev = evv.next()
                            S.op("dve", lambda e, p2=p2, ev=ev: e.tensor_copy(ev.ap, p2.ap[:, 0:64]), reads=[p2.b], writes=[ev.b])
                            store(vcmp[g, nchunk * 128:(nchunk + 1) * 128, :], ev)

        phase_compress()
        phase_reset()

        def attention_l1():
            mask_c = AR.alloc(128, (4, 512), BF16, "mask_c")
            load(mask_c, cd["mask_c"])
            mask_cmp = AR.alloc(128, (5, 512), BF16, "mask_cmp")
            load(mask_cmp, cd["mask_cmp"])
            mask_win = AR.alloc(128, (8, 512), BF16, "mask_win")
            load(mask_win, cd["mask_win"])
            ewide = AR.alloc(128, (SEQ,), BF16, "ewide")
            load(ewide, cd["ewide"])
            ovl = AR.alloc(128, (4, 128), BF16, "ovl")
            load(ovl, cd["ovl"])
            tka = AR.alloc(128, (256,), F32, "tka")
            load(tka, cd["topk_a"])
            tkm = AR.alloc(128, (256,), F32, "tkm")
            load(tkm, cd["topk_m"])
            bcmp = AR.alloc(128, (8 * NQT * 4,), F32, "bcmp")
            load(bcmp, cd["bias_cmp"])
            KcA = AR.alloc(70, (512,), BF16, "KcA")
            dma("sp", KcA.ap[64:70, :], cd["kaug_cmp"], [], [KcA.b], "LKcA")
            Vc = AR.alloc(128, (4, 64), BF16, "Vc")
            KsA = AR.alloc(70, (SEQ,), BF16, "KsA")
            KwA = AR.alloc(70, (SEQ,), BF16, "KwA")
            dma("sp", KsA.ap[64:70, :], cd["kaug"], [], [KsA.b], "LKsA")
            dma("sp", KwA.ap[64:70, :], cd["kaug"], [], [KwA.b], "LKwA")
            Vs = AR.alloc(128, (NKT, 128), BF16, "Vs")
            Vw = AR.alloc(128, (NKT, 128), BF16, "Vw")
            S.op("dve", lambda e: e.memset(Vs.ap[:, :, 64:128], 1.0), writes=[Vs.b])
            S.op("dve", lambda e: e.memset(Vw.ap[:, :, 64:128], 1.0), writes=[Vw.b])
            QAs = [[AR.alloc(70, (512,), BF16, f"QA{r}{i}") for i in range(2)] for r in range(4)]
            GBs = [[AR.alloc(64, (3, 512), BF16, f"GB{r}{i}") for i in range(2)] for r in range(4)]
            Pc = Ring([AR.alloc(128, (512,), BF16, f"Pc{i}") for i in range(10)])
            Ps = Ring([AR.alloc(128, (512,), BF16, f"Pp{i}") for i in range(4)])
            pcn = Ring([AR.alloc(128, (512,), BF16, f"pcn{i}") for i in range(4)])
            rdc = AR.alloc(128, (512,), F32, "rdc")
            ocmp = [AR.alloc(64, (512,), F32, f"ocmp{r}") for r in range(4)]
            selbT = AR.alloc(128, (512,), BF16, "selbT")
            imp2 = AR.alloc(128, (128,), F32, "imp2")
            tmp2 = AR.alloc(128, (128,), F32, "tmp2")
            selm = AR.alloc(128, (128,), F32, "selm")
            selb = AR.alloc(128, (128,), BF16, "selb")
            v8a = AR.alloc(128, (8,), F32, "v8a")
            v8b = AR.alloc(128, (8,), F32, "v8b")
            rs2 = [AR.alloc(128, (512,), F32, f"rs2{i}") for i in range(2)]
            ob2 = [AR.alloc(64, (512,), F32, f"ob2{i}") for i in range(2)]
            acc = AR.alloc(64, (512,), F32, "acc")
            t1 = AR.alloc(64, (512,), F32, "t1")
            oev = Ring([AR.alloc(64, (512,), BF16, f"oev{i}") for i in range(2)])
            Ys = Ring([PS(0, "Y0"), PS(1, "Y1"), PS(2, "Y2")])
            NUMs = [PS(3, "NUMa"), PS(5, "NUMb")]
            DENs = [PS(4, "DENa"), PS(6, "DENb")]
            IMP = PS(7, "IMP")
            TRP = T(pbanks[6][:, :].bitcast(BF16)[:, 0:512], DENs[1].b)

            def tile_job(pipe, K, kap, QAr, extra, bias, V, vap, NUM, DEN, den_parts, P, first, last, tail):
                Y = Ys.next()

                def st0():
                    nx = len(extra)
                    mm(Y, Y.ap, K, kap, QAr, QAr.ap, True, nx == 0)
                    for xi, (lt, lap, rt, rap) in enumerate(extra):
                        mm(Y, Y.ap, lt, lap, rt, rap, False, xi == nx - 1)
                    S.op("act", lambda e: e.activation(P.ap, Y.ap, AF.Exp, bias=bias[0]), reads=[Y.b, bias[1]], writes=[P.b])

                def st2():
                    if DEN is None:
                        mm(NUM, NUM.ap, V, vap, P, P.ap, first, last)
                    else:
                        mm(NUM, NUM.ap[0:64, :], V, vap, P, P.ap, first, last)
                        mm(DEN, DEN.ap[0:den_parts, :], ones, ones.ap[:, 0:den_parts], P, P.ap, first, last)
                    if last and tail is not None:
                        tail()

                pipe.job([(0, st0), (2, st2)])

            for g in range(2):
                dma("sp", KcA.ap[0:64, :], kcmp[g], [], [KcA.b], "LKcA")
                dma("sp", Vc.ap, vcmp[g].rearrange("(c p) d -> p c d", p=128), [], [Vc.b], "LVc")
                dma("sp", KsA.ap[0:64, :], kf1[4 + g], [], [KsA.b], "LKsA")
                dma("sp", KwA.ap[0:64, :], kf1[6 + g], [], [KwA.b], "LKwA")
                for g4 in range(4):
                    dma("sp", Vs.ap[:, g4 * 16:(g4 + 1) * 16, 0:64], v1[g4 * 2048:(g4 + 1) * 2048, g * 64:(g + 1) * 64].rearrange("(t p) c -> p t c", p=128), [], [Vs.b], "LVs")
                    dma("sp", Vw.ap[:, g4 * 16:(g4 + 1) * 16, 0:64], v1[g4 * 2048:(g4 + 1) * 2048, 128 + g * 64:128 + (g + 1) * 64].rearrange("(t p) c -> p t c", p=128), [], [Vw.b], "LVw")
                for qt in range(NQT):
                    pipe = Pipe()
                    QA = [QAs[r][qt % 2] for r in range(4)]
                    GB = [GBs[r][qt % 2] for r in range(4)]
                    for r in range(4):
                        h = 4 * g + r
                        dma("sp", QA[r].ap[0:64, :], q1[h, :, qt * 512:(qt + 1) * 512], [], [QA[r].b], "L" + QA[r].b.name)
                        dma("sp", QA[r].ap[64:70, :], cd["qaug"][:, 2 + h, :], [], [QA[r].b], "L" + QA[r].b.name)
                        for c3 in range(3):
                            dma("sp", GB[r].ap[:, c3, :], gts[3 * h + c3, qt * 512:(qt + 1) * 512].partition_broadcast(64), [], [GB[r].b], "L" + GB[r].b.name)
                    chunks = [c for c in range(4) if 4 * c <= qt]
                    for r in range(4):
                        h = 4 * g + r
                        NUM = NUMs[r % 2]
                        DEN = DENs[r % 2]
                        Pl = [(c, Pc.next()) for c in chunks]

                        def cmp_tail(r=r, NUM=NUM, DEN=DEN, Pl=Pl):
                            S.op("dve", lambda e: e.tensor_scalar(rdc.ap, DEN.ap, 1e-30, None, ALU.add), reads=[DEN.b], writes=[rdc.b])
                            S.op("dve", lambda e: e.reciprocal(rdc.ap, rdc.ap), reads=[rdc.b], writes=[rdc.b])
                            S.op("dve", lambda e: e.tensor_tensor(ocmp[r].ap, NUM.ap[0:64, :], rdc.ap[0:64, :], ALU.mult), reads=[NUM.b, rdc.b], writes=[ocmp[r].b])
                            for ci, (c, P) in enumerate(Pl):
                                pn = pcn.next()
                                S.op("dve", lambda e, pn=pn, P=P: e.tensor_tensor(pn.ap, P.ap, rdc.ap, ALU.mult), reads=[P.b, rdc.b], writes=[pn.b])
                                for qs in range(4):
                                    mm(IMP, IMP.ap[:, qs * 128:(qs + 1) * 128], pn, pn.ap[:, qs * 128:(qs + 1) * 128], ovl, ovl.ap[:, c, :],
                                       r == 0 and ci == 0, r == 3 and ci == len(Pl) - 1)

                        for ci, (c, P) in enumerate(Pl):
                            rel = qt - 4 * c
                            extra = [(ident, ident.ap, mask_cmp, mask_cmp.ap[:, rel, :])] if rel <= 4 else []
                            col = (h * NQT + qt) * 4 + c
                            tile_job(pipe, KcA, KcA.ap[:, c * 128:(c + 1) * 128], QA[r], extra, (bcmp.ap[:, col:col + 1], bcmp.b),
                                     Vc, Vc.ap[:, c, :], NUM, DEN, 128, P, ci == 0, ci == len(Pl) - 1, cmp_tail)
                    pipe.flush()
                    for qs in range(4):
                        off = 127 - 2 * (4 * qt + qs)
                        S.op("dve", lambda e, qs=qs, off=off: e.tensor_tensor(tmp2.ap, IMP.ap[:, qs * 128:(qs + 1) * 128], tkm.ap[:, off:off + 128], ALU.mult),
                             reads=[IMP.b, tkm.b], writes=[tmp2.b])
                        S.op("dve", lambda e, off=off: e.tensor_tensor(imp2.ap, tmp2.ap, tka.ap[:, off:off + 128], ALU.add), reads=[tmp2.b, tka.b], writes=[imp2.b])
                        S.op("dve", lambda e: e.memset(imp2.ap[:, 0:1], 1.0e6), reads=[], writes=[imp2.b])
                        S.op("dve", lambda e: e.max(v8a.ap, imp2.ap), reads=[imp2.b], writes=[v8a.b])
                        S.op("dve", lambda e: e.match_replace(tmp2.ap, v8a.ap, imp2.ap, -9.0), reads=[imp2.b, v8a.b], writes=[tmp2.b])
                        S.op("dve", lambda e: e.max(v8b.ap, tmp2.ap), reads=[tmp2.b], writes=[v8b.b])
                        S.op("dve", lambda e: e.tensor_scalar(selm.ap, imp2.ap, v8b.ap[:, 7:8], 0.0, ALU.is_ge, ALU.add), reads=[imp2.b, v8b.b], writes=[selm.b])
                        S.op("dve", lambda e: e.scalar_tensor_tensor(selm.ap, imp2.ap, 0.0, selm.ap, ALU.is_ge, ALU.mult), reads=[imp2.b, selm.b], writes=[selm.b])
                        S.op("dve", lambda e: e.tensor_scalar(selb.ap, selm.ap, -1.0, -NEG, ALU.add, ALU.mult), reads=[selm.b], writes=[selb.b])
                        S.op("pe", lambda e, qs=qs: e.transpose(TRP.ap[:, qs * 128:(qs + 1) * 128], selb.ap, ident.ap), reads=[selb.b, ident.b], writes=[TRP.b])
                    S.op("dve", lambda e: e.tensor_copy(selbT.ap, TRP.ap[:, 0:512]), reads=[TRP.b], writes=[selbT.b])
                    for r in range(4):
                        h = 4 * g + r
                        si = 2 + h
                        back = alibi_back(NSA_SLOPES[4 + r]) if g == 0 else None

                        def sel_tail(r=r, GBr=GB[r]):
                            S.op("dve", lambda e: e.reciprocal(rs2[0].ap[64:128, :], NUMs[0].ap[64:128, :]), reads=[NUMs[0].b], writes=[rs2[0].b])
                            S.op("dve", lambda e: e.tensor_tensor(ob2[0].ap, NUMs[0].ap[0:64, :], rs2[0].ap[64:128, :], ALU.mult), reads=[NUMs[0].b, rs2[0].b], writes=[ob2[0].b])
                            S.op("pool", lambda e: e.tensor_tensor(acc.ap, GBr.ap[:, 1, :], ob2[0].ap, ALU.mult), reads=[GBr.b, ob2[0].b], writes=[acc.b])
                            S.op("pool", lambda e: e.tensor_tensor(t1.ap, GBr.ap[:, 0, :], ocmp[r].ap, ALU.mult), reads=[GBr.b, ocmp[r].b], writes=[t1.b])
                            S.op("pool", lambda e: e.tensor_tensor(acc.ap, acc.ap, t1.ap, ALU.add), reads=[acc.b, t1.b], writes=[acc.b])

                        def win_tail(r=r, h=h, qt=qt, GBr=GB[r]):
                            S.op("dve", lambda e: e.reciprocal(rs2[1].ap[64:128, :], NUMs[1].ap[64:128, :]), reads=[NUMs[1].b], writes=[rs2[1].b])
                            S.op("dve", lambda e: e.tensor_tensor(ob2[1].ap, NUMs[1].ap[0:64, :], rs2[1].ap[64:128, :], ALU.mult), reads=[NUMs[1].b, rs2[1].b], writes=[ob2[1].b])
                            S.op("pool", lambda e: e.tensor_tensor(t1.ap, GBr.ap[:, 2, :], ob2[1].ap, ALU.mult), reads=[GBr.b, ob2[1].b], writes=[t1.b])
                            ev = oev.next()
                            S.op("pool", lambda e: e.tensor_tensor(ev.ap, acc.ap, t1.ap, ALU.add), reads=[acc.b, t1.b], writes=[ev.b])
                            store(ot_p[h * 64:(h + 1) * 64, qt * 512:(qt + 1) * 512], ev)

                        kts = ktiles_for(qt, back)
                        for i, kt in enumerate(kts):
                            o = kt - 4 * qt
                            extra = [(ewide, ewide.ap[:, kt * 128:(kt + 1) * 128], selbT, selbT.ap)]
                            if o >= 0:
                                extra.append((ident, ident.ap, mask_c, mask_c.ap[:, o, :]))
                            tile_job(pipe, KsA, KsA.ap[:, kt * 128:(kt + 1) * 128], QA[r], extra, (bias_ap(si, 4 * qt - kt), bias_tab.b),
                                     Vs, Vs.ap[:, kt, :], NUMs[0], None, 64, Ps.next(), i == 0, i == len(kts) - 1, sel_tail)
                        kts = [kt for kt in range(4 * qt + 3, 4 * qt - 5, -1) if kt >= 0]
                        for i, kt in enumerate(kts):
                            o = kt - 4 * qt
                            extra = [(ident, ident.ap, mask_win, mask_win.ap[:, o + 4, :])]
                            tile_job(pipe, KwA, KwA.ap[:, kt * 128:(kt + 1) * 128], QA[r], extra, (bias_ap(si, 4 * qt - kt), bias_tab.b),
                                     Vw, Vw.ap[:, kt, :], NUMs[1], None, 64, Ps.next(), i == 0, i == len(kts) - 1, win_tail)
                    pipe.flush()

        attention_l1()
        all_gather_ot()
        if STOP_AFTER == "L1ATT":
            S.barrier()
            cpb = Ring([AR.alloc(128, (1024,), BF16, f"cpb{i}") for i in range(2)])
            cpf = Ring([AR.alloc(128, (1024,), F32, f"cpf{i}") for i in range(2)])
            ov = out_d.rearrange("(a b) c -> a (b c)", a=1024)
            for k in range(8):
                for cb in range(8):
                    b_ = cpb.next()
                    f_ = cpf.next()
                    load(b_, ot[k * 128:(k + 1) * 128, cb * 1024:(cb + 1) * 1024])
                    S.op("dve", lambda e, b_=b_, f_=f_: e.tensor_copy(f_.ap, b_.ap), reads=[b_.b], writes=[f_.b])
                    store(ov[k * 128:(k + 1) * 128, cb * 1024:(cb + 1) * 1024], f_)
            S.barrier()
            S.finalize()
            return nc, S
        phase_reset()
        phase_mlp(x2, od_w_out, mlp_w1[1], mlp_w2[1], 3, out_d, final_gain=final_norm)
        S.barrier()
        S.finalize()
    return nc, S


def _cols(*ranges):
    return np.concatenate([np.arange(a, b) for a, b in ranges])


def kernel(**inputs):
    consts = [make_consts(0), make_consts(1)]
    nc, S = build_program(consts[0])
    x = np.ascontiguousarray(inputs["x"], dtype=np.float32)
    B = x.shape[0]
    shared = {}
    for k in ("attn_norm", "mlp_norm", "final_norm", "mlp_w1", "mlp_w2"):
        shared[k] = np.ascontiguousarray(inputs[k], dtype=np.float32)
    for k in ("ev_subln", "od_cmp_pos_k", "od_cmp_k_w1", "od_cmp_k_w2", "od_cmp_pos_v", "od_cmp_v_w1", "od_cmp_v_w2"):
        shared[k] = np.ascontiguousarray(inputs[k][0], dtype=np.float32)
    shared["lamv"] = np.ascontiguousarray(np.stack([inputs["ev_lam_q1"][0], inputs["ev_lam_k1"][0],
                                                    inputs["ev_lam_q2"][0], inputs["ev_lam_k2"][0]]), dtype=np.float32)
    w0 = np.asarray(inputs["ev_w_in"][0], dtype=np.float32)
    w1 = np.asarray(inputs["od_w_in"][0], dtype=np.float32)
    per = []
    feat0 = []
    feat1 = []
    for p in range(2):
        sb = (256 * p, 256 * p + 256)
        dA, dB = DIFF_ASSIGN[p]
        c0 = _cols((0 + sb[0], 0 + sb[1]), (512 + sb[0], 512 + sb[1]),
                   (1536 + 128 * dA, 1536 + 128 * dA + 128), (1536 + 128 * dB, 1536 + 128 * dB + 128),
                   (2048 + 128 * dA, 2048 + 128 * dA + 128), (2048 + 128 * dB, 2048 + 128 * dB + 128),
                   (1024 + sb[0], 1024 + sb[1]),
                   (2560 + 128 * dA, 2560 + 128 * dA + 128), (2560 + 128 * dB, 2560 + 128 * dB + 128))
        gA, gB = GROUP_ASSIGN[p]
        kv = lambda base: [(base + 64 * gA, base + 64 * gA + 64), (base + 64 * gB, base + 64 * gB + 64)]
        c1 = _cols((256 * gA, 256 * gA + 256), (256 * gB, 256 * gB + 256),
                   *kv(1024), *kv(1280), *kv(1536), *kv(2048),
                   (2560 + 12 * gA, 2560 + 12 * gA + 12), (2560 + 12 * gB, 2560 + 12 * gB + 12),
                   *kv(1792), *kv(2304))
        per.append({"ev_w_in": np.ascontiguousarray(w0[:, c0]), "od_w_in": np.ascontiguousarray(w1[:, c1])})
        feat0.append(_cols(sb, (512 + 128 * dA, 512 + 128 * dA + 128), (512 + 128 * dB, 512 + 128 * dB + 128)))
        feat1.append(_cols((256 * gA, 256 * gA + 256), (256 * gB, 256 * gB + 256)))
    perm0 = np.concatenate(feat0)
    perm1 = np.concatenate(feat1)
    shared["ev_w_out"] = np.ascontiguousarray(np.asarray(inputs["ev_w_out"][0], dtype=np.float32)[perm0])
    shared["od_w_out"] = np.ascontiguousarray(np.asarray(inputs["od_w_out"][0], dtype=np.float32)[perm1])
    in_maps = []
    for core in range(8):
        p = core % 2
        m = dict(shared)
        m.update(per[p])
        for k, v in consts[p].items():
            m["c_" + k] = v
        xb = x[(core // 2) % B]
        m["x"] = xb
        m["xh"] = np.ascontiguousarray(xb[p * (SEQ // 2):(p + 1) * (SEQ // 2)])
        ms = np.zeros((128, 2), np.float32)
        ms[:, p] = 1.0
        m["msel"] = ms
        in_maps.append(m)
    res = run_bass_kernel_spmd(nc, in_maps, core_ids=list(range(8)))
    out = np.stack([np.concatenate([np.asarray(res.results[2 * b + p]["out"], dtype=np.float32) for p in range(2)], axis=0)
                    for b in range(B)])
    return out
```
